# Optimizing a Trainium2 kernel written in Bass

```python
import jax, jax.numpy as jnp
from jax import lax
import numpy as np

D_MODEL = 2048
BATCH = 2
SEQ = 8192
DEPTH = 4

GRID_W = 64
CTX_LEN = 256
N_MIXERS = 3
N_A_LAYERS = (DEPTH + 2) // 3
N_B_LAYERS = (DEPTH + 1) // 3
N_C_LAYERS = DEPTH // 3
RMS_EPS = 1e-6

A_DK = 128
A_HEADS = D_MODEL // A_DK
A_DV = D_MODEL // A_HEADS
A_QK = A_HEADS * A_DK
A_WIDTH = A_HEADS * A_DV
A_IN_COLS = 3 * A_QK + 2 * A_WIDTH
A_CHUNK = 64

B_HEAD_DIM = 64
B_Q_HEADS = D_MODEL // B_HEAD_DIM
B_KV_HEADS = 4
B_WIDTH = B_Q_HEADS * B_HEAD_DIM
B_KV_WIDTH = B_KV_HEADS * B_HEAD_DIM
B_IN_COLS = 2 * B_WIDTH + 2 * B_KV_WIDTH
B_WINDOW = 128
B_BLOCK = 128
ROPE_BASE = 10000.0

C_HEAD = 64
C_HEADS = D_MODEL // C_HEAD
C_WIDTH = C_HEADS * C_HEAD
C_DECAY_LORA = 96
C_ICLR_LORA = 96
C_GN_EPS = 64e-5

kernel_name = "hybrid_hgrn2_swa_rwkv7_prefix_dit"


def _rmsnorm(x, w):
    xf = x.astype(jnp.float32)
    y = xf * lax.rsqrt(jnp.mean(xf * xf, axis=-1, keepdims=True) + RMS_EPS)
    return (y * w.astype(jnp.float32)).astype(x.dtype)


def _groupnorm(y, w, b):
    yf = y.astype(jnp.float32)
    mu = jnp.mean(yf, axis=-1, keepdims=True)
    var = jnp.mean(jnp.square(yf - mu), axis=-1, keepdims=True)
    z = ((yf - mu) * lax.rsqrt(var + C_GN_EPS)).reshape(y.shape[0], y.shape[1], -1)
    return (z * w.astype(jnp.float32) + b.astype(jnp.float32)).astype(y.dtype)


def _split_heads(t, n_heads, head_dim):
    return t.reshape(t.shape[0], t.shape[1], n_heads, head_dim)


def _context_then_latent(scan_fn, ctx_in, lat_in, s0, reverse):
    flip = (lambda t: jnp.flip(t, axis=1)) if reverse else (lambda t: t)
    y_ctx, s_ctx = scan_fn(*[flip(t) for t in ctx_in], s0)
    y_lat, _ = scan_fn(*[flip(t) for t in lat_in], s_ctx)
    return flip(y_ctx), flip(y_lat)


def _gla_chunk_scan(q, k, v, g, s0):
    B, T, H, _ = q.shape
    DV = v.shape[-1]
    n = T // A_CHUNK

    def chunks(t):
        return t.astype(jnp.float32).reshape(B, n, A_CHUNK, H, t.shape[-1]).transpose(1, 0, 3, 2, 4)

    incl = jnp.tril(jnp.ones((A_CHUNK, A_CHUNK), bool))[:, :, None]

    def step(S, inp):
        qc, kc, vc, gc = inp
        bcum = jnp.cumsum(gc, axis=2)
        pair = jnp.exp(jnp.where(incl, bcum[:, :, :, None, :] - bcum[:, :, None, :, :], -jnp.inf))
        att = jnp.einsum('bhtk,bhtsk,bhsk->bhts', qc, pair, kc)
        o = jnp.einsum('bhts,bhsv->bhtv', att, vc) + jnp.einsum('bhtk,bhkv->bhtv', qc * jnp.exp(bcum), S)
        b_end = bcum[:, :, -1:, :]
        S = jnp.exp(b_end[:, :, 0, :])[..., None] * S + jnp.einsum('bhsk,bhsv->bhkv', kc * jnp.exp(b_end - bcum), vc)
        return S, o

    S, o = lax.scan(step, s0, (chunks(q), chunks(k), chunks(v), chunks(g)))
    o = o.transpose(1, 0, 3, 2, 4).reshape(B, T, H, DV)
    return o.astype(v.dtype), S


def _hgrn2_project(h, w_in, lb):
    q, i, zf, zb, gate = jnp.split(h @ w_in, [A_QK, A_QK + A_WIDTH, 2 * A_QK + A_WIDTH, 3 * A_QK + A_WIDTH], axis=-1)
    q = _split_heads(q, A_HEADS, A_DK) * (A_DK ** -0.5)
    v = _split_heads(i, A_HEADS, A_DV)
    lb_h = lb.astype(jnp.float32).reshape(A_HEADS, A_DK)

    def forget(z):
        z = _split_heads(z, A_HEADS, A_DK).astype(jnp.float32)
        logf = jnp.logaddexp(jnp.log(lb_h), jnp.log1p(-lb_h) + jax.nn.log_sigmoid(z))
        return -jnp.expm1(logf), logf

    return q, v, forget(zf), forget(zb), gate


def _hgrn2_mixer(h_lat, h_ctx, w_in, lb, onorm_w, w_out, need_ctx):
    ql, vl, (kfl, gfl), (kbl, gbl), gate_l = _hgrn2_project(h_lat, w_in, lb)
    qc, vc, (kfc, gfc), (kbc, gbc), gate_c = _hgrn2_project(h_ctx, w_in, lb)
    s0 = jnp.zeros((h_lat.shape[0], A_HEADS, A_DK, A_DV), jnp.float32)
    yc_f, yl_f = _context_then_latent(_gla_chunk_scan, (qc, kfc, vc, gfc), (ql, kfl, vl, gfl), s0, False)
    yc_b, yl_b = _context_then_latent(_gla_chunk_scan, (qc, kbc, vc, gbc), (ql, kbl, vl, gbl), s0, True)

    def readout(y, gate):
        y = _rmsnorm(y, onorm_w).reshape(y.shape[0], y.shape[1], A_WIDTH)
        return (y * jax.nn.silu(gate)) @ w_out

    y_lat = readout(yl_f + yl_b, gate_l)
    y_ctx = readout(yc_f + yc_b, gate_c) if need_ctx else None
    return y_lat, y_ctx


def _axial_rope(t, row, col):
    quarter = t.shape[-1] // 4
    inv = ROPE_BASE ** (-jnp.arange(quarter, dtype=jnp.float32) / quarter)

    def rot(u, pos):
        ang = pos[:, None] * inv[None, :]
        cos = jnp.cos(ang)[None, :, None, :]
        sin = jnp.sin(ang)[None, :, None, :]
        u1, u2 = u[..., :quarter], u[..., quarter:]
        return jnp.concatenate([u1 * cos - u2 * sin, u2 * cos + u1 * sin], axis=-1)

    tf = t.astype(jnp.float32)
    half = 2 * quarter
    return jnp.concatenate([rot(tf[..., :half], row), rot(tf[..., half:], col)], axis=-1).astype(t.dtype)


def _sink_softmax_attend(q, keys, values, masks, sink):
    bsz, nq, G, R, hd = q.shape
    scale = hd ** -0.5
    logits = []
    for kk, m in zip(keys, masks):
        s = jnp.einsum('bqgrd,bkgd->bgrqk', q, kk).astype(jnp.float32) * scale
        logits.append(s if m is None else jnp.where(m, s, -jnp.inf))
    logits.append(jnp.broadcast_to(sink.astype(jnp.float32)[None, :, :, None, None], (bsz, G, R, nq, 1)))
    p = jax.nn.softmax(jnp.concatenate(logits, axis=-1), axis=-1)
    out = None
    start = 0
    for vv in values:
        n = vv.shape[1]
        o = jnp.einsum('bgrqk,bkgd->bqgrd', p[..., start:start + n].astype(vv.dtype), vv)
        out = o if out is None else out + o
        start += n
    return out


def _swa_latent(q, k, v, k_ctx, v_ctx, sink):
    bsz, T, G, R, hd = q.shape
    nblk = T // B_BLOCK
    span = B_BLOCK + 2 * B_WINDOW
    kp = jnp.pad(k, ((0, 0), (B_WINDOW, B_WINDOW), (0, 0), (0, 0)))
    vp = jnp.pad(v, ((0, 0), (B_WINDOW, B_WINDOW), (0, 0), (0, 0)))
    qb = q.reshape(bsz, nblk, B_BLOCK, G, R, hd)
    q_off = jnp.arange(B_BLOCK)
    k_off = jnp.arange(span) - B_WINDOW
    band = jnp.abs(k_off[None, :] - q_off[:, None]) <= B_WINDOW

    def block(j):
        start = j * B_BLOCK
        kpos = start + k_off
        valid = band & ((kpos >= 0) & (kpos < T))[None, :]
        qj = lax.dynamic_index_in_dim(qb, j, axis=1, keepdims=False)
        kj = lax.dynamic_slice_in_dim(kp, start, span, axis=1)
        vj = lax.dynamic_slice_in_dim(vp, start, span, axis=1)
        return _sink_softmax_attend(qj, [kj, k_ctx], [vj, v_ctx], [valid, None], sink)

    out = lax.map(block, jnp.arange(nblk))
    return jnp.moveaxis(out, 0, 1).reshape(bsz, T, G * R * hd)


def _swa_project(h, w_in):
    q, k, v, gate = jnp.split(h @ w_in, [B_WIDTH, B_WIDTH + B_KV_WIDTH, B_WIDTH + 2 * B_KV_WIDTH], axis=-1)
    return (_split_heads(q, B_Q_HEADS, B_HEAD_DIM), _split_heads(k, B_KV_HEADS, B_HEAD_DIM),
            _split_heads(v, B_KV_HEADS, B_HEAD_DIM), gate)


def _swa_mixer(h_lat, h_ctx, w_in, sink, w_out, row, col, need_ctx):
    bsz, T, _ = h_lat.shape
    G, R = B_KV_HEADS, B_Q_HEADS // B_KV_HEADS
    ql, kl, vl, gate_l = _swa_project(h_lat, w_in)
    qc, kc, vc, gate_c = _swa_project(h_ctx, w_in)
    ql = _axial_rope(ql, row, col)
    kl = _axial_rope(kl, row, col)
    sink_gr = sink.reshape(G, R)
    o_lat = _swa_latent(ql.reshape(bsz, T, G, R, B_HEAD_DIM), kl, vl, kc, vc, sink_gr)
    y_lat = (o_lat * jax.nn.silu(gate_l)) @ w_out
    y_ctx = None
    if need_ctx:
        L = h_ctx.shape[1]
        o_ctx = _sink_softmax_attend(qc.reshape(bsz, L, G, R, B_HEAD_DIM), [kc], [vc], [None], sink_gr)
        y_ctx = (o_ctx.reshape(bsz, L, B_WIDTH) * jax.nn.silu(gate_c)) @ w_out
    return y_lat, y_ctx


def _centred_shift(x):
    xp = jnp.pad(x, ((0, 0), (1, 1), (0, 0)))
    return 0.5 * (xp[:, :-2] + xp[:, 2:])


def _rwkv7_scan(r, w, k, v, a, b, s0):
    def step(S, inp):
        rt, wt, kt, vt, at, bt = inp
        sa = jnp.einsum('bhvk,bhk->bhv', S, at)
        S = S * wt[:, :, None, :] + sa[..., None] * bt[:, :, None, :] + vt[..., None] * kt[:, :, None, :]
        return S, jnp.einsum('bhvk,bhk->bhv', S, rt)

    xs = tuple(jnp.moveaxis(t.astype(jnp.float32), 1, 0) for t in (r, w, k, v, a, b))
    S, y = lax.scan(step, s0, xs)
    return jnp.moveaxis(y, 0, 1).astype(v.dtype), S


def _rwkv7_shared(h, mu, w_in, k_k):
    xx = _centred_shift(h) - h
    lerp = lambda n: h + xx * mu[n]
    r = lerp(0) @ w_in[0]
    k = lerp(1) @ w_in[1]
    v = lerp(2) @ w_in[2]
    gate = lerp(3) @ w_in[3]
    kk = _split_heads(k * k_k, C_HEADS, C_HEAD).astype(jnp.float32)
    kk = (kk / jnp.maximum(jnp.sqrt(jnp.sum(kk * kk, axis=-1, keepdims=True)), 1e-12)).astype(k.dtype)
    return lerp(4), lerp(5), r, k, v, kk, gate


def _rwkv7_direction(shared, w0, w1, w2, a0, a1, a2, k_a):
    xw, xa, r, k, v, kk, _ = shared
    w_log = -jax.nn.softplus(-(w0 + jnp.tanh(xw @ w1) @ w2)) - 0.5
    decay = jnp.exp(-jnp.exp(w_log.astype(jnp.float32)))
    iclr = jax.nn.sigmoid(a0 + (xa @ a1) @ a2)
    k_d = k * (1.0 + (iclr - 1.0) * k_a)
    hs = lambda t: _split_heads(t, C_HEADS, C_HEAD)
    return (hs(r), hs(decay), hs(k_d), hs(v), -kk, kk * hs(iclr))


def _rwkv7_bonus(scan_in, r_k):
    rh, _, kh, vh, _, _ = scan_in
    return jnp.sum(rh * kh * r_k, axis=-1, keepdims=True) * vh


def _rwkv7_mixer(h_lat, h_ctx, mu, w_in, w0, w1, w2, a0, a1, a2, k_k, k_a, r_k, ln_w, ln_b, w_out, need_ctx):
    sh_lat = _rwkv7_shared(h_lat, mu, w_in, k_k)
    sh_ctx = _rwkv7_shared(h_ctx, mu, w_in, k_k)
    s0 = jnp.zeros((h_lat.shape[0], C_HEADS, C_HEAD, C_HEAD), jnp.float32)
    y_l, bon_l, y_c, bon_c = [], [], [], []
    for d in range(2):
        in_lat = _rwkv7_direction(sh_lat, w0[d], w1[d], w2[d], a0[d], a1[d], a2[d], k_a)
        in_ctx = _rwkv7_direction(sh_ctx, w0[d], w1[d], w2[d], a0[d], a1[d], a2[d], k_a)
        yc, yl = _context_then_latent(_rwkv7_scan, in_ctx, in_lat, s0, d == 1)
        y_l.append(yl)
        bon_l.append(_rwkv7_bonus(in_lat, r_k))
        if need_ctx:
            y_c.append(yc)
            bon_c.append(_rwkv7_bonus(in_ctx, r_k))

    def readout(y, bonus, gate):
        o = _groupnorm(y, ln_w, ln_b) + bonus.reshape(bonus.shape[0], bonus.shape[1], C_WIDTH)
        return (o * jax.nn.silu(gate)) @ w_out

    y_lat = readout(y_l[0] + y_l[1], bon_l[0] + bon_l[1], sh_lat[-1])
    y_ctx = readout(y_c[0] + y_c[1], bon_c[0] + bon_c[1], sh_ctx[-1]) if need_ctx else None
    return y_lat, y_ctx


def setup_inputs(seed: int = 0) -> dict:
    key = jax.random.key(seed)
    ks = iter(jax.random.split(key, 32))
    nrm = lambda shape, s: jax.random.normal(next(ks), shape, jnp.float32) * s
    D = D_MODEL
    return {
        "x": nrm((BATCH, SEQ, D), 1.0),
        "c": nrm((BATCH, D), 1.0),
        "ctx": nrm((BATCH, CTX_LEN, D), 1.0),
        "c_ctx": nrm((D,), 1.0),
        "norm_w": 1.0 + nrm((DEPTH, D), 0.02),
        "mod_w": nrm((DEPTH, D, 3 * D), 0.5 * D ** -0.5),
        "mod_b": nrm((DEPTH, 3 * D), 0.02),
        "a_w_in": nrm((N_A_LAYERS, D, A_IN_COLS), D ** -0.5),
        "a_lb_raw": nrm((N_A_LAYERS, A_QK), 0.5),
        "a_onorm_w": 1.0 + nrm((N_A_LAYERS, A_DV), 0.02),
        "a_w_out": nrm((N_A_LAYERS, A_WIDTH, D), A_WIDTH ** -0.5),
        "b_w_in": nrm((N_B_LAYERS, D, B_IN_COLS), D ** -0.5),
        "b_sink": nrm((N_B_LAYERS, B_Q_HEADS), 0.5),
        "b_w_out": nrm((N_B_LAYERS, B_WIDTH, D), B_WIDTH ** -0.5),
        "c_mu": jax.random.uniform(next(ks), (N_C_LAYERS, 6, D), jnp.float32),
        "c_w_in": nrm((N_C_LAYERS, 4, D, C_WIDTH), D ** -0.5),
        "c_w0": -1.0 + nrm((N_C_LAYERS, 2, C_WIDTH), 0.5),
        "c_w1": nrm((N_C_LAYERS, 2, D, C_DECAY_LORA), D ** -0.5),
        "c_w2": nrm((N_C_LAYERS, 2, C_DECAY_LORA, C_WIDTH), 0.1 * C_DECAY_LORA ** -0.5),
        "c_a0": nrm((N_C_LAYERS, 2, C_WIDTH), 0.1),
        "c_a1": nrm((N_C_LAYERS, 2, D, C_ICLR_LORA), D ** -0.5),
        "c_a2": nrm((N_C_LAYERS, 2, C_ICLR_LORA, C_WIDTH), 0.5 * C_ICLR_LORA ** -0.5),
        "c_k_k": 0.85 + nrm((N_C_LAYERS, C_WIDTH), 0.02),
        "c_k_a": 1.0 + nrm((N_C_LAYERS, C_WIDTH), 0.02),
        "c_r_k": nrm((N_C_LAYERS, C_HEADS, C_HEAD), 0.1),
        "c_ln_w": 1.0 + nrm((N_C_LAYERS, C_WIDTH), 0.02),
        "c_ln_b": nrm((N_C_LAYERS, C_WIDTH), 0.02),
        "c_w_out": nrm((N_C_LAYERS, C_WIDTH, D), C_WIDTH ** -0.5),
        "final_norm_w": 1.0 + nrm((D,), 0.02),
    }


def reference(x, c, ctx, c_ctx, norm_w, mod_w, mod_b, a_w_in, a_lb_raw, a_onorm_w, a_w_out,
              b_w_in, b_sink, b_w_out, c_mu, c_w_in, c_w0, c_w1, c_w2, c_a0, c_a1, c_a2,
              c_k_k, c_k_a, c_r_k, c_ln_w, c_ln_b, c_w_out, final_norm_w):
    n_lat = x.shape[1]
    n_rows = n_lat // GRID_W
    row = jnp.repeat(jnp.arange(n_rows), GRID_W).astype(jnp.float32)
    col = jnp.tile(jnp.arange(GRID_W), n_rows).astype(jnp.float32)
    lb_all = jnp.cumsum(jax.nn.softmax(a_lb_raw.astype(jnp.float32), axis=0), axis=0)
    lb_all = lb_all - lb_all[0]
    for i in range(DEPTH):
        j = i // N_MIXERS
        kind = i % N_MIXERS
        need_ctx = i < DEPTH - 1
        mod_l = jax.nn.silu(c) @ mod_w[i] + mod_b[i]
        mod_c = jax.nn.silu(c_ctx) @ mod_w[i] + mod_b[i]
        sh_l, sc_l, g_l = jnp.split(mod_l[:, None, :], 3, axis=-1)
        sh_c, sc_c, g_c = jnp.split(mod_c[None, None, :], 3, axis=-1)
        h_lat = _rmsnorm(x, norm_w[i]) * (1.0 + sc_l) + sh_l
        h_ctx = _rmsnorm(ctx, norm_w[i]) * (1.0 + sc_c) + sh_c
        if kind == 0:
            y_lat, y_ctx = _hgrn2_mixer(h_lat, h_ctx, a_w_in[j], lb_all[j], a_onorm_w[j], a_w_out[j], need_ctx)
        elif kind == 1:
            y_lat, y_ctx = _swa_mixer(h_lat, h_ctx, b_w_in[j], b_sink[j], b_w_out[j], row, col, need_ctx)
        else:
            y_lat, y_ctx = _rwkv7_mixer(h_lat, h_ctx, c_mu[j], c_w_in[j], c_w0[j], c_w1[j], c_w2[j],
                                        c_a0[j], c_a1[j], c_a2[j], c_k_k[j], c_k_a[j], c_r_k[j],
                                        c_ln_w[j], c_ln_b[j], c_w_out[j], need_ctx)
        x = x + g_l * y_lat
        if need_ctx:
            ctx = ctx + g_c * y_ctx
    return _rmsnorm(x, final_norm_w)
```

```python
import numpy as np
from contextlib import ExitStack
import concourse.bass as bass
import concourse.mybir as mybir
from concourse.bass_utils import run_bass_kernel_spmd

F32 = mybir.dt.float32
BF16 = mybir.dt.bfloat16
AF = mybir.ActivationFunctionType
ALU = mybir.AluOpType
AX = mybir.AxisListType

NCORES = 8
D = 2048
KC = D // 128


class Prog:
    CE = ("pe", "act", "dve", "pool")

    def __init__(self, nc, ndma=8):
        self.nc = nc
        self.es = ExitStack()
        self.eng = {"pe": nc.tensor, "act": nc.scalar, "dve": nc.vector, "pool": nc.gpsimd, "sp": nc.sync}
        self.sem = {}
        self.cnt = {}
        for e in self.CE:
            self.sem[("e", e)] = self.es.enter_context(nc.semaphore("s_" + e))
            self.cnt[("e", e)] = 0
        self.ndma = ndma
        self.dma_rr = {}
        for q in ("sp", "pool", "act"):
            self.dma_rr[q] = 0
            for i in range(ndma):
                k = ("d", q, i)
                self.sem[k] = self.es.enter_context(nc.semaphore("d_%s%d" % (q, i)))
                self.cnt[k] = 0
        self.known = {e: {} for e in self.eng}
        self.last_w = {}
        self.readers = {}
        self.ninstr = 0
        self.uid = 0

    def sb(self, shape, dtype=F32, name=None, stack=None):
        self.uid += 1
        return (stack or self.es).enter_context(self.nc.sbuf_tensor(name or ("sb%d" % self.uid), list(shape), dtype))

    def barrier(self):
        for e in ("pe", "act", "dve", "pool", "sp"):
            kn = self.known[e]
            for k, v in self.cnt.items():
                if v > 0 and kn.get(k, 0) < v:
                    self.eng[e].wait_ge(self.sem[k], v)
                    kn[k] = v
                    self.ninstr += 1

    def ps(self, shape, dtype=F32, name=None):
        self.uid += 1
        return self.es.enter_context(self.nc.psum_tensor(name or ("ps%d" % self.uid), list(shape), dtype))

    def dram(self, name, shape, dtype=F32, kind="ExternalInput"):
        return self.nc.dram_tensor(name, list(shape), dtype, kind=kind).ap()

    def _deps(self, e, reads, writes):
        deps = {}

        def add(kv):
            k, v = kv
            if e == "pe" and k == ("e", "pe"):
                return
            if deps.get(k, 0) < v:
                deps[k] = v

        for b in reads:
            if b in self.last_w:
                add(self.last_w[b])
        for b in writes:
            if b in self.last_w:
                add(self.last_w[b])
            for r in self.readers.get(b, ()):
                add(r)
        kn = self.known[e]
        for k, v in deps.items():
            if kn.get(k, 0) < v:
                self.eng[e].wait_ge(self.sem[k], v)
                self.ninstr += 1
                kn[k] = v

    def _record(self, key, val, reads, writes):
        for b in writes:
            self.last_w[b] = (key, val)
            self.readers[b] = []
        for b in reads:
            self.readers.setdefault(b, []).append((key, val))
            if len(self.readers[b]) > 64:
                mx = {}
                for k, v in self.readers[b]:
                    if mx.get(k, 0) < v:
                        mx[k] = v
                self.readers[b] = list(mx.items())

    _defer = None
    _atomic = None

    def begin(self):
        self._defer = []

    def end(self):
        l, self._defer = self._defer, None
        return l

    def atomic_begin(self):
        if self._defer is not None:
            self._atomic = []

    def atomic_end(self):
        if self._defer is not None:
            self._defer.append(self._atomic)
            self._atomic = None

    def interleave(self, lists):
        n = max(len(l) for l in lists)
        for i in range(n):
            for l in lists:
                if i < len(l):
                    for (kind, args, kw) in l[i]:
                        if kind == "op":
                            self.op(*args, **kw)
                        else:
                            self.dma(*args, **kw)

    def _rec(self, kind, args, kw):
        item = (kind, args, kw)
        if self._atomic is not None:
            self._atomic.append(item)
        else:
            self._defer.append([item])

    def op(self, e, fn, reads=(), writes=()):
        if self._defer is not None:
            return self._rec("op", (e, fn, list(reads), list(writes)), {})
        self._deps(e, reads, writes)
        key = ("e", e)
        self.cnt[key] += 1
        ins = fn(self.eng[e])
        ins.then_inc(self.sem[key], 1)
        self.ninstr += 1
        self._record(key, self.cnt[key], reads, writes)
        return ins

    def dma(self, q, out, in_, reads=(), writes=(), **kw):
        if self._defer is not None:
            return self._rec("dma", (q, out, in_, list(reads), list(writes)), kw)
        i = self.dma_rr[q]
        self.dma_rr[q] = (i + 1) % self.ndma
        key = ("d", q, i)
        kn = self.known[q]
        if kn.get(key, 0) < self.cnt[key]:
            self.eng[q].wait_ge(self.sem[key], self.cnt[key])
            kn[key] = self.cnt[key]
            self.ninstr += 1
        self._deps(q, reads, writes)
        self.cnt[key] += 16
        self.eng[q].dma_start(out=out, in_=in_, **kw).then_inc(self.sem[key], 16)
        self.ninstr += 1
        self._record(key, self.cnt[key], reads, writes)

    def finish(self, out_keys):
        self._deps("pool", out_keys, ())
        for k, v in self.cnt.items():
            if k[0] == "d" and v > 0 and self.known["pool"].get(k, 0) < v:
                self.eng["pool"].wait_ge(self.sem[k], v)
                self.known["pool"][k] = v


def _run(nc, in_maps):
    res = run_bass_kernel_spmd(nc, in_maps, core_ids=list(range(NCORES)))
    return res.results


MOD_NCOL = 4 * 3 * D // NCORES


def build_mod():
    nc = bass.Bass("TRN2", target_bir_lowering=False)
    P = Prog(nc)
    ccT = P.dram("ccT", [128, KC, 3])
    w = P.dram("w", [128, KC, MOD_NCOL])
    b3 = P.dram("b3", [3, MOD_NCOL])
    out = P.dram("out", [3, MOD_NCOL], kind="ExternalOutput")
    s_in = P.sb([128, KC, 3])
    s_act = P.sb([128, KC, 3])
    bias = P.sb([3, MOD_NCOL])
    res = P.sb([3, MOD_NCOL])
    wt = [P.sb([128, KC, 512]) for _ in range(2)]
    pt = [P.ps([3, 512]) for _ in range(2)]
    P.dma("sp", s_in[:], ccT[:], writes=["s_in"])
    P.dma("sp", bias[:], b3[:], writes=["bias"])
    P.op("act", lambda e: e.activation(out=s_act[:], in_=s_in[:], func=AF.Silu), reads=["s_in"], writes=["s_act"])
    nb = MOD_NCOL // 512
    for j in range(nb):
        wb = wt[j % 2]
        pb = pt[j % 2]
        P.dma("sp", wb[:], w[:, :, j * 512:(j + 1) * 512], writes=[("w", j % 2)])
        for k in range(KC):
            P.op("pe", lambda e, k=k, wb=wb, pb=pb: e.matmul(pb[:], lhsT=s_act[:, k, :], rhs=wb[:, k, :],
                                                        start=(k == 0), stop=(k == KC - 1)),
                 reads=["s_act", ("w", j % 2)], writes=[("p", j % 2)])
        P.op("dve", lambda e, j=j, pb=pb: e.tensor_tensor(out=res[:, j * 512:(j + 1) * 512], in0=pb[:],
                                                       in1=bias[:, j * 512:(j + 1) * 512], op=ALU.add),
             reads=[("p", j % 2), "bias"], writes=["res"])
    P.dma("pool", out[:], res[:], reads=["res"], writes=["out"])
    P.finish(["out"])
    return nc


def run_mod(c, c_ctx, mod_w, mod_b):
    cc = np.concatenate([c, c_ctx[None, :]], axis=0).astype(np.float32)
    ccT = np.ascontiguousarray(cc.T.reshape(KC, 128, 3).transpose(1, 0, 2))
    wall = np.concatenate([mod_w[l] for l in range(4)], axis=1)
    ball = np.concatenate([mod_b[l] for l in range(4)], axis=0)
    in_maps = []
    for cid in range(NCORES):
        cols = slice(cid * MOD_NCOL, (cid + 1) * MOD_NCOL)
        wc = np.ascontiguousarray(wall[:, cols].reshape(KC, 128, MOD_NCOL).transpose(1, 0, 2))
        bc = np.ascontiguousarray(np.broadcast_to(ball[cols][None, :], (3, MOD_NCOL)))
        in_maps.append({"ccT": ccT, "w": wc, "b3": bc})
    nc = build_mod()
    res = _run(nc, in_maps)
    mod = np.concatenate([r["out"] for r in res], axis=1)
    return mod.reshape(3, 4, 3 * D)


def _segments(r0, n):
    segs = []
    o = 0
    while o < n:
        m = min(128, n - o)
        segs.append((r0 + o, m))
        o += m
    return segs


def build_P(n_lat, n_ctx, nblk, lerp=None):
    halo = 1 if lerp is not None else 0
    r_lat = n_lat + 2 * halo
    r_ctx = n_ctx + 2 * halo
    R = r_lat + r_ctx
    n_int = n_lat + n_ctx
    nc = bass.Bass("TRN2", target_bir_lowering=False)
    P = Prog(nc)
    xin = P.dram("xin", [R, D])
    nw = P.dram("nw", [128, D])
    scb = P.dram("scb", [2, 128, D])
    shb = P.dram("shb", [2, 128, D])
    wd = P.dram("w", [nblk, 128, KC, 128])
    identd = P.dram("ident", [128, 128])
    if lerp is not None:
        mud = P.dram("mu", [128, KC, 6])
        bmd = P.dram("bmask", [128, 4])
    out = P.dram("projT", [nblk * 128, n_int], kind="ExternalOutput")

    hT = P.sb([128, KC, R], BF16)
    tmp = P.sb([128, D])
    if lerp is not None:
        xxT = P.sb([128, KC, R], BF16)
        mu = P.sb([128, KC, 6])
        bm = P.sb([128, 4])
    tps = [P.ps([128, 4, 128], BF16) for _ in range(2)]
    mps = [P.ps([128, 512]) for _ in range(4)]
    es1 = ExitStack()
    ident_f = P.sb([128, 128], stack=es1)
    ident = P.sb([128, 128], BF16, stack=es1)
    S = [P.sb([128, D], stack=es1) for _ in range(2)]
    SH = [P.sb([128, D], stack=es1) for _ in range(2)]
    nwt = tmp
    xt = [P.sb([128, D], stack=es1) for _ in range(2)]
    sq = P.sb([128, D], BF16, stack=es1)
    hb = [P.sb([128, D], BF16, stack=es1) for _ in range(2)]
    ss = [P.sb([128, 1], stack=es1) for _ in range(2)]
    rstd = [P.sb([128, 1], stack=es1) for _ in range(2)]

    P.dma("sp", ident_f[:], identd[:], writes=["identf"])
    P.op("dve", lambda e: e.tensor_copy(out=ident[:], in_=ident_f[:]), reads=["identf"], writes=["ident"])
    P.dma("sp", nwt[:], nw[:], writes=["tmp"])
    for i in range(2):
        P.dma("sp", S[i][:], scb[i], writes=[("S", i)])
        P.dma("sp", SH[i][:], shb[i], writes=[("SH", i)])
        P.op("dve", lambda e, i=i: e.scalar_tensor_tensor(out=S[i][:], in0=S[i][:], scalar=1.0, in1=nwt[:],
                                                          op0=ALU.add, op1=ALU.mult),
             reads=[("S", i), "tmp"], writes=[("S", i)])
    if lerp is not None:
        P.dma("sp", mu[:], mud[:], writes=["mu"])
        P.dma("sp", bm[:], bmd[:], writes=["bm"])

    segs = [(r, m, 0) for (r, m) in _segments(0, r_lat)] + [(r, m, 1) for (r, m) in _segments(r_lat, r_ctx)]
    for si, (r0, m, mi) in enumerate(segs):
        b = si % 2
        P.dma("sp", xt[b][:m, :], xin[r0:r0 + m, :], writes=[("xt", b)])
        P.op("act", lambda e, b=b, m=m: e.activation(out=sq[:m, :], in_=xt[b][:m, :], func=AF.Square,
                                                      accum_out=ss[b][:m, :]),
             reads=[("xt", b)], writes=["sq", ("ss", b)])
        P.op("dve", lambda e, b=b, m=m: e.tensor_scalar(out=rstd[b][:m, :], in0=ss[b][:m, :], scalar1=1.0 / D,
                                                         scalar2=1e-6, op0=ALU.mult, op1=ALU.add),
             reads=[("ss", b)], writes=[("rstd", b)])
        P.op("act", lambda e, b=b, m=m: e.activation(out=rstd[b][:m, :], in_=rstd[b][:m, :], func=AF.Sqrt),
             reads=[("rstd", b)], writes=[("rstd", b)])
        P.op("dve", lambda e, b=b, m=m: e.reciprocal(out=rstd[b][:m, :], in_=rstd[b][:m, :]),
             reads=[("rstd", b)], writes=[("rstd", b)])
        P.op("dve", lambda e, b=b, m=m, mi=mi: e.scalar_tensor_tensor(out=tmp[:m, :], in0=xt[b][:m, :],
                                                                     scalar=rstd[b][:m, :], in1=S[mi][:m, :],
                                                                     op0=ALU.mult, op1=ALU.mult),
             reads=[("xt", b), ("rstd", b), ("S", mi)], writes=["tmp"])
        P.op("dve", lambda e, b=b, m=m, mi=mi: e.tensor_tensor(out=hb[b][:m, :], in0=tmp[:m, :], in1=SH[mi][:m, :],
                                                              op=ALU.add),
             reads=["tmp", ("SH", mi)], writes=[("hb", b)])
        for kg in range(KC // 4):
            tb = (si * 4 + kg) % 2
            for kk in range(4):
                k = kg * 4 + kk
                P.op("pe", lambda e, b=b, m=m, k=k, kk=kk, tb=tb: e.transpose(out=tps[tb][:, kk, :m],
                                                                              in_=hb[b][:m, k * 128:(k + 1) * 128],
                                                                              identity=ident[:m, :m]),
                     reads=[("hb", b), "ident"], writes=[("tps", tb)])
            eng = "act" if kg % 2 == 0 else "dve"
            if eng == "act":
                P.op("act", lambda e, m=m, kg=kg, tb=tb, r0=r0: e.copy(out=hT[:, kg * 4:(kg + 1) * 4, r0:r0 + m],
                                                                       in_=tps[tb][:, :, :m]),
                     reads=[("tps", tb)], writes=["hT"])
            else:
                P.op("dve", lambda e, m=m, kg=kg, tb=tb, r0=r0: e.tensor_copy(out=hT[:, kg * 4:(kg + 1) * 4, r0:r0 + m],
                                                                              in_=tps[tb][:, :, :m]),
                     reads=[("tps", tb)], writes=["hT"])

    if lerp is not None:
        for ci, col in enumerate([0, r_lat - 1, r_lat, R - 1]):
            P.op("dve", lambda e, ci=ci, col=col: e.tensor_scalar(out=hT[:, :, col:col + 1], in0=hT[:, :, col:col + 1],
                                                                   scalar1=bm[:, ci:ci + 1], scalar2=None, op0=ALU.mult),
                 reads=["hT", "bm"], writes=["hT"])
        for (c0, n) in [(0, r_lat), (r_lat, r_ctx)]:
            for k in range(KC):
                w_ = n - 2
                P.op("dve", lambda e, k=k, c0=c0, w_=w_: e.tensor_tensor(out=tmp[:, :w_], in0=hT[:, k, c0:c0 + w_],
                                                                        in1=hT[:, k, c0 + 2:c0 + 2 + w_], op=ALU.add),
                     reads=["hT"], writes=["tmp"])
                P.op("dve", lambda e, k=k, c0=c0, w_=w_: e.scalar_tensor_tensor(out=xxT[:, k, c0 + 1:c0 + 1 + w_],
                                                                               in0=tmp[:, :w_], scalar=0.5,
                                                                               in1=hT[:, k, c0 + 1:c0 + 1 + w_],
                                                                               op0=ALU.mult, op1=ALU.subtract),
                     reads=["tmp", "hT"], writes=["xxT"])

    P.barrier()
    es1.close()
    wf = [P.sb([128, KC, 128]) for _ in range(2)]
    wb = [P.sb([128, KC, 128], BF16) for _ in range(2)]
    stage = [P.sb([128, n_int]) for _ in range(2)]
    if lerp is not None:
        wb2 = [P.sb([128, KC, 128], BF16) for _ in range(2)]
    groups = []
    o = 0
    while o < n_lat:
        n = min(512, n_lat - o)
        groups.append((halo + o, o, n))
        o += n
    groups.append((r_lat + halo, n_lat, n_ctx))
    gi = 0
    for j in range(nblk):
        b = j % 2
        P.dma("sp", wf[b][:], wd[j], writes=[("wf", b)])
        if j % 2 == 0:
            P.op("act", lambda e, b=b: e.copy(out=wb[b][:], in_=wf[b][:]), reads=[("wf", b)], writes=[("wb", b)])
        else:
            P.op("pool", lambda e, b=b: e.tensor_copy(out=wb[b][:], in_=wf[b][:]), reads=[("wf", b)], writes=[("wb", b)])
        if lerp is not None:
            n_mu = lerp[j]
            P.op("pool", lambda e, b=b, n_mu=n_mu: e.tensor_tensor(out=wb2[b][:], in0=wf[b][:],
                                                                   in1=mu[:, :, n_mu:n_mu + 1].to_broadcast([128, KC, 128]),
                                                                   op=ALU.mult),
                 reads=[("wf", b), "mu"], writes=[("wb2", b)])
        for (hc, oc, n) in groups:
            pb = gi % 4
            gi += 1
            nmm = KC * (2 if lerp is not None else 1)
            for k in range(KC):
                P.op("pe", lambda e, b=b, k=k, pb=pb, hc=hc, n=n: e.matmul(mps[pb][:, :n], lhsT=wb[b][:, k, :],
                                                                         rhs=hT[:, k, hc:hc + n],
                                                                         start=(k == 0), stop=(k == nmm - 1)),
                     reads=[("wb", b), "hT"], writes=[("mps", pb)])
            if lerp is not None:
                for k in range(KC):
                    P.op("pe", lambda e, b=b, k=k, pb=pb, hc=hc, n=n: e.matmul(mps[pb][:, :n], lhsT=wb2[b][:, k, :],
                                                                             rhs=xxT[:, k, hc:hc + n],
                                                                             start=False, stop=(k == KC - 1)),
                         reads=[("wb2", b), "xxT"], writes=[("mps", pb)])
            if gi % 2 == 0:
                P.op("act", lambda e, b=b, pb=pb, oc=oc, n=n: e.copy(out=stage[b][:, oc:oc + n], in_=mps[pb][:, :n]),
                     reads=[("mps", pb)], writes=[("stage", b)])
            else:
                P.op("dve", lambda e, b=b, pb=pb, oc=oc, n=n: e.tensor_copy(out=stage[b][:, oc:oc + n], in_=mps[pb][:, :n]),
                     reads=[("mps", pb)], writes=[("stage", b)])
        P.dma("pool", out[j * 128:(j + 1) * 128, :], stage[b][:], reads=[("stage", b)], writes=["out"])
    P.finish(["out"])
    return nc


def _bc(v):
    return np.ascontiguousarray(np.broadcast_to(np.asarray(v, np.float32)[None, :], (128, v.shape[-1])))


def _wblocks(w):
    ncols = w.shape[1]
    nblk = (ncols + 127) // 128
    if nblk * 128 != ncols:
        w = np.concatenate([w, np.zeros((D, nblk * 128 - ncols), np.float32)], axis=1)
    return np.ascontiguousarray(w.reshape(KC, 128, nblk, 128).transpose(2, 1, 0, 3))


N_LAT = 2048
N_CTX = 64
T_LAT = 8192
T_CTX = 256


def _core_rows(x, ctx, cid, halo=0):
    b, q = cid // 4, cid % 4

    def take(a, lo, hi):
        T = a.shape[0]
        rows = []
        if lo < 0:
            rows.append(np.zeros((-lo, a.shape[1]), a.dtype))
        rows.append(a[max(lo, 0):min(hi, T)])
        if hi > T:
            rows.append(np.zeros((hi - T, a.shape[1]), a.dtype))
        return np.concatenate(rows, axis=0) if len(rows) > 1 else rows[0]

    lat = take(x[b], q * N_LAT - halo, (q + 1) * N_LAT + halo)
    cx = take(ctx[b], q * N_CTX - halo, (q + 1) * N_CTX + halo)
    return np.ascontiguousarray(np.concatenate([lat, cx], axis=0))


def _gather_rows(outs, width):
    x = np.empty((2, T_LAT, width), np.float32)
    ctx = np.empty((2, T_CTX, width), np.float32)
    for cid in range(NCORES):
        b, q = cid // 4, cid % 4
        x[b, q * N_LAT:(q + 1) * N_LAT] = outs[cid][:N_LAT]
        ctx[b, q * N_CTX:(q + 1) * N_CTX] = outs[cid][N_LAT:]
    return x, ctx


def run_P(x, ctx, mod_i, norm_w_i, wcat, lerp=None, mu=None):
    wb = _wblocks(wcat)
    nblk = wb.shape[0]
    halo = 1 if lerp is not None else 0
    nc = build_P(N_LAT, N_CTX, nblk, lerp)
    ident = np.eye(128, dtype=np.float32)
    nw = _bc(norm_w_i)
    in_maps = []
    for cid in range(NCORES):
        b, q = cid // 4, cid % 4
        m = {"xin": _core_rows(x, ctx, cid, halo), "nw": nw, "w": wb, "ident": ident,
             "scb": np.stack([_bc(mod_i[b, D:2 * D]), _bc(mod_i[2, D:2 * D])]),
             "shb": np.stack([_bc(mod_i[b, 0:D]), _bc(mod_i[2, 0:D])])}
        if lerp is not None:
            m["mu"] = np.ascontiguousarray(mu.T.reshape(KC, 128, 6).transpose(1, 0, 2))
            bm = np.ones((128, 4), np.float32)
            if q == 0:
                bm[:, 0] = 0.0
                bm[:, 2] = 0.0
            if q == 3:
                bm[:, 1] = 0.0
                bm[:, 3] = 0.0
            m["bmask"] = bm
        in_maps.append(m)
    res = _run(nc, in_maps)
    outs = [np.ascontiguousarray(r["projT"].T) for r in res]
    return _gather_rows(outs, nblk * 128)


O_INS = {"a": ["m0", "m1", "gate"], "b": ["m0", "gate"], "c": ["m0", "m1", "m2", "m3", "gate"]}


def build_O(n_lat, n_ctx, kind, final=False):
    R = n_lat + n_ctx
    nc = bass.Bass("TRN2", target_bir_lowering=False)
    P = Prog(nc)
    xin = P.dram("xin", [R, D])
    ins_d = {nm: P.dram(nm, [R, D]) for nm in O_INS[kind]}
    wod = P.dram("wo", [128, KC, D])
    gbd = P.dram("gb", [2, 128, D])
    identd = P.dram("ident", [128, 128])
    nbc = {"a": ["onw"], "b": [], "c": ["lnw", "lnb"]}[kind] + (["fnw"] if final else [])
    bc_d = {nm: P.dram(nm, [128, D]) for nm in nbc}
    out = P.dram("xout", [R, D], kind="ExternalOutput")

    ident_f = P.sb([128, 128])
    ident = P.sb([128, 128], BF16)
    wo = P.sb([128, KC, D], BF16)
    wst = [P.sb([128, D]) for _ in range(2)]
    gb = [P.sb([128, D]) for _ in range(2)]
    bc = {nm: P.sb([128, D]) for nm in nbc}
    xt = [P.sb([128, D]) for _ in range(2)]
    xo = [P.sb([128, D]) for _ in range(2)]
    it = {nm: [P.sb([128, 512]) for _ in range(2)] for nm in O_INS[kind]}
    t1 = P.sb([128, 512])
    t2 = P.sb([128, 512])
    t3 = P.sb([128, 512])
    sg = P.sb([128, 512])
    st = [P.sb([128, 8]) for _ in range(3)]
    zb = [P.sb([128, D], BF16) for _ in range(2)]
    zT = [P.sb([128, KC, 128], BF16) for _ in range(2)]
    tps = [P.ps([128, 4, 128], BF16) for _ in range(2)]
    mps = [P.ps([128, 512]) for _ in range(4)]
    fs = [P.sb([128, 1]) for _ in range(2)]

    P.dma("sp", ident_f[:], identd[:], writes=["identf"])
    P.op("dve", lambda e: e.tensor_copy(out=ident[:], in_=ident_f[:]), reads=["identf"], writes=["ident"])
    for i in range(2):
        P.dma("sp", gb[i][:], gbd[i], writes=[("gb", i)])
    for nm in nbc:
        P.dma("sp", bc[nm][:], bc_d[nm], writes=[nm])
    for k in range(KC):
        b = k % 2
        P.dma("sp", wst[b][:], wod[:, k, :], writes=[("wst", b)])
        if k % 2 == 0:
            P.op("act", lambda e, b=b, k=k: e.copy(out=wo[:, k, :], in_=wst[b][:]), reads=[("wst", b)], writes=["wo"])
        else:
            P.op("pool", lambda e, b=b, k=k: e.tensor_copy(out=wo[:, k, :], in_=wst[b][:]), reads=[("wst", b)], writes=["wo"])

    G = 128 if kind == "a" else 64
    ng = 512 // G
    segs = [(r, m, 0) for (r, m) in _segments(0, n_lat)] + [(r, m, 1) for (r, m) in _segments(n_lat, n_ctx)]
    li = 0
    for si, (r0, m, mi) in enumerate(segs):
        b = si % 2
        P.dma("sp", xt[b][:m, :], xin[r0:r0 + m, :], writes=[("xt", b)])
        for cg in range(4):
            cs = slice(cg * 512, (cg + 1) * 512)
            lb = li % 2
            li += 1
            T = {}
            for nm in O_INS[kind]:
                P.dma("sp", it[nm][lb][:m, :], ins_d[nm][r0:r0 + m, cs], writes=[(nm, lb)])
                T[nm] = it[nm][lb]
            gk = ("gate", lb)
            P.op("act", lambda e, m=m, T=T: e.activation(out=sg[:m, :], in_=T["gate"][:m, :], func=AF.Silu),
                 reads=[gk], writes=["sg"])
            if kind == "b":
                P.op("dve", lambda e, m=m, T=T, b=b, cs=cs: e.tensor_tensor(out=zb[b][:m, cs], in0=T["m0"][:m, :],
                                                                          in1=sg[:m, :], op=ALU.mult),
                     reads=[("m0", lb), "sg"], writes=[("zb", b)])
                continue
            P.op("dve", lambda e, m=m, T=T: e.tensor_tensor(out=t1[:m, :], in0=T["m0"][:m, :], in1=T["m1"][:m, :], op=ALU.add),
                 reads=[("m0", lb), ("m1", lb)], writes=["t1"])
            y3 = t1[:m, :].rearrange("p (g c) -> p g c", c=G)
            if kind == "c":
                P.op("dve", lambda e, m=m, y3=y3: e.tensor_reduce(out=st[0][:m, :ng], in_=y3, axis=AX.X, op=ALU.add),
                     reads=["t1"], writes=["st0"])
                P.op("dve", lambda e, m=m: e.tensor_scalar(out=st[0][:m, :ng], in0=st[0][:m, :ng], scalar1=-1.0 / G,
                                                           scalar2=None, op0=ALU.mult),
                     reads=["st0"], writes=["st0"])
                P.op("dve", lambda e, m=m, y3=y3: e.tensor_tensor(out=y3, in0=y3,
                                                                in1=st[0][:m, :ng].unsqueeze(2).to_broadcast([m, ng, G]),
                                                                op=ALU.add),
                     reads=["t1", "st0"], writes=["t1"])
            P.op("pool", lambda e, m=m: e.tensor_tensor(out=t2[:m, :], in0=t1[:m, :], in1=t1[:m, :], op=ALU.mult),
                 reads=["t1"], writes=["t2"])
            P.op("dve", lambda e, m=m: e.tensor_reduce(out=st[1][:m, :ng], in_=t2[:m, :].rearrange("p (g c) -> p g c", c=G),
                                                       axis=AX.X, op=ALU.add),
                 reads=["t2"], writes=["st1"])
            eps = 1e-6 if kind == "a" else 64e-5
            P.op("dve", lambda e, m=m, eps=eps: e.tensor_scalar(out=st[1][:m, :ng], in0=st[1][:m, :ng], scalar1=1.0 / G,
                                                                scalar2=eps, op0=ALU.mult, op1=ALU.add),
                 reads=["st1"], writes=["st1"])
            P.op("act", lambda e, m=m: e.activation(out=st[1][:m, :ng], in_=st[1][:m, :ng], func=AF.Sqrt),
                 reads=["st1"], writes=["st1"])
            P.op("dve", lambda e, m=m: e.reciprocal(out=st[2][:m, :ng], in_=st[1][:m, :ng]),
                 reads=["st1"], writes=["st2"])
            P.op("dve", lambda e, m=m, y3=y3: e.tensor_tensor(out=y3, in0=y3,
                                                            in1=st[2][:m, :ng].unsqueeze(2).to_broadcast([m, ng, G]),
                                                            op=ALU.mult),
                 reads=["t1", "st2"], writes=["t1"])
            if kind == "a":
                P.op("pool", lambda e, m=m, cs=cs: e.tensor_tensor(out=t2[:m, :], in0=t1[:m, :], in1=bc["onw"][:m, cs], op=ALU.mult),
                     reads=["t1", "onw"], writes=["t2"])
            else:
                P.op("pool", lambda e, m=m, cs=cs: e.tensor_tensor(out=t2[:m, :], in0=t1[:m, :], in1=bc["lnw"][:m, cs], op=ALU.mult),
                     reads=["t1", "lnw"], writes=["t2"])
                P.op("pool", lambda e, m=m, T=T: e.tensor_tensor(out=t3[:m, :], in0=T["m2"][:m, :], in1=T["m3"][:m, :], op=ALU.add),
                     reads=[("m2", lb), ("m3", lb)], writes=["t3"])
                P.op("pool", lambda e, m=m, cs=cs: e.tensor_tensor(out=t3[:m, :], in0=t3[:m, :], in1=bc["lnb"][:m, cs], op=ALU.add),
                     reads=["t3", "lnb"], writes=["t3"])
                P.op("dve", lambda e, m=m: e.tensor_tensor(out=t2[:m, :], in0=t2[:m, :], in1=t3[:m, :], op=ALU.add),
                     reads=["t2", "t3"], writes=["t2"])
            P.op("dve", lambda e, m=m, b=b, cs=cs: e.tensor_tensor(out=zb[b][:m, cs], in0=t2[:m, :], in1=sg[:m, :], op=ALU.mult),
                 reads=["t2", "sg"], writes=[("zb", b)])
        for kg in range(KC // 4):
            tb = (si * 4 + kg) % 2
            for kk in range(4):
                k = kg * 4 + kk
                P.op("pe", lambda e, b=b, m=m, k=k, kk=kk, tb=tb: e.transpose(out=tps[tb][:, kk, :m],
                                                                              in_=zb[b][:m, k * 128:(k + 1) * 128],
                                                                              identity=ident[:m, :m]),
                     reads=[("zb", b), "ident"], writes=[("tps", tb)])
            if kg % 2 == 0:
                P.op("act", lambda e, m=m, kg=kg, tb=tb, b=b: e.copy(out=zT[b][:, kg * 4:(kg + 1) * 4, :m], in_=tps[tb][:, :, :m]),
                     reads=[("tps", tb)], writes=[("zT", b)])
            else:
                P.op("dve", lambda e, m=m, kg=kg, tb=tb, b=b: e.tensor_copy(out=zT[b][:, kg * 4:(kg + 1) * 4, :m], in_=tps[tb][:, :, :m]),
                     reads=[("tps", tb)], writes=[("zT", b)])
        for cg in range(4):
            cs = slice(cg * 512, (cg + 1) * 512)
            pb = cg
            for k in range(KC):
                P.op("pe", lambda e, b=b, m=m, k=k, pb=pb, cs=cs: e.matmul(mps[pb][:m, :], lhsT=zT[b][:, k, :m], rhs=wo[:, k, cs],
                                                                         start=(k == 0), stop=(k == KC - 1)),
                     reads=[("zT", b), "wo"], writes=[("mps", pb)])
            P.op("dve", lambda e, m=m, pb=pb, cs=cs, mi=mi: e.tensor_tensor(out=t1[:m, :], in0=mps[pb][:m, :], in1=gb[mi][:m, cs], op=ALU.mult),
                 reads=[("mps", pb), ("gb", mi)], writes=["t1"])
            P.op("pool", lambda e, m=m, b=b, cs=cs: e.tensor_tensor(out=xo[b][:m, cs], in0=t1[:m, :], in1=xt[b][:m, cs], op=ALU.add),
                 reads=["t1", ("xt", b)], writes=[("xo", b)])
        if final:
            P.op("act", lambda e, b=b, m=m: e.activation(out=xt[b][:m, :], in_=xo[b][:m, :], func=AF.Square, accum_out=fs[0][:m, :]),
                 reads=[("xo", b)], writes=[("xt", b), "fs0"])
            P.op("dve", lambda e, m=m: e.tensor_scalar(out=fs[0][:m, :], in0=fs[0][:m, :], scalar1=1.0 / D, scalar2=1e-6,
                                                       op0=ALU.mult, op1=ALU.add), reads=["fs0"], writes=["fs0"])
            P.op("act", lambda e, m=m: e.activation(out=fs[0][:m, :], in_=fs[0][:m, :], func=AF.Sqrt), reads=["fs0"], writes=["fs0"])
            P.op("dve", lambda e, m=m: e.reciprocal(out=fs[1][:m, :], in_=fs[0][:m, :]), reads=["fs0"], writes=["fs1"])
            P.op("dve", lambda e, m=m, b=b: e.scalar_tensor_tensor(out=xo[b][:m, :], in0=xo[b][:m, :], scalar=fs[1][:m, :],
                                                                   in1=bc["fnw"][:m, :], op0=ALU.mult, op1=ALU.mult),
                 reads=[("xo", b), "fs1", "fnw"], writes=[("xo", b)])
        P.dma("pool", out[r0:r0 + m, :], xo[b][:m, :], reads=[("xo", b)], writes=["out"])
    P.finish(["out"])
    return nc


def run_O(x, ctx, kind, mix_lat, mix_ctx, w_out, g_lat, g_ctx, bcs, final=False, need_ctx=True):
    n_ctx = N_CTX if need_ctx else 0
    nc = build_O(N_LAT, n_ctx, kind, final)
    ident = np.eye(128, dtype=np.float32)
    wo = np.ascontiguousarray(w_out.reshape(KC, 128, D).transpose(1, 0, 2))
    in_maps = []
    for cid in range(NCORES):
        b, q = cid // 4, cid % 4

        def rows(lat, cx):
            parts = [lat[b, q * N_LAT:(q + 1) * N_LAT]]
            if need_ctx:
                parts.append(cx[b, q * N_CTX:(q + 1) * N_CTX])
            return np.ascontiguousarray(np.concatenate(parts, axis=0))

        m = {"xin": rows(x, ctx), "wo": wo, "ident": ident, "gb": np.stack([_bc(g_lat[b]), _bc(g_ctx)])}
        for nm in O_INS[kind]:
            m[nm] = rows(mix_lat[nm], mix_ctx[nm] if need_ctx else None)
        for nm, v in bcs.items():
            m[nm] = _bc(v)
        in_maps.append(m)
    res = _run(nc, in_maps)
    xo = np.empty((2, T_LAT, D), np.float32)
    co = np.empty((2, T_CTX, D), np.float32) if need_ctx else None
    for cid in range(NCORES):
        b, q = cid // 4, cid % 4
        o = res[cid]["xout"]
        xo[b, q * N_LAT:(q + 1) * N_LAT] = o[:N_LAT]
        if need_ctx:
            co[b, q * N_CTX:(q + 1) * N_CTX] = o[N_LAT:]
    return xo, co


L_SEQ = T_CTX + T_LAT
MA_SHARE_PSUM = False
CH = 64


def build_Ma(nrec, L, jlayer):
    ntile = L // 128
    nc = bass.Bass("TRN2", target_bir_lowering=False)
    P = Prog(nc)
    qd = P.dram("qT", [nrec, 128, L])
    zd = P.dram("zT", [nrec, 128, L])
    vd = P.dram("v", [nrec, L // CH, CH, 128])
    lbd = P.dram("lbr", [nrec, 128, 2])
    maskd = P.dram("mask", [CH, CH])
    identd = P.dram("ident", [128, 128])
    out = P.dram("oT", [nrec, 128, L], kind="ExternalOutput")

    ident_f = P.sb([128, 128])
    ident = P.sb([128, 128], BF16)
    mask = P.sb([CH, CH])
    ones = P.sb([128, CH])
    lbr = P.sb([128, nrec, 2])
    lb = P.sb([128, nrec])
    oml = P.sb([128, nrec])
    S = [P.sb([128, 128]) for _ in range(nrec)]
    Sb = [P.sb([128, 128], BF16) for _ in range(nrec)]
    NB = nrec
    zt = [P.sb([128, 128]) for _ in range(NB)]
    qt = [P.sb([128, 128]) for _ in range(NB)]
    vt = [P.sb([CH, 2, 128]) for _ in range(NB)]
    vb = [P.sb([CH, 2, 128], BF16) for _ in range(NB)]
    e1 = [P.sb([128, 128]) for _ in range(NB)]
    ft = [P.sb([128, 128]) for _ in range(NB)]
    gt = [P.sb([128, 128]) for _ in range(NB)]
    kq = [P.sb([128, 128]) for _ in range(NB)]
    bc = [P.sb([128, 128]) for _ in range(NB)]
    bm = [P.sb([128, 128]) for _ in range(NB)]
    be = [P.sb([128, 128]) for _ in range(NB)]
    ex = [[P.sb([128, 128]) for _ in range(4)] for _ in range(NB)]
    qh = [P.sb([128, 128], BF16) for _ in range(NB)]
    khI = [[P.sb([128, 128], BF16) for _ in range(4)] for _ in range(NB)]
    bk = [P.sb([128, 128]) for _ in range(NB)]
    qtl = [P.sb([128, 128], BF16) for _ in range(NB)]
    ktl = [P.sb([128, 128], BF16) for _ in range(NB)]
    gam = [P.sb([128, 2]) for _ in range(NB)]
    att_all = [[P.sb([CH, CH], BF16) for _ in range(2)] for _ in range(nrec)]
    ktm_all = [[P.sb([CH, 128], BF16) for _ in range(2)] for _ in range(nrec)]
    osb = [P.sb([128, 128]) for _ in range(NB)]
    pA_bank = [P.ps([128, 512]) for _ in range(2)]
    pO_bank = [P.ps([128, 512]) for _ in range(2)]
    pT_bank = [P.ps([128, 1024], BF16) for _ in range(2)]
    pS_bank = [P.ps([128, 512]) for _ in range(2)]

    P.dma("sp", ident_f[:], identd[:], writes=["identf"])
    P.op("dve", lambda e: e.tensor_copy(out=ident[:], in_=ident_f[:]), reads=["identf"], writes=["ident"])
    P.dma("sp", mask[:], maskd[:], writes=["mask"])
    P.op("pool", lambda e: e.memset(ones[:], 1.0), writes=["ones"])
    for p_ in range(2):
        P.op("dve", lambda e, p_=p_: e.memset(pA_bank[p_][:], 0.0), writes=[("pAb", p_)])
    for r_ in range(nrec):
        for p_ in range(2):
            P.op("dve", lambda e: e.engine_nop(), reads=[("pAb", p_)], writes=[("pA", r_, p_)]) if False else None
    for r in range(nrec):
        P.dma("sp", lbr[:, r, :], lbd[r], writes=["lbr"])
        P.op("pool", lambda e, r=r: e.memset(S[r][:], 0.0), writes=[("S", r)])
        P.op("pool", lambda e, r=r: e.memset(Sb[r][:], 0.0), writes=[("Sb", r)])
    if jlayer == 0:
        P.op("pool", lambda e: e.memset(lb[:], 0.0), writes=["lb"])
        P.op("pool", lambda e: e.memset(oml[:], 1.0), writes=["oml"])
    else:
        P.op("dve", lambda e: e.tensor_tensor(out=lb[:], in0=lbr[:, :, 0], in1=lbr[:, :, 1], op=ALU.subtract),
             reads=["lbr"], writes=["lb"])
        P.op("act", lambda e: e.activation(out=lb[:], in_=lb[:], func=AF.Exp), reads=["lb"], writes=["lb"])
        P.op("dve", lambda e: e.tensor_scalar(out=lb[:], in0=lb[:], scalar1=1.0, scalar2=None, op0=ALU.add),
             reads=["lb"], writes=["lb"])
        P.op("dve", lambda e: e.reciprocal(out=lb[:], in_=lb[:]), reads=["lb"], writes=["lb"])
        P.op("dve", lambda e: e.tensor_scalar(out=oml[:], in0=lb[:], scalar1=-1.0, scalar2=1.0, op0=ALU.mult, op1=ALU.add),
             reads=["lb"], writes=["oml"])

    QS = 128.0 ** -0.5
    it = 0
    ci = 0
    for t in range(ntile):
        ts = slice(t * 128, (t + 1) * 128)
        lists = []
        for r in range(nrec):
            b = r
            it += 1
            K = lambda nm, b=b: (nm, b)
            P.begin()
            P.dma("sp", zt[b][:], zd[r, :, ts], writes=[K("zt")])
            P.dma("sp", qt[b][:], qd[r, :, ts], writes=[K("qt")])
            P.dma("sp", vt[b][:], vd[r, 2 * t:2 * t + 2].rearrange("c s d -> s c d"), writes=[K("vt")])
            P.op("pool", lambda e, b=b: e.tensor_copy(out=vb[b][:], in_=vt[b][:]), reads=[K("vt")], writes=[K("vb")])
            P.op("act", lambda e, b=b: e.activation(out=e1[b][:], in_=zt[b][:], func=AF.Exp, scale=-1.0),
                 reads=[K("zt")], writes=[K("e1")])
            P.op("dve", lambda e, b=b: e.tensor_scalar(out=e1[b][:], in0=e1[b][:], scalar1=1.0, scalar2=None, op0=ALU.add),
                 reads=[K("e1")], writes=[K("e1")])
            P.op("dve", lambda e, b=b: e.reciprocal(out=ft[b][:], in_=e1[b][:]), reads=[K("e1")], writes=[K("ft")])
            if jlayer != 0:
                P.op("dve", lambda e, b=b, r=r: e.tensor_scalar(out=ft[b][:], in0=ft[b][:], scalar1=oml[:, r:r + 1],
                                                                scalar2=lb[:, r:r + 1], op0=ALU.mult, op1=ALU.add),
                     reads=[K("ft"), "oml", "lb"], writes=[K("ft")])
            P.op("act", lambda e, b=b: e.activation(out=gt[b][:], in_=ft[b][:], func=AF.Ln), reads=[K("ft")], writes=[K("gt")])
            P.op("pool", lambda e, b=b: e.tensor_scalar(out=kq[b][:], in0=ft[b][:], scalar1=-1.0, scalar2=1.0,
                                                        op0=ALU.mult, op1=ALU.add), reads=[K("ft")], writes=[K("kq")])
            for c in range(2):
                cs = slice(c * CH, (c + 1) * CH)
                P.op("dve", lambda e, b=b, cs=cs: e.tensor_tensor_scan(out=bc[b][:, cs], data0=ones[:], data1=gt[b][:, cs],
                                                                       initial=0.0, op0=ALU.mult, op1=ALU.add),
                     reads=[K("gt"), "ones"], writes=[K("bc")])
            bc3 = bc[b][:].rearrange("p (c s) -> p c s", s=CH)
            bc4 = bc[b][:].rearrange("p (c i s) -> p c i s", s=16, i=4)
            P.op("dve", lambda e, b=b, bc4=bc4: e.tensor_tensor(out=bm[b][:].rearrange("p (c i s) -> p c i s", s=16, i=4), in0=bc4,
                                                                in1=bc4[:, :, :, 0:1].to_broadcast([128, 2, 4, 16]),
                                                                op=ALU.subtract), reads=[K("bc")], writes=[K("bm")])
            P.op("dve", lambda e, b=b, bc3=bc3: e.tensor_tensor(out=be[b][:].rearrange("p (c s) -> p c s", s=CH), in0=bc3,
                                                                in1=bc3[:, :, CH - 1:CH].to_broadcast([128, 2, CH]),
                                                                op=ALU.subtract), reads=[K("bc")], writes=[K("be")])
            P.op("act", lambda e, b=b: e.activation(out=ex[b][0][:], in_=bm[b][:], func=AF.Exp), reads=[K("bm")], writes=[K("ex0")])
            P.op("act", lambda e, b=b: e.activation(out=ex[b][2][:], in_=bc[b][:], func=AF.Exp), reads=[K("bc")], writes=[K("ex2")])
            P.op("act", lambda e, b=b: e.activation(out=ex[b][3][:], in_=be[b][:], func=AF.Exp, scale=-1.0), reads=[K("be")], writes=[K("ex3")])
            P.op("act", lambda e, b=b, bc3=bc3: e.activation(out=gam[b][:], in_=bc3[:, :, CH - 1], func=AF.Exp),
                 reads=[K("bc")], writes=[K("gam")])
            P.op("dve", lambda e, b=b: e.scalar_tensor_tensor(out=qh[b][:], in0=qt[b][:], scalar=QS, in1=ex[b][0][:],
                                                              op0=ALU.mult, op1=ALU.mult), reads=[K("qt"), K("ex0")], writes=[K("qh")])
            kq3 = kq[b][:].rearrange("p (c s) -> p c s", s=CH)
            for I in range(4):
                n = 16 * (I + 1)
                bk3 = bk[b][:].rearrange("p (c s) -> p c s", s=CH)
                P.op("dve", lambda e, b=b, bc3=bc3, bk3=bk3, n=n, I=I: e.tensor_tensor(
                    out=bk3[:, :, :n], in0=bc3[:, :, :n], in1=bc3[:, :, 16 * I:16 * I + 1].to_broadcast([128, 2, n]),
                    op=ALU.subtract), reads=[K("bc")], writes=[K("bk")])
                P.op("act", lambda e, b=b, bk3=bk3, n=n: e.activation(out=bk3[:, :, :n], in_=bk3[:, :, :n], func=AF.Exp, scale=-1.0),
                     reads=[K("bk")], writes=[K("bk")])
                P.op("pool", lambda e, b=b, bk3=bk3, kq3=kq3, n=n, I=I: e.tensor_tensor(
                    out=khI[b][I][:].rearrange("p (c s) -> p c s", s=CH)[:, :, :n], in0=kq3[:, :, :n], in1=bk3[:, :, :n], op=ALU.mult),
                     reads=[K("kq"), K("bk")], writes=[K("kh")])
            P.op("dve", lambda e, b=b: e.scalar_tensor_tensor(out=qtl[b][:], in0=qt[b][:], scalar=QS, in1=ex[b][2][:],
                                                              op0=ALU.mult, op1=ALU.mult), reads=[K("qt"), K("ex2")], writes=[K("qtl")])
            P.op("pool", lambda e, b=b: e.tensor_tensor(out=ktl[b][:], in0=kq[b][:], in1=ex[b][3][:], op=ALU.mult),
                 reads=[K("kq"), K("ex3")], writes=[K("ktl")])
            for c in range(2):
                cs = slice(c * CH, (c + 1) * CH)
                ci += 1
                if MA_SHARE_PSUM:
                    p = (r, c)
                    pk = r
                    pA_t = pA_bank[c][:, r * CH:(r + 1) * CH]
                    pO_t = pO_bank[c][:, r * CH:(r + 1) * CH]
                    pT_t = pT_bank[c][:, r * 128:(r + 1) * 128]
                    pS_t = pS_bank[r // 4][:, (r % 4) * 128:(r % 4 + 1) * 128]
                else:
                    p = (r + c) % 2
                    pk = p
                    pA_t = pA_bank[p][:, 0:CH]
                    pO_t = pO_bank[p][:, 0:CH]
                    pT_t = pT_bank[p][:, 0:128]
                    pS_t = pS_bank[p][:, 0:128]
                att_t = att_all[r][c]
                ktm_t = ktm_all[r][c]
                P.atomic_begin()
                for I in range(4):
                    n = 16 * (I + 1)
                    P.op("pe", lambda e, b=b, c=c, p=p, I=I, n=n, pA_t=pA_t: e.matmul(pA_t[:n, 16 * I:16 * I + 16],
                                                                         lhsT=khI[b][I][:, c * CH:c * CH + n],
                                                                         rhs=qh[b][:, c * CH + 16 * I:c * CH + 16 * I + 16],
                                                                         start=True, stop=True),
                         reads=[K("kh"), K("qh")], writes=[("pA", p)])
                P.op("dve", lambda e, att_t=att_t, pA_t=pA_t: e.tensor_tensor(out=att_t[:], in0=pA_t[:CH, :], in1=mask[:], op=ALU.mult),
                     reads=[("pA", p), "mask"], writes=[("att", r, c)])
                P.atomic_end()
                P.atomic_begin()
                P.op("pe", lambda e, b=b, c=c, att_t=att_t, pO_t=pO_t: e.matmul(pO_t, lhsT=vb[b][:, c, :], rhs=att_t[:], start=True, stop=False),
                     reads=[K("vb"), ("att", r, c), ("Sb", r), K("qtl")], writes=[("pO", p)])
                P.op("pe", lambda e, b=b, cs=cs, r=r, pO_t=pO_t: e.matmul(pO_t, lhsT=Sb[r][:], rhs=qtl[b][:, cs], start=False, stop=True),
                     reads=[("Sb", r), K("qtl")], writes=[("pO", p)])
                P.op("act", lambda e, b=b, cs=cs, pO_t=pO_t: e.copy(out=osb[b][:, cs], in_=pO_t),
                     reads=[("pO", p)], writes=[K("osb")])
                P.atomic_end()
                P.atomic_begin()
                P.op("pe", lambda e, b=b, cs=cs, pT_t=pT_t: e.transpose(out=pT_t[:CH, :], in_=ktl[b][:, cs], identity=ident[:]),
                     reads=[K("ktl"), "ident"], writes=[("pT", p)])
                P.op("act", lambda e, ktm_t=ktm_t, pT_t=pT_t: e.copy(out=ktm_t[:], in_=pT_t[:CH, :]), reads=[("pT", p)], writes=[("ktm", r, c)])
                P.atomic_end()
                P.atomic_begin()
                P.op("pe", lambda e, b=b, c=c, ktm_t=ktm_t, pS_t=pS_t: e.matmul(pS_t, lhsT=ktm_t[:], rhs=vb[b][:, c, :], start=True, stop=True),
                     reads=[("ktm", r, c), K("vb")], writes=[("pS", pk)])
                P.op("dve", lambda e, b=b, c=c, r=r, pS_t=pS_t: e.scalar_tensor_tensor(out=S[r][:], in0=S[r][:], scalar=gam[b][:, c:c + 1],
                                                                                in1=pS_t, op0=ALU.mult, op1=ALU.add),
                     reads=[("S", r), K("gam"), ("pS", pk)], writes=[("S", r)])
                P.atomic_end()
                P.op("pool", lambda e, r=r: e.tensor_copy(out=Sb[r][:], in_=S[r][:]), reads=[("S", r)], writes=[("Sb", r)])
            P.dma("pool", out[r, :, ts], osb[b][:], reads=[K("osb")], writes=["out"])
            lists.append(P.end())
        P.interleave(lists)
    P.finish(["out"])
    return nc


def _seq(lat_b, ctx_b, rev):
    if rev:
        return np.concatenate([ctx_b[::-1], lat_b[::-1]], axis=0)
    return np.concatenate([ctx_b, lat_b], axis=0)


def _unseq(s, rev):
    c, l = s[:T_CTX], s[T_CTX:]
    if rev:
        return l[::-1], c[::-1]
    return l, c


def run_Ma(pl, pc, a_lb_raw, jlayer):
    nrec = 8
    nc = build_Ma(nrec, L_SEQ, jlayer)
    ident = np.eye(128, dtype=np.float32)
    mask = np.triu(np.ones((CH, CH), np.float32))
    in_maps = []
    for cid in range(NCORES):
        b, hg = cid // 4, cid % 4
        qT = np.empty((nrec, 128, L_SEQ), np.float32)
        zT = np.empty((nrec, 128, L_SEQ), np.float32)
        v = np.empty((nrec, L_SEQ // CH, CH, 128), np.float32)
        lbr = np.empty((nrec, 128, 2), np.float32)
        for hl in range(4):
            h = hg * 4 + hl
            hc = slice(h * 128, (h + 1) * 128)
            for dr in range(2):
                r = hl * 2 + dr
                zoff = 2 * D + dr * D
                sq = _seq(pl[b][:, hc], pc[b][:, hc], dr == 1)
                sv = _seq(pl[b][:, D + h * 128:D + (h + 1) * 128], pc[b][:, D + h * 128:D + (h + 1) * 128], dr == 1)
                sz = _seq(pl[b][:, zoff + h * 128:zoff + (h + 1) * 128], pc[b][:, zoff + h * 128:zoff + (h + 1) * 128], dr == 1)
                qT[r] = sq.T
                zT[r] = sz.T
                v[r] = sv.reshape(L_SEQ // CH, CH, 128)
                lbr[r] = a_lb_raw[:, hc].T
        in_maps.append({"qT": qT, "zT": zT, "v": v, "lbr": lbr, "mask": mask, "ident": ident})
    res = _run(nc, in_maps)
    o_lat = [np.empty((2, T_LAT, D), np.float32) for _ in range(2)]
    o_ctx = [np.empty((2, T_CTX, D), np.float32) for _ in range(2)]
    for cid in range(NCORES):
        b, hg = cid // 4, cid % 4
        oT = res[cid]["oT"]
        for hl in range(4):
            h = hg * 4 + hl
            for dr in range(2):
                l, c = _unseq(oT[hl * 2 + dr].T, dr == 1)
                o_lat[dr][b][:, h * 128:(h + 1) * 128] = l
                o_ctx[dr][b][:, h * 128:(h + 1) * 128] = c
    return o_lat, o_ctx


def layer_a(x, ctx, mod_i, norm_w_i, w_in, a_lb_raw, jlayer, onorm_w, w_out, need_ctx, final_w=None):
    pl, pc = run_P(x, ctx, mod_i, norm_w_i, w_in)
    o_lat, o_ctx = run_Ma(pl, pc, a_lb_raw, jlayer)
    ml = {"m0": o_lat[0], "m1": o_lat[1], "gate": pl[:, :, 4 * D:5 * D]}
    mc = {"m0": o_ctx[0], "m1": o_ctx[1], "gate": pc[:, :, 4 * D:5 * D]}
    bcs = {"onw": np.tile(onorm_w, 16)}
    if final_w is not None:
        bcs["fnw"] = final_w
    return run_O(x, ctx, "a", ml, mc, w_out, mod_i[0:2, 2 * D:3 * D], mod_i[2, 2 * D:3 * D], bcs,
                 final=final_w is not None, need_ctx=need_ctx)


NQH = 8
HD = 64
NBLK = T_LAT // 128


def build_Mb(need_ctx):
    nc = bass.Bass("TRN2", target_bir_lowering=False)
    P = Prog(nc)
    qd = P.dram("qT", [HD, NQH, T_LAT])
    qsd = P.dram("qsT", [HD, NQH, T_LAT])
    kd = P.dram("kT", [HD, T_LAT])
    ksd = P.dram("ksT", [HD, T_LAT])
    vd = P.dram("v", [128, NBLK, HD])
    qcd = P.dram("qcT", [HD, NQH, T_CTX])
    kcd = P.dram("kcT", [HD, T_CTX])
    vcd = P.dram("vc", [128, 2, HD])
    posd = P.dram("pos", [HD, T_LAT])
    fid = P.dram("fidx", [HD, 2])
    sinkd = P.dram("sink", [128, NQH])
    mld = P.dram("maskl", [128, 128])
    mrd = P.dram("maskr", [128, 128])
    n_out = T_LAT + (T_CTX if need_ctx else 0)
    out = P.dram("o", [n_out, NQH * HD], kind="ExternalOutput")

    PI = float(np.pi)
    CW = 2048
    cosT = P.sb([HD, T_LAT])
    sinT = P.sb([HD, T_LAT])
    posc = P.sb([HD, CW])
    tmpA = P.sb([HD, CW])
    tmpB = P.sb([HD, CW])
    fidx = P.sb([HD, 2])
    inv = P.sb([HD, 1])
    kr = P.sb([HD, T_LAT], BF16)
    kcb = P.sb([HD, T_CTX], BF16)
    kcf = P.sb([HD, T_CTX])
    vf = P.sb([128, NBLK, HD])
    vx = P.sb([128, NBLK, HD + 1], BF16)
    vcf = P.sb([128, 2, HD])
    vcx = P.sb([128, 2, HD + 1], BF16)
    esink = P.sb([128, NQH])
    ml = P.sb([128, 128], BF16)
    mr = P.sb([128, 128], BF16)
    mlf = P.sb([128, 128])
    mrf = P.sb([128, 128])
    qf = [P.sb([HD, NQH, 128]) for _ in range(2)]
    qsf = [P.sb([HD, NQH, 128]) for _ in range(2)]
    qt1 = P.sb([HD, NQH, 128])
    qt2 = P.sb([HD, NQH, 128])
    qr = [P.sb([HD, NQH, 128], BF16) for _ in range(2)]
    E = [P.sb([128, NQH, 128], BF16) for _ in range(5)]
    pS = [P.ps([128, 1024]) for _ in range(2)]
    pO = [P.ps([128, 4, 128]) for _ in range(2)]
    den = P.sb([128, NQH])
    osb = [P.sb([128, NQH, HD]) for _ in range(2)]

    P.dma("sp", fidx[:], fid[:], writes=["fidx"])
    P.dma("sp", esink[:], sinkd[:], writes=["esink"])
    P.dma("sp", mlf[:], mld[:], writes=["mlf"])
    P.dma("sp", mrf[:], mrd[:], writes=["mrf"])
    P.op("dve", lambda e: e.tensor_copy(out=ml[:], in_=mlf[:]), reads=["mlf"], writes=["ml"])
    P.op("dve", lambda e: e.tensor_copy(out=mr[:], in_=mrf[:]), reads=["mrf"], writes=["mr"])
    P.op("act", lambda e: e.activation(out=esink[:], in_=esink[:], func=AF.Exp), reads=["esink"], writes=["esink"])
    P.op("act", lambda e: e.activation(out=inv[:], in_=fidx[:, 0:1], func=AF.Exp, scale=-float(np.log(10000.0)) / 16.0),
         reads=["fidx"], writes=["inv"])

    def sincos(dst_full, shift, key, cc):
        cs_ = slice(cc * CW, (cc + 1) * CW)
        dst = dst_full[:, cs_]
        P.op("dve", lambda e: e.tensor_scalar(out=tmpA[:], in0=posc[:], scalar1=inv[:, 0:1], scalar2=shift, op0=ALU.mult, op1=ALU.add),
             reads=["pos", "inv"], writes=["tmpA"])
        ni = tmpB[:].bitcast(mybir.dt.int32)
        P.op("dve", lambda e: e.tensor_scalar(out=dst, in0=tmpA[:], scalar1=1.0 / (2 * PI), scalar2=0.5, op0=ALU.mult, op1=ALU.add),
             reads=["tmpA"], writes=[key])
        P.op("dve", lambda e: e.tensor_copy(out=ni, in_=dst), reads=[key], writes=["tmpB"])
        P.op("dve", lambda e: e.tensor_copy(out=dst, in_=ni), reads=["tmpB"], writes=[key])
        P.op("dve", lambda e: e.scalar_tensor_tensor(out=tmpA[:], in0=dst, scalar=-2 * PI, in1=tmpA[:], op0=ALU.mult, op1=ALU.add),
             reads=[key, "tmpA"], writes=["tmpA"])
        P.op("dve", lambda e: e.tensor_scalar(out=tmpB[:], in0=tmpA[:], scalar1=-PI, scalar2=2 * PI, op0=ALU.is_lt, op1=ALU.mult),
             reads=["tmpA"], writes=["tmpB"])
        P.op("dve", lambda e: e.tensor_tensor(out=tmpA[:], in0=tmpA[:], in1=tmpB[:], op=ALU.add), reads=["tmpA", "tmpB"], writes=["tmpA"])
        P.op("dve", lambda e: e.tensor_scalar(out=tmpB[:], in0=tmpA[:], scalar1=PI, scalar2=-2 * PI, op0=ALU.is_gt, op1=ALU.mult),
             reads=["tmpA"], writes=["tmpB"])
        P.op("dve", lambda e: e.tensor_tensor(out=tmpA[:], in0=tmpA[:], in1=tmpB[:], op=ALU.add), reads=["tmpA", "tmpB"], writes=["tmpA"])
        P.op("dve", lambda e: e.tensor_scalar(out=tmpA[:], in0=tmpA[:], scalar1=PI, scalar2=-PI, op0=ALU.min, op1=ALU.max),
             reads=["tmpA"], writes=["tmpA"])
        P.op("act", lambda e: e.activation(out=dst, in_=tmpA[:], func=AF.Sin), reads=["tmpA"], writes=[key])

    for cc in range(T_LAT // CW):
        P.dma("sp", posc[:], posd[:, cc * CW:(cc + 1) * CW], writes=["pos"])
        sincos(sinT, 0.0, "sinT", cc)
        sincos(cosT, PI / 2, "cosT", cc)
    P.op("dve", lambda e: e.tensor_scalar(out=sinT[:], in0=sinT[:], scalar1=fidx[:, 1:2], scalar2=None, op0=ALU.mult),
         reads=["sinT", "fidx"], writes=["sinT"])

    for cc in range(T_LAT // CW):
        cs_ = slice(cc * CW, (cc + 1) * CW)
        P.dma("sp", tmpA[:], kd[:, cs_], writes=["tmpA"])
        P.dma("sp", tmpB[:], ksd[:, cs_], writes=["tmpB"])
        P.op("dve", lambda e, cs_=cs_: e.tensor_tensor(out=tmpA[:], in0=tmpA[:], in1=cosT[:, cs_], op=ALU.mult), reads=["tmpA", "cosT"], writes=["tmpA"])
        P.op("pool", lambda e, cs_=cs_: e.tensor_tensor(out=tmpB[:], in0=tmpB[:], in1=sinT[:, cs_], op=ALU.mult), reads=["tmpB", "sinT"], writes=["tmpB"])
        P.op("dve", lambda e, cs_=cs_: e.tensor_tensor(out=kr[:, cs_], in0=tmpA[:], in1=tmpB[:], op=ALU.add), reads=["tmpA", "tmpB"], writes=["kr"])
    P.dma("sp", kcf[:], kcd[:], writes=["kcf"])
    P.op("dve", lambda e: e.tensor_copy(out=kcb[:], in_=kcf[:]), reads=["kcf"], writes=["kcb"])
    P.dma("sp", vf[:], vd[:], writes=["vf"])
    P.dma("sp", vcf[:], vcd[:], writes=["vcf"])
    P.op("pool", lambda e: e.memset(vx[:], 1.0), writes=["vx"])
    P.op("pool", lambda e: e.memset(vcx[:], 1.0), writes=["vcx"])
    P.op("pool", lambda e: e.tensor_copy(out=vx[:, :, 0:HD], in_=vf[:]), reads=["vf"], writes=["vx"])
    P.op("pool", lambda e: e.tensor_copy(out=vcx[:, :, 0:HD], in_=vcf[:]), reads=["vcf"], writes=["vcx"])

    blocks = [("lat", j) for j in range(NBLK)]
    if need_ctx:
        blocks += [("ctx", j) for j in range(2)]
    for bi, (typ, j) in enumerate(blocks):
        b = bi % 2
        ts = slice(j * 128, (j + 1) * 128)
        if typ == "lat":
            P.dma("sp", qf[b][:], qd[:, :, ts], writes=[("qf", b)])
            P.dma("sp", qsf[b][:], qsd[:, :, ts], writes=[("qsf", b)])
            P.op("dve", lambda e, b=b, ts=ts: e.tensor_tensor(out=qt1[:], in0=qf[b][:], in1=cosT[:, None, ts].to_broadcast([HD, NQH, 128]),
                                                            op=ALU.mult), reads=[("qf", b), "cosT"], writes=["qt1"])
            P.op("pool", lambda e, b=b, ts=ts: e.tensor_tensor(out=qt2[:], in0=qsf[b][:], in1=sinT[:, None, ts].to_broadcast([HD, NQH, 128]),
                                                             op=ALU.mult), reads=[("qsf", b), "sinT"], writes=["qt2"])
            P.op("dve", lambda e, b=b: e.tensor_tensor(out=qr[b][:], in0=qt1[:], in1=qt2[:], op=ALU.add),
                 reads=["qt1", "qt2"], writes=[("qr", b)])
            kbs = []
            if j > 0:
                kbs.append(("lat", j - 1, "ml"))
            kbs.append(("lat", j, None))
            if j < NBLK - 1:
                kbs.append(("lat", j + 1, "mr"))
            kbs += [("ctx", 0, None), ("ctx", 1, None)]
            orow = j * 128
        else:
            P.dma("sp", qf[b][:], qcd[:, :, ts], writes=[("qf", b)])
            P.op("dve", lambda e, b=b: e.tensor_copy(out=qr[b][:], in_=qf[b][:]), reads=[("qf", b)], writes=[("qr", b)])
            kbs = [("ctx", 0, None), ("ctx", 1, None)]
            orow = T_LAT + j * 128
        for ki, (kt, kj, mk) in enumerate(kbs):
            p = ki % 2
            ksl = slice(kj * 128, (kj + 1) * 128)
            kap = kr[:, ksl] if kt == "lat" else kcb[:, ksl]
            kkey = "kr" if kt == "lat" else "kcb"
            for hh in range(2):
                P.op("pe", lambda e, b=b, p=p, hh=hh, kap=kap: e.matmul(pS[p][:, hh * 512:(hh + 1) * 512], lhsT=kap,
                                                                      rhs=qr[b][:, hh * 4:(hh + 1) * 4, :], start=True, stop=True),
                     reads=[kkey, ("qr", b)], writes=[("pS", p)])
            P.op("act", lambda e, p=p, ki=ki: e.activation(out=E[ki][:].rearrange("p h q -> p (h q)"), in_=pS[p][:], func=AF.Exp, scale=HD ** -0.5),
                 reads=[("pS", p)], writes=[("E", ki)])
            if mk is not None:
                mt = ml if mk == "ml" else mr
                P.op("dve", lambda e, ki=ki, mt=mt: e.tensor_tensor(out=E[ki][:], in0=E[ki][:], in1=mt[:, None, :].to_broadcast([128, NQH, 128]),
                                                                   op=ALU.mult), reads=[("E", ki), mk], writes=[("E", ki)])
        nk = len(kbs)
        for h in range(NQH):
            po = h // 4
            for ki, (kt, kj, mk) in enumerate(kbs):
                vap = vx[:, kj, :] if kt == "lat" else vcx[:, kj, :]
                vkey = "vx" if kt == "lat" else "vcx"
                P.op("pe", lambda e, h=h, po=po, ki=ki, vap=vap: e.matmul(pO[po][:, h % 4, 0:HD + 1], lhsT=E[ki][:, h, :], rhs=vap,
                                                                        start=(ki == 0), stop=(ki == nk - 1)),
                     reads=[("E", ki), vkey], writes=[("pO", po)])
        for po in range(2):
            hs = slice(po * 4, (po + 1) * 4)
            P.op("dve", lambda e, po=po, hs=hs: e.tensor_tensor(out=den[:, hs], in0=pO[po][:, :, HD], in1=esink[:, hs], op=ALU.add),
                 reads=[("pO", po), "esink"], writes=["den"])
            P.op("dve", lambda e, hs=hs: e.reciprocal(out=den[:, hs], in_=den[:, hs]), reads=["den"], writes=["den"])
            P.op("dve", lambda e, po=po, hs=hs, b=b: e.tensor_tensor(out=osb[b][:, hs, :], in0=pO[po][:, :, 0:HD],
                                                                    in1=den[:, hs].unsqueeze(2).to_broadcast([128, 4, HD]), op=ALU.mult),
                 reads=[("pO", po), "den"], writes=[("osb", b)])
        P.dma("pool", out[orow:orow + 128, :], osb[b][:].rearrange("p h d -> p (h d)"), reads=[("osb", b)], writes=["out"])
    P.finish(["out"])
    return nc


def _rope_swap(a):
    hd = a.shape[-1]
    idx = np.arange(hd)
    idx = (idx // 32) * 32 + ((idx % 32) + 16) % 32
    return a[..., idx]


def run_Mb(pl, pc, sink, need_ctx):
    nc = build_Mb(need_ctx)
    t = np.arange(T_LAT)
    pos = np.empty((HD, T_LAT), np.float32)
    pos[:32] = (t // 64)[None, :]
    pos[32:] = (t % 64)[None, :]
    d = np.arange(HD)
    fidx = np.stack([(d % 16).astype(np.float32), np.where((d % 32) < 16, -1.0, 1.0).astype(np.float32)], axis=1)
    jj, ii = np.meshgrid(np.arange(128), np.arange(128), indexing="ij")
    maskl = (jj >= ii).astype(np.float32)
    maskr = (jj <= ii).astype(np.float32)
    in_maps = []
    for cid in range(NCORES):
        b, g = cid // 4, cid % 4
        q = pl[b][:, g * 512:(g + 1) * 512].reshape(T_LAT, NQH, HD)
        k = pl[b][:, D + g * HD:D + (g + 1) * HD]
        v = pl[b][:, D + 256 + g * HD:D + 256 + (g + 1) * HD]
        qc = pc[b][:, g * 512:(g + 1) * 512].reshape(T_CTX, NQH, HD)
        kc = pc[b][:, D + g * HD:D + (g + 1) * HD]
        vc = pc[b][:, D + 256 + g * HD:D + 256 + (g + 1) * HD]
        m = {"qT": np.ascontiguousarray(q.transpose(2, 1, 0)), "qsT": np.ascontiguousarray(_rope_swap(q).transpose(2, 1, 0)),
             "kT": np.ascontiguousarray(k.T), "ksT": np.ascontiguousarray(_rope_swap(k).T),
             "v": np.ascontiguousarray(v.reshape(NBLK, 128, HD).transpose(1, 0, 2)),
             "qcT": np.ascontiguousarray(qc.transpose(2, 1, 0)), "kcT": np.ascontiguousarray(kc.T),
             "vc": np.ascontiguousarray(vc.reshape(2, 128, HD).transpose(1, 0, 2)),
             "pos": pos, "fidx": fidx, "sink": _bc(sink[g * NQH:(g + 1) * NQH]), "maskl": maskl, "maskr": maskr}
        in_maps.append(m)
    res = _run(nc, in_maps)
    ol = np.empty((2, T_LAT, D), np.float32)
    oc = np.empty((2, T_CTX, D), np.float32) if need_ctx else None
    for cid in range(NCORES):
        b, g = cid // 4, cid % 4
        o = res[cid]["o"]
        ol[b][:, g * 512:(g + 1) * 512] = o[:T_LAT]
        if need_ctx:
            oc[b][:, g * 512:(g + 1) * 512] = o[T_LAT:]
    return ol, oc


def layer_b(x, ctx, mod_i, norm_w_i, w_in, sink, w_out, need_ctx, final_w=None):
    pl, pc = run_P(x, ctx, mod_i, norm_w_i, w_in)
    ol, oc = run_Mb(pl, pc, sink, need_ctx)
    ml = {"m0": ol, "gate": pl[:, :, D + 512:]}
    mc = {"m0": oc, "gate": pc[:, :, D + 512:]}
    bcs = {}
    if final_w is not None:
        bcs["fnw"] = final_w
    return run_O(x, ctx, "b", ml, mc, w_out, mod_i[0:2, 2 * D:3 * D], mod_i[2, 2 * D:3 * D], bcs,
                 final=final_w is not None, need_ctx=need_ctx)


NH_C = 8
MC_INTERLEAVE = False
MC_FP32R = True
MC_PIPE = True


def R32(ap):
    return ap.bitcast(mybir.dt.float32r) if MC_FP32R else ap
HC = 64
LORA = 96


def build_Mc(L):
    ntile = L // 128
    W = NH_C * HC
    nc = bass.Bass("TRN2", target_bir_lowering=False)
    P = Prog(nc)
    rd = P.dram("r", [2, L, W])
    kd = P.dram("k", [2, L, W])
    vd = P.dram("v", [2, L, W])
    lwd = P.dram("lwT", [2, LORA, L])
    lad = P.dram("laT", [2, LORA, L])
    w2d = P.dram("w2", [2, LORA, W])
    a2d = P.dram("a2", [2, LORA, W])
    w0d = P.dram("w0b", [2, 128, W])
    a0d = P.dram("a0b", [2, 128, W])
    kkd = P.dram("kkb", [128, W])
    kad = P.dram("kab", [128, W])
    rkd = P.dram("rkb", [128, W])
    trid = P.dram("tri", [128, 128])
    mupd = P.dram("mup", [128, 128])
    mlod = P.dram("mlo", [128, 128])
    identd = P.dram("ident", [128, 128])
    seld = P.dram("sel", [128, 1])
    yout = P.dram("y", [2, L, W], kind="ExternalOutput")
    bout = P.dram("bon", [2, L, W], kind="ExternalOutput")

    def T2(n, dt=F32, shape=(128, W)):
        return [P.sb(list(shape), dt) for _ in range(n)]

    ident = P.sb([128, 128])
    identb = P.sb([128, 128], BF16)
    tri = P.sb([128, 128])
    mup = P.sb([128, 128])
    mupi = P.sb([128, 128])
    mlo = P.sb([128, 128])
    sel = P.sb([128, 1])
    w2 = T2(2, F32, (LORA, W))
    a2 = T2(2, F32, (LORA, W))
    w0b = T2(2)
    a0b = T2(2)
    kkb = P.sb([128, W])
    kab = P.sb([128, W])
    omka = P.sb([128, W])
    rkb = P.sb([128, W])
    rt, kt, vt = T2(2), T2(2), T2(2)
    lw = T2(2, F32, (LORA, 128))
    la = T2(2, F32, (LORA, 128))
    th_ = T2(2, F32, (LORA, 128))
    zt_, ld_, iclr_, kkr_, kk_, t1_, kdt_ = T2(2), T2(2), T2(2), T2(2), T2(2), T2(2), T2(2)
    ein_, eneg_ = T2(2), T2(2)
    st8_ = [[P.sb([128, NH_C]) for _ in range(3)] for _ in range(2)]
    Ah_, Bh_, Kh_, Rh_, Vb_ = T2(4, BF16), T2(4, BF16), T2(4, BF16), T2(4, BF16), T2(4, BF16)
    XT_ = [{nm: P.sb([HC, NH_C, 128], BF16) for nm in ("a", "b", "k", "r")} for _ in range(4)]
    gamT_ = [P.sb([HC, NH_C]) for _ in range(4)]
    Nf_ = [[[P.sb([128, 4, 128]) for _ in range(2)] for _ in range(2)] for _ in range(2)]
    NTf_ = [[[P.sb([128, 4, 128]) for _ in range(2)] for _ in range(2)] for _ in range(2)]
    TT_ = [[P.sb([128, 4, 128]) for _ in range(2)] for _ in range(2)]
    TTb_ = [[P.sb([128, 4, 128], BF16) for _ in range(2)] for _ in range(2)]
    AakT_ = [[P.sb([128, 4, 128], BF16) for _ in range(2)] for _ in range(2)]
    AVb_ = [[P.sb([128, 4, HC], BF16) for _ in range(2)] for _ in range(2)]
    ArbT_ = [P.sb([128, NH_C, 128], BF16) for _ in range(2)]
    ArkT_ = [P.sb([128, NH_C, 128], BF16) for _ in range(2)]
    TAb_ = [P.sb([128, NH_C, HC], BF16) for _ in range(2)]
    TVb_ = [P.sb([128, NH_C, HC], BF16) for _ in range(2)]
    MTb_ = [P.sb([HC, NH_C, HC], BF16) for _ in range(2)]
    RQTb_ = [P.sb([HC, NH_C, 128], BF16) for _ in range(2)]
    Pb = [P.sb([HC, NH_C, HC], BF16) for _ in range(2)]
    ysb = T2(2, F32, (128, NH_C, HC))
    pz = P.ps([128, 512])
    ptr = [P.ps([128, 1024], BF16) for _ in range(2)]
    big = [P.ps([128, 4, 128]) for _ in range(2)]
    py = P.ps([128, NH_C, HC])
    pp = P.ps([128, NH_C, HC])
    pg = P.ps([128, 512])

    for (t_, d_, k_) in [(ident, identd, "ident"), (tri, trid, "tri"), (mup, mupd, "mup"), (mlo, mlod, "mlo"), (sel, seld, "sel"),
                         (kkb, kkd, "kkb"), (kab, kad, "kab"), (rkb, rkd, "rkb")]:
        P.dma("sp", t_[:], d_[:], writes=[k_])
    for d in range(2):
        P.dma("sp", w2[d][:], w2d[d], writes=[("w2", d)])
        P.dma("sp", a2[d][:], a2d[d], writes=[("a2", d)])
        P.dma("sp", w0b[d][:], w0d[d], writes=[("w0b", d)])
        P.dma("sp", a0b[d][:], a0d[d], writes=[("a0b", d)])
        P.op("pool", lambda e, d=d: e.memset(Pb[d][:], 0.0), writes=[("Pb", d)])
    P.op("dve", lambda e: e.tensor_copy(out=identb[:], in_=ident[:]), reads=["ident"], writes=["identb"])
    P.op("dve", lambda e: e.tensor_tensor(out=mupi[:], in0=mup[:], in1=ident[:], op=ALU.add), reads=["mup", "ident"], writes=["mupi"])
    P.op("dve", lambda e: e.tensor_scalar(out=omka[:], in0=kab[:], scalar1=-1.0, scalar2=None, op0=ALU.mult),
         reads=["kab"], writes=["omka"])
    P.op("dve", lambda e: e.tensor_scalar(out=omka[:], in0=omka[:], scalar1=1.0, scalar2=None, op0=ALU.add),
         reads=["omka"], writes=["omka"])

    bi = [0]

    def emit_dir(t, d):
        rows = slice(t * 128, (t + 1) * 128)
        b = d
        dp = d * 2 + (t % 2)
        IFACE = ("Ah", "Bh", "Kh", "Rh", "Vb", "XTa", "XTb", "XTk", "XTr", "gamT")
        K = lambda nm: (nm, dp) if nm in IFACE else (nm, d)
        th, zt, ld, iclr, kkr, kk, t1, kdt = th_[d], zt_[d], ld_[d], iclr_[d], kkr_[d], kk_[d], t1_[d], kdt_[d]
        bt = kkr
        sq = zt
        ein, eneg, st8 = ein_[d], eneg_[d], st8_[d]
        eex = zt
        Ah, Bh, Kh, Rh, Vb, XT, gamT = Ah_[dp], Bh_[dp], Kh_[dp], Rh_[dp], Vb_[dp], XT_[dp], gamT_[dp]
        ArbT, ArkT, TAb, TVb, MTb, RQTb = ArbT_[d], ArkT_[d], TAb_[d], TVb_[d], MTb_[d], RQTb_[d]
        main = []
        P._defer = main

        def sigmoid_tail(dst, key):
            P.op("act", lambda e: e.activation(out=zt[:], in_=zt[:], func=AF.Exp, scale=-1.0), reads=[K("zt")], writes=[K("zt")])
            P.op("dve", lambda e: e.tensor_scalar(out=zt[:], in0=zt[:], scalar1=1.0, scalar2=None, op0=ALU.add), reads=[K("zt")], writes=[K("zt")])
            P.op("dve", lambda e: e.reciprocal(out=dst, in_=zt[:]), reads=[K("zt")], writes=[key])

        P.dma("sp", rt[b][:], rd[d, rows, :], writes=[K("rt")])
        P.dma("sp", kt[b][:], kd[d, rows, :], writes=[K("kt")])
        P.dma("sp", vt[b][:], vd[d, rows, :], writes=[K("vt")])
        P.dma("sp", lw[b][:], lwd[d, :, rows], writes=[K("lw")])
        P.dma("sp", la[b][:], lad[d, :, rows], writes=[K("la")])
        P.op("act", lambda e: e.activation(out=th[:], in_=lw[b][:], func=AF.Tanh), reads=[K("lw")], writes=[K("th")])
        P.atomic_begin()
        P.op("pe", lambda e: e.matmul(pz[:], lhsT=th[:], rhs=w2[d][:], start=True, stop=True),
             reads=[K("th"), ("w2", d)], writes=["pz"])
        P.op("dve", lambda e: e.tensor_tensor(out=zt[:], in0=pz[:], in1=w0b[d][:], op=ALU.add), reads=["pz", ("w0b", d)], writes=[K("zt")])
        P.atomic_end()
        sigmoid_tail(ld[:], K("ld"))
        P.op("pool", lambda e: e.tensor_scalar(out=ld[:], in0=ld[:], scalar1=-float(np.exp(-0.5)), scalar2=None, op0=ALU.mult),
             reads=[K("ld")], writes=[K("ld")])
        P.atomic_begin()
        P.op("pe", lambda e: e.matmul(pz[:], lhsT=la[b][:], rhs=a2[d][:], start=True, stop=True),
             reads=[K("la"), ("a2", d)], writes=["pz"])
        P.op("dve", lambda e: e.tensor_tensor(out=zt[:], in0=pz[:], in1=a0b[d][:], op=ALU.add), reads=["pz", ("a0b", d)], writes=[K("zt")])
        P.atomic_end()
        sigmoid_tail(iclr[:], K("iclr"))
        h3 = lambda ap: ap.rearrange("p (h c) -> p h c", c=HC)
        P.op("pool", lambda e: e.tensor_tensor(out=kkr[:], in0=kt[b][:], in1=kkb[:], op=ALU.mult), reads=[K("kt"), "kkb"], writes=[K("kkr")])
        P.op("pool", lambda e: e.tensor_tensor(out=sq[:], in0=kkr[:], in1=kkr[:], op=ALU.mult), reads=[K("kkr")], writes=[K("zt")])
        P.op("dve", lambda e: e.tensor_reduce(out=st8[0][:], in_=h3(sq[:]), axis=AX.X, op=ALU.add), reads=[K("zt")], writes=[K("st0")])
        P.op("act", lambda e: e.activation(out=st8[0][:], in_=st8[0][:], func=AF.Sqrt), reads=[K("st0")], writes=[K("st0")])
        P.op("dve", lambda e: e.tensor_scalar(out=st8[0][:], in0=st8[0][:], scalar1=1e-12, scalar2=None, op0=ALU.max), reads=[K("st0")], writes=[K("st0")])
        P.op("dve", lambda e: e.reciprocal(out=st8[1][:], in_=st8[0][:]), reads=[K("st0")], writes=[K("st1")])
        P.op("dve", lambda e: e.tensor_tensor(out=h3(kk[:]), in0=h3(kkr[:]), in1=st8[1][:].unsqueeze(2).to_broadcast([128, NH_C, HC]), op=ALU.mult),
             reads=[K("kkr"), K("st1")], writes=[K("kk")])
        P.op("dve", lambda e: e.tensor_tensor(out=t1[:], in0=iclr[:], in1=kab[:], op=ALU.mult), reads=[K("iclr"), "kab"], writes=[K("t1")])
        P.op("pool", lambda e: e.tensor_tensor(out=t1[:], in0=t1[:], in1=omka[:], op=ALU.add), reads=[K("t1"), "omka"], writes=[K("t1")])
        P.op("dve", lambda e: e.tensor_tensor(out=kdt[:], in0=kt[b][:], in1=t1[:], op=ALU.mult), reads=[K("kt"), K("t1")], writes=[K("kdt")])
        P.op("pool", lambda e: e.tensor_tensor(out=bt[:], in0=kk[:], in1=iclr[:], op=ALU.mult), reads=[K("kk"), K("iclr")], writes=[K("kkr")])
        P.op("dve", lambda e: e.tensor_tensor(out=t1[:], in0=rt[b][:], in1=kdt[:], op=ALU.mult), reads=[K("rt"), K("kdt"), K("t1")], writes=[K("t1")])
        P.op("pool", lambda e: e.tensor_tensor(out=t1[:], in0=t1[:], in1=rkb[:], op=ALU.mult), reads=[K("t1"), "rkb"], writes=[K("t1")])
        P.op("dve", lambda e: e.tensor_reduce(out=st8[2][:], in_=h3(t1[:]), axis=AX.X, op=ALU.add), reads=[K("t1")], writes=[K("st2")])
        P.op("dve", lambda e: e.tensor_tensor(out=h3(t1[:]), in0=h3(vt[b][:]), in1=st8[2][:].unsqueeze(2).to_broadcast([128, NH_C, HC]), op=ALU.mult),
             reads=[K("vt"), K("st2"), K("t1")], writes=[K("t1")])
        P.dma("pool", bout[d, rows, :], t1[:], reads=[K("t1")], writes=["bout"])
        P.atomic_begin()
        P.op("pe", lambda e: e.matmul(pz[:], lhsT=tri[:], rhs=ld[:], start=True, stop=True), reads=["tri", K("ld")], writes=["pz"])
        P.op("act", lambda e: e.activation(out=ein[:], in_=pz[:], func=AF.Exp), reads=["pz"], writes=[K("ein")])
        P.op("act", lambda e: e.activation(out=eneg[:], in_=pz[:], func=AF.Exp, scale=-1.0), reads=["pz"], writes=[K("eneg")])
        P.op("dve", lambda e: e.tensor_tensor(out=eex[:], in0=pz[:], in1=ld[:], op=ALU.subtract), reads=["pz", K("ld")], writes=[K("zt")])
        P.atomic_end()
        P.op("act", lambda e: e.activation(out=eex[:], in_=eex[:], func=AF.Exp), reads=[K("zt")], writes=[K("zt")])
        P.op("dve", lambda e: e.scalar_tensor_tensor(out=Ah[:], in0=kk[:], scalar=-1.0, in1=eex[:], op0=ALU.mult, op1=ALU.mult),
             reads=[K("kk"), K("zt")], writes=[K("Ah")])
        P.op("pool", lambda e: e.tensor_tensor(out=Bh[:], in0=bt[:], in1=eneg[:], op=ALU.mult), reads=[K("kkr"), K("eneg")], writes=[K("Bh")])
        P.op("dve", lambda e: e.tensor_tensor(out=Kh[:], in0=kdt[:], in1=eneg[:], op=ALU.mult), reads=[K("kdt"), K("eneg")], writes=[K("Kh")])
        P.op("pool", lambda e: e.tensor_tensor(out=Rh[:], in0=rt[b][:], in1=ein[:], op=ALU.mult), reads=[K("rt"), K("ein")], writes=[K("Rh")])
        P.op("act", lambda e: e.copy(out=Vb[:], in_=vt[b][:]), reads=[K("vt")], writes=[K("Vb")])
        P.atomic_begin()
        for h in range(NH_C):
            P.op("pe", lambda e, h=h: e.matmul(pg[:HC, h:h + 1], lhsT=ein[:, h * HC:(h + 1) * HC], rhs=sel[:], start=True, stop=True),
                 reads=[K("ein"), "sel"], writes=["pg"])
        P.op("act", lambda e: e.copy(out=gamT[:], in_=pg[:HC, :NH_C]), reads=["pg"], writes=[K("gamT")])
        P.atomic_end()
        for xi, (nm, src_t, skey) in enumerate([("a", Ah, "Ah"), ("b", Bh, "Bh"), ("k", Kh, "Kh"), ("r", Rh, "Rh")]):
            pt = ptr[xi % 2]
            P.atomic_begin()
            for h in range(NH_C):
                P.op("pe", lambda e, h=h, pt=pt, src_t=src_t: e.transpose(out=pt[:HC, h * 128:(h + 1) * 128], in_=src_t[:, h * HC:(h + 1) * HC],
                                                                          identity=identb[:]),
                     reads=[K(skey), "identb"], writes=[("ptr", xi % 2)])
            if xi % 2 == 0:
                P.op("act", lambda e, nm=nm, pt=pt: e.copy(out=XT[nm][:].rearrange("p h t -> p (h t)"), in_=pt[:HC, :]),
                     reads=[("ptr", xi % 2)], writes=[K("XT" + nm)])
                P.atomic_end()
            else:
                P.op("dve", lambda e, nm=nm, pt=pt: e.tensor_copy(out=XT[nm][:].rearrange("p h t -> p (h t)"), in_=pt[:HC, :]),
                     reads=[("ptr", xi % 2)], writes=[K("XT" + nm)])
                P.atomic_end()
        qlists = []

        def emit_quad(qd_):
            ql = []
            P._defer = ql
            qlists.append(ql)
            Q = lambda nm, qd_=qd_: (nm, d, qd_)
            hs = [qd_ * 4 + i for i in range(4)]
            Nf, NTf, TT, TTb, AakT, AVb = Nf_[d][qd_], NTf_[d][qd_], TT_[d][qd_], TTb_[d][qd_], AakT_[d][qd_], AVb_[d][qd_]

            def mm4(lhs_fn, rhs_fn, rows_, cols_, rkeys, hs=hs):
                p = bi[0] % 2
                bi[0] += 1
                P.atomic_begin()
                for i, h in enumerate(hs):
                    P.op("pe", lambda e, i=i, h=h, p=p: e.matmul(big[p][:rows_, i, :cols_], lhsT=lhs_fn(i, h), rhs=rhs_fn(i, h),
                                                               start=True, stop=True), reads=rkeys, writes=[("big", p)])
                return p

            def EV(*a_, **k_):
                P.op(*a_, **k_)
                P.atomic_end()

            msk = lambda m_: m_[:, None, :].to_broadcast([128, 4, 128])
            p = mm4(lambda i, h: XT["a"][:, h, :], lambda i, h: XT["b"][:, h, :], 128, 128, [K("XTa"), K("XTb")])
            EV("dve", lambda e, p=p: e.tensor_tensor(out=R32(Nf[0][:]), in0=big[p][:], in1=msk(mlo), op=ALU.mult),
                 reads=[("big", p), "mlo"], writes=[Q("Nf0")])
            p = mm4(lambda i, h: XT["b"][:, h, :], lambda i, h: XT["a"][:, h, :], 128, 128, [K("XTa"), K("XTb")])
            EV("dve", lambda e, p=p: e.tensor_tensor(out=R32(NTf[0][:]), in0=big[p][:], in1=msk(mup), op=ALU.mult),
                 reads=[("big", p), "mup"], writes=[Q("NTf0")])
            P.op("dve", lambda e: e.tensor_tensor(out=R32(TT[:]), in0=NTf[0][:], in1=msk(ident), op=ALU.add),
                 reads=[Q("NTf0"), "ident"], writes=[Q("TT")])
            p = mm4(lambda i, h: XT["k"][:, h, :], lambda i, h: XT["a"][:, h, :], 128, 128, [K("XTa"), K("XTk")])
            EV("dve", lambda e, p=p: e.tensor_tensor(out=AakT[:], in0=big[p][:], in1=msk(mup), op=ALU.mult),
                 reads=[("big", p), "mup"], writes=[Q("AakT")])
            p = mm4(lambda i, h: XT["b"][:, h, :], lambda i, h: XT["r"][:, h, :], 128, 128, [K("XTr"), K("XTb")])
            EV("dve", lambda e, p=p, qd_=qd_: e.tensor_tensor(out=ArbT[:, qd_ * 4:(qd_ + 1) * 4, :], in0=big[p][:], in1=msk(mupi), op=ALU.mult),
                 reads=[("big", p), "mupi"], writes=[Q("ArbT")])
            p = mm4(lambda i, h: XT["k"][:, h, :], lambda i, h: XT["r"][:, h, :], 128, 128, [K("XTr"), K("XTk")])
            EV("dve", lambda e, p=p, qd_=qd_: e.tensor_tensor(out=ArkT[:, qd_ * 4:(qd_ + 1) * 4, :], in0=big[p][:], in1=msk(mupi), op=ALU.mult),
                 reads=[("big", p), "mupi"], writes=[Q("ArkT")])
            cur = 0
            for lvl in range(1, 7):
                nxt = 1 - cur
                p = mm4(lambda i, h, cur=cur: R32(NTf[cur][:, i, :]), lambda i, h, cur=cur: R32(Nf[cur][:, i, :]), 128, 128, [Q("Nf%d" % cur), Q("NTf%d" % cur)])
                EV("act", lambda e, p=p, nxt=nxt: e.copy(out=R32(Nf[nxt][:]), in_=big[p][:]), reads=[("big", p)], writes=[Q("Nf%d" % nxt)])
                if lvl < 6:
                    p = mm4(lambda i, h, cur=cur: R32(Nf[cur][:, i, :]), lambda i, h, cur=cur: R32(NTf[cur][:, i, :]), 128, 128, [Q("Nf%d" % cur), Q("NTf%d" % cur)])
                    EV("act", lambda e, p=p, nxt=nxt: e.copy(out=R32(NTf[nxt][:]), in_=big[p][:]), reads=[("big", p)], writes=[Q("NTf%d" % nxt)])
                p = mm4(lambda i, h, nxt=nxt: R32(Nf[nxt][:, i, :]), lambda i, h: R32(TT[:, i, :]), 128, 128, [Q("Nf%d" % nxt), Q("TT")])
                EV("dve", lambda e, p=p: e.tensor_tensor(out=R32(TT[:]), in0=big[p][:], in1=TT[:], op=ALU.add), reads=[("big", p), Q("TT")], writes=[Q("TT")])
                cur = nxt
            P.op("act", lambda e: e.copy(out=TTb[:], in_=TT[:]), reads=[Q("TT")], writes=[Q("TTb")])
            qs = slice(qd_ * 4, (qd_ + 1) * 4)
            p = mm4(lambda i, h: TTb[:, i, :], lambda i, h: Ah[:, h * HC:(h + 1) * HC], 128, HC, [Q("TTb"), K("Ah")])
            EV("act", lambda e, p=p, qs=qs: e.copy(out=TAb[:, qs, :], in_=big[p][:, :, :HC]), reads=[("big", p)], writes=[Q("TAb")])
            p = mm4(lambda i, h: AakT[:, i, :], lambda i, h: Vb[:, h * HC:(h + 1) * HC], 128, HC, [Q("AakT"), K("Vb")])
            EV("dve", lambda e, p=p: e.tensor_copy(out=AVb[:], in_=big[p][:, :, :HC]), reads=[("big", p)], writes=[Q("AVb")])
            p = mm4(lambda i, h: TTb[:, i, :], lambda i, h: AVb[:, i, :], 128, HC, [Q("TTb"), Q("AVb")])
            EV("act", lambda e, p=p, qs=qs: e.copy(out=TVb[:, qs, :], in_=big[p][:, :, :HC]), reads=[("big", p)], writes=[Q("TVb")])
            p = mm4(lambda i, h: TAb[:, h, :], lambda i, h: Bh[:, h * HC:(h + 1) * HC], HC, HC, [Q("TAb"), K("Bh")])
            EV("dve", lambda e, p=p, qs=qs: e.tensor_tensor(out=MTb[:, qs, :], in0=big[p][:HC, :, :HC],
                                                            in1=ident[:HC, None, :HC].to_broadcast([HC, 4, HC]), op=ALU.add),
                 reads=[("big", p), "ident"], writes=[Q("MTb")])
            p = mm4(lambda i, h: TAb[:, h, :], lambda i, h: ArbT[:, h, :], HC, 128, [Q("TAb"), Q("ArbT")])
            EV("dve", lambda e, p=p, qs=qs: e.tensor_tensor(out=RQTb[:, qs, :], in0=big[p][:HC, :, :], in1=XT["r"][:, qs, :], op=ALU.add),
                 reads=[("big", p), K("XTr")], writes=[Q("RQTb")])
        for qd_ in range(2):
            emit_quad(qd_)
        tail = []
        P._defer = tail
        allq = lambda nm: [(nm, d, 0), (nm, d, 1)]
        P.atomic_begin()
        for h in range(NH_C):
            hc = slice(h * HC, (h + 1) * HC)
            P.op("pe", lambda e, h=h: e.matmul(py[:, h, :], lhsT=ArbT[:, h, :], rhs=TVb[:, h, :], start=True, stop=False),
                 reads=allq("ArbT") + allq("TVb") + allq("ArkT") + allq("RQTb") + [K("Vb"), ("Pb", d)], writes=["py"])
            P.op("pe", lambda e, h=h, hc=hc: e.matmul(py[:, h, :], lhsT=ArkT[:, h, :], rhs=Vb[:, hc], start=False, stop=False),
                 reads=[], writes=["py"])
            P.op("pe", lambda e, h=h: e.matmul(py[:, h, :], lhsT=RQTb[:, h, :], rhs=Pb[d][:, h, :], start=False, stop=True),
                 reads=[("Pb", d)], writes=["py"])
        P.op("act", lambda e: e.copy(out=ysb[b][:], in_=py[:]), reads=["py"], writes=[K("ysb")])
        P.atomic_end()
        P.dma("pool", yout[d, rows, :], ysb[b][:].rearrange("p h c -> p (h c)"), reads=[K("ysb")], writes=["yout"])
        P.atomic_begin()
        for h in range(NH_C):
            hc = slice(h * HC, (h + 1) * HC)
            P.op("pe", lambda e, h=h, hc=hc: e.matmul(pp[:HC, h, :], lhsT=Bh[:, hc], rhs=TVb[:, h, :], start=True, stop=False),
                 reads=allq("TVb") + allq("MTb") + [K("Bh"), K("Kh"), K("Vb"), ("Pb", d)], writes=["pp"])
            P.op("pe", lambda e, h=h, hc=hc: e.matmul(pp[:HC, h, :], lhsT=Kh[:, hc], rhs=Vb[:, hc], start=False, stop=False),
                 reads=[], writes=["pp"])
            P.op("pe", lambda e, h=h: e.matmul(pp[:HC, h, :], lhsT=MTb[:, h, :], rhs=Pb[d][:, h, :], start=False, stop=True),
                 reads=[("Pb", d)], writes=["pp"])
        P.op("dve", lambda e: e.tensor_tensor(out=Pb[d][:], in0=pp[:HC, :, :], in1=gamT[:].unsqueeze(2).to_broadcast([HC, NH_C, HC]),
                                              op=ALU.mult), reads=["pp", K("gamT")], writes=[("Pb", d)])
        P.atomic_end()
        P._defer = None
        return main, qlists, tail

    for t in range(ntile):
        if t == 0:
            nxt_parts = [emit_dir(0, d) for d in range(2)]
            for d in range(2):
                P.interleave([nxt_parts[d][0]])
        parts = nxt_parts
        streams = parts[0][1] + parts[1][1]
        if t + 1 < ntile:
            nxt_parts = [emit_dir(t + 1, d) for d in range(2)]
            if MC_PIPE:
                streams = streams + [nxt_parts[0][0] + nxt_parts[1][0]]
        P.interleave(streams)
        if t + 1 < ntile and not MC_PIPE:
            for d in range(2):
                P.interleave([nxt_parts[d][0]])
        for d in range(2):
            P.interleave([parts[d][2]])
    P.finish(["yout", "bout"])
    return nc


def run_Mc(pl, pc, prm):
    W = NH_C * HC
    nc = build_Mc(L_SEQ)
    tt, ss = np.meshgrid(np.arange(128), np.arange(128), indexing="xy")
    consts = {"tri": (ss <= tt).astype(np.float32), "mup": (ss < tt).astype(np.float32), "mlo": (ss > tt).astype(np.float32),
              "ident": np.eye(128, dtype=np.float32), "sel": (np.arange(128) == 127).astype(np.float32)[:, None]}
    rk_flat = prm["r_k"].reshape(-1)
    in_maps = []
    for cid in range(NCORES):
        b, hg = cid // 4, cid % 4
        cols = slice(hg * W, (hg + 1) * W)
        m = dict(consts)
        for nm, off in (("r", 0), ("k", D), ("v", 2 * D)):
            m[nm] = np.stack([_seq(pl[b][:, off + hg * W:off + (hg + 1) * W], pc[b][:, off + hg * W:off + (hg + 1) * W], dr == 1)
                              for dr in range(2)])
        m["lwT"] = np.stack([np.ascontiguousarray(_seq(pl[b][:, 4 * D + dr * 128:4 * D + dr * 128 + LORA],
                                                       pc[b][:, 4 * D + dr * 128:4 * D + dr * 128 + LORA], dr == 1).T) for dr in range(2)])
        m["laT"] = np.stack([np.ascontiguousarray(_seq(pl[b][:, 4 * D + 256 + dr * 128:4 * D + 256 + dr * 128 + LORA],
                                                       pc[b][:, 4 * D + 256 + dr * 128:4 * D + 256 + dr * 128 + LORA], dr == 1).T) for dr in range(2)])
        m["w2"] = np.ascontiguousarray(prm["w2"][:, :, cols])
        m["a2"] = np.ascontiguousarray(prm["a2"][:, :, cols])
        m["w0b"] = np.stack([_bc(prm["w0"][dr, cols]) for dr in range(2)])
        m["a0b"] = np.stack([_bc(prm["a0"][dr, cols]) for dr in range(2)])
        m["kkb"] = _bc(prm["k_k"][cols])
        m["kab"] = _bc(prm["k_a"][cols])
        m["rkb"] = _bc(rk_flat[cols])
        in_maps.append(m)
    res = _run(nc, in_maps)
    names = ["m0", "m1", "m2", "m3"]
    lat = {nm: np.empty((2, T_LAT, D), np.float32) for nm in names}
    cx = {nm: np.empty((2, T_CTX, D), np.float32) for nm in names}
    for cid in range(NCORES):
        b, hg = cid // 4, cid % 4
        cols = slice(hg * W, (hg + 1) * W)
        for dr in range(2):
            for key, nm in (("y", "m%d" % dr), ("bon", "m%d" % (2 + dr))):
                l, c = _unseq(res[cid][key][dr], dr == 1)
                lat[nm][b][:, cols] = l
                cx[nm][b][:, cols] = c
    return lat, cx


def layer_c(x, ctx, mod_i, norm_w_i, prm, need_ctx, final_w=None):
    pad = lambda w: np.concatenate([w, np.zeros((D, 128 - w.shape[1]), np.float32)], axis=1)
    wcat = np.concatenate([prm["w_in"][0], prm["w_in"][1], prm["w_in"][2], prm["w_in"][3],
                           pad(prm["w1"][0]), pad(prm["w1"][1]), pad(prm["a1"][0]), pad(prm["a1"][1])], axis=1)
    lerp = [0] * 16 + [1] * 16 + [2] * 16 + [3] * 16 + [4, 4, 5, 5]
    pl, pc = run_P(x, ctx, mod_i, norm_w_i, wcat, lerp=lerp, mu=prm["mu"])
    lat, cx = run_Mc(pl, pc, prm)
    lat["gate"] = pl[:, :, 3 * D:4 * D]
    cx["gate"] = pc[:, :, 3 * D:4 * D]
    bcs = {"lnw": prm["ln_w"], "lnb": prm["ln_b"]}
    if final_w is not None:
        bcs["fnw"] = final_w
    return run_O(x, ctx, "c", lat, cx, prm["w_out"], mod_i[0:2, 2 * D:3 * D], mod_i[2, 2 * D:3 * D], bcs,
                 final=final_w is not None, need_ctx=need_ctx)


def kernel(x, c, ctx, c_ctx, norm_w, mod_w, mod_b, a_w_in, a_lb_raw, a_onorm_w, a_w_out,
           b_w_in, b_sink, b_w_out, c_mu, c_w_in, c_w0, c_w1, c_w2, c_a0, c_a1, c_a2,
           c_k_k, c_k_a, c_r_k, c_ln_w, c_ln_b, c_w_out, final_norm_w):
    f = lambda a: np.asarray(a, dtype=np.float32)
    x, c, ctx, c_ctx = f(x), f(c), f(ctx), f(c_ctx)
    mod = run_mod(c, f(c_ctx), f(mod_w), f(mod_b))
    depth = 4
    for i in range(depth):
        j, kind = i // 3, i % 3
        need_ctx = i < depth - 1
        fw = f(final_norm_w) if i == depth - 1 else None
        mod_i = np.ascontiguousarray(mod[:, i])
        if kind == 0:
            x, ctx = layer_a(x, ctx, mod_i, f(norm_w[i]), f(a_w_in[j]), f(a_lb_raw), j, f(a_onorm_w[j]), f(a_w_out[j]), need_ctx, fw)
        elif kind == 1:
            x, ctx_n = layer_b(x, ctx, mod_i, f(norm_w[i]), f(b_w_in[j]), f(b_sink[j]), f(b_w_out[j]), need_ctx, fw)
            ctx = ctx_n if need_ctx else ctx
        else:
            prm = {"mu": f(c_mu[j]), "w_in": f(c_w_in[j]), "w0": f(c_w0[j]), "w1": f(c_w1[j]), "w2": f(c_w2[j]),
                   "a0": f(c_a0[j]), "a1": f(c_a1[j]), "a2": f(c_a2[j]), "k_k": f(c_k_k[j]), "k_a": f(c_k_a[j]),
                   "r_k": f(c_r_k[j]), "ln_w": f(c_ln_w[j]), "ln_b": f(c_ln_b[j]), "w_out": f(c_w_out[j])}
            x, ctx_n = layer_c(x, ctx, mod_i, f(norm_w[i]), prm, need_ctx, fw)
            ctx = ctx_n if need_ctx else ctx
    return x.astype(np.float32)
```

```python
import numpy as np
from contextlib import ExitStack
import concourse.bass as bass
import concourse.mybir as mybir
from concourse.bass_utils import run_bass_kernel_spmd

F32 = mybir.dt.float32
BF16 = mybir.dt.bfloat16
AF = mybir.ActivationFunctionType
ALU = mybir.AluOpType
AX = mybir.AxisListType

NCORES = 8
D = 2048
KC = D // 128


class Prog:
    CE = ("pe", "act", "dve", "pool")

    def __init__(self, nc, ndma=8):
        self.nc = nc
        self.es = ExitStack()
        self.eng = {"pe": nc.tensor, "act": nc.scalar, "dve": nc.vector, "pool": nc.gpsimd, "sp": nc.sync}
        self.sem = {}
        self.cnt = {}
        for e in self.CE:
            self.sem[("e", e)] = self.es.enter_context(nc.semaphore("s_" + e))
            self.cnt[("e", e)] = 0
        self.ndma = ndma
        self.dma_rr = {}
        for q in ("sp", "pool", "act"):
            self.dma_rr[q] = 0
            for i in range(ndma):
                k = ("d", q, i)
                self.sem[k] = self.es.enter_context(nc.semaphore("d_%s%d" % (q, i)))
                self.cnt[k] = 0
        self.known = {e: {} for e in self.eng}
        self.last_w = {}
        self.readers = {}
        self.ninstr = 0
        self.uid = 0

    def sb(self, shape, dtype=F32, name=None, stack=None):
        self.uid += 1
        return (stack or self.es).enter_context(self.nc.sbuf_tensor(name or ("sb%d" % self.uid), list(shape), dtype))

    def barrier(self):
        for e in ("pe", "act", "dve", "pool", "sp"):
            kn = self.known[e]
            for k, v in self.cnt.items():
                if v > 0 and kn.get(k, 0) < v:
                    self.eng[e].wait_ge(self.sem[k], v)
                    kn[k] = v
                    self.ninstr += 1

    def ps(self, shape, dtype=F32, name=None):
        self.uid += 1
        return self.es.enter_context(self.nc.psum_tensor(name or ("ps%d" % self.uid), list(shape), dtype))

    def dram(self, name, shape, dtype=F32, kind="ExternalInput"):
        return self.nc.dram_tensor(name, list(shape), dtype, kind=kind).ap()

    def _deps(self, e, reads, writes):
        deps = {}

        def add(kv):
            k, v = kv
            if e == "pe" and k == ("e", "pe"):
                return
            if deps.get(k, 0) < v:
                deps[k] = v

        for b in reads:
            if b in self.last_w:
                add(self.last_w[b])
        for b in writes:
            if b in self.last_w:
                add(self.last_w[b])
            for r in self.readers.get(b, ()):
                add(r)
        kn = self.known[e]
        for k, v in deps.items():
            if kn.get(k, 0) < v:
                self.eng[e].wait_ge(self.sem[k], v)
                self.ninstr += 1
                kn[k] = v

    def _record(self, key, val, reads, writes):
        for b in writes:
            self.last_w[b] = (key, val)
            self.readers[b] = []
        for b in reads:
            self.readers.setdefault(b, []).append((key, val))
            if len(self.readers[b]) > 64:
                mx = {}
                for k, v in self.readers[b]:
                    if mx.get(k, 0) < v:
                        mx[k] = v
                self.readers[b] = list(mx.items())

    _defer = None
    _atomic = None

    def begin(self):
        self._defer = []

    def end(self):
        l, self._defer = self._defer, None
        return l

    def atomic_begin(self):
        if self._defer is not None:
            self._atomic = []

    def atomic_end(self):
        if self._defer is not None:
            self._defer.append(self._atomic)
            self._atomic = None

    def interleave(self, lists):
        n = max(len(l) for l in lists)
        for i in range(n):
            for l in lists:
                if i < len(l):
                    for (kind, args, kw) in l[i]:
                        if kind == "op":
                            self.op(*args, **kw)
                        else:
                            self.dma(*args, **kw)

    def _rec(self, kind, args, kw):
        item = (kind, args, kw)
        if self._atomic is not None:
            self._atomic.append(item)
        else:
            self._defer.append([item])

    def op(self, e, fn, reads=(), writes=()):
        if self._defer is not None:
            return self._rec("op", (e, fn, list(reads), list(writes)), {})
        self._deps(e, reads, writes)
        key = ("e", e)
        self.cnt[key] += 1
        ins = fn(self.eng[e])
        ins.then_inc(self.sem[key], 1)
        self.ninstr += 1
        self._record(key, self.cnt[key], reads, writes)
        return ins

    def dma(self, q, out, in_, reads=(), writes=(), **kw):
        if self._defer is not None:
            return self._rec("dma", (q, out, in_, list(reads), list(writes)), kw)
        i = self.dma_rr[q]
        self.dma_rr[q] = (i + 1) % self.ndma
        key = ("d", q, i)
        kn = self.known[q]
        if kn.get(key, 0) < self.cnt[key]:
            self.eng[q].wait_ge(self.sem[key], self.cnt[key])
            kn[key] = self.cnt[key]
            self.ninstr += 1
        self._deps(q, reads, writes)
        self.cnt[key] += 16
        self.eng[q].dma_start(out=out, in_=in_, **kw).then_inc(self.sem[key], 16)
        self.ninstr += 1
        self._record(key, self.cnt[key], reads, writes)

    def finish(self, out_keys):
        self._deps("pool", out_keys, ())
        for k, v in self.cnt.items():
            if k[0] == "d" and v > 0 and self.known["pool"].get(k, 0) < v:
                self.eng["pool"].wait_ge(self.sem[k], v)
                self.known["pool"][k] = v


def _run(nc, in_maps):
    res = run_bass_kernel_spmd(nc, in_maps, core_ids=list(range(NCORES)))
    return res.results


MOD_NCOL = 4 * 3 * D // NCORES


def build_mod():
    nc = bass.Bass("TRN2", target_bir_lowering=False)
    P = Prog(nc)
    ccT = P.dram("ccT", [128, KC, 3])
    w = P.dram("w", [128, KC, MOD_NCOL])
    b3 = P.dram("b3", [3, MOD_NCOL])
    out = P.dram("out", [3, MOD_NCOL], kind="ExternalOutput")
    s_in = P.sb([128, KC, 3])
    s_act = P.sb([128, KC, 3])
    bias = P.sb([3, MOD_NCOL])
    res = P.sb([3, MOD_NCOL])
    wt = [P.sb([128, KC, 512]) for _ in range(2)]
    pt = [P.ps([3, 512]) for _ in range(2)]
    P.dma("sp", s_in[:], ccT[:], writes=["s_in"])
    P.dma("sp", bias[:], b3[:], writes=["bias"])
    P.op("act", lambda e: e.activation(out=s_act[:], in_=s_in[:], func=AF.Silu), reads=["s_in"], writes=["s_act"])
    nb = MOD_NCOL // 512
    for j in range(nb):
        wb = wt[j % 2]
        pb = pt[j % 2]
        P.dma("sp", wb[:], w[:, :, j * 512:(j + 1) * 512], writes=[("w", j % 2)])
        for k in range(KC):
            P.op("pe", lambda e, k=k, wb=wb, pb=pb: e.matmul(pb[:], lhsT=s_act[:, k, :], rhs=wb[:, k, :],
                                                        start=(k == 0), stop=(k == KC - 1)),
                 reads=["s_act", ("w", j % 2)], writes=[("p", j % 2)])
        P.op("dve", lambda e, j=j, pb=pb: e.tensor_tensor(out=res[:, j * 512:(j + 1) * 512], in0=pb[:],
                                                       in1=bias[:, j * 512:(j + 1) * 512], op=ALU.add),
             reads=[("p", j % 2), "bias"], writes=["res"])
    P.dma("pool", out[:], res[:], reads=["res"], writes=["out"])
    P.finish(["out"])
    return nc


def run_mod(c, c_ctx, mod_w, mod_b):
    cc = np.concatenate([c, c_ctx[None, :]], axis=0).astype(np.float32)
    ccT = np.ascontiguousarray(cc.T.reshape(KC, 128, 3).transpose(1, 0, 2))
    wall = np.concatenate([mod_w[l] for l in range(4)], axis=1)
    ball = np.concatenate([mod_b[l] for l in range(4)], axis=0)
    in_maps = []
    for cid in range(NCORES):
        cols = slice(cid * MOD_NCOL, (cid + 1) * MOD_NCOL)
        wc = np.ascontiguousarray(wall[:, cols].reshape(KC, 128, MOD_NCOL).transpose(1, 0, 2))
        bc = np.ascontiguousarray(np.broadcast_to(ball[cols][None, :], (3, MOD_NCOL)))
        in_maps.append({"ccT": ccT, "w": wc, "b3": bc})
    nc = build_mod()
    res = _run(nc, in_maps)
    mod = np.concatenate([r["out"] for r in res], axis=1)
    return mod.reshape(3, 4, 3 * D)


def _segments(r0, n):
    segs = []
    o = 0
    while o < n:
        m = min(128, n - o)
        segs.append((r0 + o, m))
        o += m
    return segs


def build_P(n_lat, n_ctx, nblk, lerp=None):
    halo = 1 if lerp is not None else 0
    r_lat = n_lat + 2 * halo
    r_ctx = n_ctx + 2 * halo
    R = r_lat + r_ctx
    n_int = n_lat + n_ctx
    nc = bass.Bass("TRN2", target_bir_lowering=False)
    P = Prog(nc)
    xin = P.dram("xin", [R, D])
    nw = P.dram("nw", [128, D])
    scb = P.dram("scb", [2, 128, D])
    shb = P.dram("shb", [2, 128, D])
    wd = P.dram("w", [nblk, 128, KC, 128])
    identd = P.dram("ident", [128, 128])
    if lerp is not None:
        mud = P.dram("mu", [128, KC, 6])
        bmd = P.dram("bmask", [128, 4])
    out = P.dram("projT", [nblk * 128, n_int], kind="ExternalOutput")

    hT = P.sb([128, KC, R], BF16)
    tmp = P.sb([128, D])
    if lerp is not None:
        xxT = P.sb([128, KC, R], BF16)
        mu = P.sb([128, KC, 6])
        bm = P.sb([128, 4])
    tps = [P.ps([128, 4, 128], BF16) for _ in range(2)]
    mps = [P.ps([128, 512]) for _ in range(4)]
    es1 = ExitStack()
    ident_f = P.sb([128, 128], stack=es1)
    ident = P.sb([128, 128], BF16, stack=es1)
    S = [P.sb([128, D], stack=es1) for _ in range(2)]
    SH = [P.sb([128, D], stack=es1) for _ in range(2)]
    nwt = tmp
    xt = [P.sb([128, D], stack=es1) for _ in range(2)]
    sq = P.sb([128, D], BF16, stack=es1)
    hb = [P.sb([128, D], BF16, stack=es1) for _ in range(2)]
    ss = [P.sb([128, 1], stack=es1) for _ in range(2)]
    rstd = [P.sb([128, 1], stack=es1) for _ in range(2)]

    P.dma("sp", ident_f[:], identd[:], writes=["identf"])
    P.op("dve", lambda e: e.tensor_copy(out=ident[:], in_=ident_f[:]), reads=["identf"], writes=["ident"])
    P.dma("sp", nwt[:], nw[:], writes=["tmp"])
    for i in range(2):
        P.dma("sp", S[i][:], scb[i], writes=[("S", i)])
        P.dma("sp", SH[i][:], shb[i], writes=[("SH", i)])
        P.op("dve", lambda e, i=i: e.scalar_tensor_tensor(out=S[i][:], in0=S[i][:], scalar=1.0, in1=nwt[:],
                                                          op0=ALU.add, op1=ALU.mult),
             reads=[("S", i), "tmp"], writes=[("S", i)])
    if lerp is not None:
        P.dma("sp", mu[:], mud[:], writes=["mu"])
        P.dma("sp", bm[:], bmd[:], writes=["bm"])

    segs = [(r, m, 0) for (r, m) in _segments(0, r_lat)] + [(r, m, 1) for (r, m) in _segments(r_lat, r_ctx)]
    for si, (r0, m, mi) in enumerate(segs):
        b = si % 2
        P.dma("sp", xt[b][:m, :], xin[r0:r0 + m, :], writes=[("xt", b)])
        P.op("act", lambda e, b=b, m=m: e.activation(out=sq[:m, :], in_=xt[b][:m, :], func=AF.Square,
                                                      accum_out=ss[b][:m, :]),
             reads=[("xt", b)], writes=["sq", ("ss", b)])
        P.op("dve", lambda e, b=b, m=m: e.tensor_scalar(out=rstd[b][:m, :], in0=ss[b][:m, :], scalar1=1.0 / D,
                                                         scalar2=1e-6, op0=ALU.mult, op1=ALU.add),
             reads=[("ss", b)], writes=[("rstd", b)])
        P.op("act", lambda e, b=b, m=m: e.activation(out=rstd[b][:m, :], in_=rstd[b][:m, :], func=AF.Sqrt),
             reads=[("rstd", b)], writes=[("rstd", b)])
        P.op("dve", lambda e, b=b, m=m: e.reciprocal(out=rstd[b][:m, :], in_=rstd[b][:m, :]),
             reads=[("rstd", b)], writes=[("rstd", b)])
        P.op("dve", lambda e, b=b, m=m, mi=mi: e.scalar_tensor_tensor(out=tmp[:m, :], in0=xt[b][:m, :],
                                                                     scalar=rstd[b][:m, :], in1=S[mi][:m, :],
                                                                     op0=ALU.mult, op1=ALU.mult),
             reads=[("xt", b), ("rstd", b), ("S", mi)], writes=["tmp"])
        P.op("dve", lambda e, b=b, m=m, mi=mi: e.tensor_tensor(out=hb[b][:m, :], in0=tmp[:m, :], in1=SH[mi][:m, :],
                                                              op=ALU.add),
             reads=["tmp", ("SH", mi)], writes=[("hb", b)])
        for kg in range(KC // 4):
            tb = (si * 4 + kg) % 2
            for kk in range(4):
                k = kg * 4 + kk
                P.op("pe", lambda e, b=b, m=m, k=k, kk=kk, tb=tb: e.transpose(out=tps[tb][:, kk, :m],
                                                                              in_=hb[b][:m, k * 128:(k + 1) * 128],
                                                                              identity=ident[:m, :m]),
                     reads=[("hb", b), "ident"], writes=[("tps", tb)])
            eng = "act" if kg % 2 == 0 else "dve"
            if eng == "act":
                P.op("act", lambda e, m=m, kg=kg, tb=tb, r0=r0: e.copy(out=hT[:, kg * 4:(kg + 1) * 4, r0:r0 + m],
                                                                       in_=tps[tb][:, :, :m]),
                     reads=[("tps", tb)], writes=["hT"])
            else:
                P.op("dve", lambda e, m=m, kg=kg, tb=tb, r0=r0: e.tensor_copy(out=hT[:, kg * 4:(kg + 1) * 4, r0:r0 + m],
                                                                              in_=tps[tb][:, :, :m]),
                     reads=[("tps", tb)], writes=["hT"])

    if lerp is not None:
        for ci, col in enumerate([0, r_lat - 1, r_lat, R - 1]):
            P.op("dve", lambda e, ci=ci, col=col: e.tensor_scalar(out=hT[:, :, col:col + 1], in0=hT[:, :, col:col + 1],
                                                                   scalar1=bm[:, ci:ci + 1], scalar2=None, op0=ALU.mult),
                 reads=["hT", "bm"], writes=["hT"])
        for (c0, n) in [(0, r_lat), (r_lat, r_ctx)]:
            for k in range(KC):
                w_ = n - 2
                P.op("dve", lambda e, k=k, c0=c0, w_=w_: e.tensor_tensor(out=tmp[:, :w_], in0=hT[:, k, c0:c0 + w_],
                                                                        in1=hT[:, k, c0 + 2:c0 + 2 + w_], op=ALU.add),
                     reads=["hT"], writes=["tmp"])
                P.op("dve", lambda e, k=k, c0=c0, w_=w_: e.scalar_tensor_tensor(out=xxT[:, k, c0 + 1:c0 + 1 + w_],
                                                                               in0=tmp[:, :w_], scalar=0.5,
                                                                               in1=hT[:, k, c0 + 1:c0 + 1 + w_],
                                                                               op0=ALU.mult, op1=ALU.subtract),
                     reads=["tmp", "hT"], writes=["xxT"])

    P.barrier()
    es1.close()
    wf = [P.sb([128, KC, 128]) for _ in range(2)]
    wb = [P.sb([128, KC, 128], BF16) for _ in range(2)]
    stage = [P.sb([128, n_int]) for _ in range(2)]
    if lerp is not None:
        wb2 = [P.sb([128, KC, 128], BF16) for _ in range(2)]
    groups = []
    o = 0
    while o < n_lat:
        n = min(512, n_lat - o)
        groups.append((halo + o, o, n))
        o += n
    groups.append((r_lat + halo, n_lat, n_ctx))
    gi = 0
    for j in range(nblk):
        b = j % 2
        P.dma("sp", wf[b][:], wd[j], writes=[("wf", b)])
        if j % 2 == 0:
            P.op("act", lambda e, b=b: e.copy(out=wb[b][:], in_=wf[b][:]), reads=[("wf", b)], writes=[("wb", b)])
        else:
            P.op("pool", lambda e, b=b: e.tensor_copy(out=wb[b][:], in_=wf[b][:]), reads=[("wf", b)], writes=[("wb", b)])
        if lerp is not None:
            n_mu = lerp[j]
            P.op("pool", lambda e, b=b, n_mu=n_mu: e.tensor_tensor(out=wb2[b][:], in0=wf[b][:],
                                                                   in1=mu[:, :, n_mu:n_mu + 1].to_broadcast([128, KC, 128]),
                                                                   op=ALU.mult),
                 reads=[("wf", b), "mu"], writes=[("wb2", b)])
        for (hc, oc, n) in groups:
            pb = gi % 4
            gi += 1
            nmm = KC * (2 if lerp is not None else 1)
            for k in range(KC):
                P.op("pe", lambda e, b=b, k=k, pb=pb, hc=hc, n=n: e.matmul(mps[pb][:, :n], lhsT=wb[b][:, k, :],
                                                                         rhs=hT[:, k, hc:hc + n],
                                                                         start=(k == 0), stop=(k == nmm - 1)),
                     reads=[("wb", b), "hT"], writes=[("mps", pb)])
            if lerp is not None:
                for k in range(KC):
                    P.op("pe", lambda e, b=b, k=k, pb=pb, hc=hc, n=n: e.matmul(mps[pb][:, :n], lhsT=wb2[b][:, k, :],
                                                                             rhs=xxT[:, k, hc:hc + n],
                                                                             start=False, stop=(k == KC - 1)),
                         reads=[("wb2", b), "xxT"], writes=[("mps", pb)])
            if gi % 2 == 0:
                P.op("act", lambda e, b=b, pb=pb, oc=oc, n=n: e.copy(out=stage[b][:, oc:oc + n], in_=mps[pb][:, :n]),
                     reads=[("mps", pb)], writes=[("stage", b)])
            else:
                P.op("dve", lambda e, b=b, pb=pb, oc=oc, n=n: e.tensor_copy(out=stage[b][:, oc:oc + n], in_=mps[pb][:, :n]),
                     reads=[("mps", pb)], writes=[("stage", b)])
        P.dma("pool", out[j * 128:(j + 1) * 128, :], stage[b][:], reads=[("stage", b)], writes=["out"])
    P.finish(["out"])
    return nc


def _bc(v):
    return np.ascontiguousarray(np.broadcast_to(np.asarray(v, np.float32)[None, :], (128, v.shape[-1])))


def _wblocks(w):
    ncols = w.shape[1]
    nblk = (ncols + 127) // 128
    if nblk * 128 != ncols:
        w = np.concatenate([w, np.zeros((D, nblk * 128 - ncols), np.float32)], axis=1)
    return np.ascontiguousarray(w.reshape(KC, 128, nblk, 128).transpose(2, 1, 0, 3))


N_LAT = 2048
N_CTX = 64
T_LAT = 8192
T_CTX = 256


def _core_rows(x, ctx, cid, halo=0):
    b, q = cid // 4, cid % 4

    def take(a, lo, hi):
        T = a.shape[0]
        rows = []
        if lo < 0:
            rows.append(np.zeros((-lo, a.shape[1]), a.dtype))
        rows.append(a[max(lo, 0):min(hi, T)])
        if hi > T:
            rows.append(np.zeros((hi - T, a.shape[1]), a.dtype))
        return np.concatenate(rows, axis=0) if len(rows) > 1 else rows[0]

    lat = take(x[b], q * N_LAT - halo, (q + 1) * N_LAT + halo)
    cx = take(ctx[b], q * N_CTX - halo, (q + 1) * N_CTX + halo)
    return np.ascontiguousarray(np.concatenate([lat, cx], axis=0))


def _gather_rows(outs, width):
    x = np.empty((2, T_LAT, width), np.float32)
    ctx = np.empty((2, T_CTX, width), np.float32)
    for cid in range(NCORES):
        b, q = cid // 4, cid % 4
        x[b, q * N_LAT:(q + 1) * N_LAT] = outs[cid][:N_LAT]
        ctx[b, q * N_CTX:(q + 1) * N_CTX] = outs[cid][N_LAT:]
    return x, ctx


def run_P(x, ctx, mod_i, norm_w_i, wcat, lerp=None, mu=None):
    wb = _wblocks(wcat)
    nblk = wb.shape[0]
    halo = 1 if lerp is not None else 0
    nc = build_P(N_LAT, N_CTX, nblk, lerp)
    ident = np.eye(128, dtype=np.float32)
    nw = _bc(norm_w_i)
    in_maps = []
    for cid in range(NCORES):
        b, q = cid // 4, cid % 4
        m = {"xin": _core_rows(x, ctx, cid, halo), "nw": nw, "w": wb, "ident": ident,
             "scb": np.stack([_bc(mod_i[b, D:2 * D]), _bc(mod_i[2, D:2 * D])]),
             "shb": np.stack([_bc(mod_i[b, 0:D]), _bc(mod_i[2, 0:D])])}
        if lerp is not None:
            m["mu"] = np.ascontiguousarray(mu.T.reshape(KC, 128, 6).transpose(1, 0, 2))
            bm = np.ones((128, 4), np.float32)
            if q == 0:
                bm[:, 0] = 0.0
                bm[:, 2] = 0.0
            if q == 3:
                bm[:, 1] = 0.0
                bm[:, 3] = 0.0
            m["bmask"] = bm
        in_maps.append(m)
    res = _run(nc, in_maps)
    outs = [np.ascontiguousarray(r["projT"].T) for r in res]
    return _gather_rows(outs, nblk * 128)


O_INS = {"a": ["m0", "m1", "gate"], "b": ["m0", "gate"], "c": ["m0", "m1", "m2", "m3", "gate"]}


def build_O(n_lat, n_ctx, kind, final=False):
    R = n_lat + n_ctx
    nc = bass.Bass("TRN2", target_bir_lowering=False)
    P = Prog(nc)
    xin = P.dram("xin", [R, D])
    ins_d = {nm: P.dram(nm, [R, D]) for nm in O_INS[kind]}
    wod = P.dram("wo", [128, KC, D])
    gbd = P.dram("gb", [2, 128, D])
    identd = P.dram("ident", [128, 128])
    nbc = {"a": ["onw"], "b": [], "c": ["lnw", "lnb"]}[kind] + (["fnw"] if final else [])
    bc_d = {nm: P.dram(nm, [128, D]) for nm in nbc}
    out = P.dram("xout", [R, D], kind="ExternalOutput")

    ident_f = P.sb([128, 128])
    ident = P.sb([128, 128], BF16)
    wo = P.sb([128, KC, D], BF16)
    wst = [P.sb([128, D]) for _ in range(2)]
    gb = [P.sb([128, D]) for _ in range(2)]
    bc = {nm: P.sb([128, D]) for nm in nbc}
    xt = [P.sb([128, D]) for _ in range(2)]
    xo = [P.sb([128, D]) for _ in range(2)]
    it = {nm: [P.sb([128, 512]) for _ in range(2)] for nm in O_INS[kind]}
    t1 = P.sb([128, 512])
    t2 = P.sb([128, 512])
    t3 = P.sb([128, 512])
    sg = P.sb([128, 512])
    st = [P.sb([128, 8]) for _ in range(3)]
    zb = [P.sb([128, D], BF16) for _ in range(2)]
    zT = [P.sb([128, KC, 128], BF16) for _ in range(2)]
    tps = [P.ps([128, 4, 128], BF16) for _ in range(2)]
    mps = [P.ps([128, 512]) for _ in range(4)]
    fs = [P.sb([128, 1]) for _ in range(2)]

    P.dma("sp", ident_f[:], identd[:], writes=["identf"])
    P.op("dve", lambda e: e.tensor_copy(out=ident[:], in_=ident_f[:]), reads=["identf"], writes=["ident"])
    for i in range(2):
        P.dma("sp", gb[i][:], gbd[i], writes=[("gb", i)])
    for nm in nbc:
        P.dma("sp", bc[nm][:], bc_d[nm], writes=[nm])
    for k in range(KC):
        b = k % 2
        P.dma("sp", wst[b][:], wod[:, k, :], writes=[("wst", b)])
        if k % 2 == 0:
            P.op("act", lambda e, b=b, k=k: e.copy(out=wo[:, k, :], in_=wst[b][:]), reads=[("wst", b)], writes=["wo"])
        else:
            P.op("pool", lambda e, b=b, k=k: e.tensor_copy(out=wo[:, k, :], in_=wst[b][:]), reads=[("wst", b)], writes=["wo"])

    G = 128 if kind == "a" else 64
    ng = 512 // G
    segs = [(r, m, 0) for (r, m) in _segments(0, n_lat)] + [(r, m, 1) for (r, m) in _segments(n_lat, n_ctx)]
    li = 0
    for si, (r0, m, mi) in enumerate(segs):
        b = si % 2
        P.dma("sp", xt[b][:m, :], xin[r0:r0 + m, :], writes=[("xt", b)])
        for cg in range(4):
            cs = slice(cg * 512, (cg + 1) * 512)
            lb = li % 2
            li += 1
            T = {}
            for nm in O_INS[kind]:
                P.dma("sp", it[nm][lb][:m, :], ins_d[nm][r0:r0 + m, cs], writes=[(nm, lb)])
                T[nm] = it[nm][lb]
            gk = ("gate", lb)
            P.op("act", lambda e, m=m, T=T: e.activation(out=sg[:m, :], in_=T["gate"][:m, :], func=AF.Silu),
                 reads=[gk], writes=["sg"])
            if kind == "b":
                P.op("dve", lambda e, m=m, T=T, b=b, cs=cs: e.tensor_tensor(out=zb[b][:m, cs], in0=T["m0"][:m, :],
                                                                          in1=sg[:m, :], op=ALU.mult),
                     reads=[("m0", lb), "sg"], writes=[("zb", b)])
                continue
            P.op("dve", lambda e, m=m, T=T: e.tensor_tensor(out=t1[:m, :], in0=T["m0"][:m, :], in1=T["m1"][:m, :], op=ALU.add),
                 reads=[("m0", lb), ("m1", lb)], writes=["t1"])
            y3 = t1[:m, :].rearrange("p (g c) -> p g c", c=G)
            if kind == "c":
                P.op("dve", lambda e, m=m, y3=y3: e.tensor_reduce(out=st[0][:m, :ng], in_=y3, axis=AX.X, op=ALU.add),
                     reads=["t1"], writes=["st0"])
                P.op("dve", lambda e, m=m: e.tensor_scalar(out=st[0][:m, :ng], in0=st[0][:m, :ng], scalar1=-1.0 / G,
                                                           scalar2=None, op0=ALU.mult),
                     reads=["st0"], writes=["st0"])
                P.op("dve", lambda e, m=m, y3=y3: e.tensor_tensor(out=y3, in0=y3,
                                                                in1=st[0][:m, :ng].unsqueeze(2).to_broadcast([m, ng, G]),
                                                                op=ALU.add),
                     reads=["t1", "st0"], writes=["t1"])
            P.op("pool", lambda e, m=m: e.tensor_tensor(out=t2[:m, :], in0=t1[:m, :], in1=t1[:m, :], op=ALU.mult),
                 reads=["t1"], writes=["t2"])
            P.op("dve", lambda e, m=m: e.tensor_reduce(out=st[1][:m, :ng], in_=t2[:m, :].rearrange("p (g c) -> p g c", c=G),
                                                       axis=AX.X, op=ALU.add),
                 reads=["t2"], writes=["st1"])
            eps = 1e-6 if kind == "a" else 64e-5
            P.op("dve", lambda e, m=m, eps=eps: e.tensor_scalar(out=st[1][:m, :ng], in0=st[1][:m, :ng], scalar1=1.0 / G,
                                                                scalar2=eps, op0=ALU.mult, op1=ALU.add),
                 reads=["st1"], writes=["st1"])
            P.op("act", lambda e, m=m: e.activation(out=st[1][:m, :ng], in_=st[1][:m, :ng], func=AF.Sqrt),
                 reads=["st1"], writes=["st1"])
            P.op("dve", lambda e, m=m: e.reciprocal(out=st[2][:m, :ng], in_=st[1][:m, :ng]),
                 reads=["st1"], writes=["st2"])
            P.op("dve", lambda e, m=m, y3=y3: e.tensor_tensor(out=y3, in0=y3,
                                                            in1=st[2][:m, :ng].unsqueeze(2).to_broadcast([m, ng, G]),
                                                            op=ALU.mult),
                 reads=["t1", "st2"], writes=["t1"])
            if kind == "a":
                P.op("pool", lambda e, m=m, cs=cs: e.tensor_tensor(out=t2[:m, :], in0=t1[:m, :], in1=bc["onw"][:m, cs], op=ALU.mult),
                     reads=["t1", "onw"], writes=["t2"])
            else:
                P.op("pool", lambda e, m=m, cs=cs: e.tensor_tensor(out=t2[:m, :], in0=t1[:m, :], in1=bc["lnw"][:m, cs], op=ALU.mult),
                     reads=["t1", "lnw"], writes=["t2"])
                P.op("pool", lambda e, m=m, T=T: e.tensor_tensor(out=t3[:m, :], in0=T["m2"][:m, :], in1=T["m3"][:m, :], op=ALU.add),
                     reads=[("m2", lb), ("m3", lb)], writes=["t3"])
                P.op("pool", lambda e, m=m, cs=cs: e.tensor_tensor(out=t3[:m, :], in0=t3[:m, :], in1=bc["lnb"][:m, cs], op=ALU.add),
                     reads=["t3", "lnb"], writes=["t3"])
                P.op("dve", lambda e, m=m: e.tensor_tensor(out=t2[:m, :], in0=t2[:m, :], in1=t3[:m, :], op=ALU.add),
                     reads=["t2", "t3"], writes=["t2"])
            P.op("dve", lambda e, m=m, b=b, cs=cs: e.tensor_tensor(out=zb[b][:m, cs], in0=t2[:m, :], in1=sg[:m, :], op=ALU.mult),
                 reads=["t2", "sg"], writes=[("zb", b)])
        for kg in range(KC // 4):
            tb = (si * 4 + kg) % 2
            for kk in range(4):
                k = kg * 4 + kk
                P.op("pe", lambda e, b=b, m=m, k=k, kk=kk, tb=tb: e.transpose(out=tps[tb][:, kk, :m],
                                                                              in_=zb[b][:m, k * 128:(k + 1) * 128],
                                                                              identity=ident[:m, :m]),
                     reads=[("zb", b), "ident"], writes=[("tps", tb)])
            if kg % 2 == 0:
                P.op("act", lambda e, m=m, kg=kg, tb=tb, b=b: e.copy(out=zT[b][:, kg * 4:(kg + 1) * 4, :m], in_=tps[tb][:, :, :m]),
                     reads=[("tps", tb)], writes=[("zT", b)])
            else:
                P.op("dve", lambda e, m=m, kg=kg, tb=tb, b=b: e.tensor_copy(out=zT[b][:, kg * 4:(kg + 1) * 4, :m], in_=tps[tb][:, :, :m]),
                     reads=[("tps", tb)], writes=[("zT", b)])
        for cg in range(4):
            cs = slice(cg * 512, (cg + 1) * 512)
            pb = cg
            for k in range(KC):
                P.op("pe", lambda e, b=b, m=m, k=k, pb=pb, cs=cs: e.matmul(mps[pb][:m, :], lhsT=zT[b][:, k, :m], rhs=wo[:, k, cs],
                                                                         start=(k == 0), stop=(k == KC - 1)),
                     reads=[("zT", b), "wo"], writes=[("mps", pb)])
            P.op("dve", lambda e, m=m, pb=pb, cs=cs, mi=mi: e.tensor_tensor(out=t1[:m, :], in0=mps[pb][:m, :], in1=gb[mi][:m, cs], op=ALU.mult),
                 reads=[("mps", pb), ("gb", mi)], writes=["t1"])
            P.op("pool", lambda e, m=m, b=b, cs=cs: e.tensor_tensor(out=xo[b][:m, cs], in0=t1[:m, :], in1=xt[b][:m, cs], op=ALU.add),
                 reads=["t1", ("xt", b)], writes=[("xo", b)])
        if final:
            P.op("act", lambda e, b=b, m=m: e.activation(out=xt[b][:m, :], in_=xo[b][:m, :], func=AF.Square, accum_out=fs[0][:m, :]),
                 reads=[("xo", b)], writes=[("xt", b), "fs0"])
            P.op("dve", lambda e, m=m: e.tensor_scalar(out=fs[0][:m, :], in0=fs[0][:m, :], scalar1=1.0 / D, scalar2=1e-6,
                                                       op0=ALU.mult, op1=ALU.add), reads=["fs0"], writes=["fs0"])
            P.op("act", lambda e, m=m: e.activation(out=fs[0][:m, :], in_=fs[0][:m, :], func=AF.Sqrt), reads=["fs0"], writes=["fs0"])
            P.op("dve", lambda e, m=m: e.reciprocal(out=fs[1][:m, :], in_=fs[0][:m, :]), reads=["fs0"], writes=["fs1"])
            P.op("dve", lambda e, m=m, b=b: e.scalar_tensor_tensor(out=xo[b][:m, :], in0=xo[b][:m, :], scalar=fs[1][:m, :],
                                                                   in1=bc["fnw"][:m, :], op0=ALU.mult, op1=ALU.mult),
                 reads=[("xo", b), "fs1", "fnw"], writes=[("xo", b)])
        P.dma("pool", out[r0:r0 + m, :], xo[b][:m, :], reads=[("xo", b)], writes=["out"])
    P.finish(["out"])
    return nc


def run_O(x, ctx, kind, mix_lat, mix_ctx, w_out, g_lat, g_ctx, bcs, final=False, need_ctx=True):
    n_ctx = N_CTX if need_ctx else 0
    nc = build_O(N_LAT, n_ctx, kind, final)
    ident = np.eye(128, dtype=np.float32)
    wo = np.ascontiguousarray(w_out.reshape(KC, 128, D).transpose(1, 0, 2))
    in_maps = []
    for cid in range(NCORES):
        b, q = cid // 4, cid % 4

        def rows(lat, cx):
            parts = [lat[b, q * N_LAT:(q + 1) * N_LAT]]
            if need_ctx:
                parts.append(cx[b, q * N_CTX:(q + 1) * N_CTX])
            return np.ascontiguousarray(np.concatenate(parts, axis=0))

        m = {"xin": rows(x, ctx), "wo": wo, "ident": ident, "gb": np.stack([_bc(g_lat[b]), _bc(g_ctx)])}
        for nm in O_INS[kind]:
            m[nm] = rows(mix_lat[nm], mix_ctx[nm] if need_ctx else None)
        for nm, v in bcs.items():
            m[nm] = _bc(v)
        in_maps.append(m)
    res = _run(nc, in_maps)
    xo = np.empty((2, T_LAT, D), np.float32)
    co = np.empty((2, T_CTX, D), np.float32) if need_ctx else None
    for cid in range(NCORES):
        b, q = cid // 4, cid % 4
        o = res[cid]["xout"]
        xo[b, q * N_LAT:(q + 1) * N_LAT] = o[:N_LAT]
        if need_ctx:
            co[b, q * N_CTX:(q + 1) * N_CTX] = o[N_LAT:]
    return xo, co


L_SEQ = T_CTX + T_LAT
MA_SHARE_PSUM = False
CH = 64


def build_Ma(nrec, L, jlayer):
    ntile = L // 128
    NR = nrec
    WN = NR * 128
    NG = NR * 2
    nc = bass.Bass("TRN2", target_bir_lowering=False)
    P = Prog(nc)
    qd = P.dram("qT", [nrec, 128, L])
    zd = P.dram("zT", [nrec, 128, L])
    vd = P.dram("v", [nrec, L // CH, CH, 128])
    lbd = P.dram("lbr", [nrec, 128, 2])
    maskd = P.dram("mask", [CH, CH])
    identd = P.dram("ident", [128, 128])
    out = P.dram("oT", [nrec, 128, L], kind="ExternalOutput")

    ident_f = P.sb([128, 128])
    ident = P.sb([128, 128], BF16)
    mask = P.sb([CH, CH])
    m01 = P.sb([128, WN])
    lbr = P.sb([128, nrec, 2])
    lb = P.sb([128, nrec])
    oml = P.sb([128, nrec])
    S = [P.sb([128, 128]) for _ in range(nrec)]
    Sb = [P.sb([128, 128], BF16) for _ in range(nrec)]
    Z = [P.sb([128, NR, 128]) for _ in range(2)]
    Qw = [P.sb([128, NR, 128]) for _ in range(2)]
    Vt = [P.sb([CH, NR, 2, 128]) for _ in range(2)]
    Vb = [P.sb([CH, NR, 2, 128], BF16) for _ in range(2)]
    e1, ft, gt, kq, bc, bm, be, ex0, ex2, ex3, bk = [P.sb([128, WN]) for _ in range(11)]
    qh = [P.sb([128, WN], BF16) for _ in range(2)]
    qtl = [P.sb([128, WN], BF16) for _ in range(2)]
    ktl = [P.sb([128, WN], BF16) for _ in range(2)]
    khI = [[P.sb([128, WN], BF16) for _ in range(4)] for _ in range(2)]
    gam = [P.sb([128, NG]) for _ in range(2)]
    osb = [P.sb([128, NR, 128]) for _ in range(2)]
    att_all = [[P.sb([CH, CH], BF16) for _ in range(2)] for _ in range(nrec)]
    ktm_all = [[P.sb([CH, 128], BF16) for _ in range(2)] for _ in range(nrec)]
    pA_bank = [P.ps([128, 512]) for _ in range(2)]
    pO_bank = [P.ps([128, 512]) for _ in range(2)]
    pT_bank = [P.ps([128, 1024], BF16) for _ in range(2)]
    pS_bank = [P.ps([128, 512]) for _ in range(2)]

    P.dma("sp", ident_f[:], identd[:], writes=["identf"])
    P.op("dve", lambda e: e.tensor_copy(out=ident[:], in_=ident_f[:]), reads=["identf"], writes=["ident"])
    P.dma("sp", mask[:], maskd[:], writes=["mask"])
    P.op("pool", lambda e: e.memset(m01[:], 1.0), writes=["m01"])
    P.op("pool", lambda e: e.memset(m01[:].rearrange("p (g s) -> p g s", s=CH)[:, :, 0:1], 0.0), reads=["m01"], writes=["m01"])
    for p_ in range(2):
        P.op("dve", lambda e, p_=p_: e.memset(pA_bank[p_][:], 0.0), writes=[("pA", p_)])
    for r in range(nrec):
        P.dma("sp", lbr[:, r, :], lbd[r], writes=["lbr"])
        P.op("pool", lambda e, r=r: e.memset(S[r][:], 0.0), writes=[("S", r)])
        P.op("pool", lambda e, r=r: e.memset(Sb[r][:], 0.0), writes=[("Sb", r)])
    if jlayer != 0:
        P.op("dve", lambda e: e.tensor_tensor(out=lb[:], in0=lbr[:, :, 0], in1=lbr[:, :, 1], op=ALU.subtract),
             reads=["lbr"], writes=["lb"])
        P.op("act", lambda e: e.activation(out=lb[:], in_=lb[:], func=AF.Exp), reads=["lb"], writes=["lb"])
        P.op("dve", lambda e: e.tensor_scalar(out=lb[:], in0=lb[:], scalar1=1.0, scalar2=None, op0=ALU.add),
             reads=["lb"], writes=["lb"])
        P.op("dve", lambda e: e.reciprocal(out=lb[:], in_=lb[:]), reads=["lb"], writes=["lb"])
        P.op("dve", lambda e: e.tensor_scalar(out=oml[:], in0=lb[:], scalar1=-1.0, scalar2=1.0, op0=ALU.mult, op1=ALU.add),
             reads=["lb"], writes=["oml"])

    QS = 128.0 ** -0.5

    NH = 2
    HW_ = WN // NH
    GH = NG // NH

    def emit_W(t, hf):
        tp = t % 2
        ts = slice(t * 128, (t + 1) * 128)
        KT = lambda nm: (nm, tp, hf)
        KH = lambda nm: (nm, hf)
        cs = slice(hf * HW_, (hf + 1) * HW_)
        rs = slice(hf * (NR // NH), (hf + 1) * (NR // NH))
        nr = NR // NH
        lst = []
        P._defer = lst
        z2 = Z[tp][:, rs, :].rearrange("p r t -> p (r t)")
        q2 = Qw[tp][:, rs, :].rearrange("p r t -> p (r t)")
        g3 = lambda ap: ap.rearrange("p (g s) -> p g s", s=CH)
        P.dma("sp", Z[tp][:, rs, :], zd[rs, :, ts].rearrange("r p t -> p r t"), writes=[KT("Z")])
        P.dma("sp", Qw[tp][:, rs, :], qd[rs, :, ts].rearrange("r p t -> p r t"), writes=[KT("Q")])
        for r_ in range(rs.start, rs.stop):
            P.dma("sp", Vt[tp][:, r_], vd[r_, 2 * t:2 * t + 2].rearrange("c s d -> s c d"), writes=[KT("Vt")])
        P.op("pool", lambda e: e.tensor_copy(out=Vb[tp][:, rs], in_=Vt[tp][:, rs]), reads=[KT("Vt")], writes=[KT("Vb")])
        P.op("act", lambda e: e.activation(out=e1[:, cs], in_=z2, func=AF.Exp, scale=-1.0), reads=[KT("Z")], writes=[KH("e1")])
        P.op("dve", lambda e: e.tensor_scalar(out=e1[:, cs], in0=e1[:, cs], scalar1=1.0, scalar2=None, op0=ALU.add), reads=[KH("e1")], writes=[KH("e1")])
        P.op("act", lambda e: e.activation(out=e1[:, cs], in_=e1[:, cs], func=AF.Ln), reads=[KH("e1")], writes=[KH("e1")])
        P.op("act", lambda e: e.activation(out=ft[:, cs], in_=e1[:, cs], func=AF.Exp, scale=-1.0), reads=[KH("e1")], writes=[KH("ft")])
        if jlayer != 0:
            f3 = ft[:, cs].rearrange("p (r t) -> p r t", t=128)
            P.op("dve", lambda e: e.tensor_tensor(out=f3, in0=f3, in1=oml[:, rs].unsqueeze(2).to_broadcast([128, nr, 128]), op=ALU.mult),
                 reads=[KH("ft"), "oml"], writes=[KH("ft")])
            P.op("dve", lambda e: e.tensor_tensor(out=f3, in0=f3, in1=lb[:, rs].unsqueeze(2).to_broadcast([128, nr, 128]), op=ALU.add),
                 reads=[KH("ft"), "lb"], writes=[KH("ft")])
            P.op("act", lambda e: e.activation(out=gt[:, cs], in_=ft[:, cs], func=AF.Ln), reads=[KH("ft")], writes=[KH("gt")])
        else:
            P.op("dve", lambda e: e.tensor_scalar(out=gt[:, cs], in0=e1[:, cs], scalar1=-1.0, scalar2=None, op0=ALU.mult),
                 reads=[KH("e1")], writes=[KH("gt")])
        P.op("pool", lambda e: e.tensor_scalar(out=kq[:, cs], in0=ft[:, cs], scalar1=-1.0, scalar2=1.0, op0=ALU.mult, op1=ALU.add),
             reads=[KH("ft")], writes=[KH("kq")])
        P.op("dve", lambda e: e.tensor_tensor_scan(out=bc[:, cs], data0=m01[:, cs], data1=gt[:, cs], initial=0.0, op0=ALU.mult, op1=ALU.add),
             reads=[KH("gt"), "m01"], writes=[KH("bc")])
        bc3 = g3(bc[:, cs])
        bc4 = bc[:, cs].rearrange("p (g i s) -> p g i s", i=4, s=16)
        P.op("dve", lambda e: e.tensor_tensor(out=bm[:, cs].rearrange("p (g i s) -> p g i s", i=4, s=16), in0=bc4,
                                              in1=bc4[:, :, :, 0:1].to_broadcast([128, GH, 4, 16]), op=ALU.subtract),
             reads=[KH("bc")], writes=[KH("bm")])
        P.op("dve", lambda e: e.tensor_tensor(out=g3(be[:, cs]), in0=bc3, in1=bc3[:, :, CH - 1:CH].to_broadcast([128, GH, CH]), op=ALU.subtract),
             reads=[KH("bc")], writes=[KH("be")])
        P.op("act", lambda e: e.activation(out=ex0[:, cs], in_=bm[:, cs], func=AF.Exp), reads=[KH("bm")], writes=[KH("ex0")])
        P.op("act", lambda e: e.activation(out=ex2[:, cs], in_=bc[:, cs], func=AF.Exp), reads=[KH("bc")], writes=[KH("ex2")])
        P.op("act", lambda e: e.activation(out=ex3[:, cs], in_=be[:, cs], func=AF.Exp, scale=-1.0), reads=[KH("be")], writes=[KH("ex3")])
        P.op("act", lambda e: e.activation(out=gam[tp][:, hf * GH:(hf + 1) * GH], in_=bc3[:, :, CH - 1], func=AF.Exp), reads=[KH("bc")], writes=[KT("gam")])
        P.op("dve", lambda e: e.scalar_tensor_tensor(out=qh[tp][:, cs], in0=q2, scalar=QS, in1=ex0[:, cs], op0=ALU.mult, op1=ALU.mult),
             reads=[KT("Q"), KH("ex0")], writes=[KT("qh")])
        P.op("dve", lambda e: e.scalar_tensor_tensor(out=qtl[tp][:, cs], in0=q2, scalar=QS, in1=ex2[:, cs], op0=ALU.mult, op1=ALU.mult),
             reads=[KT("Q"), KH("ex2")], writes=[KT("qtl")])
        P.op("pool", lambda e: e.tensor_tensor(out=ktl[tp][:, cs], in0=kq[:, cs], in1=ex3[:, cs], op=ALU.mult), reads=[KH("kq"), KH("ex3")], writes=[KT("ktl")])
        kq3 = g3(kq[:, cs])
        bk3 = g3(bk[:, cs])
        for I in range(4):
            n = 16 * (I + 1)
            P.op("dve", lambda e, n=n, I=I: e.tensor_tensor(out=bk3[:, :, :n], in0=bc3[:, :, :n],
                                                          in1=bc3[:, :, 16 * I:16 * I + 1].to_broadcast([128, GH, n]), op=ALU.subtract),
                 reads=[KH("bc"), KH("bk")], writes=[KH("bk")])
            P.op("act", lambda e, n=n: e.activation(out=bk3[:, :, :n], in_=bk3[:, :, :n], func=AF.Exp, scale=-1.0), reads=[KH("bk")], writes=[KH("bk")])
            eng_ = "pool" if I % 2 == 0 else "dve"
            P.op(eng_, lambda e, n=n, I=I: e.tensor_tensor(out=g3(khI[tp][I][:, cs])[:, :, :n], in0=kq3[:, :, :n], in1=bk3[:, :, :n], op=ALU.mult),
                 reads=[KH("kq"), KH("bk")], writes=[KT("kh%d" % I)])
        P._defer = None
        return lst

    def emit_R(t, r):
        tp = t % 2
        hf_ = r // (NR // NH)
        KT = lambda nm: (nm, tp, hf_) if nm != "osb" else (nm, tp)
        lst = []
        P._defer = lst
        for c in range(2):
            g = r * 2 + c
            go = g * CH
            p = (r + c) % 2
            pA_t = pA_bank[p][:, 0:CH]
            pO_t = pO_bank[p][:, 0:CH]
            pT_t = pT_bank[p][:, 0:128]
            pS_t = pS_bank[p][:, 0:128]
            att_t = att_all[r][c]
            ktm_t = ktm_all[r][c]
            vb_t = Vb[tp][:, r, c, :]
            P.atomic_begin()
            for I in range(4):
                n = 16 * (I + 1)
                P.op("pe", lambda e, I=I, n=n, pA_t=pA_t, go=go: e.matmul(pA_t[:n, 16 * I:16 * I + 16], lhsT=khI[tp][I][:, go:go + n],
                                                                         rhs=qh[tp][:, go + 16 * I:go + 16 * I + 16], start=True, stop=True),
                     reads=[KT("kh%d" % I), KT("qh")], writes=[("pA", p)])
            P.op("dve", lambda e, att_t=att_t, pA_t=pA_t: e.tensor_tensor(out=att_t[:], in0=pA_t[:CH, :], in1=mask[:], op=ALU.mult),
                 reads=[("pA", p), "mask"], writes=[("att", r, c)])
            P.atomic_end()
            P.atomic_begin()
            P.op("pe", lambda e, att_t=att_t, pO_t=pO_t, vb_t=vb_t: e.matmul(pO_t, lhsT=vb_t, rhs=att_t[:], start=True, stop=False),
                 reads=[KT("Vb"), ("att", r, c), ("Sb", r), KT("qtl")], writes=[("pO", p)])
            P.op("pe", lambda e, pO_t=pO_t, go=go: e.matmul(pO_t, lhsT=Sb[r][:], rhs=qtl[tp][:, go:go + CH], start=False, stop=True),
                 reads=[("Sb", r), KT("qtl")], writes=[("pO", p)])
            P.op("act", lambda e, pO_t=pO_t, c=c: e.copy(out=osb[tp][:, r, c * CH:(c + 1) * CH], in_=pO_t),
                 reads=[("pO", p)], writes=[KT("osb")])
            P.atomic_end()
            P.atomic_begin()
            P.op("pe", lambda e, pT_t=pT_t, go=go: e.transpose(out=pT_t[:CH, :], in_=ktl[tp][:, go:go + CH], identity=ident[:]),
                 reads=[KT("ktl"), "ident"], writes=[("pT", p)])
            P.op("act", lambda e, ktm_t=ktm_t, pT_t=pT_t: e.copy(out=ktm_t[:], in_=pT_t[:CH, :]), reads=[("pT", p)], writes=[("ktm", r, c)])
            P.atomic_end()
            P.atomic_begin()
            P.op("pe", lambda e, ktm_t=ktm_t, pS_t=pS_t, vb_t=vb_t: e.matmul(pS_t, lhsT=ktm_t[:], rhs=vb_t, start=True, stop=True),
                 reads=[("ktm", r, c), KT("Vb")], writes=[("pS", p)])
            P.op("dve", lambda e, pS_t=pS_t, g=g: e.scalar_tensor_tensor(out=S[r][:], in0=S[r][:], scalar=gam[tp][:, g:g + 1],
                                                                        in1=pS_t, op0=ALU.mult, op1=ALU.add),
                 reads=[("S", r), KT("gam"), ("pS", p)], writes=[("S", r)])
            P.atomic_end()
            P.op("pool", lambda e: e.tensor_copy(out=Sb[r][:], in_=S[r][:]), reads=[("S", r)], writes=[("Sb", r)])
        P._defer = None
        return lst

    P.interleave([emit_W(0, hf) for hf in range(NH)])
    for t in range(ntile):
        ts = slice(t * 128, (t + 1) * 128)
        streams = [emit_R(t, r) for r in range(nrec)]
        if t + 1 < ntile:
            streams += [emit_W(t + 1, hf) for hf in range(NH)]
        P.interleave(streams)
        P.dma("pool", out[:, :, ts].rearrange("r p t -> p r t"), osb[t % 2][:], reads=[("osb", t % 2)], writes=["out"])
    P.finish(["out"])
    return nc


def _seq(lat_b, ctx_b, rev):
    if rev:
        return np.concatenate([ctx_b[::-1], lat_b[::-1]], axis=0)
    return np.concatenate([ctx_b, lat_b], axis=0)


def _unseq(s, rev):
    c, l = s[:T_CTX], s[T_CTX:]
    if rev:
        return l[::-1], c[::-1]
    return l, c


def run_Ma(pl, pc, a_lb_raw, jlayer):
    nrec = 8
    nc = build_Ma(nrec, L_SEQ, jlayer)
    ident = np.eye(128, dtype=np.float32)
    mask = np.triu(np.ones((CH, CH), np.float32))
    in_maps = []
    for cid in range(NCORES):
        b, hg = cid // 4, cid % 4
        qT = np.empty((nrec, 128, L_SEQ), np.float32)
        zT = np.empty((nrec, 128, L_SEQ), np.float32)
        v = np.empty((nrec, L_SEQ // CH, CH, 128), np.float32)
        lbr = np.empty((nrec, 128, 2), np.float32)
        for hl in range(4):
            h = hg * 4 + hl
            hc = slice(h * 128, (h + 1) * 128)
            for dr in range(2):
                r = hl * 2 + dr
                zoff = 2 * D + dr * D
                sq = _seq(pl[b][:, hc], pc[b][:, hc], dr == 1)
                sv = _seq(pl[b][:, D + h * 128:D + (h + 1) * 128], pc[b][:, D + h * 128:D + (h + 1) * 128], dr == 1)
                sz = _seq(pl[b][:, zoff + h * 128:zoff + (h + 1) * 128], pc[b][:, zoff + h * 128:zoff + (h + 1) * 128], dr == 1)
                qT[r] = sq.T
                zT[r] = sz.T
                v[r] = sv.reshape(L_SEQ // CH, CH, 128)
                lbr[r] = a_lb_raw[:, hc].T
        in_maps.append({"qT": qT, "zT": zT, "v": v, "lbr": lbr, "mask": mask, "ident": ident})
    res = _run(nc, in_maps)
    o_lat = [np.empty((2, T_LAT, D), np.float32) for _ in range(2)]
    o_ctx = [np.empty((2, T_CTX, D), np.float32) for _ in range(2)]
    for cid in range(NCORES):
        b, hg = cid // 4, cid % 4
        oT = res[cid]["oT"]
        for hl in range(4):
            h = hg * 4 + hl
            for dr in range(2):
                l, c = _unseq(oT[hl * 2 + dr].T, dr == 1)
                o_lat[dr][b][:, h * 128:(h + 1) * 128] = l
                o_ctx[dr][b][:, h * 128:(h + 1) * 128] = c
    return o_lat, o_ctx


def layer_a(x, ctx, mod_i, norm_w_i, w_in, a_lb_raw, jlayer, onorm_w, w_out, need_ctx, final_w=None):
    pl, pc = run_P(x, ctx, mod_i, norm_w_i, w_in)
    o_lat, o_ctx = run_Ma(pl, pc, a_lb_raw, jlayer)
    ml = {"m0": o_lat[0], "m1": o_lat[1], "gate": pl[:, :, 4 * D:5 * D]}
    mc = {"m0": o_ctx[0], "m1": o_ctx[1], "gate": pc[:, :, 4 * D:5 * D]}
    bcs = {"onw": np.tile(onorm_w, 16)}
    if final_w is not None:
        bcs["fnw"] = final_w
    return run_O(x, ctx, "a", ml, mc, w_out, mod_i[0:2, 2 * D:3 * D], mod_i[2, 2 * D:3 * D], bcs,
                 final=final_w is not None, need_ctx=need_ctx)


NQH = 8
HD = 64
NBLK = T_LAT // 128


def build_Mb(need_ctx):
    nc = bass.Bass("TRN2", target_bir_lowering=False)
    P = Prog(nc)
    qd = P.dram("qT", [HD, NQH, T_LAT])
    qsd = P.dram("qsT", [HD, NQH, T_LAT])
    kd = P.dram("kT", [HD, T_LAT])
    ksd = P.dram("ksT", [HD, T_LAT])
    vd = P.dram("v", [128, NBLK, HD])
    qcd = P.dram("qcT", [HD, NQH, T_CTX])
    kcd = P.dram("kcT", [HD, T_CTX])
    vcd = P.dram("vc", [128, 2, HD])
    posd = P.dram("pos", [HD, T_LAT])
    fid = P.dram("fidx", [HD, 2])
    sinkd = P.dram("sink", [128, NQH])
    mld = P.dram("maskl", [128, 128])
    mrd = P.dram("maskr", [128, 128])
    n_out = T_LAT + (T_CTX if need_ctx else 0)
    out = P.dram("o", [n_out, NQH * HD], kind="ExternalOutput")

    PI = float(np.pi)
    CW = 2048
    cosT = P.sb([HD, T_LAT])
    sinT = P.sb([HD, T_LAT])
    posc = P.sb([HD, CW])
    tmpA = P.sb([HD, CW])
    tmpB = P.sb([HD, CW])
    fidx = P.sb([HD, 2])
    inv = P.sb([HD, 1])
    kr = P.sb([HD, T_LAT], BF16)
    kcb = P.sb([HD, T_CTX], BF16)
    kcf = P.sb([HD, T_CTX])
    vf = P.sb([128, NBLK, HD])
    vx = P.sb([128, NBLK, HD + 1], BF16)
    vcf = P.sb([128, 2, HD])
    vcx = P.sb([128, 2, HD + 1], BF16)
    esink = P.sb([128, NQH])
    ml = P.sb([128, 128], BF16)
    mr = P.sb([128, 128], BF16)
    mlf = P.sb([128, 128])
    mrf = P.sb([128, 128])
    qf = [P.sb([HD, NQH, 128]) for _ in range(2)]
    qsf = [P.sb([HD, NQH, 128]) for _ in range(2)]
    qt1 = P.sb([HD, NQH, 128])
    qt2 = P.sb([HD, NQH, 128])
    qr = [P.sb([HD, NQH, 128], BF16) for _ in range(2)]
    E = [P.sb([128, NQH, 128], BF16) for _ in range(5)]
    pS = [P.ps([128, 1024]) for _ in range(2)]
    pO = [P.ps([128, 4, 128]) for _ in range(2)]
    den = P.sb([128, NQH])
    osb = [P.sb([128, NQH, HD]) for _ in range(2)]

    P.dma("sp", fidx[:], fid[:], writes=["fidx"])
    P.dma("sp", esink[:], sinkd[:], writes=["esink"])
    P.dma("sp", mlf[:], mld[:], writes=["mlf"])
    P.dma("sp", mrf[:], mrd[:], writes=["mrf"])
    P.op("dve", lambda e: e.tensor_copy(out=ml[:], in_=mlf[:]), reads=["mlf"], writes=["ml"])
    P.op("dve", lambda e: e.tensor_copy(out=mr[:], in_=mrf[:]), reads=["mrf"], writes=["mr"])
    P.op("act", lambda e: e.activation(out=esink[:], in_=esink[:], func=AF.Exp), reads=["esink"], writes=["esink"])
    P.op("act", lambda e: e.activation(out=inv[:], in_=fidx[:, 0:1], func=AF.Exp, scale=-float(np.log(10000.0)) / 16.0),
         reads=["fidx"], writes=["inv"])

    def sincos(dst_full, shift, key, cc):
        cs_ = slice(cc * CW, (cc + 1) * CW)
        dst = dst_full[:, cs_]
        P.op("dve", lambda e: e.tensor_scalar(out=tmpA[:], in0=posc[:], scalar1=inv[:, 0:1], scalar2=shift, op0=ALU.mult, op1=ALU.add),
             reads=["pos", "inv"], writes=["tmpA"])
        ni = tmpB[:].bitcast(mybir.dt.int32)
        P.op("dve", lambda e: e.tensor_scalar(out=dst, in0=tmpA[:], scalar1=1.0 / (2 * PI), scalar2=0.5, op0=ALU.mult, op1=ALU.add),
             reads=["tmpA"], writes=[key])
        P.op("dve", lambda e: e.tensor_copy(out=ni, in_=dst), reads=[key], writes=["tmpB"])
        P.op("dve", lambda e: e.tensor_copy(out=dst, in_=ni), reads=["tmpB"], writes=[key])
        P.op("dve", lambda e: e.scalar_tensor_tensor(out=tmpA[:], in0=dst, scalar=-2 * PI, in1=tmpA[:], op0=ALU.mult, op1=ALU.add),
             reads=[key, "tmpA"], writes=["tmpA"])
        P.op("dve", lambda e: e.tensor_scalar(out=tmpB[:], in0=tmpA[:], scalar1=-PI, scalar2=2 * PI, op0=ALU.is_lt, op1=ALU.mult),
             reads=["tmpA"], writes=["tmpB"])
        P.op("dve", lambda e: e.tensor_tensor(out=tmpA[:], in0=tmpA[:], in1=tmpB[:], op=ALU.add), reads=["tmpA", "tmpB"], writes=["tmpA"])
        P.op("dve", lambda e: e.tensor_scalar(out=tmpB[:], in0=tmpA[:], scalar1=PI, scalar2=-2 * PI, op0=ALU.is_gt, op1=ALU.mult),
             reads=["tmpA"], writes=["tmpB"])
        P.op("dve", lambda e: e.tensor_tensor(out=tmpA[:], in0=tmpA[:], in1=tmpB[:], op=ALU.add), reads=["tmpA", "tmpB"], writes=["tmpA"])
        P.op("dve", lambda e: e.tensor_scalar(out=tmpA[:], in0=tmpA[:], scalar1=PI, scalar2=-PI, op0=ALU.min, op1=ALU.max),
             reads=["tmpA"], writes=["tmpA"])
        P.op("act", lambda e: e.activation(out=dst, in_=tmpA[:], func=AF.Sin), reads=["tmpA"], writes=[key])

    for cc in range(T_LAT // CW):
        P.dma("sp", posc[:], posd[:, cc * CW:(cc + 1) * CW], writes=["pos"])
        sincos(sinT, 0.0, "sinT", cc)
        sincos(cosT, PI / 2, "cosT", cc)
    P.op("dve", lambda e: e.tensor_scalar(out=sinT[:], in0=sinT[:], scalar1=fidx[:, 1:2], scalar2=None, op0=ALU.mult),
         reads=["sinT", "fidx"], writes=["sinT"])

    for cc in range(T_LAT // CW):
        cs_ = slice(cc * CW, (cc + 1) * CW)
        P.dma("sp", tmpA[:], kd[:, cs_], writes=["tmpA"])
        P.dma("sp", tmpB[:], ksd[:, cs_], writes=["tmpB"])
        P.op("dve", lambda e, cs_=cs_: e.tensor_tensor(out=tmpA[:], in0=tmpA[:], in1=cosT[:, cs_], op=ALU.mult), reads=["tmpA", "cosT"], writes=["tmpA"])
        P.op("pool", lambda e, cs_=cs_: e.tensor_tensor(out=tmpB[:], in0=tmpB[:], in1=sinT[:, cs_], op=ALU.mult), reads=["tmpB", "sinT"], writes=["tmpB"])
        P.op("dve", lambda e, cs_=cs_: e.tensor_tensor(out=kr[:, cs_], in0=tmpA[:], in1=tmpB[:], op=ALU.add), reads=["tmpA", "tmpB"], writes=["kr"])
    P.dma("sp", kcf[:], kcd[:], writes=["kcf"])
    P.op("dve", lambda e: e.tensor_copy(out=kcb[:], in_=kcf[:]), reads=["kcf"], writes=["kcb"])
    P.dma("sp", vf[:], vd[:], writes=["vf"])
    P.dma("sp", vcf[:], vcd[:], writes=["vcf"])
    P.op("pool", lambda e: e.memset(vx[:], 1.0), writes=["vx"])
    P.op("pool", lambda e: e.memset(vcx[:], 1.0), writes=["vcx"])
    P.op("pool", lambda e: e.tensor_copy(out=vx[:, :, 0:HD], in_=vf[:]), reads=["vf"], writes=["vx"])
    P.op("pool", lambda e: e.tensor_copy(out=vcx[:, :, 0:HD], in_=vcf[:]), reads=["vcf"], writes=["vcx"])

    blocks = [("lat", j) for j in range(NBLK)]
    if need_ctx:
        blocks += [("ctx", j) for j in range(2)]
    for bi, (typ, j) in enumerate(blocks):
        b = bi % 2
        ts = slice(j * 128, (j + 1) * 128)
        if typ == "lat":
            P.dma("sp", qf[b][:], qd[:, :, ts], writes=[("qf", b)])
            P.dma("sp", qsf[b][:], qsd[:, :, ts], writes=[("qsf", b)])
            P.op("dve", lambda e, b=b, ts=ts: e.tensor_tensor(out=qt1[:], in0=qf[b][:], in1=cosT[:, None, ts].to_broadcast([HD, NQH, 128]),
                                                            op=ALU.mult), reads=[("qf", b), "cosT"], writes=["qt1"])
            P.op("pool", lambda e, b=b, ts=ts: e.tensor_tensor(out=qt2[:], in0=qsf[b][:], in1=sinT[:, None, ts].to_broadcast([HD, NQH, 128]),
                                                             op=ALU.mult), reads=[("qsf", b), "sinT"], writes=["qt2"])
            P.op("dve", lambda e, b=b: e.tensor_tensor(out=qr[b][:], in0=qt1[:], in1=qt2[:], op=ALU.add),
                 reads=["qt1", "qt2"], writes=[("qr", b)])
            kbs = []
            if j > 0:
                kbs.append(("lat", j - 1, "ml"))
            kbs.append(("lat", j, None))
            if j < NBLK - 1:
                kbs.append(("lat", j + 1, "mr"))
            kbs += [("ctx", 0, None), ("ctx", 1, None)]
            orow = j * 128
        else:
            P.dma("sp", qf[b][:], qcd[:, :, ts], writes=[("qf", b)])
            P.op("dve", lambda e, b=b: e.tensor_copy(out=qr[b][:], in_=qf[b][:]), reads=[("qf", b)], writes=[("qr", b)])
            kbs = [("ctx", 0, None), ("ctx", 1, None)]
            orow = T_LAT + j * 128
        for ki, (kt, kj, mk) in enumerate(kbs):
            p = ki % 2
            ksl = slice(kj * 128, (kj + 1) * 128)
            kap = kr[:, ksl] if kt == "lat" else kcb[:, ksl]
            kkey = "kr" if kt == "lat" else "kcb"
            for hh in range(2):
                P.op("pe", lambda e, b=b, p=p, hh=hh, kap=kap: e.matmul(pS[p][:, hh * 512:(hh + 1) * 512], lhsT=kap,
                                                                      rhs=qr[b][:, hh * 4:(hh + 1) * 4, :], start=True, stop=True),
                     reads=[kkey, ("qr", b)], writes=[("pS", p)])
            P.op("act", lambda e, p=p, ki=ki: e.activation(out=E[ki][:].rearrange("p h q -> p (h q)"), in_=pS[p][:], func=AF.Exp, scale=HD ** -0.5),
                 reads=[("pS", p)], writes=[("E", ki)])
            if mk is not None:
                mt = ml if mk == "ml" else mr
                P.op("dve", lambda e, ki=ki, mt=mt: e.tensor_tensor(out=E[ki][:], in0=E[ki][:], in1=mt[:, None, :].to_broadcast([128, NQH, 128]),
                                                                   op=ALU.mult), reads=[("E", ki), mk], writes=[("E", ki)])
        nk = len(kbs)
        for h in range(NQH):
            po = h // 4
            for ki, (kt, kj, mk) in enumerate(kbs):
                vap = vx[:, kj, :] if kt == "lat" else vcx[:, kj, :]
                vkey = "vx" if kt == "lat" else "vcx"
                P.op("pe", lambda e, h=h, po=po, ki=ki, vap=vap: e.matmul(pO[po][:, h % 4, 0:HD + 1], lhsT=E[ki][:, h, :], rhs=vap,
                                                                        start=(ki == 0), stop=(ki == nk - 1)),
                     reads=[("E", ki), vkey], writes=[("pO", po)])
        for po in range(2):
            hs = slice(po * 4, (po + 1) * 4)
            P.op("dve", lambda e, po=po, hs=hs: e.tensor_tensor(out=den[:, hs], in0=pO[po][:, :, HD], in1=esink[:, hs], op=ALU.add),
                 reads=[("pO", po), "esink"], writes=["den"])
            P.op("dve", lambda e, hs=hs: e.reciprocal(out=den[:, hs], in_=den[:, hs]), reads=["den"], writes=["den"])
            P.op("dve", lambda e, po=po, hs=hs, b=b: e.tensor_tensor(out=osb[b][:, hs, :], in0=pO[po][:, :, 0:HD],
                                                                    in1=den[:, hs].unsqueeze(2).to_broadcast([128, 4, HD]), op=ALU.mult),
                 reads=[("pO", po), "den"], writes=[("osb", b)])
        P.dma("pool", out[orow:orow + 128, :], osb[b][:].rearrange("p h d -> p (h d)"), reads=[("osb", b)], writes=["out"])
    P.finish(["out"])
    return nc


def _rope_swap(a):
    hd = a.shape[-1]
    idx = np.arange(hd)
    idx = (idx // 32) * 32 + ((idx % 32) + 16) % 32
    return a[..., idx]


def run_Mb(pl, pc, sink, need_ctx):
    nc = build_Mb(need_ctx)
    t = np.arange(T_LAT)
    pos = np.empty((HD, T_LAT), np.float32)
    pos[:32] = (t // 64)[None, :]
    pos[32:] = (t % 64)[None, :]
    d = np.arange(HD)
    fidx = np.stack([(d % 16).astype(np.float32), np.where((d % 32) < 16, -1.0, 1.0).astype(np.float32)], axis=1)
    jj, ii = np.meshgrid(np.arange(128), np.arange(128), indexing="ij")
    maskl = (jj >= ii).astype(np.float32)
    maskr = (jj <= ii).astype(np.float32)
    in_maps = []
    for cid in range(NCORES):
        b, g = cid // 4, cid % 4
        q = pl[b][:, g * 512:(g + 1) * 512].reshape(T_LAT, NQH, HD)
        k = pl[b][:, D + g * HD:D + (g + 1) * HD]
        v = pl[b][:, D + 256 + g * HD:D + 256 + (g + 1) * HD]
        qc = pc[b][:, g * 512:(g + 1) * 512].reshape(T_CTX, NQH, HD)
        kc = pc[b][:, D + g * HD:D + (g + 1) * HD]
        vc = pc[b][:, D + 256 + g * HD:D + 256 + (g + 1) * HD]
        m = {"qT": np.ascontiguousarray(q.transpose(2, 1, 0)), "qsT": np.ascontiguousarray(_rope_swap(q).transpose(2, 1, 0)),
             "kT": np.ascontiguousarray(k.T), "ksT": np.ascontiguousarray(_rope_swap(k).T),
             "v": np.ascontiguousarray(v.reshape(NBLK, 128, HD).transpose(1, 0, 2)),
             "qcT": np.ascontiguousarray(qc.transpose(2, 1, 0)), "kcT": np.ascontiguousarray(kc.T),
             "vc": np.ascontiguousarray(vc.reshape(2, 128, HD).transpose(1, 0, 2)),
             "pos": pos, "fidx": fidx, "sink": _bc(sink[g * NQH:(g + 1) * NQH]), "maskl": maskl, "maskr": maskr}
        in_maps.append(m)
    res = _run(nc, in_maps)
    ol = np.empty((2, T_LAT, D), np.float32)
    oc = np.empty((2, T_CTX, D), np.float32) if need_ctx else None
    for cid in range(NCORES):
        b, g = cid // 4, cid % 4
        o = res[cid]["o"]
        ol[b][:, g * 512:(g + 1) * 512] = o[:T_LAT]
        if need_ctx:
            oc[b][:, g * 512:(g + 1) * 512] = o[T_LAT:]
    return ol, oc


def layer_b(x, ctx, mod_i, norm_w_i, w_in, sink, w_out, need_ctx, final_w=None):
    pl, pc = run_P(x, ctx, mod_i, norm_w_i, w_in)
    ol, oc = run_Mb(pl, pc, sink, need_ctx)
    ml = {"m0": ol, "gate": pl[:, :, D + 512:]}
    mc = {"m0": oc, "gate": pc[:, :, D + 512:]}
    bcs = {}
    if final_w is not None:
        bcs["fnw"] = final_w
    return run_O(x, ctx, "b", ml, mc, w_out, mod_i[0:2, 2 * D:3 * D], mod_i[2, 2 * D:3 * D], bcs,
                 final=final_w is not None, need_ctx=need_ctx)


NH_C = 8
MC_INTERLEAVE = False
MC_FP32R = True
MC_PIPE = True


def R32(ap):
    return ap.bitcast(mybir.dt.float32r) if MC_FP32R else ap
HC = 64
LORA = 96


def build_Mc(L):
    ntile = L // 128
    W = NH_C * HC
    nc = bass.Bass("TRN2", target_bir_lowering=False)
    P = Prog(nc)
    rd = P.dram("r", [2, L, W])
    kd = P.dram("k", [2, L, W])
    vd = P.dram("v", [2, L, W])
    lwd = P.dram("lwT", [2, LORA, L])
    lad = P.dram("laT", [2, LORA, L])
    w2d = P.dram("w2", [2, LORA, W])
    a2d = P.dram("a2", [2, LORA, W])
    w0d = P.dram("w0b", [2, 128, W])
    a0d = P.dram("a0b", [2, 128, W])
    kkd = P.dram("kkb", [128, W])
    kad = P.dram("kab", [128, W])
    rkd = P.dram("rkb", [128, W])
    trid = P.dram("tri", [128, 128])
    mupd = P.dram("mup", [128, 128])
    mlod = P.dram("mlo", [128, 128])
    identd = P.dram("ident", [128, 128])
    seld = P.dram("sel", [128, 1])
    yout = P.dram("y", [2, L, W], kind="ExternalOutput")
    bout = P.dram("bon", [2, L, W], kind="ExternalOutput")

    def T2(n, dt=F32, shape=(128, W)):
        return [P.sb(list(shape), dt) for _ in range(n)]

    ident = P.sb([128, 128])
    identb = P.sb([128, 128], BF16)
    tri = P.sb([128, 128])
    mup = P.sb([128, 128])
    mupi = P.sb([128, 128])
    mlo = P.sb([128, 128])
    sel = P.sb([128, 1])
    w2 = T2(2, F32, (LORA, W))
    a2 = T2(2, F32, (LORA, W))
    w0b = T2(2)
    a0b = T2(2)
    kkb = P.sb([128, W])
    kab = P.sb([128, W])
    omka = P.sb([128, W])
    rkb = P.sb([128, W])
    rt, kt, vt = T2(2), T2(2), T2(2)
    lw = T2(2, F32, (LORA, 128))
    la = T2(2, F32, (LORA, 128))
    th_ = T2(2, F32, (LORA, 128))
    zt_, ld_, iclr_, kkr_, kk_, t1_, kdt_ = T2(2), T2(2), T2(2), T2(2), T2(2), T2(2), T2(2)
    ein_, eneg_ = T2(2), T2(2)
    st8_ = [[P.sb([128, NH_C]) for _ in range(3)] for _ in range(2)]
    Ah_, Bh_, Kh_, Rh_, Vb_ = T2(4, BF16), T2(4, BF16), T2(4, BF16), T2(4, BF16), T2(4, BF16)
    XT_ = [{nm: P.sb([HC, NH_C, 128], BF16) for nm in ("a", "b", "k", "r")} for _ in range(4)]
    gamT_ = [P.sb([HC, NH_C]) for _ in range(4)]
    Nf_ = [[[P.sb([128, 4, 128]) for _ in range(2)] for _ in range(2)] for _ in range(2)]
    NTf_ = [[[P.sb([128, 4, 128]) for _ in range(2)] for _ in range(2)] for _ in range(2)]
    TT_ = [[P.sb([128, 4, 128]) for _ in range(2)] for _ in range(2)]
    TTb_ = [[P.sb([128, 4, 128], BF16) for _ in range(2)] for _ in range(2)]
    AakT_ = [[P.sb([128, 4, 128], BF16) for _ in range(2)] for _ in range(2)]
    AVb_ = [[P.sb([128, 4, HC], BF16) for _ in range(2)] for _ in range(2)]
    ArbT_ = [P.sb([128, NH_C, 128], BF16) for _ in range(2)]
    ArkT_ = [P.sb([128, NH_C, 128], BF16) for _ in range(2)]
    TAb_ = [P.sb([128, NH_C, HC], BF16) for _ in range(2)]
    TVb_ = [P.sb([128, NH_C, HC], BF16) for _ in range(2)]
    MTb_ = [P.sb([HC, NH_C, HC], BF16) for _ in range(2)]
    RQTb_ = [P.sb([HC, NH_C, 128], BF16) for _ in range(2)]
    Pb = [P.sb([HC, NH_C, HC], BF16) for _ in range(2)]
    ysb = T2(2, F32, (128, NH_C, HC))
    pz = P.ps([128, 512])
    ptr = [P.ps([128, 1024], BF16) for _ in range(2)]
    big = [P.ps([128, 4, 128]) for _ in range(2)]
    py = P.ps([128, NH_C, HC])
    pp = P.ps([128, NH_C, HC])
    pg = P.ps([128, 512])

    for (t_, d_, k_) in [(ident, identd, "ident"), (tri, trid, "tri"), (mup, mupd, "mup"), (mlo, mlod, "mlo"), (sel, seld, "sel"),
                         (kkb, kkd, "kkb"), (kab, kad, "kab"), (rkb, rkd, "rkb")]:
        P.dma("sp", t_[:], d_[:], writes=[k_])
    for d in range(2):
        P.dma("sp", w2[d][:], w2d[d], writes=[("w2", d)])
        P.dma("sp", a2[d][:], a2d[d], writes=[("a2", d)])
        P.dma("sp", w0b[d][:], w0d[d], writes=[("w0b", d)])
        P.dma("sp", a0b[d][:], a0d[d], writes=[("a0b", d)])
        P.op("pool", lambda e, d=d: e.memset(Pb[d][:], 0.0), writes=[("Pb", d)])
    P.op("dve", lambda e: e.tensor_copy(out=identb[:], in_=ident[:]), reads=["ident"], writes=["identb"])
    P.op("dve", lambda e: e.tensor_tensor(out=mupi[:], in0=mup[:], in1=ident[:], op=ALU.add), reads=["mup", "ident"], writes=["mupi"])
    P.op("dve", lambda e: e.tensor_scalar(out=omka[:], in0=kab[:], scalar1=-1.0, scalar2=None, op0=ALU.mult),
         reads=["kab"], writes=["omka"])
    P.op("dve", lambda e: e.tensor_scalar(out=omka[:], in0=omka[:], scalar1=1.0, scalar2=None, op0=ALU.add),
         reads=["omka"], writes=["omka"])

    bi = [0]

    def emit_dir(t, d):
        rows = slice(t * 128, (t + 1) * 128)
        b = d
        dp = d * 2 + (t % 2)
        IFACE = ("Ah", "Bh", "Kh", "Rh", "Vb", "XTa", "XTb", "XTk", "XTr", "gamT")
        K = lambda nm: (nm, dp) if nm in IFACE else (nm, d)
        th, zt, ld, iclr, kkr, kk, t1, kdt = th_[d], zt_[d], ld_[d], iclr_[d], kkr_[d], kk_[d], t1_[d], kdt_[d]
        bt = kkr
        sq = zt
        ein, eneg, st8 = ein_[d], eneg_[d], st8_[d]
        eex = zt
        Ah, Bh, Kh, Rh, Vb, XT, gamT = Ah_[dp], Bh_[dp], Kh_[dp], Rh_[dp], Vb_[dp], XT_[dp], gamT_[dp]
        ArbT, ArkT, TAb, TVb, MTb, RQTb = ArbT_[d], ArkT_[d], TAb_[d], TVb_[d], MTb_[d], RQTb_[d]
        main = []
        P._defer = main

        def sigmoid_tail(dst, key):
            P.op("act", lambda e: e.activation(out=zt[:], in_=zt[:], func=AF.Exp, scale=-1.0), reads=[K("zt")], writes=[K("zt")])
            P.op("dve", lambda e: e.tensor_scalar(out=zt[:], in0=zt[:], scalar1=1.0, scalar2=None, op0=ALU.add), reads=[K("zt")], writes=[K("zt")])
            P.op("dve", lambda e: e.reciprocal(out=dst, in_=zt[:]), reads=[K("zt")], writes=[key])

        P.dma("sp", rt[b][:], rd[d, rows, :], writes=[K("rt")])
        P.dma("sp", kt[b][:], kd[d, rows, :], writes=[K("kt")])
        P.dma("sp", vt[b][:], vd[d, rows, :], writes=[K("vt")])
        P.dma("sp", lw[b][:], lwd[d, :, rows], writes=[K("lw")])
        P.dma("sp", la[b][:], lad[d, :, rows], writes=[K("la")])
        P.op("act", lambda e: e.activation(out=th[:], in_=lw[b][:], func=AF.Tanh), reads=[K("lw")], writes=[K("th")])
        P.atomic_begin()
        P.op("pe", lambda e: e.matmul(pz[:], lhsT=th[:], rhs=w2[d][:], start=True, stop=True),
             reads=[K("th"), ("w2", d)], writes=["pz"])
        P.op("dve", lambda e: e.tensor_tensor(out=zt[:], in0=pz[:], in1=w0b[d][:], op=ALU.add), reads=["pz", ("w0b", d)], writes=[K("zt")])
        P.atomic_end()
        sigmoid_tail(ld[:], K("ld"))
        P.op("pool", lambda e: e.tensor_scalar(out=ld[:], in0=ld[:], scalar1=-float(np.exp(-0.5)), scalar2=None, op0=ALU.mult),
             reads=[K("ld")], writes=[K("ld")])
        P.atomic_begin()
        P.op("pe", lambda e: e.matmul(pz[:], lhsT=la[b][:], rhs=a2[d][:], start=True, stop=True),
             reads=[K("la"), ("a2", d)], writes=["pz"])
        P.op("dve", lambda e: e.tensor_tensor(out=zt[:], in0=pz[:], in1=a0b[d][:], op=ALU.add), reads=["pz", ("a0b", d)], writes=[K("zt")])
        P.atomic_end()
        sigmoid_tail(iclr[:], K("iclr"))
        h3 = lambda ap: ap.rearrange("p (h c) -> p h c", c=HC)
        P.op("pool", lambda e: e.tensor_tensor(out=kkr[:], in0=kt[b][:], in1=kkb[:], op=ALU.mult), reads=[K("kt"), "kkb"], writes=[K("kkr")])
        P.op("pool", lambda e: e.tensor_tensor(out=sq[:], in0=kkr[:], in1=kkr[:], op=ALU.mult), reads=[K("kkr")], writes=[K("zt")])
        P.op("dve", lambda e: e.tensor_reduce(out=st8[0][:], in_=h3(sq[:]), axis=AX.X, op=ALU.add), reads=[K("zt")], writes=[K("st0")])
        P.op("act", lambda e: e.activation(out=st8[0][:], in_=st8[0][:], func=AF.Sqrt), reads=[K("st0")], writes=[K("st0")])
        P.op("dve", lambda e: e.tensor_scalar(out=st8[0][:], in0=st8[0][:], scalar1=1e-12, scalar2=None, op0=ALU.max), reads=[K("st0")], writes=[K("st0")])
        P.op("dve", lambda e: e.reciprocal(out=st8[1][:], in_=st8[0][:]), reads=[K("st0")], writes=[K("st1")])
        P.op("dve", lambda e: e.tensor_tensor(out=h3(kk[:]), in0=h3(kkr[:]), in1=st8[1][:].unsqueeze(2).to_broadcast([128, NH_C, HC]), op=ALU.mult),
             reads=[K("kkr"), K("st1")], writes=[K("kk")])
        P.op("dve", lambda e: e.tensor_tensor(out=t1[:], in0=iclr[:], in1=kab[:], op=ALU.mult), reads=[K("iclr"), "kab"], writes=[K("t1")])
        P.op("pool", lambda e: e.tensor_tensor(out=t1[:], in0=t1[:], in1=omka[:], op=ALU.add), reads=[K("t1"), "omka"], writes=[K("t1")])
        P.op("dve", lambda e: e.tensor_tensor(out=kdt[:], in0=kt[b][:], in1=t1[:], op=ALU.mult), reads=[K("kt"), K("t1")], writes=[K("kdt")])
        P.op("pool", lambda e: e.tensor_tensor(out=bt[:], in0=kk[:], in1=iclr[:], op=ALU.mult), reads=[K("kk"), K("iclr")], writes=[K("kkr")])
        P.op("dve", lambda e: e.tensor_tensor(out=t1[:], in0=rt[b][:], in1=kdt[:], op=ALU.mult), reads=[K("rt"), K("kdt"), K("t1")], writes=[K("t1")])
        P.op("pool", lambda e: e.tensor_tensor(out=t1[:], in0=t1[:], in1=rkb[:], op=ALU.mult), reads=[K("t1"), "rkb"], writes=[K("t1")])
        P.op("dve", lambda e: e.tensor_reduce(out=st8[2][:], in_=h3(t1[:]), axis=AX.X, op=ALU.add), reads=[K("t1")], writes=[K("st2")])
        P.op("dve", lambda e: e.tensor_tensor(out=h3(t1[:]), in0=h3(vt[b][:]), in1=st8[2][:].unsqueeze(2).to_broadcast([128, NH_C, HC]), op=ALU.mult),
             reads=[K("vt"), K("st2"), K("t1")], writes=[K("t1")])
        P.dma("pool", bout[d, rows, :], t1[:], reads=[K("t1")], writes=["bout"])
        P.atomic_begin()
        P.op("pe", lambda e: e.matmul(pz[:], lhsT=tri[:], rhs=ld[:], start=True, stop=True), reads=["tri", K("ld")], writes=["pz"])
        P.op("act", lambda e: e.activation(out=ein[:], in_=pz[:], func=AF.Exp), reads=["pz"], writes=[K("ein")])
        P.op("act", lambda e: e.activation(out=eneg[:], in_=pz[:], func=AF.Exp, scale=-1.0), reads=["pz"], writes=[K("eneg")])
        P.op("dve", lambda e: e.tensor_tensor(out=eex[:], in0=pz[:], in1=ld[:], op=ALU.subtract), reads=["pz", K("ld")], writes=[K("zt")])
        P.atomic_end()
        P.op("act", lambda e: e.activation(out=eex[:], in_=eex[:], func=AF.Exp), reads=[K("zt")], writes=[K("zt")])
        P.op("dve", lambda e: e.scalar_tensor_tensor(out=Ah[:], in0=kk[:], scalar=-1.0, in1=eex[:], op0=ALU.mult, op1=ALU.mult),
             reads=[K("kk"), K("zt")], writes=[K("Ah")])
        P.op("pool", lambda e: e.tensor_tensor(out=Bh[:], in0=bt[:], in1=eneg[:], op=ALU.mult), reads=[K("kkr"), K("eneg")], writes=[K("Bh")])
        P.op("dve", lambda e: e.tensor_tensor(out=Kh[:], in0=kdt[:], in1=eneg[:], op=ALU.mult), reads=[K("kdt"), K("eneg")], writes=[K("Kh")])
        P.op("pool", lambda e: e.tensor_tensor(out=Rh[:], in0=rt[b][:], in1=ein[:], op=ALU.mult), reads=[K("rt"), K("ein")], writes=[K("Rh")])
        P.op("act", lambda e: e.copy(out=Vb[:], in_=vt[b][:]), reads=[K("vt")], writes=[K("Vb")])
        P.atomic_begin()
        for h in range(NH_C):
            P.op("pe", lambda e, h=h: e.matmul(pg[:HC, h:h + 1], lhsT=ein[:, h * HC:(h + 1) * HC], rhs=sel[:], start=True, stop=True),
                 reads=[K("ein"), "sel"], writes=["pg"])
        P.op("act", lambda e: e.copy(out=gamT[:], in_=pg[:HC, :NH_C]), reads=["pg"], writes=[K("gamT")])
        P.atomic_end()
        for xi, (nm, src_t, skey) in enumerate([("a", Ah, "Ah"), ("b", Bh, "Bh"), ("k", Kh, "Kh"), ("r", Rh, "Rh")]):
            pt = ptr[xi % 2]
            P.atomic_begin()
            for h in range(NH_C):
                P.op("pe", lambda e, h=h, pt=pt, src_t=src_t: e.transpose(out=pt[:HC, h * 128:(h + 1) * 128], in_=src_t[:, h * HC:(h + 1) * HC],
                                                                          identity=identb[:]),
                     reads=[K(skey), "identb"], writes=[("ptr", xi % 2)])
            if xi % 2 == 0:
                P.op("act", lambda e, nm=nm, pt=pt: e.copy(out=XT[nm][:].rearrange("p h t -> p (h t)"), in_=pt[:HC, :]),
                     reads=[("ptr", xi % 2)], writes=[K("XT" + nm)])
                P.atomic_end()
            else:
                P.op("dve", lambda e, nm=nm, pt=pt: e.tensor_copy(out=XT[nm][:].rearrange("p h t -> p (h t)"), in_=pt[:HC, :]),
                     reads=[("ptr", xi % 2)], writes=[K("XT" + nm)])
                P.atomic_end()
        qlists = []

        def emit_quad(qd_):
            ql = []
            P._defer = ql
            qlists.append(ql)
            Q = lambda nm, qd_=qd_: (nm, d, qd_)
            hs = [qd_ * 4 + i for i in range(4)]
            Nf, NTf, TT, TTb, AakT, AVb = Nf_[d][qd_], NTf_[d][qd_], TT_[d][qd_], TTb_[d][qd_], AakT_[d][qd_], AVb_[d][qd_]

            def mm4(lhs_fn, rhs_fn, rows_, cols_, rkeys, hs=hs):
                p = bi[0] % 2
                bi[0] += 1
                P.atomic_begin()
                for i, h in enumerate(hs):
                    P.op("pe", lambda e, i=i, h=h, p=p: e.matmul(big[p][:rows_, i, :cols_], lhsT=lhs_fn(i, h), rhs=rhs_fn(i, h),
                                                               start=True, stop=True), reads=rkeys, writes=[("big", p)])
                return p

            def EV(*a_, **k_):
                P.op(*a_, **k_)
                P.atomic_end()

            msk = lambda m_: m_[:, None, :].to_broadcast([128, 4, 128])
            p = mm4(lambda i, h: XT["a"][:, h, :], lambda i, h: XT["b"][:, h, :], 128, 128, [K("XTa"), K("XTb")])
            EV("dve", lambda e, p=p: e.tensor_tensor(out=R32(Nf[0][:]), in0=big[p][:], in1=msk(mlo), op=ALU.mult),
                 reads=[("big", p), "mlo"], writes=[Q("Nf0")])
            p = mm4(lambda i, h: XT["b"][:, h, :], lambda i, h: XT["a"][:, h, :], 128, 128, [K("XTa"), K("XTb")])
            EV("dve", lambda e, p=p: e.tensor_tensor(out=R32(NTf[0][:]), in0=big[p][:], in1=msk(mup), op=ALU.mult),
                 reads=[("big", p), "mup"], writes=[Q("NTf0")])
            P.op("dve", lambda e: e.tensor_tensor(out=R32(TT[:]), in0=NTf[0][:], in1=msk(ident), op=ALU.add),
                 reads=[Q("NTf0"), "ident"], writes=[Q("TT")])
            p = mm4(lambda i, h: XT["k"][:, h, :], lambda i, h: XT["a"][:, h, :], 128, 128, [K("XTa"), K("XTk")])
            EV("dve", lambda e, p=p: e.tensor_tensor(out=AakT[:], in0=big[p][:], in1=msk(mup), op=ALU.mult),
                 reads=[("big", p), "mup"], writes=[Q("AakT")])
            p = mm4(lambda i, h: XT["b"][:, h, :], lambda i, h: XT["r"][:, h, :], 128, 128, [K("XTr"), K("XTb")])
            EV("dve", lambda e, p=p, qd_=qd_: e.tensor_tensor(out=ArbT[:, qd_ * 4:(qd_ + 1) * 4, :], in0=big[p][:], in1=msk(mupi), op=ALU.mult),
                 reads=[("big", p), "mupi"], writes=[Q("ArbT")])
            p = mm4(lambda i, h: XT["k"][:, h, :], lambda i, h: XT["r"][:, h, :], 128, 128, [K("XTr"), K("XTk")])
            EV("dve", lambda e, p=p, qd_=qd_: e.tensor_tensor(out=ArkT[:, qd_ * 4:(qd_ + 1) * 4, :], in0=big[p][:], in1=msk(mupi), op=ALU.mult),
                 reads=[("big", p), "mupi"], writes=[Q("ArkT")])
            cur = 0
            for lvl in range(1, 7):
                nxt = 1 - cur
                p = mm4(lambda i, h, cur=cur: R32(NTf[cur][:, i, :]), lambda i, h, cur=cur: R32(Nf[cur][:, i, :]), 128, 128, [Q("Nf%d" % cur), Q("NTf%d" % cur)])
                EV("act", lambda e, p=p, nxt=nxt: e.copy(out=R32(Nf[nxt][:]), in_=big[p][:]), reads=[("big", p)], writes=[Q("Nf%d" % nxt)])
                if lvl < 6:
                    p = mm4(lambda i, h, cur=cur: R32(Nf[cur][:, i, :]), lambda i, h, cur=cur: R32(NTf[cur][:, i, :]), 128, 128, [Q("Nf%d" % cur), Q("NTf%d" % cur)])
                    EV("act", lambda e, p=p, nxt=nxt: e.copy(out=R32(NTf[nxt][:]), in_=big[p][:]), reads=[("big", p)], writes=[Q("NTf%d" % nxt)])
                p = mm4(lambda i, h, nxt=nxt: R32(Nf[nxt][:, i, :]), lambda i, h: R32(TT[:, i, :]), 128, 128, [Q("Nf%d" % nxt), Q("TT")])
                EV("dve", lambda e, p=p: e.tensor_tensor(out=R32(TT[:]), in0=big[p][:], in1=TT[:], op=ALU.add), reads=[("big", p), Q("TT")], writes=[Q("TT")])
                cur = nxt
            P.op("act", lambda e: e.copy(out=TTb[:], in_=TT[:]), reads=[Q("TT")], writes=[Q("TTb")])
            qs = slice(qd_ * 4, (qd_ + 1) * 4)
            p = mm4(lambda i, h: TTb[:, i, :], lambda i, h: Ah[:, h * HC:(h + 1) * HC], 128, HC, [Q("TTb"), K("Ah")])
            EV("act", lambda e, p=p, qs=qs: e.copy(out=TAb[:, qs, :], in_=big[p][:, :, :HC]), reads=[("big", p)], writes=[Q("TAb")])
            p = mm4(lambda i, h: AakT[:, i, :], lambda i, h: Vb[:, h * HC:(h + 1) * HC], 128, HC, [Q("AakT"), K("Vb")])
            EV("dve", lambda e, p=p: e.tensor_copy(out=AVb[:], in_=big[p][:, :, :HC]), reads=[("big", p)], writes=[Q("AVb")])
            p = mm4(lambda i, h: TTb[:, i, :], lambda i, h: AVb[:, i, :], 128, HC, [Q("TTb"), Q("AVb")])
            EV("act", lambda e, p=p, qs=qs: e.copy(out=TVb[:, qs, :], in_=big[p][:, :, :HC]), reads=[("big", p)], writes=[Q("TVb")])
            p = mm4(lambda i, h: TAb[:, h, :], lambda i, h: Bh[:, h * HC:(h + 1) * HC], HC, HC, [Q("TAb"), K("Bh")])
            EV("dve", lambda e, p=p, qs=qs: e.tensor_tensor(out=MTb[:, qs, :], in0=big[p][:HC, :, :HC],
                                                            in1=ident[:HC, None, :HC].to_broadcast([HC, 4, HC]), op=ALU.add),
                 reads=[("big", p), "ident"], writes=[Q("MTb")])
            p = mm4(lambda i, h: TAb[:, h, :], lambda i, h: ArbT[:, h, :], HC, 128, [Q("TAb"), Q("ArbT")])
            EV("dve", lambda e, p=p, qs=qs: e.tensor_tensor(out=RQTb[:, qs, :], in0=big[p][:HC, :, :], in1=XT["r"][:, qs, :], op=ALU.add),
                 reads=[("big", p), K("XTr")], writes=[Q("RQTb")])
        for qd_ in range(2):
            emit_quad(qd_)
        tail = []
        P._defer = tail
        allq = lambda nm: [(nm, d, 0), (nm, d, 1)]
        P.atomic_begin()
        for h in range(NH_C):
            hc = slice(h * HC, (h + 1) * HC)
            P.op("pe", lambda e, h=h: e.matmul(py[:, h, :], lhsT=ArbT[:, h, :], rhs=TVb[:, h, :], start=True, stop=False),
                 reads=allq("ArbT") + allq("TVb") + allq("ArkT") + allq("RQTb") + [K("Vb"), ("Pb", d)], writes=["py"])
            P.op("pe", lambda e, h=h, hc=hc: e.matmul(py[:, h, :], lhsT=ArkT[:, h, :], rhs=Vb[:, hc], start=False, stop=False),
                 reads=[], writes=["py"])
            P.op("pe", lambda e, h=h: e.matmul(py[:, h, :], lhsT=RQTb[:, h, :], rhs=Pb[d][:, h, :], start=False, stop=True),
                 reads=[("Pb", d)], writes=["py"])
        P.op("act", lambda e: e.copy(out=ysb[b][:], in_=py[:]), reads=["py"], writes=[K("ysb")])
        P.atomic_end()
        P.dma("pool", yout[d, rows, :], ysb[b][:].rearrange("p h c -> p (h c)"), reads=[K("ysb")], writes=["yout"])
        P.atomic_begin()
        for h in range(NH_C):
            hc = slice(h * HC, (h + 1) * HC)
            P.op("pe", lambda e, h=h, hc=hc: e.matmul(pp[:HC, h, :], lhsT=Bh[:, hc], rhs=TVb[:, h, :], start=True, stop=False),
                 reads=allq("TVb") + allq("MTb") + [K("Bh"), K("Kh"), K("Vb"), ("Pb", d)], writes=["pp"])
            P.op("pe", lambda e, h=h, hc=hc: e.matmul(pp[:HC, h, :], lhsT=Kh[:, hc], rhs=Vb[:, hc], start=False, stop=False),
                 reads=[], writes=["pp"])
            P.op("pe", lambda e, h=h: e.matmul(pp[:HC, h, :], lhsT=MTb[:, h, :], rhs=Pb[d][:, h, :], start=False, stop=True),
                 reads=[("Pb", d)], writes=["pp"])
        P.op("dve", lambda e: e.tensor_tensor(out=Pb[d][:], in0=pp[:HC, :, :], in1=gamT[:].unsqueeze(2).to_broadcast([HC, NH_C, HC]),
                                              op=ALU.mult), reads=["pp", K("gamT")], writes=[("Pb", d)])
        P.atomic_end()
        P._defer = None
        return main, qlists, tail

    for t in range(ntile):
        if t == 0:
            nxt_parts = [emit_dir(0, d) for d in range(2)]
            for d in range(2):
                P.interleave([nxt_parts[d][0]])
        parts = nxt_parts
        streams = parts[0][1] + parts[1][1]
        if t + 1 < ntile:
            nxt_parts = [emit_dir(t + 1, d) for d in range(2)]
            if MC_PIPE:
                streams = streams + [nxt_parts[0][0] + nxt_parts[1][0]]
        P.interleave(streams)
        if t + 1 < ntile and not MC_PIPE:
            for d in range(2):
                P.interleave([nxt_parts[d][0]])
        for d in range(2):
            P.interleave([parts[d][2]])
    P.finish(["yout", "bout"])
    return nc


def run_Mc(pl, pc, prm):
    W = NH_C * HC
    nc = build_Mc(L_SEQ)
    tt, ss = np.meshgrid(np.arange(128), np.arange(128), indexing="xy")
    consts = {"tri": (ss <= tt).astype(np.float32), "mup": (ss < tt).astype(np.float32), "mlo": (ss > tt).astype(np.float32),
              "ident": np.eye(128, dtype=np.float32), "sel": (np.arange(128) == 127).astype(np.float32)[:, None]}
    rk_flat = prm["r_k"].reshape(-1)
    in_maps = []
    for cid in range(NCORES):
        b, hg = cid // 4, cid % 4
        cols = slice(hg * W, (hg + 1) * W)
        m = dict(consts)
        for nm, off in (("r", 0), ("k", D), ("v", 2 * D)):
            m[nm] = np.stack([_seq(pl[b][:, off + hg * W:off + (hg + 1) * W], pc[b][:, off + hg * W:off + (hg + 1) * W], dr == 1)
                              for dr in range(2)])
        m["lwT"] = np.stack([np.ascontiguousarray(_seq(pl[b][:, 4 * D + dr * 128:4 * D + dr * 128 + LORA],
                                                       pc[b][:, 4 * D + dr * 128:4 * D + dr * 128 + LORA], dr == 1).T) for dr in range(2)])
        m["laT"] = np.stack([np.ascontiguousarray(_seq(pl[b][:, 4 * D + 256 + dr * 128:4 * D + 256 + dr * 128 + LORA],
                                                       pc[b][:, 4 * D + 256 + dr * 128:4 * D + 256 + dr * 128 + LORA], dr == 1).T) for dr in range(2)])
        m["w2"] = np.ascontiguousarray(prm["w2"][:, :, cols])
        m["a2"] = np.ascontiguousarray(prm["a2"][:, :, cols])
        m["w0b"] = np.stack([_bc(prm["w0"][dr, cols]) for dr in range(2)])
        m["a0b"] = np.stack([_bc(prm["a0"][dr, cols]) for dr in range(2)])
        m["kkb"] = _bc(prm["k_k"][cols])
        m["kab"] = _bc(prm["k_a"][cols])
        m["rkb"] = _bc(rk_flat[cols])
        in_maps.append(m)
    res = _run(nc, in_maps)
    names = ["m0", "m1", "m2", "m3"]
    lat = {nm: np.empty((2, T_LAT, D), np.float32) for nm in names}
    cx = {nm: np.empty((2, T_CTX, D), np.float32) for nm in names}
    for cid in range(NCORES):
        b, hg = cid // 4, cid % 4
        cols = slice(hg * W, (hg + 1) * W)
        for dr in range(2):
            for key, nm in (("y", "m%d" % dr), ("bon", "m%d" % (2 + dr))):
                l, c = _unseq(res[cid][key][dr], dr == 1)
                lat[nm][b][:, cols] = l
                cx[nm][b][:, cols] = c
    return lat, cx


def layer_c(x, ctx, mod_i, norm_w_i, prm, need_ctx, final_w=None):
    pad = lambda w: np.concatenate([w, np.zeros((D, 128 - w.shape[1]), np.float32)], axis=1)
    wcat = np.concatenate([prm["w_in"][0], prm["w_in"][1], prm["w_in"][2], prm["w_in"][3],
                           pad(prm["w1"][0]), pad(prm["w1"][1]), pad(prm["a1"][0]), pad(prm["a1"][1])], axis=1)
    lerp = [0] * 16 + [1] * 16 + [2] * 16 + [3] * 16 + [4, 4, 5, 5]
    pl, pc = run_P(x, ctx, mod_i, norm_w_i, wcat, lerp=lerp, mu=prm["mu"])
    lat, cx = run_Mc(pl, pc, prm)
    lat["gate"] = pl[:, :, 3 * D:4 * D]
    cx["gate"] = pc[:, :, 3 * D:4 * D]
    bcs = {"lnw": prm["ln_w"], "lnb": prm["ln_b"]}
    if final_w is not None:
        bcs["fnw"] = final_w
    return run_O(x, ctx, "c", lat, cx, prm["w_out"], mod_i[0:2, 2 * D:3 * D], mod_i[2, 2 * D:3 * D], bcs,
                 final=final_w is not None, need_ctx=need_ctx)


def kernel(x, c, ctx, c_ctx, norm_w, mod_w, mod_b, a_w_in, a_lb_raw, a_onorm_w, a_w_out,
           b_w_in, b_sink, b_w_out, c_mu, c_w_in, c_w0, c_w1, c_w2, c_a0, c_a1, c_a2,
           c_k_k, c_k_a, c_r_k, c_ln_w, c_ln_b, c_w_out, final_norm_w):
    f = lambda a: np.asarray(a, dtype=np.float32)
    x, c, ctx, c_ctx = f(x), f(c), f(ctx), f(c_ctx)
    mod = run_mod(c, f(c_ctx), f(mod_w), f(mod_b))
    depth = 4
    for i in range(depth):
        j, kind = i // 3, i % 3
        need_ctx = i < depth - 1
        fw = f(final_norm_w) if i == depth - 1 else None
        mod_i = np.ascontiguousarray(mod[:, i])
        if kind == 0:
            x, ctx = layer_a(x, ctx, mod_i, f(norm_w[i]), f(a_w_in[j]), f(a_lb_raw), j, f(a_onorm_w[j]), f(a_w_out[j]), need_ctx, fw)
        elif kind == 1:
            x, ctx_n = layer_b(x, ctx, mod_i, f(norm_w[i]), f(b_w_in[j]), f(b_sink[j]), f(b_w_out[j]), need_ctx, fw)
            ctx = ctx_n if need_ctx else ctx
        else:
            prm = {"mu": f(c_mu[j]), "w_in": f(c_w_in[j]), "w0": f(c_w0[j]), "w1": f(c_w1[j]), "w2": f(c_w2[j]),
                   "a0": f(c_a0[j]), "a1": f(c_a1[j]), "a2": f(c_a2[j]), "k_k": f(c_k_k[j]), "k_a": f(c_k_a[j]),
                   "r_k": f(c_r_k[j]), "ln_w": f(c_ln_w[j]), "ln_b": f(c_ln_b[j]), "w_out": f(c_w_out[j])}
            x, ctx_n = layer_c(x, ctx, mod_i, f(norm_w[i]), prm, need_ctx, fw)
            ctx = ctx_n if need_ctx else ctx
    return x.astype(np.float32)
```

```python
import numpy as np
from contextlib import ExitStack
import concourse.bass as bass
import concourse.mybir as mybir
from concourse.bass_utils import run_bass_kernel_spmd

F32 = mybir.dt.float32
BF16 = mybir.dt.bfloat16
AF = mybir.ActivationFunctionType
ALU = mybir.AluOpType
AX = mybir.AxisListType

NCORES = 8
D = 2048
KC = D // 128


class Prog:
    CE = ("pe", "act", "dve", "pool")

    def __init__(self, nc, ndma=8):
        self.nc = nc
        self.es = ExitStack()
        self.eng = {"pe": nc.tensor, "act": nc.scalar, "dve": nc.vector, "pool": nc.gpsimd, "sp": nc.sync}
        self.sem = {}
        self.cnt = {}
        for e in self.CE:
            self.sem[("e", e)] = self.es.enter_context(nc.semaphore("s_" + e))
            self.cnt[("e", e)] = 0
        self.ndma = ndma
        self.dma_rr = {}
        for q in ("sp", "pool", "act"):
            self.dma_rr[q] = 0
            for i in range(ndma):
                k = ("d", q, i)
                self.sem[k] = self.es.enter_context(nc.semaphore("d_%s%d" % (q, i)))
                self.cnt[k] = 0
        self.known = {e: {} for e in self.eng}
        self.last_w = {}
        self.readers = {}
        self.ninstr = 0
        self.uid = 0

    def sb(self, shape, dtype=F32, name=None, stack=None):
        self.uid += 1
        return (stack or self.es).enter_context(self.nc.sbuf_tensor(name or ("sb%d" % self.uid), list(shape), dtype))

    def barrier(self):
        for e in ("pe", "act", "dve", "pool", "sp"):
            kn = self.known[e]
            for k, v in self.cnt.items():
                if v > 0 and kn.get(k, 0) < v:
                    self.eng[e].wait_ge(self.sem[k], v)
                    kn[k] = v
                    self.ninstr += 1

    def ps(self, shape, dtype=F32, name=None):
        self.uid += 1
        return self.es.enter_context(self.nc.psum_tensor(name or ("ps%d" % self.uid), list(shape), dtype))

    def dram(self, name, shape, dtype=F32, kind="ExternalInput"):
        return self.nc.dram_tensor(name, list(shape), dtype, kind=kind).ap()

    def _deps(self, e, reads, writes):
        deps = {}

        def add(kv):
            k, v = kv
            if e == "pe" and k == ("e", "pe"):
                return
            if deps.get(k, 0) < v:
                deps[k] = v

        for b in reads:
            if b in self.last_w:
                add(self.last_w[b])
        for b in writes:
            if b in self.last_w:
                add(self.last_w[b])
            for r in self.readers.get(b, ()):
                add(r)
        kn = self.known[e]
        for k, v in deps.items():
            if kn.get(k, 0) < v:
                self.eng[e].wait_ge(self.sem[k], v)
                self.ninstr += 1
                kn[k] = v

    def _record(self, key, val, reads, writes):
        for b in writes:
            self.last_w[b] = (key, val)
            self.readers[b] = []
        for b in reads:
            self.readers.setdefault(b, []).append((key, val))
            if len(self.readers[b]) > 64:
                mx = {}
                for k, v in self.readers[b]:
                    if mx.get(k, 0) < v:
                        mx[k] = v
                self.readers[b] = list(mx.items())

    _defer = None
    _atomic = None

    def begin(self):
        self._defer = []

    def end(self):
        l, self._defer = self._defer, None
        return l

    def atomic_begin(self):
        if self._defer is not None:
            self._atomic = []

    def atomic_end(self):
        if self._defer is not None:
            self._defer.append(self._atomic)
            self._atomic = None

    def interleave(self, lists):
        n = max(len(l) for l in lists)
        for i in range(n):
            for l in lists:
                if i < len(l):
                    for (kind, args, kw) in l[i]:
                        if kind == "op":
                            self.op(*args, **kw)
                        else:
                            self.dma(*args, **kw)

    def _rec(self, kind, args, kw):
        item = (kind, args, kw)
        if self._atomic is not None:
            self._atomic.append(item)
        else:
            self._defer.append([item])

    def op(self, e, fn, reads=(), writes=()):
        if self._defer is not None:
            return self._rec("op", (e, fn, list(reads), list(writes)), {})
        self._deps(e, reads, writes)
        key = ("e", e)
        self.cnt[key] += 1
        ins = fn(self.eng[e])
        ins.then_inc(self.sem[key], 1)
        self.ninstr += 1
        self._record(key, self.cnt[key], reads, writes)
        return ins

    def dma(self, q, out, in_, reads=(), writes=(), **kw):
        if self._defer is not None:
            return self._rec("dma", (q, out, in_, list(reads), list(writes)), kw)
        i = self.dma_rr[q]
        self.dma_rr[q] = (i + 1) % self.ndma
        key = ("d", q, i)
        kn = self.known[q]
        if kn.get(key, 0) < self.cnt[key]:
            self.eng[q].wait_ge(self.sem[key], self.cnt[key])
            kn[key] = self.cnt[key]
            self.ninstr += 1
        self._deps(q, reads, writes)
        self.cnt[key] += 16
        self.eng[q].dma_start(out=out, in_=in_, **kw).then_inc(self.sem[key], 16)
        self.ninstr += 1
        self._record(key, self.cnt[key], reads, writes)

    def finish(self, out_keys):
        self._deps("pool", out_keys, ())
        for k, v in self.cnt.items():
            if k[0] == "d" and v > 0 and self.known["pool"].get(k, 0) < v:
                self.eng["pool"].wait_ge(self.sem[k], v)
                self.known["pool"][k] = v


def _run(nc, in_maps):
    res = run_bass_kernel_spmd(nc, in_maps, core_ids=list(range(NCORES)))
    return res.results


MOD_NCOL = 4 * 3 * D // NCORES


def build_mod():
    nc = bass.Bass("TRN2", target_bir_lowering=False)
    P = Prog(nc)
    ccT = P.dram("ccT", [128, KC, 3])
    w = P.dram("w", [128, KC, MOD_NCOL])
    b3 = P.dram("b3", [3, MOD_NCOL])
    out = P.dram("out", [3, MOD_NCOL], kind="ExternalOutput")
    s_in = P.sb([128, KC, 3])
    s_act = P.sb([128, KC, 3])
    bias = P.sb([3, MOD_NCOL])
    res = P.sb([3, MOD_NCOL])
    wt = [P.sb([128, KC, 512]) for _ in range(2)]
    pt = [P.ps([3, 512]) for _ in range(2)]
    P.dma("sp", s_in[:], ccT[:], writes=["s_in"])
    P.dma("sp", bias[:], b3[:], writes=["bias"])
    P.op("act", lambda e: e.activation(out=s_act[:], in_=s_in[:], func=AF.Silu), reads=["s_in"], writes=["s_act"])
    nb = MOD_NCOL // 512
    for j in range(nb):
        wb = wt[j % 2]
        pb = pt[j % 2]
        P.dma("sp", wb[:], w[:, :, j * 512:(j + 1) * 512], writes=[("w", j % 2)])
        for k in range(KC):
            P.op("pe", lambda e, k=k, wb=wb, pb=pb: e.matmul(pb[:], lhsT=s_act[:, k, :], rhs=wb[:, k, :],
                                                        start=(k == 0), stop=(k == KC - 1)),
                 reads=["s_act", ("w", j % 2)], writes=[("p", j % 2)])
        P.op("dve", lambda e, j=j, pb=pb: e.tensor_tensor(out=res[:, j * 512:(j + 1) * 512], in0=pb[:],
                                                       in1=bias[:, j * 512:(j + 1) * 512], op=ALU.add),
             reads=[("p", j % 2), "bias"], writes=["res"])
    P.dma("pool", out[:], res[:], reads=["res"], writes=["out"])
    P.finish(["out"])
    return nc


def run_mod(c, c_ctx, mod_w, mod_b):
    cc = np.concatenate([c, c_ctx[None, :]], axis=0).astype(np.float32)
    ccT = np.ascontiguousarray(cc.T.reshape(KC, 128, 3).transpose(1, 0, 2))
    wall = np.concatenate([mod_w[l] for l in range(4)], axis=1)
    ball = np.concatenate([mod_b[l] for l in range(4)], axis=0)
    in_maps = []
    for cid in range(NCORES):
        cols = slice(cid * MOD_NCOL, (cid + 1) * MOD_NCOL)
        wc = np.ascontiguousarray(wall[:, cols].reshape(KC, 128, MOD_NCOL).transpose(1, 0, 2))
        bc = np.ascontiguousarray(np.broadcast_to(ball[cols][None, :], (3, MOD_NCOL)))
        in_maps.append({"ccT": ccT, "w": wc, "b3": bc})
    nc = build_mod()
    res = _run(nc, in_maps)
    mod = np.concatenate([r["out"] for r in res], axis=1)
    return mod.reshape(3, 4, 3 * D)


def _segments(r0, n):
    segs = []
    o = 0
    while o < n:
        m = min(128, n - o)
        segs.append((r0 + o, m))
        o += m
    return segs


def build_P(n_lat, n_ctx, nblk, lerp=None):
    halo = 1 if lerp is not None else 0
    r_lat = n_lat + 2 * halo
    r_ctx = n_ctx + 2 * halo
    R = r_lat + r_ctx
    n_int = n_lat + n_ctx
    nc = bass.Bass("TRN2", target_bir_lowering=False)
    P = Prog(nc)
    xin = P.dram("xin", [R, D])
    nw = P.dram("nw", [128, D])
    scb = P.dram("scb", [2, 128, D])
    shb = P.dram("shb", [2, 128, D])
    wd = P.dram("w", [nblk, 128, KC, 128])
    identd = P.dram("ident", [128, 128])
    if lerp is not None:
        mud = P.dram("mu", [128, KC, 6])
        bmd = P.dram("bmask", [128, 4])
    out = P.dram("projT", [nblk * 128, n_int], kind="ExternalOutput")

    hT = P.sb([128, KC, R], BF16)
    tmp = P.sb([128, D])
    if lerp is not None:
        xxT = P.sb([128, KC, R], BF16)
        mu = P.sb([128, KC, 6])
        bm = P.sb([128, 4])
    tps = [P.ps([128, 4, 128], BF16) for _ in range(2)]
    mps = [P.ps([128, 512]) for _ in range(4)]
    es1 = ExitStack()
    ident_f = P.sb([128, 128], stack=es1)
    ident = P.sb([128, 128], BF16, stack=es1)
    S = [P.sb([128, D], stack=es1) for _ in range(2)]
    SH = [P.sb([128, D], stack=es1) for _ in range(2)]
    nwt = tmp
    xt = [P.sb([128, D], stack=es1) for _ in range(2)]
    sq = P.sb([128, D], BF16, stack=es1)
    hb = [P.sb([128, D], BF16, stack=es1) for _ in range(2)]
    ss = [P.sb([128, 1], stack=es1) for _ in range(2)]
    rstd = [P.sb([128, 1], stack=es1) for _ in range(2)]

    P.dma("sp", ident_f[:], identd[:], writes=["identf"])
    P.op("dve", lambda e: e.tensor_copy(out=ident[:], in_=ident_f[:]), reads=["identf"], writes=["ident"])
    P.dma("sp", nwt[:], nw[:], writes=["tmp"])
    for i in range(2):
        P.dma("sp", S[i][:], scb[i], writes=[("S", i)])
        P.dma("sp", SH[i][:], shb[i], writes=[("SH", i)])
        P.op("dve", lambda e, i=i: e.scalar_tensor_tensor(out=S[i][:], in0=S[i][:], scalar=1.0, in1=nwt[:],
                                                          op0=ALU.add, op1=ALU.mult),
             reads=[("S", i), "tmp"], writes=[("S", i)])
    if lerp is not None:
        P.dma("sp", mu[:], mud[:], writes=["mu"])
        P.dma("sp", bm[:], bmd[:], writes=["bm"])

    segs = [(r, m, 0) for (r, m) in _segments(0, r_lat)] + [(r, m, 1) for (r, m) in _segments(r_lat, r_ctx)]
    for si, (r0, m, mi) in enumerate(segs):
        b = si % 2
        P.dma("sp", xt[b][:m, :], xin[r0:r0 + m, :], writes=[("xt", b)])
        P.op("act", lambda e, b=b, m=m: e.activation(out=sq[:m, :], in_=xt[b][:m, :], func=AF.Square,
                                                      accum_out=ss[b][:m, :]),
             reads=[("xt", b)], writes=["sq", ("ss", b)])
        P.op("dve", lambda e, b=b, m=m: e.tensor_scalar(out=rstd[b][:m, :], in0=ss[b][:m, :], scalar1=1.0 / D,
                                                         scalar2=1e-6, op0=ALU.mult, op1=ALU.add),
             reads=[("ss", b)], writes=[("rstd", b)])
        P.op("act", lambda e, b=b, m=m: e.activation(out=rstd[b][:m, :], in_=rstd[b][:m, :], func=AF.Sqrt),
             reads=[("rstd", b)], writes=[("rstd", b)])
        P.op("dve", lambda e, b=b, m=m: e.reciprocal(out=rstd[b][:m, :], in_=rstd[b][:m, :]),
             reads=[("rstd", b)], writes=[("rstd", b)])
        P.op("dve", lambda e, b=b, m=m, mi=mi: e.scalar_tensor_tensor(out=tmp[:m, :], in0=xt[b][:m, :],
                                                                     scalar=rstd[b][:m, :], in1=S[mi][:m, :],
                                                                     op0=ALU.mult, op1=ALU.mult),
             reads=[("xt", b), ("rstd", b), ("S", mi)], writes=["tmp"])
        P.op("dve", lambda e, b=b, m=m, mi=mi: e.tensor_tensor(out=hb[b][:m, :], in0=tmp[:m, :], in1=SH[mi][:m, :],
                                                              op=ALU.add),
             reads=["tmp", ("SH", mi)], writes=[("hb", b)])
        for kg in range(KC // 4):
            tb = (si * 4 + kg) % 2
            for kk in range(4):
                k = kg * 4 + kk
                P.op("pe", lambda e, b=b, m=m, k=k, kk=kk, tb=tb: e.transpose(out=tps[tb][:, kk, :m],
                                                                              in_=hb[b][:m, k * 128:(k + 1) * 128],
                                                                              identity=ident[:m, :m]),
                     reads=[("hb", b), "ident"], writes=[("tps", tb)])
            eng = "act" if kg % 2 == 0 else "dve"
            if eng == "act":
                P.op("act", lambda e, m=m, kg=kg, tb=tb, r0=r0: e.copy(out=hT[:, kg * 4:(kg + 1) * 4, r0:r0 + m],
                                                                       in_=tps[tb][:, :, :m]),
                     reads=[("tps", tb)], writes=["hT"])
            else:
                P.op("dve", lambda e, m=m, kg=kg, tb=tb, r0=r0: e.tensor_copy(out=hT[:, kg * 4:(kg + 1) * 4, r0:r0 + m],
                                                                              in_=tps[tb][:, :, :m]),
                     reads=[("tps", tb)], writes=["hT"])

    if lerp is not None:
        for ci, col in enumerate([0, r_lat - 1, r_lat, R - 1]):
            P.op("dve", lambda e, ci=ci, col=col: e.tensor_scalar(out=hT[:, :, col:col + 1], in0=hT[:, :, col:col + 1],
                                                                   scalar1=bm[:, ci:ci + 1], scalar2=None, op0=ALU.mult),
                 reads=["hT", "bm"], writes=["hT"])
        for (c0, n) in [(0, r_lat), (r_lat, r_ctx)]:
            for k in range(KC):
                w_ = n - 2
                P.op("dve", lambda e, k=k, c0=c0, w_=w_: e.tensor_tensor(out=tmp[:, :w_], in0=hT[:, k, c0:c0 + w_],
                                                                        in1=hT[:, k, c0 + 2:c0 + 2 + w_], op=ALU.add),
                     reads=["hT"], writes=["tmp"])
                P.op("dve", lambda e, k=k, c0=c0, w_=w_: e.scalar_tensor_tensor(out=xxT[:, k, c0 + 1:c0 + 1 + w_],
                                                                               in0=tmp[:, :w_], scalar=0.5,
                                                                               in1=hT[:, k, c0 + 1:c0 + 1 + w_],
                                                                               op0=ALU.mult, op1=ALU.subtract),
                     reads=["tmp", "hT"], writes=["xxT"])

    P.barrier()
    es1.close()
    wf = [P.sb([128, KC, 128]) for _ in range(2)]
    wb = [P.sb([128, KC, 128], BF16) for _ in range(2)]
    stage = [P.sb([128, n_int]) for _ in range(2)]
    if lerp is not None:
        wb2 = [P.sb([128, KC, 128], BF16) for _ in range(2)]
    groups = []
    o = 0
    while o < n_lat:
        n = min(512, n_lat - o)
        groups.append((halo + o, o, n))
        o += n
    groups.append((r_lat + halo, n_lat, n_ctx))
    gi = 0
    for j in range(nblk):
        b = j % 2
        P.dma("sp", wf[b][:], wd[j], writes=[("wf", b)])
        if j % 2 == 0:
            P.op("act", lambda e, b=b: e.copy(out=wb[b][:], in_=wf[b][:]), reads=[("wf", b)], writes=[("wb", b)])
        else:
            P.op("pool", lambda e, b=b: e.tensor_copy(out=wb[b][:], in_=wf[b][:]), reads=[("wf", b)], writes=[("wb", b)])
        if lerp is not None:
            n_mu = lerp[j]
            P.op("pool", lambda e, b=b, n_mu=n_mu: e.tensor_tensor(out=wb2[b][:], in0=wf[b][:],
                                                                   in1=mu[:, :, n_mu:n_mu + 1].to_broadcast([128, KC, 128]),
                                                                   op=ALU.mult),
                 reads=[("wf", b), "mu"], writes=[("wb2", b)])
        for (hc, oc, n) in groups:
            pb = gi % 4
            gi += 1
            nmm = KC * (2 if lerp is not None else 1)
            for k in range(KC):
                P.op("pe", lambda e, b=b, k=k, pb=pb, hc=hc, n=n: e.matmul(mps[pb][:, :n], lhsT=wb[b][:, k, :],
                                                                         rhs=hT[:, k, hc:hc + n],
                                                                         start=(k == 0), stop=(k == nmm - 1)),
                     reads=[("wb", b), "hT"], writes=[("mps", pb)])
            if lerp is not None:
                for k in range(KC):
                    P.op("pe", lambda e, b=b, k=k, pb=pb, hc=hc, n=n: e.matmul(mps[pb][:, :n], lhsT=wb2[b][:, k, :],
                                                                             rhs=xxT[:, k, hc:hc + n],
                                                                             start=False, stop=(k == KC - 1)),
                         reads=[("wb2", b), "xxT"], writes=[("mps", pb)])
            if gi % 2 == 0:
                P.op("act", lambda e, b=b, pb=pb, oc=oc, n=n: e.copy(out=stage[b][:, oc:oc + n], in_=mps[pb][:, :n]),
                     reads=[("mps", pb)], writes=[("stage", b)])
            else:
                P.op("dve", lambda e, b=b, pb=pb, oc=oc, n=n: e.tensor_copy(out=stage[b][:, oc:oc + n], in_=mps[pb][:, :n]),
                     reads=[("mps", pb)], writes=[("stage", b)])
        P.dma("pool", out[j * 128:(j + 1) * 128, :], stage[b][:], reads=[("stage", b)], writes=["out"])
    P.finish(["out"])
    return nc


def _bc(v):
    return np.ascontiguousarray(np.broadcast_to(np.asarray(v, np.float32)[None, :], (128, v.shape[-1])))


def _wblocks(w):
    ncols = w.shape[1]
    nblk = (ncols + 127) // 128
    if nblk * 128 != ncols:
        w = np.concatenate([w, np.zeros((D, nblk * 128 - ncols), np.float32)], axis=1)
    return np.ascontiguousarray(w.reshape(KC, 128, nblk, 128).transpose(2, 1, 0, 3))


N_LAT = 2048
N_CTX = 64
T_LAT = 8192
T_CTX = 256


def _core_rows(x, ctx, cid, halo=0):
    b, q = cid // 4, cid % 4

    def take(a, lo, hi):
        T = a.shape[0]
        rows = []
        if lo < 0:
            rows.append(np.zeros((-lo, a.shape[1]), a.dtype))
        rows.append(a[max(lo, 0):min(hi, T)])
        if hi > T:
            rows.append(np.zeros((hi - T, a.shape[1]), a.dtype))
        return np.concatenate(rows, axis=0) if len(rows) > 1 else rows[0]

    lat = take(x[b], q * N_LAT - halo, (q + 1) * N_LAT + halo)
    cx = take(ctx[b], q * N_CTX - halo, (q + 1) * N_CTX + halo)
    return np.ascontiguousarray(np.concatenate([lat, cx], axis=0))


def _gather_rows(outs, width):
    x = np.empty((2, T_LAT, width), np.float32)
    ctx = np.empty((2, T_CTX, width), np.float32)
    for cid in range(NCORES):
        b, q = cid // 4, cid % 4
        x[b, q * N_LAT:(q + 1) * N_LAT] = outs[cid][:N_LAT]
        ctx[b, q * N_CTX:(q + 1) * N_CTX] = outs[cid][N_LAT:]
    return x, ctx


def run_P(x, ctx, mod_i, norm_w_i, wcat, lerp=None, mu=None):
    wb = _wblocks(wcat)
    nblk = wb.shape[0]
    halo = 1 if lerp is not None else 0
    nc = build_P(N_LAT, N_CTX, nblk, lerp)
    ident = np.eye(128, dtype=np.float32)
    nw = _bc(norm_w_i)
    in_maps = []
    for cid in range(NCORES):
        b, q = cid // 4, cid % 4
        m = {"xin": _core_rows(x, ctx, cid, halo), "nw": nw, "w": wb, "ident": ident,
             "scb": np.stack([_bc(mod_i[b, D:2 * D]), _bc(mod_i[2, D:2 * D])]),
             "shb": np.stack([_bc(mod_i[b, 0:D]), _bc(mod_i[2, 0:D])])}
        if lerp is not None:
            m["mu"] = np.ascontiguousarray(mu.T.reshape(KC, 128, 6).transpose(1, 0, 2))
            bm = np.ones((128, 4), np.float32)
            if q == 0:
                bm[:, 0] = 0.0
                bm[:, 2] = 0.0
            if q == 3:
                bm[:, 1] = 0.0
                bm[:, 3] = 0.0
            m["bmask"] = bm
        in_maps.append(m)
    res = _run(nc, in_maps)
    outs = [np.ascontiguousarray(r["projT"].T) for r in res]
    return _gather_rows(outs, nblk * 128)


O_INS = {"a": ["m0", "m1", "gate"], "b": ["m0", "gate"], "c": ["m0", "m1", "m2", "m3", "gate"]}


def build_O(n_lat, n_ctx, kind, final=False):
    R = n_lat + n_ctx
    nc = bass.Bass("TRN2", target_bir_lowering=False)
    P = Prog(nc)
    xin = P.dram("xin", [R, D])
    ins_d = {nm: P.dram(nm, [R, D]) for nm in O_INS[kind]}
    wod = P.dram("wo", [128, KC, D])
    gbd = P.dram("gb", [2, 128, D])
    identd = P.dram("ident", [128, 128])
    nbc = {"a": ["onw"], "b": [], "c": ["lnw", "lnb"]}[kind] + (["fnw"] if final else [])
    bc_d = {nm: P.dram(nm, [128, D]) for nm in nbc}
    out = P.dram("xout", [R, D], kind="ExternalOutput")

    ident_f = P.sb([128, 128])
    ident = P.sb([128, 128], BF16)
    wo = P.sb([128, KC, D], BF16)
    gb = [P.sb([128, D]) for _ in range(2)]
    bc = {nm: P.sb([128, D]) for nm in nbc}
    xt = [P.sb([128, D]) for _ in range(2)]
    xo = [P.sb([128, D]) for _ in range(2)]
    it = {nm: [P.sb([128, 512]) for _ in range(2)] for nm in O_INS[kind]}
    t1s = [P.sb([128, 512]) for _ in range(2)]
    t2s = [P.sb([128, 512]) for _ in range(2)]
    t3s = [P.sb([128, 512]) for _ in range(2)]
    sgs = [P.sb([128, 512]) for _ in range(2)]
    sts = [[P.sb([128, 8]) for _ in range(3)] for _ in range(2)]
    zb = [P.sb([128, D], BF16) for _ in range(2)]
    zT = [P.sb([128, KC, 128], BF16) for _ in range(2)]
    tps = [P.ps([128, 4, 128], BF16) for _ in range(2)]
    mps = [P.ps([128, 512]) for _ in range(4)]
    fs = [P.sb([128, 1]) for _ in range(2)]

    P.dma("sp", ident_f[:], identd[:], writes=["identf"])
    P.op("dve", lambda e: e.tensor_copy(out=ident[:], in_=ident_f[:]), reads=["identf"], writes=["ident"])
    for i in range(2):
        P.dma("sp", gb[i][:], gbd[i], writes=[("gb", i)])
    for nm in nbc:
        P.dma("sp", bc[nm][:], bc_d[nm], writes=[nm])
    for k in range(KC):
        b = k % 2
        P.dma("sp", xt[b][:], wod[:, k, :], writes=[("xt", b)])
        if k % 2 == 0:
            P.op("act", lambda e, b=b, k=k: e.copy(out=wo[:, k, :], in_=xt[b][:]), reads=[("xt", b)], writes=["wo"])
        else:
            P.op("pool", lambda e, b=b, k=k: e.tensor_copy(out=wo[:, k, :], in_=xt[b][:]), reads=[("xt", b)], writes=["wo"])

    G = 128 if kind == "a" else 64
    ng = 512 // G
    segs = [(r, m, 0) for (r, m) in _segments(0, n_lat)] + [(r, m, 1) for (r, m) in _segments(n_lat, n_ctx)]
    li = 0
    for si, (r0, m, mi) in enumerate(segs):
        b = si % 2
        P.dma("sp", xt[b][:m, :], xin[r0:r0 + m, :], writes=[("xt", b)])
        gstreams = [[], []]

        def emit_group(cg):
            sidx = cg // 2
            t1, t2, t3, sg, st = t1s[sidx], t2s[sidx], t3s[sidx], sgs[sidx], sts[sidx]
            SK = lambda nm: (nm, sidx)
            cs = slice(cg * 512, (cg + 1) * 512)
            lb = sidx
            P._defer = gstreams[sidx]
            T = {}
            for nm in O_INS[kind]:
                P.dma("sp", it[nm][lb][:m, :], ins_d[nm][r0:r0 + m, cs], writes=[(nm, lb)])
                T[nm] = it[nm][lb]
            gk = ("gate", lb)
            P.op("act", lambda e, m=m, T=T: e.activation(out=sg[:m, :], in_=T["gate"][:m, :], func=AF.Silu),
                 reads=[gk], writes=[SK("sg")])
            if kind == "b":
                P.op("dve", lambda e, m=m, T=T, b=b, cs=cs: e.tensor_tensor(out=zb[b][:m, cs], in0=T["m0"][:m, :],
                                                                          in1=sg[:m, :], op=ALU.mult),
                     reads=[("m0", lb), SK("sg")], writes=[("zb", b)])
                P._defer = None
                return
            P.op("dve", lambda e, m=m, T=T: e.tensor_tensor(out=t1[:m, :], in0=T["m0"][:m, :], in1=T["m1"][:m, :], op=ALU.add),
                 reads=[("m0", lb), ("m1", lb)], writes=[SK("t1")])
            y3 = t1[:m, :].rearrange("p (g c) -> p g c", c=G)
            if kind == "c":
                P.op("dve", lambda e, m=m, y3=y3: e.tensor_reduce(out=st[0][:m, :ng], in_=y3, axis=AX.X, op=ALU.add),
                     reads=[SK("t1")], writes=[SK("st0")])
                P.op("dve", lambda e, m=m: e.tensor_scalar(out=st[0][:m, :ng], in0=st[0][:m, :ng], scalar1=-1.0 / G,
                                                           scalar2=None, op0=ALU.mult),
                     reads=[SK("st0")], writes=[SK("st0")])
                P.op("dve", lambda e, m=m, y3=y3: e.tensor_tensor(out=y3, in0=y3,
                                                                in1=st[0][:m, :ng].unsqueeze(2).to_broadcast([m, ng, G]),
                                                                op=ALU.add),
                     reads=[SK("t1"), SK("st0")], writes=[SK("t1")])
            P.op("pool", lambda e, m=m: e.tensor_tensor(out=t2[:m, :], in0=t1[:m, :], in1=t1[:m, :], op=ALU.mult),
                 reads=[SK("t1")], writes=[SK("t2")])
            P.op("dve", lambda e, m=m: e.tensor_reduce(out=st[1][:m, :ng], in_=t2[:m, :].rearrange("p (g c) -> p g c", c=G),
                                                       axis=AX.X, op=ALU.add),
                 reads=[SK("t2")], writes=[SK("st1")])
            eps = 1e-6 if kind == "a" else 64e-5
            P.op("dve", lambda e, m=m, eps=eps: e.tensor_scalar(out=st[1][:m, :ng], in0=st[1][:m, :ng], scalar1=1.0 / G,
                                                                scalar2=eps, op0=ALU.mult, op1=ALU.add),
                 reads=[SK("st1")], writes=[SK("st1")])
            P.op("act", lambda e, m=m: e.activation(out=st[1][:m, :ng], in_=st[1][:m, :ng], func=AF.Sqrt),
                 reads=[SK("st1")], writes=[SK("st1")])
            P.op("dve", lambda e, m=m: e.reciprocal(out=st[2][:m, :ng], in_=st[1][:m, :ng]),
                 reads=[SK("st1")], writes=[SK("st2")])
            P.op("dve", lambda e, m=m, y3=y3: e.tensor_tensor(out=y3, in0=y3,
                                                            in1=st[2][:m, :ng].unsqueeze(2).to_broadcast([m, ng, G]),
                                                            op=ALU.mult),
                 reads=[SK("t1"), SK("st2")], writes=[SK("t1")])
            if kind == "a":
                P.op("pool", lambda e, m=m, cs=cs: e.tensor_tensor(out=t2[:m, :], in0=t1[:m, :], in1=bc["onw"][:m, cs], op=ALU.mult),
                     reads=[SK("t1"), "onw"], writes=[SK("t2")])
            else:
                P.op("pool", lambda e, m=m, cs=cs: e.tensor_tensor(out=t2[:m, :], in0=t1[:m, :], in1=bc["lnw"][:m, cs], op=ALU.mult),
                     reads=[SK("t1"), "lnw"], writes=[SK("t2")])
                P.op("pool", lambda e, m=m, T=T: e.tensor_tensor(out=t3[:m, :], in0=T["m2"][:m, :], in1=T["m3"][:m, :], op=ALU.add),
                     reads=[("m2", lb), ("m3", lb)], writes=[SK("t3")])
                P.op("pool", lambda e, m=m, cs=cs: e.tensor_tensor(out=t3[:m, :], in0=t3[:m, :], in1=bc["lnb"][:m, cs], op=ALU.add),
                     reads=[SK("t3"), "lnb"], writes=[SK("t3")])
                P.op("dve", lambda e, m=m: e.tensor_tensor(out=t2[:m, :], in0=t2[:m, :], in1=t3[:m, :], op=ALU.add),
                     reads=[SK("t2"), SK("t3")], writes=[SK("t2")])
            P.op("dve", lambda e, m=m, b=b, cs=cs: e.tensor_tensor(out=zb[b][:m, cs], in0=t2[:m, :], in1=sg[:m, :], op=ALU.mult),
                 reads=[SK("t2"), SK("sg")], writes=[("zb", b)])
            P._defer = None

        for cg in range(4):
            emit_group(cg)
        P.interleave(gstreams)
        for kg in range(KC // 4):
            tb = (si * 4 + kg) % 2
            for kk in range(4):
                k = kg * 4 + kk
                P.op("pe", lambda e, b=b, m=m, k=k, kk=kk, tb=tb: e.transpose(out=tps[tb][:, kk, :m],
                                                                              in_=zb[b][:m, k * 128:(k + 1) * 128],
                                                                              identity=ident[:m, :m]),
                     reads=[("zb", b), "ident"], writes=[("tps", tb)])
            if kg % 2 == 0:
                P.op("act", lambda e, m=m, kg=kg, tb=tb, b=b: e.copy(out=zT[b][:, kg * 4:(kg + 1) * 4, :m], in_=tps[tb][:, :, :m]),
                     reads=[("tps", tb)], writes=[("zT", b)])
            else:
                P.op("dve", lambda e, m=m, kg=kg, tb=tb, b=b: e.tensor_copy(out=zT[b][:, kg * 4:(kg + 1) * 4, :m], in_=tps[tb][:, :, :m]),
                     reads=[("tps", tb)], writes=[("zT", b)])
        for cg in range(4):
            cs = slice(cg * 512, (cg + 1) * 512)
            pb = cg
            for k in range(KC):
                P.op("pe", lambda e, b=b, m=m, k=k, pb=pb, cs=cs: e.matmul(mps[pb][:m, :], lhsT=zT[b][:, k, :m], rhs=wo[:, k, cs],
                                                                         start=(k == 0), stop=(k == KC - 1)),
                     reads=[("zT", b), "wo"], writes=[("mps", pb)])
            te = t1s[cg % 2]
            P.op("dve", lambda e, m=m, pb=pb, cs=cs, mi=mi, te=te: e.tensor_tensor(out=te[:m, :], in0=mps[pb][:m, :], in1=gb[mi][:m, cs], op=ALU.mult),
                 reads=[("mps", pb), ("gb", mi)], writes=[("t1", cg % 2)])
            P.op("pool", lambda e, m=m, b=b, cs=cs, te=te: e.tensor_tensor(out=xo[b][:m, cs], in0=te[:m, :], in1=xt[b][:m, cs], op=ALU.add),
                 reads=[("t1", cg % 2), ("xt", b)], writes=[("xo", b)])
        if final:
            P.op("act", lambda e, b=b, m=m: e.activation(out=xt[b][:m, :], in_=xo[b][:m, :], func=AF.Square, accum_out=fs[0][:m, :]),
                 reads=[("xo", b)], writes=[("xt", b), "fs0"])
            P.op("dve", lambda e, m=m: e.tensor_scalar(out=fs[0][:m, :], in0=fs[0][:m, :], scalar1=1.0 / D, scalar2=1e-6,
                                                       op0=ALU.mult, op1=ALU.add), reads=["fs0"], writes=["fs0"])
            P.op("act", lambda e, m=m: e.activation(out=fs[0][:m, :], in_=fs[0][:m, :], func=AF.Sqrt), reads=["fs0"], writes=["fs0"])
            P.op("dve", lambda e, m=m: e.reciprocal(out=fs[1][:m, :], in_=fs[0][:m, :]), reads=["fs0"], writes=["fs1"])
            P.op("dve", lambda e, m=m, b=b: e.scalar_tensor_tensor(out=xo[b][:m, :], in0=xo[b][:m, :], scalar=fs[1][:m, :],
                                                                   in1=bc["fnw"][:m, :], op0=ALU.mult, op1=ALU.mult),
                 reads=[("xo", b), "fs1", "fnw"], writes=[("xo", b)])
        P.dma("pool", out[r0:r0 + m, :], xo[b][:m, :], reads=[("xo", b)], writes=["out"])
    P.finish(["out"])
    return nc


def run_O(x, ctx, kind, mix_lat, mix_ctx, w_out, g_lat, g_ctx, bcs, final=False, need_ctx=True):
    n_ctx = N_CTX if need_ctx else 0
    nc = build_O(N_LAT, n_ctx, kind, final)
    ident = np.eye(128, dtype=np.float32)
    wo = np.ascontiguousarray(w_out.reshape(KC, 128, D).transpose(1, 0, 2))
    in_maps = []
    for cid in range(NCORES):
        b, q = cid // 4, cid % 4

        def rows(lat, cx):
            parts = [lat[b, q * N_LAT:(q + 1) * N_LAT]]
            if need_ctx:
                parts.append(cx[b, q * N_CTX:(q + 1) * N_CTX])
            return np.ascontiguousarray(np.concatenate(parts, axis=0))

        m = {"xin": rows(x, ctx), "wo": wo, "ident": ident, "gb": np.stack([_bc(g_lat[b]), _bc(g_ctx)])}
        for nm in O_INS[kind]:
            m[nm] = rows(mix_lat[nm], mix_ctx[nm] if need_ctx else None)
        for nm, v in bcs.items():
            m[nm] = _bc(v)
        in_maps.append(m)
    res = _run(nc, in_maps)
    xo = np.empty((2, T_LAT, D), np.float32)
    co = np.empty((2, T_CTX, D), np.float32) if need_ctx else None
    for cid in range(NCORES):
        b, q = cid // 4, cid % 4
        o = res[cid]["xout"]
        xo[b, q * N_LAT:(q + 1) * N_LAT] = o[:N_LAT]
        if need_ctx:
            co[b, q * N_CTX:(q + 1) * N_CTX] = o[N_LAT:]
    return xo, co


L_SEQ = T_CTX + T_LAT
MA_SHARE_PSUM = False
CH = 64


def build_Ma(nrec, L, jlayer):
    ntile = L // 128
    NR = nrec
    WN = NR * 128
    NG = NR * 2
    nc = bass.Bass("TRN2", target_bir_lowering=False)
    P = Prog(nc)
    qd = P.dram("qT", [nrec, 128, L])
    zd = P.dram("zT", [nrec, 128, L])
    vd = P.dram("v", [nrec, L // CH, CH, 128])
    lbd = P.dram("lbr", [nrec, 128, 2])
    maskd = P.dram("mask", [CH, CH])
    identd = P.dram("ident", [128, 128])
    out = P.dram("oT", [nrec, 128, L], kind="ExternalOutput")

    ident_f = P.sb([128, 128])
    ident = P.sb([128, 128], BF16)
    mask = P.sb([CH, CH])
    m01 = P.sb([128, WN])
    lbr = P.sb([128, nrec, 2])
    lb = P.sb([128, nrec])
    oml = P.sb([128, nrec])
    S = [P.sb([128, 128]) for _ in range(nrec)]
    Sb = [P.sb([128, 128], BF16) for _ in range(nrec)]
    Z = [P.sb([128, NR, 128]) for _ in range(2)]
    Qw = [P.sb([128, NR, 128]) for _ in range(2)]
    Vt = [P.sb([CH, NR, 2, 128]) for _ in range(2)]
    Vb = [P.sb([CH, NR, 2, 128], BF16) for _ in range(2)]
    e1, ft, gt, kq, bc, bm, be, ex0, ex2, ex3, bk = [P.sb([128, WN]) for _ in range(11)]
    qh = [P.sb([128, WN], BF16) for _ in range(2)]
    qtl = [P.sb([128, WN], BF16) for _ in range(2)]
    ktl = [P.sb([128, WN], BF16) for _ in range(2)]
    khI = [[P.sb([128, WN], BF16) for _ in range(4)] for _ in range(2)]
    gam = [P.sb([128, NG]) for _ in range(2)]
    osb = [P.sb([128, NR, 128]) for _ in range(2)]
    att_all = [[P.sb([CH, CH], BF16) for _ in range(2)] for _ in range(nrec)]
    ktm_all = [[P.sb([CH, 128], BF16) for _ in range(2)] for _ in range(nrec)]
    pA_bank = [P.ps([128, 512]) for _ in range(2)]
    pO_bank = [P.ps([128, 512]) for _ in range(2)]
    pT_bank = [P.ps([128, 1024], BF16) for _ in range(2)]
    pS_bank = [P.ps([128, 512]) for _ in range(2)]

    P.dma("sp", ident_f[:], identd[:], writes=["identf"])
    P.op("dve", lambda e: e.tensor_copy(out=ident[:], in_=ident_f[:]), reads=["identf"], writes=["ident"])
    P.dma("sp", mask[:], maskd[:], writes=["mask"])
    P.op("pool", lambda e: e.memset(m01[:], 1.0), writes=["m01"])
    P.op("pool", lambda e: e.memset(m01[:].rearrange("p (g s) -> p g s", s=CH)[:, :, 0:1], 0.0), reads=["m01"], writes=["m01"])
    for p_ in range(2):
        P.op("dve", lambda e, p_=p_: e.memset(pA_bank[p_][:], 0.0), writes=[("pA", p_)])
    for r in range(nrec):
        P.dma("sp", lbr[:, r, :], lbd[r], writes=["lbr"])
        P.op("pool", lambda e, r=r: e.memset(S[r][:], 0.0), writes=[("S", r)])
        P.op("pool", lambda e, r=r: e.memset(Sb[r][:], 0.0), writes=[("Sb", r)])
    if jlayer != 0:
        P.op("dve", lambda e: e.tensor_tensor(out=lb[:], in0=lbr[:, :, 0], in1=lbr[:, :, 1], op=ALU.subtract),
             reads=["lbr"], writes=["lb"])
        P.op("act", lambda e: e.activation(out=lb[:], in_=lb[:], func=AF.Exp), reads=["lb"], writes=["lb"])
        P.op("dve", lambda e: e.tensor_scalar(out=lb[:], in0=lb[:], scalar1=1.0, scalar2=None, op0=ALU.add),
             reads=["lb"], writes=["lb"])
        P.op("dve", lambda e: e.reciprocal(out=lb[:], in_=lb[:]), reads=["lb"], writes=["lb"])
        P.op("dve", lambda e: e.tensor_scalar(out=oml[:], in0=lb[:], scalar1=-1.0, scalar2=1.0, op0=ALU.mult, op1=ALU.add),
             reads=["lb"], writes=["oml"])

    QS = 128.0 ** -0.5

    NH = 2
    HW_ = WN // NH
    GH = NG // NH

    def emit_W(t, hf):
        tp = t % 2
        ts = slice(t * 128, (t + 1) * 128)
        KT = lambda nm: (nm, tp, hf)
        KH = lambda nm: (nm, hf)
        cs = slice(hf * HW_, (hf + 1) * HW_)
        rs = slice(hf * (NR // NH), (hf + 1) * (NR // NH))
        nr = NR // NH
        lst = []
        P._defer = lst
        z2 = Z[tp][:, rs, :].rearrange("p r t -> p (r t)")
        q2 = Qw[tp][:, rs, :].rearrange("p r t -> p (r t)")
        g3 = lambda ap: ap.rearrange("p (g s) -> p g s", s=CH)
        P.dma("sp", Z[tp][:, rs, :], zd[rs, :, ts].rearrange("r p t -> p r t"), writes=[KT("Z")])
        P.dma("sp", Qw[tp][:, rs, :], qd[rs, :, ts].rearrange("r p t -> p r t"), writes=[KT("Q")])
        for r_ in range(rs.start, rs.stop):
            P.dma("sp", Vt[tp][:, r_], vd[r_, 2 * t:2 * t + 2].rearrange("c s d -> s c d"), writes=[KT("Vt")])
        P.op("pool", lambda e: e.tensor_copy(out=Vb[tp][:, rs], in_=Vt[tp][:, rs]), reads=[KT("Vt")], writes=[KT("Vb")])
        P.op("act", lambda e: e.activation(out=e1[:, cs], in_=z2, func=AF.Exp, scale=-1.0), reads=[KT("Z")], writes=[KH("e1")])
        P.op("dve", lambda e: e.tensor_scalar(out=e1[:, cs], in0=e1[:, cs], scalar1=1.0, scalar2=None, op0=ALU.add), reads=[KH("e1")], writes=[KH("e1")])
        P.op("act", lambda e: e.activation(out=e1[:, cs], in_=e1[:, cs], func=AF.Ln), reads=[KH("e1")], writes=[KH("e1")])
        P.op("act", lambda e: e.activation(out=ft[:, cs], in_=e1[:, cs], func=AF.Exp, scale=-1.0), reads=[KH("e1")], writes=[KH("ft")])
        if jlayer != 0:
            f3 = ft[:, cs].rearrange("p (r t) -> p r t", t=128)
            P.op("dve", lambda e: e.tensor_tensor(out=f3, in0=f3, in1=oml[:, rs].unsqueeze(2).to_broadcast([128, nr, 128]), op=ALU.mult),
                 reads=[KH("ft"), "oml"], writes=[KH("ft")])
            P.op("dve", lambda e: e.tensor_tensor(out=f3, in0=f3, in1=lb[:, rs].unsqueeze(2).to_broadcast([128, nr, 128]), op=ALU.add),
                 reads=[KH("ft"), "lb"], writes=[KH("ft")])
            P.op("act", lambda e: e.activation(out=gt[:, cs], in_=ft[:, cs], func=AF.Ln), reads=[KH("ft")], writes=[KH("gt")])
        else:
            P.op("dve", lambda e: e.tensor_scalar(out=gt[:, cs], in0=e1[:, cs], scalar1=-1.0, scalar2=None, op0=ALU.mult),
                 reads=[KH("e1")], writes=[KH("gt")])
        P.op("pool", lambda e: e.tensor_scalar(out=kq[:, cs], in0=ft[:, cs], scalar1=-1.0, scalar2=1.0, op0=ALU.mult, op1=ALU.add),
             reads=[KH("ft")], writes=[KH("kq")])
        P.op("dve", lambda e: e.tensor_tensor_scan(out=bc[:, cs], data0=m01[:, cs], data1=gt[:, cs], initial=0.0, op0=ALU.mult, op1=ALU.add),
             reads=[KH("gt"), "m01"], writes=[KH("bc")])
        bc3 = g3(bc[:, cs])
        bc4 = bc[:, cs].rearrange("p (g i s) -> p g i s", i=4, s=16)
        P.op("dve", lambda e: e.tensor_tensor(out=bm[:, cs].rearrange("p (g i s) -> p g i s", i=4, s=16), in0=bc4,
                                              in1=bc4[:, :, :, 0:1].to_broadcast([128, GH, 4, 16]), op=ALU.subtract),
             reads=[KH("bc")], writes=[KH("bm")])
        P.op("dve", lambda e: e.tensor_tensor(out=g3(be[:, cs]), in0=bc3, in1=bc3[:, :, CH - 1:CH].to_broadcast([128, GH, CH]), op=ALU.subtract),
             reads=[KH("bc")], writes=[KH("be")])
        P.op("act", lambda e: e.activation(out=ex0[:, cs], in_=bm[:, cs], func=AF.Exp), reads=[KH("bm")], writes=[KH("ex0")])
        P.op("act", lambda e: e.activation(out=ex2[:, cs], in_=bc[:, cs], func=AF.Exp), reads=[KH("bc")], writes=[KH("ex2")])
        P.op("act", lambda e: e.activation(out=ex3[:, cs], in_=be[:, cs], func=AF.Exp, scale=-1.0), reads=[KH("be")], writes=[KH("ex3")])
        P.op("act", lambda e: e.activation(out=gam[tp][:, hf * GH:(hf + 1) * GH], in_=bc3[:, :, CH - 1], func=AF.Exp), reads=[KH("bc")], writes=[KT("gam")])
        P.op("dve", lambda e: e.scalar_tensor_tensor(out=qh[tp][:, cs], in0=q2, scalar=QS, in1=ex0[:, cs], op0=ALU.mult, op1=ALU.mult),
             reads=[KT("Q"), KH("ex0")], writes=[KT("qh")])
        P.op("dve", lambda e: e.scalar_tensor_tensor(out=qtl[tp][:, cs], in0=q2, scalar=QS, in1=ex2[:, cs], op0=ALU.mult, op1=ALU.mult),
             reads=[KT("Q"), KH("ex2")], writes=[KT("qtl")])
        P.op("pool", lambda e: e.tensor_tensor(out=ktl[tp][:, cs], in0=kq[:, cs], in1=ex3[:, cs], op=ALU.mult), reads=[KH("kq"), KH("ex3")], writes=[KT("ktl")])
        kq3 = g3(kq[:, cs])
        bk3 = g3(bk[:, cs])
        for I in range(4):
            n = 16 * (I + 1)
            P.op("dve", lambda e, n=n, I=I: e.tensor_tensor(out=bk3[:, :, :n], in0=bc3[:, :, :n],
                                                          in1=bc3[:, :, 16 * I:16 * I + 1].to_broadcast([128, GH, n]), op=ALU.subtract),
                 reads=[KH("bc"), KH("bk")], writes=[KH("bk")])
            P.op("act", lambda e, n=n: e.activation(out=bk3[:, :, :n], in_=bk3[:, :, :n], func=AF.Exp, scale=-1.0), reads=[KH("bk")], writes=[KH("bk")])
            eng_ = "pool" if I % 2 == 0 else "dve"
            P.op(eng_, lambda e, n=n, I=I: e.tensor_tensor(out=g3(khI[tp][I][:, cs])[:, :, :n], in0=kq3[:, :, :n], in1=bk3[:, :, :n], op=ALU.mult),
                 reads=[KH("kq"), KH("bk")], writes=[KT("kh%d" % I)])
        P._defer = None
        return lst

    def emit_R(t, r):
        tp = t % 2
        hf_ = r // (NR // NH)
        KT = lambda nm: (nm, tp, hf_) if nm != "osb" else (nm, tp)
        lst = []
        P._defer = lst
        for c in range(2):
            g = r * 2 + c
            go = g * CH
            p = (r + c) % 2
            pA_t = pA_bank[p][:, 0:CH]
            pO_t = pO_bank[p][:, 0:CH]
            pT_t = pT_bank[p][:, 0:128]
            pS_t = pS_bank[p][:, 0:128]
            att_t = att_all[r][c]
            ktm_t = ktm_all[r][c]
            vb_t = Vb[tp][:, r, c, :]
            P.atomic_begin()
            for I in range(4):
                n = 16 * (I + 1)
                P.op("pe", lambda e, I=I, n=n, pA_t=pA_t, go=go: e.matmul(pA_t[:n, 16 * I:16 * I + 16], lhsT=khI[tp][I][:, go:go + n],
                                                                         rhs=qh[tp][:, go + 16 * I:go + 16 * I + 16], start=True, stop=True),
                     reads=[KT("kh%d" % I), KT("qh")], writes=[("pA", p)])
            P.op("dve", lambda e, att_t=att_t, pA_t=pA_t: e.tensor_tensor(out=att_t[:], in0=pA_t[:CH, :], in1=mask[:], op=ALU.mult),
                 reads=[("pA", p), "mask"], writes=[("att", r, c)])
            P.atomic_end()
            P.atomic_begin()
            P.op("pe", lambda e, att_t=att_t, pO_t=pO_t, vb_t=vb_t: e.matmul(pO_t, lhsT=vb_t, rhs=att_t[:], start=True, stop=False),
                 reads=[KT("Vb"), ("att", r, c), ("Sb", r), KT("qtl")], writes=[("pO", p)])
            P.op("pe", lambda e, pO_t=pO_t, go=go: e.matmul(pO_t, lhsT=Sb[r][:], rhs=qtl[tp][:, go:go + CH], start=False, stop=True),
                 reads=[("Sb", r), KT("qtl")], writes=[("pO", p)])
            P.op("act", lambda e, pO_t=pO_t, c=c: e.copy(out=osb[tp][:, r, c * CH:(c + 1) * CH], in_=pO_t),
                 reads=[("pO", p)], writes=[KT("osb")])
            P.atomic_end()
            P.atomic_begin()
            P.op("pe", lambda e, pT_t=pT_t, go=go: e.transpose(out=pT_t[:CH, :], in_=ktl[tp][:, go:go + CH], identity=ident[:]),
                 reads=[KT("ktl"), "ident"], writes=[("pT", p)])
            P.op("act", lambda e, ktm_t=ktm_t, pT_t=pT_t: e.copy(out=ktm_t[:], in_=pT_t[:CH, :]), reads=[("pT", p)], writes=[("ktm", r, c)])
            P.atomic_end()
            P.atomic_begin()
            P.op("pe", lambda e, ktm_t=ktm_t, pS_t=pS_t, vb_t=vb_t: e.matmul(pS_t, lhsT=ktm_t[:], rhs=vb_t, start=True, stop=True),
                 reads=[("ktm", r, c), KT("Vb")], writes=[("pS", p)])
            P.op("dve", lambda e, pS_t=pS_t, g=g: e.scalar_tensor_tensor(out=S[r][:], in0=S[r][:], scalar=gam[tp][:, g:g + 1],
                                                                        in1=pS_t, op0=ALU.mult, op1=ALU.add),
                 reads=[("S", r), KT("gam"), ("pS", p)], writes=[("S", r)])
            P.atomic_end()
            P.op("pool", lambda e: e.tensor_copy(out=Sb[r][:], in_=S[r][:]), reads=[("S", r)], writes=[("Sb", r)])
        P._defer = None
        return lst

    P.interleave([emit_W(0, hf) for hf in range(NH)])
    for t in range(ntile):
        ts = slice(t * 128, (t + 1) * 128)
        streams = [emit_R(t, r) for r in range(nrec)]
        if t + 1 < ntile:
            streams += [emit_W(t + 1, hf) for hf in range(NH)]
        P.interleave(streams)
        P.dma("pool", out[:, :, ts].rearrange("r p t -> p r t"), osb[t % 2][:], reads=[("osb", t % 2)], writes=["out"])
    P.finish(["out"])
    return nc


def _seq(lat_b, ctx_b, rev):
    if rev:
        return np.concatenate([ctx_b[::-1], lat_b[::-1]], axis=0)
    return np.concatenate([ctx_b, lat_b], axis=0)


def _unseq(s, rev):
    c, l = s[:T_CTX], s[T_CTX:]
    if rev:
        return l[::-1], c[::-1]
    return l, c


def run_Ma(pl, pc, a_lb_raw, jlayer):
    nrec = 8
    nc = build_Ma(nrec, L_SEQ, jlayer)
    ident = np.eye(128, dtype=np.float32)
    mask = np.triu(np.ones((CH, CH), np.float32))
    in_maps = []
    for cid in range(NCORES):
        b, hg = cid // 4, cid % 4
        qT = np.empty((nrec, 128, L_SEQ), np.float32)
        zT = np.empty((nrec, 128, L_SEQ), np.float32)
        v = np.empty((nrec, L_SEQ // CH, CH, 128), np.float32)
        lbr = np.empty((nrec, 128, 2), np.float32)
        for hl in range(4):
            h = hg * 4 + hl
            hc = slice(h * 128, (h + 1) * 128)
            for dr in range(2):
                r = hl * 2 + dr
                zoff = 2 * D + dr * D
                sq = _seq(pl[b][:, hc], pc[b][:, hc], dr == 1)
                sv = _seq(pl[b][:, D + h * 128:D + (h + 1) * 128], pc[b][:, D + h * 128:D + (h + 1) * 128], dr == 1)
                sz = _seq(pl[b][:, zoff + h * 128:zoff + (h + 1) * 128], pc[b][:, zoff + h * 128:zoff + (h + 1) * 128], dr == 1)
                qT[r] = sq.T
                zT[r] = sz.T
                v[r] = sv.reshape(L_SEQ // CH, CH, 128)
                lbr[r] = a_lb_raw[:, hc].T
        in_maps.append({"qT": qT, "zT": zT, "v": v, "lbr": lbr, "mask": mask, "ident": ident})
    res = _run(nc, in_maps)
    o_lat = [np.empty((2, T_LAT, D), np.float32) for _ in range(2)]
    o_ctx = [np.empty((2, T_CTX, D), np.float32) for _ in range(2)]
    for cid in range(NCORES):
        b, hg = cid // 4, cid % 4
        oT = res[cid]["oT"]
        for hl in range(4):
            h = hg * 4 + hl
            for dr in range(2):
                l, c = _unseq(oT[hl * 2 + dr].T, dr == 1)
                o_lat[dr][b][:, h * 128:(h + 1) * 128] = l
                o_ctx[dr][b][:, h * 128:(h + 1) * 128] = c
    return o_lat, o_ctx


def layer_a(x, ctx, mod_i, norm_w_i, w_in, a_lb_raw, jlayer, onorm_w, w_out, need_ctx, final_w=None):
    pl, pc = run_P(x, ctx, mod_i, norm_w_i, w_in)
    o_lat, o_ctx = run_Ma(pl, pc, a_lb_raw, jlayer)
    ml = {"m0": o_lat[0], "m1": o_lat[1], "gate": pl[:, :, 4 * D:5 * D]}
    mc = {"m0": o_ctx[0], "m1": o_ctx[1], "gate": pc[:, :, 4 * D:5 * D]}
    bcs = {"onw": np.tile(onorm_w, 16)}
    if final_w is not None:
        bcs["fnw"] = final_w
    return run_O(x, ctx, "a", ml, mc, w_out, mod_i[0:2, 2 * D:3 * D], mod_i[2, 2 * D:3 * D], bcs,
                 final=final_w is not None, need_ctx=need_ctx)


NQH = 8
HD = 64
NBLK = T_LAT // 128


def build_Mb(need_ctx):
    nc = bass.Bass("TRN2", target_bir_lowering=False)
    P = Prog(nc)
    qd = P.dram("qT", [HD, NQH, T_LAT])
    qsd = P.dram("qsT", [HD, NQH, T_LAT])
    kd = P.dram("kT", [HD, T_LAT])
    ksd = P.dram("ksT", [HD, T_LAT])
    vd = P.dram("v", [128, NBLK, HD])
    qcd = P.dram("qcT", [HD, NQH, T_CTX])
    kcd = P.dram("kcT", [HD, T_CTX])
    vcd = P.dram("vc", [128, 2, HD])
    posd = P.dram("pos", [HD, T_LAT])
    fid = P.dram("fidx", [HD, 2])
    sinkd = P.dram("sink", [128, NQH])
    mld = P.dram("maskl", [128, 128])
    mrd = P.dram("maskr", [128, 128])
    n_out = T_LAT + (T_CTX if need_ctx else 0)
    out = P.dram("o", [n_out, NQH * HD], kind="ExternalOutput")

    PI = float(np.pi)
    CW = 2048
    cosT = P.sb([HD, T_LAT])
    sinT = P.sb([HD, T_LAT])
    posc = P.sb([HD, CW])
    tmpA = P.sb([HD, CW])
    tmpB = P.sb([HD, CW])
    fidx = P.sb([HD, 2])
    inv = P.sb([HD, 1])
    kr = P.sb([HD, T_LAT], BF16)
    kcb = P.sb([HD, T_CTX], BF16)
    kcf = P.sb([HD, T_CTX])
    vf = P.sb([128, NBLK, HD])
    vx = P.sb([128, NBLK, HD + 1], BF16)
    vcf = P.sb([128, 2, HD])
    vcx = P.sb([128, 2, HD + 1], BF16)
    esink = P.sb([128, NQH])
    ml = P.sb([128, 128], BF16)
    mr = P.sb([128, 128], BF16)
    mlf = P.sb([128, 128])
    mrf = P.sb([128, 128])
    qf = [P.sb([HD, NQH, 128]) for _ in range(2)]
    qsf = [P.sb([HD, NQH, 128]) for _ in range(2)]
    qt1 = P.sb([HD, NQH, 128])
    qt2 = P.sb([HD, NQH, 128])
    qr = [P.sb([HD, NQH, 128], BF16) for _ in range(2)]
    E = [P.sb([128, NQH, 128], BF16) for _ in range(5)]
    pS = [P.ps([128, 1024]) for _ in range(2)]
    pO = [P.ps([128, 4, 128]) for _ in range(2)]
    den = P.sb([128, NQH])
    osb = [P.sb([128, NQH, HD]) for _ in range(2)]

    P.dma("sp", fidx[:], fid[:], writes=["fidx"])
    P.dma("sp", esink[:], sinkd[:], writes=["esink"])
    P.dma("sp", mlf[:], mld[:], writes=["mlf"])
    P.dma("sp", mrf[:], mrd[:], writes=["mrf"])
    P.op("dve", lambda e: e.tensor_copy(out=ml[:], in_=mlf[:]), reads=["mlf"], writes=["ml"])
    P.op("dve", lambda e: e.tensor_copy(out=mr[:], in_=mrf[:]), reads=["mrf"], writes=["mr"])
    P.op("act", lambda e: e.activation(out=esink[:], in_=esink[:], func=AF.Exp), reads=["esink"], writes=["esink"])
    P.op("act", lambda e: e.activation(out=inv[:], in_=fidx[:, 0:1], func=AF.Exp, scale=-float(np.log(10000.0)) / 16.0),
         reads=["fidx"], writes=["inv"])

    def sincos(dst_full, shift, key, cc):
        cs_ = slice(cc * CW, (cc + 1) * CW)
        dst = dst_full[:, cs_]
        P.op("dve", lambda e: e.tensor_scalar(out=tmpA[:], in0=posc[:], scalar1=inv[:, 0:1], scalar2=shift, op0=ALU.mult, op1=ALU.add),
             reads=["pos", "inv"], writes=["tmpA"])
        ni = tmpB[:].bitcast(mybir.dt.int32)
        P.op("dve", lambda e: e.tensor_scalar(out=dst, in0=tmpA[:], scalar1=1.0 / (2 * PI), scalar2=0.5, op0=ALU.mult, op1=ALU.add),
             reads=["tmpA"], writes=[key])
        P.op("dve", lambda e: e.tensor_copy(out=ni, in_=dst), reads=[key], writes=["tmpB"])
        P.op("dve", lambda e: e.tensor_copy(out=dst, in_=ni), reads=["tmpB"], writes=[key])
        P.op("dve", lambda e: e.scalar_tensor_tensor(out=tmpA[:], in0=dst, scalar=-2 * PI, in1=tmpA[:], op0=ALU.mult, op1=ALU.add),
             reads=[key, "tmpA"], writes=["tmpA"])
        P.op("dve", lambda e: e.tensor_scalar(out=tmpB[:], in0=tmpA[:], scalar1=-PI, scalar2=2 * PI, op0=ALU.is_lt, op1=ALU.mult),
             reads=["tmpA"], writes=["tmpB"])
        P.op("dve", lambda e: e.tensor_tensor(out=tmpA[:], in0=tmpA[:], in1=tmpB[:], op=ALU.add), reads=["tmpA", "tmpB"], writes=["tmpA"])
        P.op("dve", lambda e: e.tensor_scalar(out=tmpB[:], in0=tmpA[:], scalar1=PI, scalar2=-2 * PI, op0=ALU.is_gt, op1=ALU.mult),
             reads=["tmpA"], writes=["tmpB"])
        P.op("dve", lambda e: e.tensor_tensor(out=tmpA[:], in0=tmpA[:], in1=tmpB[:], op=ALU.add), reads=["tmpA", "tmpB"], writes=["tmpA"])
        P.op("dve", lambda e: e.tensor_scalar(out=tmpA[:], in0=tmpA[:], scalar1=PI, scalar2=-PI, op0=ALU.min, op1=ALU.max),
             reads=["tmpA"], writes=["tmpA"])
        P.op("act", lambda e: e.activation(out=dst, in_=tmpA[:], func=AF.Sin), reads=["tmpA"], writes=[key])

    for cc in range(T_LAT // CW):
        P.dma("sp", posc[:], posd[:, cc * CW:(cc + 1) * CW], writes=["pos"])
        sincos(sinT, 0.0, "sinT", cc)
        sincos(cosT, PI / 2, "cosT", cc)
    P.op("dve", lambda e: e.tensor_scalar(out=sinT[:], in0=sinT[:], scalar1=fidx[:, 1:2], scalar2=None, op0=ALU.mult),
         reads=["sinT", "fidx"], writes=["sinT"])

    for cc in range(T_LAT // CW):
        cs_ = slice(cc * CW, (cc + 1) * CW)
        P.dma("sp", tmpA[:], kd[:, cs_], writes=["tmpA"])
        P.dma("sp", tmpB[:], ksd[:, cs_], writes=["tmpB"])
        P.op("dve", lambda e, cs_=cs_: e.tensor_tensor(out=tmpA[:], in0=tmpA[:], in1=cosT[:, cs_], op=ALU.mult), reads=["tmpA", "cosT"], writes=["tmpA"])
        P.op("pool", lambda e, cs_=cs_: e.tensor_tensor(out=tmpB[:], in0=tmpB[:], in1=sinT[:, cs_], op=ALU.mult), reads=["tmpB", "sinT"], writes=["tmpB"])
        P.op("dve", lambda e, cs_=cs_: e.tensor_tensor(out=kr[:, cs_], in0=tmpA[:], in1=tmpB[:], op=ALU.add), reads=["tmpA", "tmpB"], writes=["kr"])
    P.dma("sp", kcf[:], kcd[:], writes=["kcf"])
    P.op("dve", lambda e: e.tensor_copy(out=kcb[:], in_=kcf[:]), reads=["kcf"], writes=["kcb"])
    P.dma("sp", vf[:], vd[:], writes=["vf"])
    P.dma("sp", vcf[:], vcd[:], writes=["vcf"])
    P.op("pool", lambda e: e.memset(vx[:], 1.0), writes=["vx"])
    P.op("pool", lambda e: e.memset(vcx[:], 1.0), writes=["vcx"])
    P.op("pool", lambda e: e.tensor_copy(out=vx[:, :, 0:HD], in_=vf[:]), reads=["vf"], writes=["vx"])
    P.op("pool", lambda e: e.tensor_copy(out=vcx[:, :, 0:HD], in_=vcf[:]), reads=["vcf"], writes=["vcx"])

    blocks = [("lat", j) for j in range(NBLK)]
    if need_ctx:
        blocks += [("ctx", j) for j in range(2)]
    for bi, (typ, j) in enumerate(blocks):
        b = bi % 2
        ts = slice(j * 128, (j + 1) * 128)
        if typ == "lat":
            P.dma("sp", qf[b][:], qd[:, :, ts], writes=[("qf", b)])
            P.dma("sp", qsf[b][:], qsd[:, :, ts], writes=[("qsf", b)])
            P.op("dve", lambda e, b=b, ts=ts: e.tensor_tensor(out=qt1[:], in0=qf[b][:], in1=cosT[:, None, ts].to_broadcast([HD, NQH, 128]),
                                                            op=ALU.mult), reads=[("qf", b), "cosT"], writes=["qt1"])
            P.op("pool", lambda e, b=b, ts=ts: e.tensor_tensor(out=qt2[:], in0=qsf[b][:], in1=sinT[:, None, ts].to_broadcast([HD, NQH, 128]),
                                                             op=ALU.mult), reads=[("qsf", b), "sinT"], writes=["qt2"])
            P.op("dve", lambda e, b=b: e.tensor_tensor(out=qr[b][:], in0=qt1[:], in1=qt2[:], op=ALU.add),
                 reads=["qt1", "qt2"], writes=[("qr", b)])
            kbs = []
            if j > 0:
                kbs.append(("lat", j - 1, "ml"))
            kbs.append(("lat", j, None))
            if j < NBLK - 1:
                kbs.append(("lat", j + 1, "mr"))
            kbs += [("ctx", 0, None), ("ctx", 1, None)]
            orow = j * 128
        else:
            P.dma("sp", qf[b][:], qcd[:, :, ts], writes=[("qf", b)])
            P.op("dve", lambda e, b=b: e.tensor_copy(out=qr[b][:], in_=qf[b][:]), reads=[("qf", b)], writes=[("qr", b)])
            kbs = [("ctx", 0, None), ("ctx", 1, None)]
            orow = T_LAT + j * 128
        for ki, (kt, kj, mk) in enumerate(kbs):
            p = ki % 2
            ksl = slice(kj * 128, (kj + 1) * 128)
            kap = kr[:, ksl] if kt == "lat" else kcb[:, ksl]
            kkey = "kr" if kt == "lat" else "kcb"
            for hh in range(2):
                P.op("pe", lambda e, b=b, p=p, hh=hh, kap=kap: e.matmul(pS[p][:, hh * 512:(hh + 1) * 512], lhsT=kap,
                                                                      rhs=qr[b][:, hh * 4:(hh + 1) * 4, :], start=True, stop=True),
                     reads=[kkey, ("qr", b)], writes=[("pS", p)])
            P.op("act", lambda e, p=p, ki=ki: e.activation(out=E[ki][:].rearrange("p h q -> p (h q)"), in_=pS[p][:], func=AF.Exp, scale=HD ** -0.5),
                 reads=[("pS", p)], writes=[("E", ki)])
            if mk is not None:
                mt = ml if mk == "ml" else mr
                P.op("dve", lambda e, ki=ki, mt=mt: e.tensor_tensor(out=E[ki][:], in0=E[ki][:], in1=mt[:, None, :].to_broadcast([128, NQH, 128]),
                                                                   op=ALU.mult), reads=[("E", ki), mk], writes=[("E", ki)])
        nk = len(kbs)
        for h in range(NQH):
            po = h // 4
            for ki, (kt, kj, mk) in enumerate(kbs):
                vap = vx[:, kj, :] if kt == "lat" else vcx[:, kj, :]
                vkey = "vx" if kt == "lat" else "vcx"
                P.op("pe", lambda e, h=h, po=po, ki=ki, vap=vap: e.matmul(pO[po][:, h % 4, 0:HD + 1], lhsT=E[ki][:, h, :], rhs=vap,
                                                                        start=(ki == 0), stop=(ki == nk - 1)),
                     reads=[("E", ki), vkey], writes=[("pO", po)])
        for po in range(2):
            hs = slice(po * 4, (po + 1) * 4)
            P.op("dve", lambda e, po=po, hs=hs: e.tensor_tensor(out=den[:, hs], in0=pO[po][:, :, HD], in1=esink[:, hs], op=ALU.add),
                 reads=[("pO", po), "esink"], writes=["den"])
            P.op("dve", lambda e, hs=hs: e.reciprocal(out=den[:, hs], in_=den[:, hs]), reads=["den"], writes=["den"])
            P.op("dve", lambda e, po=po, hs=hs, b=b: e.tensor_tensor(out=osb[b][:, hs, :], in0=pO[po][:, :, 0:HD],
                                                                    in1=den[:, hs].unsqueeze(2).to_broadcast([128, 4, HD]), op=ALU.mult),
                 reads=[("pO", po), "den"], writes=[("osb", b)])
        P.dma("pool", out[orow:orow + 128, :], osb[b][:].rearrange("p h d -> p (h d)"), reads=[("osb", b)], writes=["out"])
    P.finish(["out"])
    return nc


def _rope_swap(a):
    hd = a.shape[-1]
    idx = np.arange(hd)
    idx = (idx // 32) * 32 + ((idx % 32) + 16) % 32
    return a[..., idx]


def run_Mb(pl, pc, sink, need_ctx):
    nc = build_Mb(need_ctx)
    t = np.arange(T_LAT)
    pos = np.empty((HD, T_LAT), np.float32)
    pos[:32] = (t // 64)[None, :]
    pos[32:] = (t % 64)[None, :]
    d = np.arange(HD)
    fidx = np.stack([(d % 16).astype(np.float32), np.where((d % 32) < 16, -1.0, 1.0).astype(np.float32)], axis=1)
    jj, ii = np.meshgrid(np.arange(128), np.arange(128), indexing="ij")
    maskl = (jj >= ii).astype(np.float32)
    maskr = (jj <= ii).astype(np.float32)
    in_maps = []
    for cid in range(NCORES):
        b, g = cid // 4, cid % 4
        q = pl[b][:, g * 512:(g + 1) * 512].reshape(T_LAT, NQH, HD)
        k = pl[b][:, D + g * HD:D + (g + 1) * HD]
        v = pl[b][:, D + 256 + g * HD:D + 256 + (g + 1) * HD]
        qc = pc[b][:, g * 512:(g + 1) * 512].reshape(T_CTX, NQH, HD)
        kc = pc[b][:, D + g * HD:D + (g + 1) * HD]
        vc = pc[b][:, D + 256 + g * HD:D + 256 + (g + 1) * HD]
        m = {"qT": np.ascontiguousarray(q.transpose(2, 1, 0)), "qsT": np.ascontiguousarray(_rope_swap(q).transpose(2, 1, 0)),
             "kT": np.ascontiguousarray(k.T), "ksT": np.ascontiguousarray(_rope_swap(k).T),
             "v": np.ascontiguousarray(v.reshape(NBLK, 128, HD).transpose(1, 0, 2)),
             "qcT": np.ascontiguousarray(qc.transpose(2, 1, 0)), "kcT": np.ascontiguousarray(kc.T),
             "vc": np.ascontiguousarray(vc.reshape(2, 128, HD).transpose(1, 0, 2)),
             "pos": pos, "fidx": fidx, "sink": _bc(sink[g * NQH:(g + 1) * NQH]), "maskl": maskl, "maskr": maskr}
        in_maps.append(m)
    res = _run(nc, in_maps)
    ol = np.empty((2, T_LAT, D), np.float32)
    oc = np.empty((2, T_CTX, D), np.float32) if need_ctx else None
    for cid in range(NCORES):
        b, g = cid // 4, cid % 4
        o = res[cid]["o"]
        ol[b][:, g * 512:(g + 1) * 512] = o[:T_LAT]
        if need_ctx:
            oc[b][:, g * 512:(g + 1) * 512] = o[T_LAT:]
    return ol, oc


def layer_b(x, ctx, mod_i, norm_w_i, w_in, sink, w_out, need_ctx, final_w=None):
    pl, pc = run_P(x, ctx, mod_i, norm_w_i, w_in)
    ol, oc = run_Mb(pl, pc, sink, need_ctx)
    ml = {"m0": ol, "gate": pl[:, :, D + 512:]}
    mc = {"m0": oc, "gate": pc[:, :, D + 512:]}
    bcs = {}
    if final_w is not None:
        bcs["fnw"] = final_w
    return run_O(x, ctx, "b", ml, mc, w_out, mod_i[0:2, 2 * D:3 * D], mod_i[2, 2 * D:3 * D], bcs,
                 final=final_w is not None, need_ctx=need_ctx)


NH_C = 8
MC_INTERLEAVE = False
MC_FP32R = True
MC_PIPE = True


def R32(ap):
    return ap.bitcast(mybir.dt.float32r) if MC_FP32R else ap
HC = 64
LORA = 96


def build_Mc(L):
    ntile = L // 128
    W = NH_C * HC
    nc = bass.Bass("TRN2", target_bir_lowering=False)
    P = Prog(nc)
    rd = P.dram("r", [2, L, W])
    kd = P.dram("k", [2, L, W])
    vd = P.dram("v", [2, L, W])
    lwd = P.dram("lwT", [2, LORA, L])
    lad = P.dram("laT", [2, LORA, L])
    w2d = P.dram("w2", [2, LORA, W])
    a2d = P.dram("a2", [2, LORA, W])
    w0d = P.dram("w0b", [2, 128, W])
    a0d = P.dram("a0b", [2, 128, W])
    kkd = P.dram("kkb", [128, W])
    kad = P.dram("kab", [128, W])
    rkd = P.dram("rkb", [128, W])
    trid = P.dram("tri", [128, 128])
    mupd = P.dram("mup", [128, 128])
    mlod = P.dram("mlo", [128, 128])
    identd = P.dram("ident", [128, 128])
    seld = P.dram("sel", [128, 1])
    yout = P.dram("y", [2, L, W], kind="ExternalOutput")
    bout = P.dram("bon", [2, L, W], kind="ExternalOutput")

    def T2(n, dt=F32, shape=(128, W)):
        return [P.sb(list(shape), dt) for _ in range(n)]

    ident = P.sb([128, 128])
    identb = P.sb([128, 128], BF16)
    tri = P.sb([128, 128])
    mup = P.sb([128, 128])
    mupi = P.sb([128, 128])
    mlo = P.sb([128, 128])
    sel = P.sb([128, 1])
    w2 = T2(2, F32, (LORA, W))
    a2 = T2(2, F32, (LORA, W))
    w0b = T2(2)
    a0b = T2(2)
    kkb = P.sb([128, W])
    kab = P.sb([128, W])
    omka = P.sb([128, W])
    rkb = P.sb([128, W])
    rt, kt, vt = T2(2), T2(2), T2(2)
    lw = T2(2, F32, (LORA, 128))
    la = T2(2, F32, (LORA, 128))
    th_ = T2(2, F32, (LORA, 128))
    zt_, ld_, iclr_, kkr_, kk_, t1_, kdt_ = T2(2), T2(2), T2(2), T2(2), T2(2), T2(2), T2(2)
    ein_, eneg_ = T2(2), T2(2)
    st8_ = [[P.sb([128, NH_C]) for _ in range(3)] for _ in range(2)]
    Ah_, Bh_, Kh_, Rh_, Vb_ = T2(4, BF16), T2(4, BF16), T2(4, BF16), T2(4, BF16), T2(4, BF16)
    XT_ = [{nm: P.sb([HC, NH_C, 128], BF16) for nm in ("a", "b", "k", "r")} for _ in range(4)]
    gamT_ = [P.sb([HC, NH_C]) for _ in range(4)]
    Nf_ = [[[P.sb([128, 4, 128]) for _ in range(2)] for _ in range(2)] for _ in range(2)]
    NTf_ = [[[P.sb([128, 4, 128]) for _ in range(2)] for _ in range(2)] for _ in range(2)]
    TT_ = [[P.sb([128, 4, 128]) for _ in range(2)] for _ in range(2)]
    TTb_ = [[P.sb([128, 4, 128], BF16) for _ in range(2)] for _ in range(2)]
    AakT_ = [[P.sb([128, 4, 128], BF16) for _ in range(2)] for _ in range(2)]
    AVb_ = [[P.sb([128, 4, HC], BF16) for _ in range(2)] for _ in range(2)]
    ArbT_ = [P.sb([128, NH_C, 128], BF16) for _ in range(2)]
    ArkT_ = [P.sb([128, NH_C, 128], BF16) for _ in range(2)]
    TAb_ = [P.sb([128, NH_C, HC], BF16) for _ in range(2)]
    TVb_ = [P.sb([128, NH_C, HC], BF16) for _ in range(2)]
    MTb_ = [P.sb([HC, NH_C, HC], BF16) for _ in range(2)]
    RQTb_ = [P.sb([HC, NH_C, 128], BF16) for _ in range(2)]
    Pb = [P.sb([HC, NH_C, HC], BF16) for _ in range(2)]
    ysb = T2(2, F32, (128, NH_C, HC))
    pz = P.ps([128, 512])
    ptr = [P.ps([128, 1024], BF16) for _ in range(2)]
    big = [P.ps([128, 4, 128]) for _ in range(2)]
    py = P.ps([128, NH_C, HC])
    pp = P.ps([128, NH_C, HC])
    pg = P.ps([128, 512])

    for (t_, d_, k_) in [(ident, identd, "ident"), (tri, trid, "tri"), (mup, mupd, "mup"), (mlo, mlod, "mlo"), (sel, seld, "sel"),
                         (kkb, kkd, "kkb"), (kab, kad, "kab"), (rkb, rkd, "rkb")]:
        P.dma("sp", t_[:], d_[:], writes=[k_])
    for d in range(2):
        P.dma("sp", w2[d][:], w2d[d], writes=[("w2", d)])
        P.dma("sp", a2[d][:], a2d[d], writes=[("a2", d)])
        P.dma("sp", w0b[d][:], w0d[d], writes=[("w0b", d)])
        P.dma("sp", a0b[d][:], a0d[d], writes=[("a0b", d)])
        P.op("pool", lambda e, d=d: e.memset(Pb[d][:], 0.0), writes=[("Pb", d)])
    P.op("dve", lambda e: e.tensor_copy(out=identb[:], in_=ident[:]), reads=["ident"], writes=["identb"])
    P.op("dve", lambda e: e.tensor_tensor(out=mupi[:], in0=mup[:], in1=ident[:], op=ALU.add), reads=["mup", "ident"], writes=["mupi"])
    P.op("dve", lambda e: e.tensor_scalar(out=omka[:], in0=kab[:], scalar1=-1.0, scalar2=None, op0=ALU.mult),
         reads=["kab"], writes=["omka"])
    P.op("dve", lambda e: e.tensor_scalar(out=omka[:], in0=omka[:], scalar1=1.0, scalar2=None, op0=ALU.add),
         reads=["omka"], writes=["omka"])

    bi = [0]

    def emit_dir(t, d):
        rows = slice(t * 128, (t + 1) * 128)
        b = d
        dp = d * 2 + (t % 2)
        IFACE = ("Ah", "Bh", "Kh", "Rh", "Vb", "XTa", "XTb", "XTk", "XTr", "gamT")
        K = lambda nm: (nm, dp) if nm in IFACE else (nm, d)
        th, zt, ld, iclr, kkr, kk, t1, kdt = th_[d], zt_[d], ld_[d], iclr_[d], kkr_[d], kk_[d], t1_[d], kdt_[d]
        bt = kkr
        sq = zt
        ein, eneg, st8 = ein_[d], eneg_[d], st8_[d]
        eex = zt
        Ah, Bh, Kh, Rh, Vb, XT, gamT = Ah_[dp], Bh_[dp], Kh_[dp], Rh_[dp], Vb_[dp], XT_[dp], gamT_[dp]
        ArbT, ArkT, TAb, TVb, MTb, RQTb = ArbT_[d], ArkT_[d], TAb_[d], TVb_[d], MTb_[d], RQTb_[d]
        main = []
        P._defer = main

        def sigmoid_tail(dst, key):
            P.op("act", lambda e: e.activation(out=zt[:], in_=zt[:], func=AF.Exp, scale=-1.0), reads=[K("zt")], writes=[K("zt")])
            P.op("dve", lambda e: e.tensor_scalar(out=zt[:], in0=zt[:], scalar1=1.0, scalar2=None, op0=ALU.add), reads=[K("zt")], writes=[K("zt")])
            P.op("dve", lambda e: e.reciprocal(out=dst, in_=zt[:]), reads=[K("zt")], writes=[key])

        P.dma("sp", rt[b][:], rd[d, rows, :], writes=[K("rt")])
        P.dma("sp", kt[b][:], kd[d, rows, :], writes=[K("kt")])
        P.dma("sp", vt[b][:], vd[d, rows, :], writes=[K("vt")])
        P.dma("sp", lw[b][:], lwd[d, :, rows], writes=[K("lw")])
        P.dma("sp", la[b][:], lad[d, :, rows], writes=[K("la")])
        P.op("act", lambda e: e.activation(out=th[:], in_=lw[b][:], func=AF.Tanh), reads=[K("lw")], writes=[K("th")])
        P.atomic_begin()
        P.op("pe", lambda e: e.matmul(pz[:], lhsT=th[:], rhs=w2[d][:], start=True, stop=True),
             reads=[K("th"), ("w2", d)], writes=["pz"])
        P.op("dve", lambda e: e.tensor_tensor(out=zt[:], in0=pz[:], in1=w0b[d][:], op=ALU.add), reads=["pz", ("w0b", d)], writes=[K("zt")])
        P.atomic_end()
        sigmoid_tail(ld[:], K("ld"))
        P.op("pool", lambda e: e.tensor_scalar(out=ld[:], in0=ld[:], scalar1=-float(np.exp(-0.5)), scalar2=None, op0=ALU.mult),
             reads=[K("ld")], writes=[K("ld")])
        P.atomic_begin()
        P.op("pe", lambda e: e.matmul(pz[:], lhsT=la[b][:], rhs=a2[d][:], start=True, stop=True),
             reads=[K("la"), ("a2", d)], writes=["pz"])
        P.op("dve", lambda e: e.tensor_tensor(out=zt[:], in0=pz[:], in1=a0b[d][:], op=ALU.add), reads=["pz", ("a0b", d)], writes=[K("zt")])
        P.atomic_end()
        sigmoid_tail(iclr[:], K("iclr"))
        h3 = lambda ap: ap.rearrange("p (h c) -> p h c", c=HC)
        P.op("pool", lambda e: e.tensor_tensor(out=kkr[:], in0=kt[b][:], in1=kkb[:], op=ALU.mult), reads=[K("kt"), "kkb"], writes=[K("kkr")])
        P.op("pool", lambda e: e.tensor_tensor(out=sq[:], in0=kkr[:], in1=kkr[:], op=ALU.mult), reads=[K("kkr")], writes=[K("zt")])
        P.op("dve", lambda e: e.tensor_reduce(out=st8[0][:], in_=h3(sq[:]), axis=AX.X, op=ALU.add), reads=[K("zt")], writes=[K("st0")])
        P.op("act", lambda e: e.activation(out=st8[0][:], in_=st8[0][:], func=AF.Sqrt), reads=[K("st0")], writes=[K("st0")])
        P.op("dve", lambda e: e.tensor_scalar(out=st8[0][:], in0=st8[0][:], scalar1=1e-12, scalar2=None, op0=ALU.max), reads=[K("st0")], writes=[K("st0")])
        P.op("dve", lambda e: e.reciprocal(out=st8[1][:], in_=st8[0][:]), reads=[K("st0")], writes=[K("st1")])
        P.op("dve", lambda e: e.tensor_tensor(out=h3(kk[:]), in0=h3(kkr[:]), in1=st8[1][:].unsqueeze(2).to_broadcast([128, NH_C, HC]), op=ALU.mult),
             reads=[K("kkr"), K("st1")], writes=[K("kk")])
        P.op("dve", lambda e: e.tensor_tensor(out=t1[:], in0=iclr[:], in1=kab[:], op=ALU.mult), reads=[K("iclr"), "kab"], writes=[K("t1")])
        P.op("pool", lambda e: e.tensor_tensor(out=t1[:], in0=t1[:], in1=omka[:], op=ALU.add), reads=[K("t1"), "omka"], writes=[K("t1")])
        P.op("dve", lambda e: e.tensor_tensor(out=kdt[:], in0=kt[b][:], in1=t1[:], op=ALU.mult), reads=[K("kt"), K("t1")], writes=[K("kdt")])
        P.op("pool", lambda e: e.tensor_tensor(out=bt[:], in0=kk[:], in1=iclr[:], op=ALU.mult), reads=[K("kk"), K("iclr")], writes=[K("kkr")])
        P.op("dve", lambda e: e.tensor_tensor(out=t1[:], in0=rt[b][:], in1=kdt[:], op=ALU.mult), reads=[K("rt"), K("kdt"), K("t1")], writes=[K("t1")])
        P.op("pool", lambda e: e.tensor_tensor(out=t1[:], in0=t1[:], in1=rkb[:], op=ALU.mult), reads=[K("t1"), "rkb"], writes=[K("t1")])
        P.op("dve", lambda e: e.tensor_reduce(out=st8[2][:], in_=h3(t1[:]), axis=AX.X, op=ALU.add), reads=[K("t1")], writes=[K("st2")])
        P.op("dve", lambda e: e.tensor_tensor(out=h3(t1[:]), in0=h3(vt[b][:]), in1=st8[2][:].unsqueeze(2).to_broadcast([128, NH_C, HC]), op=ALU.mult),
             reads=[K("vt"), K("st2"), K("t1")], writes=[K("t1")])
        P.dma("pool", bout[d, rows, :], t1[:], reads=[K("t1")], writes=["bout"])
        P.atomic_begin()
        P.op("pe", lambda e: e.matmul(pz[:], lhsT=tri[:], rhs=ld[:], start=True, stop=True), reads=["tri", K("ld")], writes=["pz"])
        P.op("act", lambda e: e.activation(out=ein[:], in_=pz[:], func=AF.Exp), reads=["pz"], writes=[K("ein")])
        P.op("act", lambda e: e.activation(out=eneg[:], in_=pz[:], func=AF.Exp, scale=-1.0), reads=["pz"], writes=[K("eneg")])
        P.op("dve", lambda e: e.tensor_tensor(out=eex[:], in0=pz[:], in1=ld[:], op=ALU.subtract), reads=["pz", K("ld")], writes=[K("zt")])
        P.atomic_end()
        P.op("act", lambda e: e.activation(out=eex[:], in_=eex[:], func=AF.Exp), reads=[K("zt")], writes=[K("zt")])
        P.op("dve", lambda e: e.scalar_tensor_tensor(out=Ah[:], in0=kk[:], scalar=-1.0, in1=eex[:], op0=ALU.mult, op1=ALU.mult),
             reads=[K("kk"), K("zt")], writes=[K("Ah")])
        P.op("pool", lambda e: e.tensor_tensor(out=Bh[:], in0=bt[:], in1=eneg[:], op=ALU.mult), reads=[K("kkr"), K("eneg")], writes=[K("Bh")])
        P.op("dve", lambda e: e.tensor_tensor(out=Kh[:], in0=kdt[:], in1=eneg[:], op=ALU.mult), reads=[K("kdt"), K("eneg")], writes=[K("Kh")])
        P.op("pool", lambda e: e.tensor_tensor(out=Rh[:], in0=rt[b][:], in1=ein[:], op=ALU.mult), reads=[K("rt"), K("ein")], writes=[K("Rh")])
        P.op("act", lambda e: e.copy(out=Vb[:], in_=vt[b][:]), reads=[K("vt")], writes=[K("Vb")])
        P.atomic_begin()
        for h in range(NH_C):
            P.op("pe", lambda e, h=h: e.matmul(pg[:HC, h:h + 1], lhsT=ein[:, h * HC:(h + 1) * HC], rhs=sel[:], start=True, stop=True),
                 reads=[K("ein"), "sel"], writes=["pg"])
        P.op("act", lambda e: e.copy(out=gamT[:], in_=pg[:HC, :NH_C]), reads=["pg"], writes=[K("gamT")])
        P.atomic_end()
        for xi, (nm, src_t, skey) in enumerate([("a", Ah, "Ah"), ("b", Bh, "Bh"), ("k", Kh, "Kh"), ("r", Rh, "Rh")]):
            pt = ptr[xi % 2]
            P.atomic_begin()
            for h in range(NH_C):
                P.op("pe", lambda e, h=h, pt=pt, src_t=src_t: e.transpose(out=pt[:HC, h * 128:(h + 1) * 128], in_=src_t[:, h * HC:(h + 1) * HC],
                                                                          identity=identb[:]),
                     reads=[K(skey), "identb"], writes=[("ptr", xi % 2)])
            if xi % 2 == 0:
                P.op("act", lambda e, nm=nm, pt=pt: e.copy(out=XT[nm][:].rearrange("p h t -> p (h t)"), in_=pt[:HC, :]),
                     reads=[("ptr", xi % 2)], writes=[K("XT" + nm)])
                P.atomic_end()
            else:
                P.op("dve", lambda e, nm=nm, pt=pt: e.tensor_copy(out=XT[nm][:].rearrange("p h t -> p (h t)"), in_=pt[:HC, :]),
                     reads=[("ptr", xi % 2)], writes=[K("XT" + nm)])
                P.atomic_end()
        qlists = []

        def emit_quad(qd_):
            ql = []
            P._defer = ql
            qlists.append(ql)
            Q = lambda nm, qd_=qd_: (nm, d, qd_)
            hs = [qd_ * 4 + i for i in range(4)]
            Nf, NTf, TT, TTb, AakT, AVb = Nf_[d][qd_], NTf_[d][qd_], TT_[d][qd_], TTb_[d][qd_], AakT_[d][qd_], AVb_[d][qd_]

            def mm4(lhs_fn, rhs_fn, rows_, cols_, rkeys, hs=hs):
                p = bi[0] % 2
                bi[0] += 1
                P.atomic_begin()
                for i, h in enumerate(hs):
                    P.op("pe", lambda e, i=i, h=h, p=p: e.matmul(big[p][:rows_, i, :cols_], lhsT=lhs_fn(i, h), rhs=rhs_fn(i, h),
                                                               start=True, stop=True), reads=rkeys, writes=[("big", p)])
                return p

            def EV(*a_, **k_):
                P.op(*a_, **k_)
                P.atomic_end()

            msk = lambda m_: m_[:, None, :].to_broadcast([128, 4, 128])
            p = mm4(lambda i, h: XT["a"][:, h, :], lambda i, h: XT["b"][:, h, :], 128, 128, [K("XTa"), K("XTb")])
            EV("dve", lambda e, p=p: e.tensor_tensor(out=R32(Nf[0][:]), in0=big[p][:], in1=msk(mlo), op=ALU.mult),
                 reads=[("big", p), "mlo"], writes=[Q("Nf0")])
            p = mm4(lambda i, h: XT["b"][:, h, :], lambda i, h: XT["a"][:, h, :], 128, 128, [K("XTa"), K("XTb")])
            EV("dve", lambda e, p=p: e.tensor_tensor(out=R32(NTf[0][:]), in0=big[p][:], in1=msk(mup), op=ALU.mult),
                 reads=[("big", p), "mup"], writes=[Q("NTf0")])
            P.op("dve", lambda e: e.tensor_tensor(out=R32(TT[:]), in0=NTf[0][:], in1=msk(ident), op=ALU.add),
                 reads=[Q("NTf0"), "ident"], writes=[Q("TT")])
            p = mm4(lambda i, h: XT["k"][:, h, :], lambda i, h: XT["a"][:, h, :], 128, 128, [K("XTa"), K("XTk")])
            EV("dve", lambda e, p=p: e.tensor_tensor(out=AakT[:], in0=big[p][:], in1=msk(mup), op=ALU.mult),
                 reads=[("big", p), "mup"], writes=[Q("AakT")])
            p = mm4(lambda i, h: XT["b"][:, h, :], lambda i, h: XT["r"][:, h, :], 128, 128, [K("XTr"), K("XTb")])
            EV("dve", lambda e, p=p, qd_=qd_: e.tensor_tensor(out=ArbT[:, qd_ * 4:(qd_ + 1) * 4, :], in0=big[p][:], in1=msk(mupi), op=ALU.mult),
                 reads=[("big", p), "mupi"], writes=[Q("ArbT")])
            p = mm4(lambda i, h: XT["k"][:, h, :], lambda i, h: XT["r"][:, h, :], 128, 128, [K("XTr"), K("XTk")])
            EV("dve", lambda e, p=p, qd_=qd_: e.tensor_tensor(out=ArkT[:, qd_ * 4:(qd_ + 1) * 4, :], in0=big[p][:], in1=msk(mupi), op=ALU.mult),
                 reads=[("big", p), "mupi"], writes=[Q("ArkT")])
            cur = 0
            for lvl in range(1, 7):
                nxt = 1 - cur
                p = mm4(lambda i, h, cur=cur: R32(NTf[cur][:, i, :]), lambda i, h, cur=cur: R32(Nf[cur][:, i, :]), 128, 128, [Q("Nf%d" % cur), Q("NTf%d" % cur)])
                EV("act", lambda e, p=p, nxt=nxt: e.copy(out=R32(Nf[nxt][:]), in_=big[p][:]), reads=[("big", p)], writes=[Q("Nf%d" % nxt)])
                if lvl < 6:
                    p = mm4(lambda i, h, cur=cur: R32(Nf[cur][:, i, :]), lambda i, h, cur=cur: R32(NTf[cur][:, i, :]), 128, 128, [Q("Nf%d" % cur), Q("NTf%d" % cur)])
                    EV("act", lambda e, p=p, nxt=nxt: e.copy(out=R32(NTf[nxt][:]), in_=big[p][:]), reads=[("big", p)], writes=[Q("NTf%d" % nxt)])
                p = mm4(lambda i, h, nxt=nxt: R32(Nf[nxt][:, i, :]), lambda i, h: R32(TT[:, i, :]), 128, 128, [Q("Nf%d" % nxt), Q("TT")])
                EV("dve", lambda e, p=p: e.tensor_tensor(out=R32(TT[:]), in0=big[p][:], in1=TT[:], op=ALU.add), reads=[("big", p), Q("TT")], writes=[Q("TT")])
                cur = nxt
            P.op("act", lambda e: e.copy(out=TTb[:], in_=TT[:]), reads=[Q("TT")], writes=[Q("TTb")])
            qs = slice(qd_ * 4, (qd_ + 1) * 4)
            p = mm4(lambda i, h: TTb[:, i, :], lambda i, h: Ah[:, h * HC:(h + 1) * HC], 128, HC, [Q("TTb"), K("Ah")])
            EV("act", lambda e, p=p, qs=qs: e.copy(out=TAb[:, qs, :], in_=big[p][:, :, :HC]), reads=[("big", p)], writes=[Q("TAb")])
            p = mm4(lambda i, h: AakT[:, i, :], lambda i, h: Vb[:, h * HC:(h + 1) * HC], 128, HC, [Q("AakT"), K("Vb")])
            EV("dve", lambda e, p=p: e.tensor_copy(out=AVb[:], in_=big[p][:, :, :HC]), reads=[("big", p)], writes=[Q("AVb")])
            p = mm4(lambda i, h: TTb[:, i, :], lambda i, h: AVb[:, i, :], 128, HC, [Q("TTb"), Q("AVb")])
            EV("act", lambda e, p=p, qs=qs: e.copy(out=TVb[:, qs, :], in_=big[p][:, :, :HC]), reads=[("big", p)], writes=[Q("TVb")])
            p = mm4(lambda i, h: TAb[:, h, :], lambda i, h: Bh[:, h * HC:(h + 1) * HC], HC, HC, [Q("TAb"), K("Bh")])
            EV("dve", lambda e, p=p, qs=qs: e.tensor_tensor(out=MTb[:, qs, :], in0=big[p][:HC, :, :HC],
                                                            in1=ident[:HC, None, :HC].to_broadcast([HC, 4, HC]), op=ALU.add),
                 reads=[("big", p), "ident"], writes=[Q("MTb")])
            p = mm4(lambda i, h: TAb[:, h, :], lambda i, h: ArbT[:, h, :], HC, 128, [Q("TAb"), Q("ArbT")])
            EV("dve", lambda e, p=p, qs=qs: e.tensor_tensor(out=RQTb[:, qs, :], in0=big[p][:HC, :, :], in1=XT["r"][:, qs, :], op=ALU.add),
                 reads=[("big", p), K("XTr")], writes=[Q("RQTb")])
        for qd_ in range(2):
            emit_quad(qd_)
        tail = []
        P._defer = tail
        allq = lambda nm: [(nm, d, 0), (nm, d, 1)]
        P.atomic_begin()
        for h in range(NH_C):
            hc = slice(h * HC, (h + 1) * HC)
            P.op("pe", lambda e, h=h: e.matmul(py[:, h, :], lhsT=ArbT[:, h, :], rhs=TVb[:, h, :], start=True, stop=False),
                 reads=allq("ArbT") + allq("TVb") + allq("ArkT") + allq("RQTb") + [K("Vb"), ("Pb", d)], writes=["py"])
            P.op("pe", lambda e, h=h, hc=hc: e.matmul(py[:, h, :], lhsT=ArkT[:, h, :], rhs=Vb[:, hc], start=False, stop=False),
                 reads=[], writes=["py"])
            P.op("pe", lambda e, h=h: e.matmul(py[:, h, :], lhsT=RQTb[:, h, :], rhs=Pb[d][:, h, :], start=False, stop=True),
                 reads=[("Pb", d)], writes=["py"])
        P.op("act", lambda e: e.copy(out=ysb[b][:], in_=py[:]), reads=["py"], writes=[K("ysb")])
        P.atomic_end()
        P.dma("pool", yout[d, rows, :], ysb[b][:].rearrange("p h c -> p (h c)"), reads=[K("ysb")], writes=["yout"])
        P.atomic_begin()
        for h in range(NH_C):
            hc = slice(h * HC, (h + 1) * HC)
            P.op("pe", lambda e, h=h, hc=hc: e.matmul(pp[:HC, h, :], lhsT=Bh[:, hc], rhs=TVb[:, h, :], start=True, stop=False),
                 reads=allq("TVb") + allq("MTb") + [K("Bh"), K("Kh"), K("Vb"), ("Pb", d)], writes=["pp"])
            P.op("pe", lambda e, h=h, hc=hc: e.matmul(pp[:HC, h, :], lhsT=Kh[:, hc], rhs=Vb[:, hc], start=False, stop=False),
                 reads=[], writes=["pp"])
            P.op("pe", lambda e, h=h: e.matmul(pp[:HC, h, :], lhsT=MTb[:, h, :], rhs=Pb[d][:, h, :], start=False, stop=True),
                 reads=[("Pb", d)], writes=["pp"])
        P.op("dve", lambda e: e.tensor_tensor(out=Pb[d][:], in0=pp[:HC, :, :], in1=gamT[:].unsqueeze(2).to_broadcast([HC, NH_C, HC]),
                                              op=ALU.mult), reads=["pp", K("gamT")], writes=[("Pb", d)])
        P.atomic_end()
        P._defer = None
        return main, qlists, tail

    for t in range(ntile):
        if t == 0:
            nxt_parts = [emit_dir(0, d) for d in range(2)]
            for d in range(2):
                P.interleave([nxt_parts[d][0]])
        parts = nxt_parts
        streams = parts[0][1] + parts[1][1]
        if t + 1 < ntile:
            nxt_parts = [emit_dir(t + 1, d) for d in range(2)]
            if MC_PIPE:
                streams = streams + [nxt_parts[0][0] + nxt_parts[1][0]]
        P.interleave(streams)
        if t + 1 < ntile and not MC_PIPE:
            for d in range(2):
                P.interleave([nxt_parts[d][0]])
        for d in range(2):
            P.interleave([parts[d][2]])
    P.finish(["yout", "bout"])
    return nc


def run_Mc(pl, pc, prm):
    W = NH_C * HC
    nc = build_Mc(L_SEQ)
    tt, ss = np.meshgrid(np.arange(128), np.arange(128), indexing="xy")
    consts = {"tri": (ss <= tt).astype(np.float32), "mup": (ss < tt).astype(np.float32), "mlo": (ss > tt).astype(np.float32),
              "ident": np.eye(128, dtype=np.float32), "sel": (np.arange(128) == 127).astype(np.float32)[:, None]}
    rk_flat = prm["r_k"].reshape(-1)
    in_maps = []
    for cid in range(NCORES):
        b, hg = cid // 4, cid % 4
        cols = slice(hg * W, (hg + 1) * W)
        m = dict(consts)
        for nm, off in (("r", 0), ("k", D), ("v", 2 * D)):
            m[nm] = np.stack([_seq(pl[b][:, off + hg * W:off + (hg + 1) * W], pc[b][:, off + hg * W:off + (hg + 1) * W], dr == 1)
                              for dr in range(2)])
        m["lwT"] = np.stack([np.ascontiguousarray(_seq(pl[b][:, 4 * D + dr * 128:4 * D + dr * 128 + LORA],
                                                       pc[b][:, 4 * D + dr * 128:4 * D + dr * 128 + LORA], dr == 1).T) for dr in range(2)])
        m["laT"] = np.stack([np.ascontiguousarray(_seq(pl[b][:, 4 * D + 256 + dr * 128:4 * D + 256 + dr * 128 + LORA],
                                                       pc[b][:, 4 * D + 256 + dr * 128:4 * D + 256 + dr * 128 + LORA], dr == 1).T) for dr in range(2)])
        m["w2"] = np.ascontiguousarray(prm["w2"][:, :, cols])
        m["a2"] = np.ascontiguousarray(prm["a2"][:, :, cols])
        m["w0b"] = np.stack([_bc(prm["w0"][dr, cols]) for dr in range(2)])
        m["a0b"] = np.stack([_bc(prm["a0"][dr, cols]) for dr in range(2)])
        m["kkb"] = _bc(prm["k_k"][cols])
        m["kab"] = _bc(prm["k_a"][cols])
        m["rkb"] = _bc(rk_flat[cols])
        in_maps.append(m)
    res = _run(nc, in_maps)
    names = ["m0", "m1", "m2", "m3"]
    lat = {nm: np.empty((2, T_LAT, D), np.float32) for nm in names}
    cx = {nm: np.empty((2, T_CTX, D), np.float32) for nm in names}
    for cid in range(NCORES):
        b, hg = cid // 4, cid % 4
        cols = slice(hg * W, (hg + 1) * W)
        for dr in range(2):
            for key, nm in (("y", "m%d" % dr), ("bon", "m%d" % (2 + dr))):
                l, c = _unseq(res[cid][key][dr], dr == 1)
                lat[nm][b][:, cols] = l
                cx[nm][b][:, cols] = c
    return lat, cx


def layer_c(x, ctx, mod_i, norm_w_i, prm, need_ctx, final_w=None):
    pad = lambda w: np.concatenate([w, np.zeros((D, 128 - w.shape[1]), np.float32)], axis=1)
    wcat = np.concatenate([prm["w_in"][0], prm["w_in"][1], prm["w_in"][2], prm["w_in"][3],
                           pad(prm["w1"][0]), pad(prm["w1"][1]), pad(prm["a1"][0]), pad(prm["a1"][1])], axis=1)
    lerp = [0] * 16 + [1] * 16 + [2] * 16 + [3] * 16 + [4, 4, 5, 5]
    pl, pc = run_P(x, ctx, mod_i, norm_w_i, wcat, lerp=lerp, mu=prm["mu"])
    lat, cx = run_Mc(pl, pc, prm)
    lat["gate"] = pl[:, :, 3 * D:4 * D]
    cx["gate"] = pc[:, :, 3 * D:4 * D]
    bcs = {"lnw": prm["ln_w"], "lnb": prm["ln_b"]}
    if final_w is not None:
        bcs["fnw"] = final_w
    return run_O(x, ctx, "c", lat, cx, prm["w_out"], mod_i[0:2, 2 * D:3 * D], mod_i[2, 2 * D:3 * D], bcs,
                 final=final_w is not None, need_ctx=need_ctx)


def kernel(x, c, ctx, c_ctx, norm_w, mod_w, mod_b, a_w_in, a_lb_raw, a_onorm_w, a_w_out,
           b_w_in, b_sink, b_w_out, c_mu, c_w_in, c_w0, c_w1, c_w2, c_a0, c_a1, c_a2,
           c_k_k, c_k_a, c_r_k, c_ln_w, c_ln_b, c_w_out, final_norm_w):
    f = lambda a: np.asarray(a, dtype=np.float32)
    x, c, ctx, c_ctx = f(x), f(c), f(ctx), f(c_ctx)
    mod = run_mod(c, f(c_ctx), f(mod_w), f(mod_b))
    depth = 4
    for i in range(depth):
        j, kind = i // 3, i % 3
        need_ctx = i < depth - 1
        fw = f(final_norm_w) if i == depth - 1 else None
        mod_i = np.ascontiguousarray(mod[:, i])
        if kind == 0:
            x, ctx = layer_a(x, ctx, mod_i, f(norm_w[i]), f(a_w_in[j]), f(a_lb_raw), j, f(a_onorm_w[j]), f(a_w_out[j]), need_ctx, fw)
        elif kind == 1:
            x, ctx_n = layer_b(x, ctx, mod_i, f(norm_w[i]), f(b_w_in[j]), f(b_sink[j]), f(b_w_out[j]), need_ctx, fw)
            ctx = ctx_n if need_ctx else ctx
        else:
            prm = {"mu": f(c_mu[j]), "w_in": f(c_w_in[j]), "w0": f(c_w0[j]), "w1": f(c_w1[j]), "w2": f(c_w2[j]),
                   "a0": f(c_a0[j]), "a1": f(c_a1[j]), "a2": f(c_a2[j]), "k_k": f(c_k_k[j]), "k_a": f(c_k_a[j]),
                   "r_k": f(c_r_k[j]), "ln_w": f(c_ln_w[j]), "ln_b": f(c_ln_b[j]), "w_out": f(c_w_out[j])}
            x, ctx_n = layer_c(x, ctx, mod_i, f(norm_w[i]), prm, need_ctx, fw)
            ctx = ctx_n if need_ctx else ctx
    return x.astype(np.float32)
```

```python
import numpy as np
from contextlib import ExitStack
import concourse.bass as bass
import concourse.mybir as mybir
from concourse.bass_utils import run_bass_kernel_spmd

F32 = mybir.dt.float32
BF16 = mybir.dt.bfloat16
AF = mybir.ActivationFunctionType
ALU = mybir.AluOpType
AX = mybir.AxisListType

NCORES = 8
D = 2048
KC = D // 128


class Prog:
    CE = ("pe", "act", "dve", "pool")

    def __init__(self, nc, ndma=8):
        self.nc = nc
        self.es = ExitStack()
        self.eng = {"pe": nc.tensor, "act": nc.scalar, "dve": nc.vector, "pool": nc.gpsimd, "sp": nc.sync}
        self.sem = {}
        self.cnt = {}
        for e in self.CE:
            self.sem[("e", e)] = self.es.enter_context(nc.semaphore("s_" + e))
            self.cnt[("e", e)] = 0
        self.ndma = ndma
        self.dma_rr = {}
        for q in ("sp", "pool", "act"):
            self.dma_rr[q] = 0
            for i in range(ndma):
                k = ("d", q, i)
                self.sem[k] = self.es.enter_context(nc.semaphore("d_%s%d" % (q, i)))
                self.cnt[k] = 0
        self.known = {e: {} for e in self.eng}
        self.last_w = {}
        self.readers = {}
        self.ninstr = 0
        self.uid = 0

    def sb(self, shape, dtype=F32, name=None, stack=None):
        self.uid += 1
        return (stack or self.es).enter_context(self.nc.sbuf_tensor(name or ("sb%d" % self.uid), list(shape), dtype))

    def barrier(self):
        for e in ("pe", "act", "dve", "pool", "sp"):
            kn = self.known[e]
            for k, v in self.cnt.items():
                if v > 0 and kn.get(k, 0) < v:
                    self.eng[e].wait_ge(self.sem[k], v)
                    kn[k] = v
                    self.ninstr += 1

    def ps(self, shape, dtype=F32, name=None):
        self.uid += 1
        return self.es.enter_context(self.nc.psum_tensor(name or ("ps%d" % self.uid), list(shape), dtype))

    def dram(self, name, shape, dtype=F32, kind="ExternalInput"):
        return self.nc.dram_tensor(name, list(shape), dtype, kind=kind).ap()

    def _deps(self, e, reads, writes):
        deps = {}

        def add(kv):
            k, v = kv
            if e == "pe" and k == ("e", "pe"):
                return
            if deps.get(k, 0) < v:
                deps[k] = v

        for b in reads:
            if b in self.last_w:
                add(self.last_w[b])
        for b in writes:
            if b in self.last_w:
                add(self.last_w[b])
            for r in self.readers.get(b, ()):
                add(r)
        kn = self.known[e]
        for k, v in deps.items():
            if kn.get(k, 0) < v:
                self.eng[e].wait_ge(self.sem[k], v)
                self.ninstr += 1
                kn[k] = v

    def _record(self, key, val, reads, writes):
        for b in writes:
            self.last_w[b] = (key, val)
            self.readers[b] = []
        for b in reads:
            self.readers.setdefault(b, []).append((key, val))
            if len(self.readers[b]) > 64:
                mx = {}
                for k, v in self.readers[b]:
                    if mx.get(k, 0) < v:
                        mx[k] = v
                self.readers[b] = list(mx.items())

    _defer = None
    _atomic = None

    def begin(self):
        self._defer = []

    def end(self):
        l, self._defer = self._defer, None
        return l

    def atomic_begin(self):
        if self._defer is not None:
            self._atomic = []

    def atomic_end(self):
        if self._defer is not None:
            self._defer.append(self._atomic)
            self._atomic = None

    def interleave(self, lists):
        n = max(len(l) for l in lists)
        for i in range(n):
            for l in lists:
                if i < len(l):
                    for (kind, args, kw) in l[i]:
                        if kind == "op":
                            self.op(*args, **kw)
                        else:
                            self.dma(*args, **kw)

    def _rec(self, kind, args, kw):
        item = (kind, args, kw)
        if self._atomic is not None:
            self._atomic.append(item)
        else:
            self._defer.append([item])

    def op(self, e, fn, reads=(), writes=()):
        if self._defer is not None:
            return self._rec("op", (e, fn, list(reads), list(writes)), {})
        self._deps(e, reads, writes)
        key = ("e", e)
        self.cnt[key] += 1
        ins = fn(self.eng[e])
        ins.then_inc(self.sem[key], 1)
        self.ninstr += 1
        self._record(key, self.cnt[key], reads, writes)
        return ins

    def dma(self, q, out, in_, reads=(), writes=(), **kw):
        if self._defer is not None:
            return self._rec("dma", (q, out, in_, list(reads), list(writes)), kw)
        i = self.dma_rr[q]
        self.dma_rr[q] = (i + 1) % self.ndma
        key = ("d", q, i)
        kn = self.known[q]
        if kn.get(key, 0) < self.cnt[key]:
            self.eng[q].wait_ge(self.sem[key], self.cnt[key])
            kn[key] = self.cnt[key]
            self.ninstr += 1
        self._deps(q, reads, writes)
        self.cnt[key] += 16
        self.eng[q].dma_start(out=out, in_=in_, **kw).then_inc(self.sem[key], 16)
        self.ninstr += 1
        self._record(key, self.cnt[key], reads, writes)

    def finish(self, out_keys):
        self._deps("pool", out_keys, ())
        for k, v in self.cnt.items():
            if k[0] == "d" and v > 0 and self.known["pool"].get(k, 0) < v:
                self.eng["pool"].wait_ge(self.sem[k], v)
                self.known["pool"][k] = v


def _run(nc, in_maps):
    res = run_bass_kernel_spmd(nc, in_maps, core_ids=list(range(NCORES)))
    return res.results


MOD_NCOL = 4 * 3 * D // NCORES


def build_mod():
    nc = bass.Bass("TRN2", target_bir_lowering=False)
    P = Prog(nc)
    ccT = P.dram("ccT", [128, KC, 3])
    w = P.dram("w", [128, KC, MOD_NCOL])
    b3 = P.dram("b3", [3, MOD_NCOL])
    out = P.dram("out", [3, MOD_NCOL], kind="ExternalOutput")
    s_in = P.sb([128, KC, 3])
    s_act = P.sb([128, KC, 3])
    bias = P.sb([3, MOD_NCOL])
    res = P.sb([3, MOD_NCOL])
    wt = [P.sb([128, KC, 512]) for _ in range(2)]
    pt = [P.ps([3, 512]) for _ in range(2)]
    P.dma("sp", s_in[:], ccT[:], writes=["s_in"])
    P.dma("sp", bias[:], b3[:], writes=["bias"])
    P.op("act", lambda e: e.activation(out=s_act[:], in_=s_in[:], func=AF.Silu), reads=["s_in"], writes=["s_act"])
    nb = MOD_NCOL // 512
    for j in range(nb):
        wb = wt[j % 2]
        pb = pt[j % 2]
        P.dma("sp", wb[:], w[:, :, j * 512:(j + 1) * 512], writes=[("w", j % 2)])
        for k in range(KC):
            P.op("pe", lambda e, k=k, wb=wb, pb=pb: e.matmul(pb[:], lhsT=s_act[:, k, :], rhs=wb[:, k, :],
                                                        start=(k == 0), stop=(k == KC - 1)),
                 reads=["s_act", ("w", j % 2)], writes=[("p", j % 2)])
        P.op("dve", lambda e, j=j, pb=pb: e.tensor_tensor(out=res[:, j * 512:(j + 1) * 512], in0=pb[:],
                                                       in1=bias[:, j * 512:(j + 1) * 512], op=ALU.add),
             reads=[("p", j % 2), "bias"], writes=["res"])
    P.dma("pool", out[:], res[:], reads=["res"], writes=["out"])
    P.finish(["out"])
    return nc


def run_mod(c, c_ctx, mod_w, mod_b):
    cc = np.concatenate([c, c_ctx[None, :]], axis=0).astype(np.float32)
    ccT = np.ascontiguousarray(cc.T.reshape(KC, 128, 3).transpose(1, 0, 2))
    wall = np.concatenate([mod_w[l] for l in range(4)], axis=1)
    ball = np.concatenate([mod_b[l] for l in range(4)], axis=0)
    in_maps = []
    for cid in range(NCORES):
        cols = slice(cid * MOD_NCOL, (cid + 1) * MOD_NCOL)
        wc = np.ascontiguousarray(wall[:, cols].reshape(KC, 128, MOD_NCOL).transpose(1, 0, 2))
        bc = np.ascontiguousarray(np.broadcast_to(ball[cols][None, :], (3, MOD_NCOL)))
        in_maps.append({"ccT": ccT, "w": wc, "b3": bc})
    nc = build_mod()
    res = _run(nc, in_maps)
    mod = np.concatenate([r["out"] for r in res], axis=1)
    return mod.reshape(3, 4, 3 * D)


def _segments(r0, n):
    segs = []
    o = 0
    while o < n:
        m = min(128, n - o)
        segs.append((r0 + o, m))
        o += m
    return segs


def build_P(n_lat, n_ctx, nblk, lerp=None):
    halo = 1 if lerp is not None else 0
    r_lat = n_lat + 2 * halo
    r_ctx = n_ctx + 2 * halo
    R = r_lat + r_ctx
    n_int = n_lat + n_ctx
    nc = bass.Bass("TRN2", target_bir_lowering=False)
    P = Prog(nc)
    xin = P.dram("xin", [R, D])
    nw = P.dram("nw", [128, D])
    scb = P.dram("scb", [2, 128, D])
    shb = P.dram("shb", [2, 128, D])
    wd = P.dram("w", [nblk, 128, KC, 128])
    identd = P.dram("ident", [128, 128])
    if lerp is not None:
        mud = P.dram("mu", [128, KC, 6])
        bmd = P.dram("bmask", [128, 4])
    out = P.dram("projT", [nblk * 128, n_int], kind="ExternalOutput")

    hT = P.sb([128, KC, R], BF16)
    tmp = P.sb([128, D])
    if lerp is not None:
        xxT = P.sb([128, KC, R], BF16)
        mu = P.sb([128, KC, 6])
        bm = P.sb([128, 4])
    tps = [P.ps([128, 4, 128], BF16) for _ in range(2)]
    mps = [P.ps([128, 512]) for _ in range(4)]
    es1 = ExitStack()
    ident_f = P.sb([128, 128], stack=es1)
    ident = P.sb([128, 128], BF16, stack=es1)
    S = [P.sb([128, D], stack=es1) for _ in range(2)]
    SH = [P.sb([128, D], stack=es1) for _ in range(2)]
    nwt = tmp
    xt = [P.sb([128, D], stack=es1) for _ in range(2)]
    sq = P.sb([128, D], BF16, stack=es1)
    hb = [P.sb([128, D], BF16, stack=es1) for _ in range(2)]
    ss = [P.sb([128, 1], stack=es1) for _ in range(2)]
    rstd = [P.sb([128, 1], stack=es1) for _ in range(2)]

    P.dma("sp", ident_f[:], identd[:], writes=["identf"])
    P.op("dve", lambda e: e.tensor_copy(out=ident[:], in_=ident_f[:]), reads=["identf"], writes=["ident"])
    P.dma("sp", nwt[:], nw[:], writes=["tmp"])
    for i in range(2):
        P.dma("sp", S[i][:], scb[i], writes=[("S", i)])
        P.dma("sp", SH[i][:], shb[i], writes=[("SH", i)])
        P.op("dve", lambda e, i=i: e.scalar_tensor_tensor(out=S[i][:], in0=S[i][:], scalar=1.0, in1=nwt[:],
                                                          op0=ALU.add, op1=ALU.mult),
             reads=[("S", i), "tmp"], writes=[("S", i)])
    if lerp is not None:
        P.dma("sp", mu[:], mud[:], writes=["mu"])
        P.dma("sp", bm[:], bmd[:], writes=["bm"])

    segs = [(r, m, 0) for (r, m) in _segments(0, r_lat)] + [(r, m, 1) for (r, m) in _segments(r_lat, r_ctx)]
    for si, (r0, m, mi) in enumerate(segs):
        b = si % 2
        P.dma("sp", xt[b][:m, :], xin[r0:r0 + m, :], writes=[("xt", b)])
        P.op("act", lambda e, b=b, m=m: e.activation(out=sq[:m, :], in_=xt[b][:m, :], func=AF.Square,
                                                      accum_out=ss[b][:m, :]),
             reads=[("xt", b)], writes=["sq", ("ss", b)])
        P.op("dve", lambda e, b=b, m=m: e.tensor_scalar(out=rstd[b][:m, :], in0=ss[b][:m, :], scalar1=1.0 / D,
                                                         scalar2=1e-6, op0=ALU.mult, op1=ALU.add),
             reads=[("ss", b)], writes=[("rstd", b)])
        P.op("act", lambda e, b=b, m=m: e.activation(out=rstd[b][:m, :], in_=rstd[b][:m, :], func=AF.Sqrt),
             reads=[("rstd", b)], writes=[("rstd", b)])
        P.op("dve", lambda e, b=b, m=m: e.reciprocal(out=rstd[b][:m, :], in_=rstd[b][:m, :]),
             reads=[("rstd", b)], writes=[("rstd", b)])
        P.op("dve", lambda e, b=b, m=m, mi=mi: e.scalar_tensor_tensor(out=tmp[:m, :], in0=xt[b][:m, :],
                                                                     scalar=rstd[b][:m, :], in1=S[mi][:m, :],
                                                                     op0=ALU.mult, op1=ALU.mult),
             reads=[("xt", b), ("rstd", b), ("S", mi)], writes=["tmp"])
        P.op("dve", lambda e, b=b, m=m, mi=mi: e.tensor_tensor(out=hb[b][:m, :], in0=tmp[:m, :], in1=SH[mi][:m, :],
                                                              op=ALU.add),
             reads=["tmp", ("SH", mi)], writes=[("hb", b)])
        for kg in range(KC // 4):
            tb = (si * 4 + kg) % 2
            for kk in range(4):
                k = kg * 4 + kk
                P.op("pe", lambda e, b=b, m=m, k=k, kk=kk, tb=tb: e.transpose(out=tps[tb][:, kk, :m],
                                                                              in_=hb[b][:m, k * 128:(k + 1) * 128],
                                                                              identity=ident[:m, :m]),
                     reads=[("hb", b), "ident"], writes=[("tps", tb)])
            eng = "act" if kg % 2 == 0 else "dve"
            if eng == "act":
                P.op("act", lambda e, m=m, kg=kg, tb=tb, r0=r0: e.copy(out=hT[:, kg * 4:(kg + 1) * 4, r0:r0 + m],
                                                                       in_=tps[tb][:, :, :m]),
                     reads=[("tps", tb)], writes=["hT"])
            else:
                P.op("dve", lambda e, m=m, kg=kg, tb=tb, r0=r0: e.tensor_copy(out=hT[:, kg * 4:(kg + 1) * 4, r0:r0 + m],
                                                                              in_=tps[tb][:, :, :m]),
                     reads=[("tps", tb)], writes=["hT"])

    if lerp is not None:
        for ci, col in enumerate([0, r_lat - 1, r_lat, R - 1]):
            P.op("dve", lambda e, ci=ci, col=col: e.tensor_scalar(out=hT[:, :, col:col + 1], in0=hT[:, :, col:col + 1],
                                                                   scalar1=bm[:, ci:ci + 1], scalar2=None, op0=ALU.mult),
                 reads=["hT", "bm"], writes=["hT"])
        for (c0, n) in [(0, r_lat), (r_lat, r_ctx)]:
            for k in range(KC):
                w_ = n - 2
                P.op("dve", lambda e, k=k, c0=c0, w_=w_: e.tensor_tensor(out=tmp[:, :w_], in0=hT[:, k, c0:c0 + w_],
                                                                        in1=hT[:, k, c0 + 2:c0 + 2 + w_], op=ALU.add),
                     reads=["hT"], writes=["tmp"])
                P.op("dve", lambda e, k=k, c0=c0, w_=w_: e.scalar_tensor_tensor(out=xxT[:, k, c0 + 1:c0 + 1 + w_],
                                                                               in0=tmp[:, :w_], scalar=0.5,
                                                                               in1=hT[:, k, c0 + 1:c0 + 1 + w_],
                                                                               op0=ALU.mult, op1=ALU.subtract),
                     reads=["tmp", "hT"], writes=["xxT"])

    P.barrier()
    es1.close()
    wf = [P.sb([128, KC, 128]) for _ in range(2)]
    wb = [P.sb([128, KC, 128], BF16) for _ in range(2)]
    stage = [P.sb([128, n_int]) for _ in range(2)]
    if lerp is not None:
        wb2 = [P.sb([128, KC, 128], BF16) for _ in range(2)]
    groups = []
    o = 0
    while o < n_lat:
        n = min(512, n_lat - o)
        groups.append((halo + o, o, n))
        o += n
    groups.append((r_lat + halo, n_lat, n_ctx))
    gi = 0
    for j in range(nblk):
        b = j % 2
        P.dma("sp", wf[b][:], wd[j], writes=[("wf", b)])
        if j % 2 == 0:
            P.op("act", lambda e, b=b: e.copy(out=wb[b][:], in_=wf[b][:]), reads=[("wf", b)], writes=[("wb", b)])
        else:
            P.op("pool", lambda e, b=b: e.tensor_copy(out=wb[b][:], in_=wf[b][:]), reads=[("wf", b)], writes=[("wb", b)])
        if lerp is not None:
            n_mu = lerp[j]
            P.op("pool", lambda e, b=b, n_mu=n_mu: e.tensor_tensor(out=wb2[b][:], in0=wf[b][:],
                                                                   in1=mu[:, :, n_mu:n_mu + 1].to_broadcast([128, KC, 128]),
                                                                   op=ALU.mult),
                 reads=[("wf", b), "mu"], writes=[("wb2", b)])
        for (hc, oc, n) in groups:
            pb = gi % 4
            gi += 1
            nmm = KC * (2 if lerp is not None else 1)
            for k in range(KC):
                P.op("pe", lambda e, b=b, k=k, pb=pb, hc=hc, n=n: e.matmul(mps[pb][:, :n], lhsT=wb[b][:, k, :],
                                                                         rhs=hT[:, k, hc:hc + n],
                                                                         start=(k == 0), stop=(k == nmm - 1)),
                     reads=[("wb", b), "hT"], writes=[("mps", pb)])
            if lerp is not None:
                for k in range(KC):
                    P.op("pe", lambda e, b=b, k=k, pb=pb, hc=hc, n=n: e.matmul(mps[pb][:, :n], lhsT=wb2[b][:, k, :],
                                                                             rhs=xxT[:, k, hc:hc + n],
                                                                             start=False, stop=(k == KC - 1)),
                         reads=[("wb2", b), "xxT"], writes=[("mps", pb)])
            if gi % 2 == 0:
                P.op("act", lambda e, b=b, pb=pb, oc=oc, n=n: e.copy(out=stage[b][:, oc:oc + n], in_=mps[pb][:, :n]),
                     reads=[("mps", pb)], writes=[("stage", b)])
            else:
                P.op("dve", lambda e, b=b, pb=pb, oc=oc, n=n: e.tensor_copy(out=stage[b][:, oc:oc + n], in_=mps[pb][:, :n]),
                     reads=[("mps", pb)], writes=[("stage", b)])
        P.dma("pool", out[j * 128:(j + 1) * 128, :], stage[b][:], reads=[("stage", b)], writes=["out"])
    P.finish(["out"])
    return nc


def _bc(v):
    return np.ascontiguousarray(np.broadcast_to(np.asarray(v, np.float32)[None, :], (128, v.shape[-1])))


def _wblocks(w):
    ncols = w.shape[1]
    nblk = (ncols + 127) // 128
    if nblk * 128 != ncols:
        w = np.concatenate([w, np.zeros((D, nblk * 128 - ncols), np.float32)], axis=1)
    return np.ascontiguousarray(w.reshape(KC, 128, nblk, 128).transpose(2, 1, 0, 3))


N_LAT = 2048
N_CTX = 64
T_LAT = 8192
T_CTX = 256


def _core_rows(x, ctx, cid, halo=0):
    b, q = cid // 4, cid % 4

    def take(a, lo, hi):
        T = a.shape[0]
        rows = []
        if lo < 0:
            rows.append(np.zeros((-lo, a.shape[1]), a.dtype))
        rows.append(a[max(lo, 0):min(hi, T)])
        if hi > T:
            rows.append(np.zeros((hi - T, a.shape[1]), a.dtype))
        return np.concatenate(rows, axis=0) if len(rows) > 1 else rows[0]

    lat = take(x[b], q * N_LAT - halo, (q + 1) * N_LAT + halo)
    cx = take(ctx[b], q * N_CTX - halo, (q + 1) * N_CTX + halo)
    return np.ascontiguousarray(np.concatenate([lat, cx], axis=0))


def _gather_rows(outs, width):
    x = np.empty((2, T_LAT, width), np.float32)
    ctx = np.empty((2, T_CTX, width), np.float32)
    for cid in range(NCORES):
        b, q = cid // 4, cid % 4
        x[b, q * N_LAT:(q + 1) * N_LAT] = outs[cid][:N_LAT]
        ctx[b, q * N_CTX:(q + 1) * N_CTX] = outs[cid][N_LAT:]
    return x, ctx


def run_P(x, ctx, mod_i, norm_w_i, wcat, lerp=None, mu=None):
    wb = _wblocks(wcat)
    nblk = wb.shape[0]
    halo = 1 if lerp is not None else 0
    nc = build_P(N_LAT, N_CTX, nblk, lerp)
    ident = np.eye(128, dtype=np.float32)
    nw = _bc(norm_w_i)
    in_maps = []
    for cid in range(NCORES):
        b, q = cid // 4, cid % 4
        m = {"xin": _core_rows(x, ctx, cid, halo), "nw": nw, "w": wb, "ident": ident,
             "scb": np.stack([_bc(mod_i[b, D:2 * D]), _bc(mod_i[2, D:2 * D])]),
             "shb": np.stack([_bc(mod_i[b, 0:D]), _bc(mod_i[2, 0:D])])}
        if lerp is not None:
            m["mu"] = np.ascontiguousarray(mu.T.reshape(KC, 128, 6).transpose(1, 0, 2))
            bm = np.ones((128, 4), np.float32)
            if q == 0:
                bm[:, 0] = 0.0
                bm[:, 2] = 0.0
            if q == 3:
                bm[:, 1] = 0.0
                bm[:, 3] = 0.0
            m["bmask"] = bm
        in_maps.append(m)
    res = _run(nc, in_maps)
    outs = [np.ascontiguousarray(r["projT"].T) for r in res]
    return _gather_rows(outs, nblk * 128)


O_INS = {"a": ["m0", "m1", "gate"], "b": ["m0", "gate"], "c": ["m0", "m1", "m2", "m3", "gate"]}


def build_O(n_lat, n_ctx, kind, final=False):
    R = n_lat + n_ctx
    nc = bass.Bass("TRN2", target_bir_lowering=False)
    P = Prog(nc)
    xin = P.dram("xin", [R, D])
    ins_d = {nm: P.dram(nm, [R, D]) for nm in O_INS[kind]}
    wod = P.dram("wo", [128, KC, D])
    gbd = P.dram("gb", [2, 128, D])
    identd = P.dram("ident", [128, 128])
    nbc = {"a": ["onw"], "b": [], "c": ["lnw", "lnb"]}[kind] + (["fnw"] if final else [])
    bc_d = {nm: P.dram(nm, [128, D]) for nm in nbc}
    out = P.dram("xout", [R, D], kind="ExternalOutput")

    ident_f = P.sb([128, 128])
    ident = P.sb([128, 128], BF16)
    wo = P.sb([128, KC, D], BF16)
    gb = [P.sb([128, D]) for _ in range(2)]
    bc = {nm: P.sb([128, D]) for nm in nbc}
    xt = [P.sb([128, D]) for _ in range(2)]
    xo = [P.sb([128, D]) for _ in range(2)]
    it = {nm: [P.sb([128, 512]) for _ in range(2)] for nm in O_INS[kind]}
    t1s = [P.sb([128, 512]) for _ in range(2)]
    t2s = [P.sb([128, 512]) for _ in range(2)]
    t3s = [P.sb([128, 512]) for _ in range(2)]
    sgs = [P.sb([128, 512]) for _ in range(2)]
    sts = [[P.sb([128, 8]) for _ in range(3)] for _ in range(2)]
    zb = [P.sb([128, D], BF16) for _ in range(2)]
    zT = [P.sb([128, KC, 128], BF16) for _ in range(2)]
    tps = [P.ps([128, 4, 128], BF16) for _ in range(2)]
    mps = [P.ps([128, 512]) for _ in range(4)]
    fs = [P.sb([128, 1]) for _ in range(2)]

    P.dma("sp", ident_f[:], identd[:], writes=["identf"])
    P.op("dve", lambda e: e.tensor_copy(out=ident[:], in_=ident_f[:]), reads=["identf"], writes=["ident"])
    for i in range(2):
        P.dma("sp", gb[i][:], gbd[i], writes=[("gb", i)])
    for nm in nbc:
        P.dma("sp", bc[nm][:], bc_d[nm], writes=[nm])
    for k in range(KC):
        b = k % 2
        P.dma("sp", xt[b][:], wod[:, k, :], writes=[("xt", b)])
        if k % 2 == 0:
            P.op("act", lambda e, b=b, k=k: e.copy(out=wo[:, k, :], in_=xt[b][:]), reads=[("xt", b)], writes=["wo"])
        else:
            P.op("pool", lambda e, b=b, k=k: e.tensor_copy(out=wo[:, k, :], in_=xt[b][:]), reads=[("xt", b)], writes=["wo"])

    G = 128 if kind == "a" else 64
    ng = 512 // G
    segs = [(r, m, 0) for (r, m) in _segments(0, n_lat)] + [(r, m, 1) for (r, m) in _segments(n_lat, n_ctx)]
    li = 0
    for si, (r0, m, mi) in enumerate(segs):
        b = si % 2
        P.dma("sp", xt[b][:m, :], xin[r0:r0 + m, :], writes=[("xt", b)])
        gstreams = [[], []]

        def emit_group(cg):
            sidx = cg // 2
            t1, t2, t3, sg, st = t1s[sidx], t2s[sidx], t3s[sidx], sgs[sidx], sts[sidx]
            SK = lambda nm: (nm, sidx)
            cs = slice(cg * 512, (cg + 1) * 512)
            lb = sidx
            P._defer = gstreams[sidx]
            T = {}
            for nm in O_INS[kind]:
                P.dma("sp", it[nm][lb][:m, :], ins_d[nm][r0:r0 + m, cs], writes=[(nm, lb)])
                T[nm] = it[nm][lb]
            gk = ("gate", lb)
            P.op("act", lambda e, m=m, T=T: e.activation(out=sg[:m, :], in_=T["gate"][:m, :], func=AF.Silu),
                 reads=[gk], writes=[SK("sg")])
            if kind == "b":
                P.op("dve", lambda e, m=m, T=T, b=b, cs=cs: e.tensor_tensor(out=zb[b][:m, cs], in0=T["m0"][:m, :],
                                                                          in1=sg[:m, :], op=ALU.mult),
                     reads=[("m0", lb), SK("sg")], writes=[("zb", b)])
                P._defer = None
                return
            P.op("dve", lambda e, m=m, T=T: e.tensor_tensor(out=t1[:m, :], in0=T["m0"][:m, :], in1=T["m1"][:m, :], op=ALU.add),
                 reads=[("m0", lb), ("m1", lb)], writes=[SK("t1")])
            y3 = t1[:m, :].rearrange("p (g c) -> p g c", c=G)
            if kind == "c":
                P.op("dve", lambda e, m=m, y3=y3: e.tensor_reduce(out=st[0][:m, :ng], in_=y3, axis=AX.X, op=ALU.add),
                     reads=[SK("t1")], writes=[SK("st0")])
                P.op("dve", lambda e, m=m: e.tensor_scalar(out=st[0][:m, :ng], in0=st[0][:m, :ng], scalar1=-1.0 / G,
                                                           scalar2=None, op0=ALU.mult),
                     reads=[SK("st0")], writes=[SK("st0")])
                P.op("dve", lambda e, m=m, y3=y3: e.tensor_tensor(out=y3, in0=y3,
                                                                in1=st[0][:m, :ng].unsqueeze(2).to_broadcast([m, ng, G]),
                                                                op=ALU.add),
                     reads=[SK("t1"), SK("st0")], writes=[SK("t1")])
            P.op("pool", lambda e, m=m: e.tensor_tensor(out=t2[:m, :], in0=t1[:m, :], in1=t1[:m, :], op=ALU.mult),
                 reads=[SK("t1")], writes=[SK("t2")])
            P.op("dve", lambda e, m=m: e.tensor_reduce(out=st[1][:m, :ng], in_=t2[:m, :].rearrange("p (g c) -> p g c", c=G),
                                                       axis=AX.X, op=ALU.add),
                 reads=[SK("t2")], writes=[SK("st1")])
            eps = 1e-6 if kind == "a" else 64e-5
            P.op("dve", lambda e, m=m, eps=eps: e.tensor_scalar(out=st[1][:m, :ng], in0=st[1][:m, :ng], scalar1=1.0 / G,
                                                                scalar2=eps, op0=ALU.mult, op1=ALU.add),
                 reads=[SK("st1")], writes=[SK("st1")])
            P.op("act", lambda e, m=m: e.activation(out=st[1][:m, :ng], in_=st[1][:m, :ng], func=AF.Sqrt),
                 reads=[SK("st1")], writes=[SK("st1")])
            P.op("dve", lambda e, m=m: e.reciprocal(out=st[2][:m, :ng], in_=st[1][:m, :ng]),
                 reads=[SK("st1")], writes=[SK("st2")])
            P.op("dve", lambda e, m=m, y3=y3: e.tensor_tensor(out=y3, in0=y3,
                                                            in1=st[2][:m, :ng].unsqueeze(2).to_broadcast([m, ng, G]),
                                                            op=ALU.mult),
                 reads=[SK("t1"), SK("st2")], writes=[SK("t1")])
            if kind == "a":
                P.op("pool", lambda e, m=m, cs=cs: e.tensor_tensor(out=t2[:m, :], in0=t1[:m, :], in1=bc["onw"][:m, cs], op=ALU.mult),
                     reads=[SK("t1"), "onw"], writes=[SK("t2")])
            else:
                P.op("pool", lambda e, m=m, cs=cs: e.tensor_tensor(out=t2[:m, :], in0=t1[:m, :], in1=bc["lnw"][:m, cs], op=ALU.mult),
                     reads=[SK("t1"), "lnw"], writes=[SK("t2")])
                P.op("pool", lambda e, m=m, T=T: e.tensor_tensor(out=t3[:m, :], in0=T["m2"][:m, :], in1=T["m3"][:m, :], op=ALU.add),
                     reads=[("m2", lb), ("m3", lb)], writes=[SK("t3")])
                P.op("pool", lambda e, m=m, cs=cs: e.tensor_tensor(out=t3[:m, :], in0=t3[:m, :], in1=bc["lnb"][:m, cs], op=ALU.add),
                     reads=[SK("t3"), "lnb"], writes=[SK("t3")])
                P.op("dve", lambda e, m=m: e.tensor_tensor(out=t2[:m, :], in0=t2[:m, :], in1=t3[:m, :], op=ALU.add),
                     reads=[SK("t2"), SK("t3")], writes=[SK("t2")])
            P.op("dve", lambda e, m=m, b=b, cs=cs: e.tensor_tensor(out=zb[b][:m, cs], in0=t2[:m, :], in1=sg[:m, :], op=ALU.mult),
                 reads=[SK("t2"), SK("sg")], writes=[("zb", b)])
            P._defer = None

        for cg in range(4):
            emit_group(cg)
        P.interleave(gstreams)
        for kg in range(KC // 4):
            tb = (si * 4 + kg) % 2
            for kk in range(4):
                k = kg * 4 + kk
                P.op("pe", lambda e, b=b, m=m, k=k, kk=kk, tb=tb: e.transpose(out=tps[tb][:, kk, :m],
                                                                              in_=zb[b][:m, k * 128:(k + 1) * 128],
                                                                              identity=ident[:m, :m]),
                     reads=[("zb", b), "ident"], writes=[("tps", tb)])
            if kg % 2 == 0:
                P.op("act", lambda e, m=m, kg=kg, tb=tb, b=b: e.copy(out=zT[b][:, kg * 4:(kg + 1) * 4, :m], in_=tps[tb][:, :, :m]),
                     reads=[("tps", tb)], writes=[("zT", b)])
            else:
                P.op("dve", lambda e, m=m, kg=kg, tb=tb, b=b: e.tensor_copy(out=zT[b][:, kg * 4:(kg + 1) * 4, :m], in_=tps[tb][:, :, :m]),
                     reads=[("tps", tb)], writes=[("zT", b)])
        for cg in range(4):
            cs = slice(cg * 512, (cg + 1) * 512)
            pb = cg
            for k in range(KC):
                P.op("pe", lambda e, b=b, m=m, k=k, pb=pb, cs=cs: e.matmul(mps[pb][:m, :], lhsT=zT[b][:, k, :m], rhs=wo[:, k, cs],
                                                                         start=(k == 0), stop=(k == KC - 1)),
                     reads=[("zT", b), "wo"], writes=[("mps", pb)])
            te = t1s[cg % 2]
            P.op("dve", lambda e, m=m, pb=pb, cs=cs, mi=mi, te=te: e.tensor_tensor(out=te[:m, :], in0=mps[pb][:m, :], in1=gb[mi][:m, cs], op=ALU.mult),
                 reads=[("mps", pb), ("gb", mi)], writes=[("t1", cg % 2)])
            P.op("pool", lambda e, m=m, b=b, cs=cs, te=te: e.tensor_tensor(out=xo[b][:m, cs], in0=te[:m, :], in1=xt[b][:m, cs], op=ALU.add),
                 reads=[("t1", cg % 2), ("xt", b)], writes=[("xo", b)])
        if final:
            P.op("act", lambda e, b=b, m=m: e.activation(out=xt[b][:m, :], in_=xo[b][:m, :], func=AF.Square, accum_out=fs[0][:m, :]),
                 reads=[("xo", b)], writes=[("xt", b), "fs0"])
            P.op("dve", lambda e, m=m: e.tensor_scalar(out=fs[0][:m, :], in0=fs[0][:m, :], scalar1=1.0 / D, scalar2=1e-6,
                                                       op0=ALU.mult, op1=ALU.add), reads=["fs0"], writes=["fs0"])
            P.op("act", lambda e, m=m: e.activation(out=fs[0][:m, :], in_=fs[0][:m, :], func=AF.Sqrt), reads=["fs0"], writes=["fs0"])
            P.op("dve", lambda e, m=m: e.reciprocal(out=fs[1][:m, :], in_=fs[0][:m, :]), reads=["fs0"], writes=["fs1"])
            P.op("dve", lambda e, m=m, b=b: e.scalar_tensor_tensor(out=xo[b][:m, :], in0=xo[b][:m, :], scalar=fs[1][:m, :],
                                                                   in1=bc["fnw"][:m, :], op0=ALU.mult, op1=ALU.mult),
                 reads=[("xo", b), "fs1", "fnw"], writes=[("xo", b)])
        P.dma("pool", out[r0:r0 + m, :], xo[b][:m, :], reads=[("xo", b)], writes=["out"])
    P.finish(["out"])
    return nc


def run_O(x, ctx, kind, mix_lat, mix_ctx, w_out, g_lat, g_ctx, bcs, final=False, need_ctx=True):
    n_ctx = N_CTX if need_ctx else 0
    nc = build_O(N_LAT, n_ctx, kind, final)
    ident = np.eye(128, dtype=np.float32)
    wo = np.ascontiguousarray(w_out.reshape(KC, 128, D).transpose(1, 0, 2))
    in_maps = []
    for cid in range(NCORES):
        b, q = cid // 4, cid % 4

        def rows(lat, cx):
            parts = [lat[b, q * N_LAT:(q + 1) * N_LAT]]
            if need_ctx:
                parts.append(cx[b, q * N_CTX:(q + 1) * N_CTX])
            return np.ascontiguousarray(np.concatenate(parts, axis=0))

        m = {"xin": rows(x, ctx), "wo": wo, "ident": ident, "gb": np.stack([_bc(g_lat[b]), _bc(g_ctx)])}
        for nm in O_INS[kind]:
            m[nm] = rows(mix_lat[nm], mix_ctx[nm] if need_ctx else None)
        for nm, v in bcs.items():
            m[nm] = _bc(v)
        in_maps.append(m)
    res = _run(nc, in_maps)
    xo = np.empty((2, T_LAT, D), np.float32)
    co = np.empty((2, T_CTX, D), np.float32) if need_ctx else None
    for cid in range(NCORES):
        b, q = cid // 4, cid % 4
        o = res[cid]["xout"]
        xo[b, q * N_LAT:(q + 1) * N_LAT] = o[:N_LAT]
        if need_ctx:
            co[b, q * N_CTX:(q + 1) * N_CTX] = o[N_LAT:]
    return xo, co


L_SEQ = T_CTX + T_LAT
MA_SHARE_PSUM = False
CH = 64


def build_Ma(nrec, L, jlayer):
    ntile = L // 128
    NR = nrec
    WN = NR * 128
    NG = NR * 2
    nc = bass.Bass("TRN2", target_bir_lowering=False)
    P = Prog(nc)
    qd = P.dram("qT", [nrec, 128, L])
    zd = P.dram("zT", [nrec, 128, L])
    vd = P.dram("v", [nrec, L // CH, CH, 128])
    lbd = P.dram("lbr", [nrec, 128, 2])
    maskd = P.dram("mask", [CH, CH])
    identd = P.dram("ident", [128, 128])
    out = P.dram("oT", [nrec, 128, L], kind="ExternalOutput")

    ident_f = P.sb([128, 128])
    ident = P.sb([128, 128], BF16)
    mask = P.sb([CH, CH])
    m01 = P.sb([128, WN])
    lbr = P.sb([128, nrec, 2])
    lb = P.sb([128, nrec])
    oml = P.sb([128, nrec])
    S = [P.sb([128, 128]) for _ in range(nrec)]
    Sb = [P.sb([128, 128], BF16) for _ in range(nrec)]
    Z = [P.sb([128, NR, 128]) for _ in range(2)]
    Qw = [P.sb([128, NR, 128]) for _ in range(2)]
    Vt = [P.sb([CH, NR, 2, 128]) for _ in range(2)]
    Vb = [P.sb([CH, NR, 2, 128], BF16) for _ in range(2)]
    e1, ft, gt, kq, bc, bm, be, ex0, ex2, ex3, bk = [P.sb([128, WN]) for _ in range(11)]
    qh = [P.sb([128, WN], BF16) for _ in range(2)]
    qtl = [P.sb([128, WN], BF16) for _ in range(2)]
    ktl = [P.sb([128, WN], BF16) for _ in range(2)]
    khI = [[P.sb([128, WN], BF16) for _ in range(4)] for _ in range(2)]
    gam = [P.sb([128, NG]) for _ in range(2)]
    osb = [P.sb([128, NR, 128]) for _ in range(2)]
    att_all = [[P.sb([CH, CH], BF16) for _ in range(2)] for _ in range(nrec)]
    ktm_all = [[P.sb([CH, 128], BF16) for _ in range(2)] for _ in range(nrec)]
    pA_bank = [P.ps([128, 512]) for _ in range(2)]
    pO_bank = [P.ps([128, 512]) for _ in range(2)]
    pT_bank = [P.ps([128, 1024], BF16) for _ in range(2)]
    pS_bank = [P.ps([128, 512]) for _ in range(2)]

    P.dma("sp", ident_f[:], identd[:], writes=["identf"])
    P.op("dve", lambda e: e.tensor_copy(out=ident[:], in_=ident_f[:]), reads=["identf"], writes=["ident"])
    P.dma("sp", mask[:], maskd[:], writes=["mask"])
    P.op("pool", lambda e: e.memset(m01[:], 1.0), writes=["m01"])
    P.op("pool", lambda e: e.memset(m01[:].rearrange("p (g s) -> p g s", s=CH)[:, :, 0:1], 0.0), reads=["m01"], writes=["m01"])
    for p_ in range(2):
        P.op("dve", lambda e, p_=p_: e.memset(pA_bank[p_][:], 0.0), writes=[("pA", p_)])
    for r in range(nrec):
        P.dma("sp", lbr[:, r, :], lbd[r], writes=["lbr"])
        P.op("pool", lambda e, r=r: e.memset(S[r][:], 0.0), writes=[("S", r)])
        P.op("pool", lambda e, r=r: e.memset(Sb[r][:], 0.0), writes=[("Sb", r)])
    if jlayer != 0:
        P.op("dve", lambda e: e.tensor_tensor(out=lb[:], in0=lbr[:, :, 0], in1=lbr[:, :, 1], op=ALU.subtract),
             reads=["lbr"], writes=["lb"])
        P.op("act", lambda e: e.activation(out=lb[:], in_=lb[:], func=AF.Exp), reads=["lb"], writes=["lb"])
        P.op("dve", lambda e: e.tensor_scalar(out=lb[:], in0=lb[:], scalar1=1.0, scalar2=None, op0=ALU.add),
             reads=["lb"], writes=["lb"])
        P.op("dve", lambda e: e.reciprocal(out=lb[:], in_=lb[:]), reads=["lb"], writes=["lb"])
        P.op("dve", lambda e: e.tensor_scalar(out=oml[:], in0=lb[:], scalar1=-1.0, scalar2=1.0, op0=ALU.mult, op1=ALU.add),
             reads=["lb"], writes=["oml"])

    QS = 128.0 ** -0.5

    NH = 2
    HW_ = WN // NH
    GH = NG // NH

    def emit_W(t, hf):
        tp = t % 2
        ts = slice(t * 128, (t + 1) * 128)
        KT = lambda nm: (nm, tp, hf)
        KH = lambda nm: (nm, hf)
        cs = slice(hf * HW_, (hf + 1) * HW_)
        rs = slice(hf * (NR // NH), (hf + 1) * (NR // NH))
        nr = NR // NH
        lst = []
        P._defer = lst
        z2 = Z[tp][:, rs, :].rearrange("p r t -> p (r t)")
        q2 = Qw[tp][:, rs, :].rearrange("p r t -> p (r t)")
        g3 = lambda ap: ap.rearrange("p (g s) -> p g s", s=CH)
        P.dma("sp", Z[tp][:, rs, :], zd[rs, :, ts].rearrange("r p t -> p r t"), writes=[KT("Z")])
        P.dma("sp", Qw[tp][:, rs, :], qd[rs, :, ts].rearrange("r p t -> p r t"), writes=[KT("Q")])
        for r_ in range(rs.start, rs.stop):
            P.dma("sp", Vt[tp][:, r_], vd[r_, 2 * t:2 * t + 2].rearrange("c s d -> s c d"), writes=[KT("Vt")])
        P.op("pool", lambda e: e.tensor_copy(out=Vb[tp][:, rs], in_=Vt[tp][:, rs]), reads=[KT("Vt")], writes=[KT("Vb")])
        P.op("act", lambda e: e.activation(out=e1[:, cs], in_=z2, func=AF.Exp, scale=-1.0), reads=[KT("Z")], writes=[KH("e1")])
        P.op("dve", lambda e: e.tensor_scalar(out=e1[:, cs], in0=e1[:, cs], scalar1=1.0, scalar2=None, op0=ALU.add), reads=[KH("e1")], writes=[KH("e1")])
        P.op("act", lambda e: e.activation(out=e1[:, cs], in_=e1[:, cs], func=AF.Ln), reads=[KH("e1")], writes=[KH("e1")])
        P.op("act", lambda e: e.activation(out=ft[:, cs], in_=e1[:, cs], func=AF.Exp, scale=-1.0), reads=[KH("e1")], writes=[KH("ft")])
        if jlayer != 0:
            f3 = ft[:, cs].rearrange("p (r t) -> p r t", t=128)
            P.op("dve", lambda e: e.tensor_tensor(out=f3, in0=f3, in1=oml[:, rs].unsqueeze(2).to_broadcast([128, nr, 128]), op=ALU.mult),
                 reads=[KH("ft"), "oml"], writes=[KH("ft")])
            P.op("dve", lambda e: e.tensor_tensor(out=f3, in0=f3, in1=lb[:, rs].unsqueeze(2).to_broadcast([128, nr, 128]), op=ALU.add),
                 reads=[KH("ft"), "lb"], writes=[KH("ft")])
            P.op("act", lambda e: e.activation(out=gt[:, cs], in_=ft[:, cs], func=AF.Ln), reads=[KH("ft")], writes=[KH("gt")])
        else:
            P.op("dve", lambda e: e.tensor_scalar(out=gt[:, cs], in0=e1[:, cs], scalar1=-1.0, scalar2=None, op0=ALU.mult),
                 reads=[KH("e1")], writes=[KH("gt")])
        P.op("pool", lambda e: e.tensor_scalar(out=kq[:, cs], in0=ft[:, cs], scalar1=-1.0, scalar2=1.0, op0=ALU.mult, op1=ALU.add),
             reads=[KH("ft")], writes=[KH("kq")])
        P.op("dve", lambda e: e.tensor_tensor_scan(out=bc[:, cs], data0=m01[:, cs], data1=gt[:, cs], initial=0.0, op0=ALU.mult, op1=ALU.add),
             reads=[KH("gt"), "m01"], writes=[KH("bc")])
        bc3 = g3(bc[:, cs])
        bc4 = bc[:, cs].rearrange("p (g i s) -> p g i s", i=4, s=16)
        P.op("dve", lambda e: e.tensor_tensor(out=bm[:, cs].rearrange("p (g i s) -> p g i s", i=4, s=16), in0=bc4,
                                              in1=bc4[:, :, :, 0:1].to_broadcast([128, GH, 4, 16]), op=ALU.subtract),
             reads=[KH("bc")], writes=[KH("bm")])
        P.op("dve", lambda e: e.tensor_tensor(out=g3(be[:, cs]), in0=bc3, in1=bc3[:, :, CH - 1:CH].to_broadcast([128, GH, CH]), op=ALU.subtract),
             reads=[KH("bc")], writes=[KH("be")])
        P.op("act", lambda e: e.activation(out=ex0[:, cs], in_=bm[:, cs], func=AF.Exp), reads=[KH("bm")], writes=[KH("ex0")])
        P.op("act", lambda e: e.activation(out=ex2[:, cs], in_=bc[:, cs], func=AF.Exp), reads=[KH("bc")], writes=[KH("ex2")])
        P.op("act", lambda e: e.activation(out=ex3[:, cs], in_=be[:, cs], func=AF.Exp, scale=-1.0), reads=[KH("be")], writes=[KH("ex3")])
        P.op("act", lambda e: e.activation(out=gam[tp][:, hf * GH:(hf + 1) * GH], in_=bc3[:, :, CH - 1], func=AF.Exp), reads=[KH("bc")], writes=[KT("gam")])
        P.op("dve", lambda e: e.scalar_tensor_tensor(out=qh[tp][:, cs], in0=q2, scalar=QS, in1=ex0[:, cs], op0=ALU.mult, op1=ALU.mult),
             reads=[KT("Q"), KH("ex0")], writes=[KT("qh")])
        P.op("dve", lambda e: e.scalar_tensor_tensor(out=qtl[tp][:, cs], in0=q2, scalar=QS, in1=ex2[:, cs], op0=ALU.mult, op1=ALU.mult),
             reads=[KT("Q"), KH("ex2")], writes=[KT("qtl")])
        P.op("pool", lambda e: e.tensor_tensor(out=ktl[tp][:, cs], in0=kq[:, cs], in1=ex3[:, cs], op=ALU.mult), reads=[KH("kq"), KH("ex3")], writes=[KT("ktl")])
        kq3 = g3(kq[:, cs])
        bk3 = g3(bk[:, cs])
        for I in range(4):
            n = 16 * (I + 1)
            P.op("dve", lambda e, n=n, I=I: e.tensor_tensor(out=bk3[:, :, :n], in0=bc3[:, :, :n],
                                                          in1=bc3[:, :, 16 * I:16 * I + 1].to_broadcast([128, GH, n]), op=ALU.subtract),
                 reads=[KH("bc"), KH("bk")], writes=[KH("bk")])
            P.op("act", lambda e, n=n: e.activation(out=bk3[:, :, :n], in_=bk3[:, :, :n], func=AF.Exp, scale=-1.0), reads=[KH("bk")], writes=[KH("bk")])
            eng_ = "pool" if I % 2 == 0 else "dve"
            P.op(eng_, lambda e, n=n, I=I: e.tensor_tensor(out=g3(khI[tp][I][:, cs])[:, :, :n], in0=kq3[:, :, :n], in1=bk3[:, :, :n], op=ALU.mult),
                 reads=[KH("kq"), KH("bk")], writes=[KT("kh%d" % I)])
        P._defer = None
        return lst

    def emit_R(t, r):
        tp = t % 2
        hf_ = r // (NR // NH)
        KT = lambda nm: (nm, tp, hf_) if nm != "osb" else (nm, tp)
        lst = []
        P._defer = lst
        for c in range(2):
            g = r * 2 + c
            go = g * CH
            p = (r + c) % 2
            pA_t = pA_bank[p][:, 0:CH]
            pO_t = pO_bank[p][:, 0:CH]
            pT_t = pT_bank[p][:, 0:128]
            pS_t = pS_bank[p][:, 0:128]
            att_t = att_all[r][c]
            ktm_t = ktm_all[r][c]
            vb_t = Vb[tp][:, r, c, :]
            P.atomic_begin()
            for I in range(4):
                n = 16 * (I + 1)
                P.op("pe", lambda e, I=I, n=n, pA_t=pA_t, go=go: e.matmul(pA_t[:n, 16 * I:16 * I + 16], lhsT=khI[tp][I][:, go:go + n],
                                                                         rhs=qh[tp][:, go + 16 * I:go + 16 * I + 16], start=True, stop=True),
                     reads=[KT("kh%d" % I), KT("qh")], writes=[("pA", p)])
            P.op("dve", lambda e, att_t=att_t, pA_t=pA_t: e.tensor_tensor(out=att_t[:], in0=pA_t[:CH, :], in1=mask[:], op=ALU.mult),
                 reads=[("pA", p), "mask"], writes=[("att", r, c)])
            P.atomic_end()
            P.atomic_begin()
            P.op("pe", lambda e, att_t=att_t, pO_t=pO_t, vb_t=vb_t: e.matmul(pO_t, lhsT=vb_t, rhs=att_t[:], start=True, stop=False),
                 reads=[KT("Vb"), ("att", r, c), ("Sb", r), KT("qtl")], writes=[("pO", p)])
            P.op("pe", lambda e, pO_t=pO_t, go=go: e.matmul(pO_t, lhsT=Sb[r][:], rhs=qtl[tp][:, go:go + CH], start=False, stop=True),
                 reads=[("Sb", r), KT("qtl")], writes=[("pO", p)])
            P.op("act", lambda e, pO_t=pO_t, c=c: e.copy(out=osb[tp][:, r, c * CH:(c + 1) * CH], in_=pO_t),
                 reads=[("pO", p)], writes=[KT("osb")])
            P.atomic_end()
            P.atomic_begin()
            P.op("pe", lambda e, pT_t=pT_t, go=go: e.transpose(out=pT_t[:CH, :], in_=ktl[tp][:, go:go + CH], identity=ident[:]),
                 reads=[KT("ktl"), "ident"], writes=[("pT", p)])
            P.op("act", lambda e, ktm_t=ktm_t, pT_t=pT_t: e.copy(out=ktm_t[:], in_=pT_t[:CH, :]), reads=[("pT", p)], writes=[("ktm", r, c)])
            P.atomic_end()
            P.atomic_begin()
            P.op("pe", lambda e, ktm_t=ktm_t, pS_t=pS_t, vb_t=vb_t: e.matmul(pS_t, lhsT=ktm_t[:], rhs=vb_t, start=True, stop=True),
                 reads=[("ktm", r, c), KT("Vb")], writes=[("pS", p)])
            P.op("dve", lambda e, pS_t=pS_t, g=g: e.scalar_tensor_tensor(out=S[r][:], in0=S[r][:], scalar=gam[tp][:, g:g + 1],
                                                                        in1=pS_t, op0=ALU.mult, op1=ALU.add),
                 reads=[("S", r), KT("gam"), ("pS", p)], writes=[("S", r)])
            P.atomic_end()
            P.op("pool", lambda e: e.tensor_copy(out=Sb[r][:], in_=S[r][:]), reads=[("S", r)], writes=[("Sb", r)])
        P._defer = None
        return lst

    P.interleave([emit_W(0, hf) for hf in range(NH)])
    for t in range(ntile):
        ts = slice(t * 128, (t + 1) * 128)
        streams = [emit_R(t, r) for r in range(nrec)]
        if t + 1 < ntile:
            streams += [emit_W(t + 1, hf) for hf in range(NH)]
        P.interleave(streams)
        P.dma("pool", out[:, :, ts].rearrange("r p t -> p r t"), osb[t % 2][:], reads=[("osb", t % 2)], writes=["out"])
    P.finish(["out"])
    return nc


def _seq(lat_b, ctx_b, rev):
    if rev:
        return np.concatenate([ctx_b[::-1], lat_b[::-1]], axis=0)
    return np.concatenate([ctx_b, lat_b], axis=0)


def _unseq(s, rev):
    c, l = s[:T_CTX], s[T_CTX:]
    if rev:
        return l[::-1], c[::-1]
    return l, c


def run_Ma(pl, pc, a_lb_raw, jlayer):
    nrec = 8
    nc = build_Ma(nrec, L_SEQ, jlayer)
    ident = np.eye(128, dtype=np.float32)
    mask = np.triu(np.ones((CH, CH), np.float32))
    in_maps = []
    for cid in range(NCORES):
        b, hg = cid // 4, cid % 4
        qT = np.empty((nrec, 128, L_SEQ), np.float32)
        zT = np.empty((nrec, 128, L_SEQ), np.float32)
        v = np.empty((nrec, L_SEQ // CH, CH, 128), np.float32)
        lbr = np.empty((nrec, 128, 2), np.float32)
        for hl in range(4):
            h = hg * 4 + hl
            hc = slice(h * 128, (h + 1) * 128)
            for dr in range(2):
                r = hl * 2 + dr
                zoff = 2 * D + dr * D
                sq = _seq(pl[b][:, hc], pc[b][:, hc], dr == 1)
                sv = _seq(pl[b][:, D + h * 128:D + (h + 1) * 128], pc[b][:, D + h * 128:D + (h + 1) * 128], dr == 1)
                sz = _seq(pl[b][:, zoff + h * 128:zoff + (h + 1) * 128], pc[b][:, zoff + h * 128:zoff + (h + 1) * 128], dr == 1)
                qT[r] = sq.T
                zT[r] = sz.T
                v[r] = sv.reshape(L_SEQ // CH, CH, 128)
                lbr[r] = a_lb_raw[:, hc].T
        in_maps.append({"qT": qT, "zT": zT, "v": v, "lbr": lbr, "mask": mask, "ident": ident})
    res = _run(nc, in_maps)
    o_lat = [np.empty((2, T_LAT, D), np.float32) for _ in range(2)]
    o_ctx = [np.empty((2, T_CTX, D), np.float32) for _ in range(2)]
    for cid in range(NCORES):
        b, hg = cid // 4, cid % 4
        oT = res[cid]["oT"]
        for hl in range(4):
            h = hg * 4 + hl
            for dr in range(2):
                l, c = _unseq(oT[hl * 2 + dr].T, dr == 1)
                o_lat[dr][b][:, h * 128:(h + 1) * 128] = l
                o_ctx[dr][b][:, h * 128:(h + 1) * 128] = c
    return o_lat, o_ctx


def layer_a(x, ctx, mod_i, norm_w_i, w_in, a_lb_raw, jlayer, onorm_w, w_out, need_ctx, final_w=None):
    pl, pc = run_P(x, ctx, mod_i, norm_w_i, w_in)
    o_lat, o_ctx = run_Ma(pl, pc, a_lb_raw, jlayer)
    ml = {"m0": o_lat[0], "m1": o_lat[1], "gate": pl[:, :, 4 * D:5 * D]}
    mc = {"m0": o_ctx[0], "m1": o_ctx[1], "gate": pc[:, :, 4 * D:5 * D]}
    bcs = {"onw": np.tile(onorm_w, 16)}
    if final_w is not None:
        bcs["fnw"] = final_w
    return run_O(x, ctx, "a", ml, mc, w_out, mod_i[0:2, 2 * D:3 * D], mod_i[2, 2 * D:3 * D], bcs,
                 final=final_w is not None, need_ctx=need_ctx)


NQH = 8
HD = 64
NBLK = T_LAT // 128


def build_Mb(need_ctx):
    nc = bass.Bass("TRN2", target_bir_lowering=False)
    P = Prog(nc)
    qd = P.dram("qT", [HD, NQH, T_LAT])
    qsd = P.dram("qsT", [HD, NQH, T_LAT])
    kd = P.dram("kT", [HD, T_LAT])
    ksd = P.dram("ksT", [HD, T_LAT])
    vd = P.dram("v", [128, NBLK, HD])
    qcd = P.dram("qcT", [HD, NQH, T_CTX])
    kcd = P.dram("kcT", [HD, T_CTX])
    vcd = P.dram("vc", [128, 2, HD])
    posd = P.dram("pos", [HD, T_LAT])
    fid = P.dram("fidx", [HD, 2])
    sinkd = P.dram("sink", [128, NQH])
    mld = P.dram("maskl", [128, 128])
    mrd = P.dram("maskr", [128, 128])
    n_out = T_LAT + (T_CTX if need_ctx else 0)
    out = P.dram("o", [n_out, NQH * HD], kind="ExternalOutput")

    PI = float(np.pi)
    CW = 2048
    cosT = P.sb([HD, T_LAT])
    sinT = P.sb([HD, T_LAT])
    posc = P.sb([HD, CW])
    tmpA = P.sb([HD, CW])
    tmpB = P.sb([HD, CW])
    fidx = P.sb([HD, 2])
    inv = P.sb([HD, 1])
    kr = P.sb([HD, T_LAT], BF16)
    kcb = P.sb([HD, T_CTX], BF16)
    kcf = P.sb([HD, T_CTX])
    vf = P.sb([128, NBLK, HD])
    vx = P.sb([128, NBLK, HD + 1], BF16)
    vcf = P.sb([128, 2, HD])
    vcx = P.sb([128, 2, HD + 1], BF16)
    esink = P.sb([128, NQH])
    ml = P.sb([128, 128], BF16)
    mr = P.sb([128, 128], BF16)
    mlf = P.sb([128, 128])
    mrf = P.sb([128, 128])
    qf = [P.sb([HD, NQH, 128]) for _ in range(2)]
    qsf = [P.sb([HD, NQH, 128]) for _ in range(2)]
    qt1 = P.sb([HD, NQH, 128])
    qt2 = P.sb([HD, NQH, 128])
    qr = [P.sb([HD, NQH, 128], BF16) for _ in range(2)]
    E = [P.sb([128, NQH, 128], BF16) for _ in range(5)]
    pS = [P.ps([128, 1024]) for _ in range(2)]
    pO = [P.ps([128, 4, 128]) for _ in range(2)]
    den = P.sb([128, NQH])
    osb = [P.sb([128, NQH, HD]) for _ in range(2)]

    P.dma("sp", fidx[:], fid[:], writes=["fidx"])
    P.dma("sp", esink[:], sinkd[:], writes=["esink"])
    P.dma("sp", mlf[:], mld[:], writes=["mlf"])
    P.dma("sp", mrf[:], mrd[:], writes=["mrf"])
    P.op("dve", lambda e: e.tensor_copy(out=ml[:], in_=mlf[:]), reads=["mlf"], writes=["ml"])
    P.op("dve", lambda e: e.tensor_copy(out=mr[:], in_=mrf[:]), reads=["mrf"], writes=["mr"])
    P.op("act", lambda e: e.activation(out=esink[:], in_=esink[:], func=AF.Exp), reads=["esink"], writes=["esink"])
    P.op("act", lambda e: e.activation(out=inv[:], in_=fidx[:, 0:1], func=AF.Exp, scale=-float(np.log(10000.0)) / 16.0),
         reads=["fidx"], writes=["inv"])

    def sincos(dst_full, shift, key, cc):
        cs_ = slice(cc * CW, (cc + 1) * CW)
        dst = dst_full[:, cs_]
        P.op("dve", lambda e: e.tensor_scalar(out=tmpA[:], in0=posc[:], scalar1=inv[:, 0:1], scalar2=shift, op0=ALU.mult, op1=ALU.add),
             reads=["pos", "inv"], writes=["tmpA"])
        ni = tmpB[:].bitcast(mybir.dt.int32)
        P.op("dve", lambda e: e.tensor_scalar(out=dst, in0=tmpA[:], scalar1=1.0 / (2 * PI), scalar2=0.5, op0=ALU.mult, op1=ALU.add),
             reads=["tmpA"], writes=[key])
        P.op("dve", lambda e: e.tensor_copy(out=ni, in_=dst), reads=[key], writes=["tmpB"])
        P.op("dve", lambda e: e.tensor_copy(out=dst, in_=ni), reads=["tmpB"], writes=[key])
        P.op("dve", lambda e: e.scalar_tensor_tensor(out=tmpA[:], in0=dst, scalar=-2 * PI, in1=tmpA[:], op0=ALU.mult, op1=ALU.add),
             reads=[key, "tmpA"], writes=["tmpA"])
        P.op("dve", lambda e: e.tensor_scalar(out=tmpB[:], in0=tmpA[:], scalar1=-PI, scalar2=2 * PI, op0=ALU.is_lt, op1=ALU.mult),
             reads=["tmpA"], writes=["tmpB"])
        P.op("dve", lambda e: e.tensor_tensor(out=tmpA[:], in0=tmpA[:], in1=tmpB[:], op=ALU.add), reads=["tmpA", "tmpB"], writes=["tmpA"])
        P.op("dve", lambda e: e.tensor_scalar(out=tmpB[:], in0=tmpA[:], scalar1=PI, scalar2=-2 * PI, op0=ALU.is_gt, op1=ALU.mult),
             reads=["tmpA"], writes=["tmpB"])
        P.op("dve", lambda e: e.tensor_tensor(out=tmpA[:], in0=tmpA[:], in1=tmpB[:], op=ALU.add), reads=["tmpA", "tmpB"], writes=["tmpA"])
        P.op("dve", lambda e: e.tensor_scalar(out=tmpA[:], in0=tmpA[:], scalar1=PI, scalar2=-PI, op0=ALU.min, op1=ALU.max),
             reads=["tmpA"], writes=["tmpA"])
        P.op("act", lambda e: e.activation(out=dst, in_=tmpA[:], func=AF.Sin), reads=["tmpA"], writes=[key])

    for cc in range(T_LAT // CW):
        P.dma("sp", posc[:], posd[:, cc * CW:(cc + 1) * CW], writes=["pos"])
        sincos(sinT, 0.0, "sinT", cc)
        sincos(cosT, PI / 2, "cosT", cc)
    P.op("dve", lambda e: e.tensor_scalar(out=sinT[:], in0=sinT[:], scalar1=fidx[:, 1:2], scalar2=None, op0=ALU.mult),
         reads=["sinT", "fidx"], writes=["sinT"])

    for cc in range(T_LAT // CW):
        cs_ = slice(cc * CW, (cc + 1) * CW)
        P.dma("sp", tmpA[:], kd[:, cs_], writes=["tmpA"])
        P.dma("sp", tmpB[:], ksd[:, cs_], writes=["tmpB"])
        P.op("dve", lambda e, cs_=cs_: e.tensor_tensor(out=tmpA[:], in0=tmpA[:], in1=cosT[:, cs_], op=ALU.mult), reads=["tmpA", "cosT"], writes=["tmpA"])
        P.op("pool", lambda e, cs_=cs_: e.tensor_tensor(out=tmpB[:], in0=tmpB[:], in1=sinT[:, cs_], op=ALU.mult), reads=["tmpB", "sinT"], writes=["tmpB"])
        P.op("dve", lambda e, cs_=cs_: e.tensor_tensor(out=kr[:, cs_], in0=tmpA[:], in1=tmpB[:], op=ALU.add), reads=["tmpA", "tmpB"], writes=["kr"])
    P.dma("sp", kcf[:], kcd[:], writes=["kcf"])
    P.op("dve", lambda e: e.tensor_copy(out=kcb[:], in_=kcf[:]), reads=["kcf"], writes=["kcb"])
    P.dma("sp", vf[:], vd[:], writes=["vf"])
    P.dma("sp", vcf[:], vcd[:], writes=["vcf"])
    P.op("pool", lambda e: e.memset(vx[:], 1.0), writes=["vx"])
    P.op("pool", lambda e: e.memset(vcx[:], 1.0), writes=["vcx"])
    P.op("pool", lambda e: e.tensor_copy(out=vx[:, :, 0:HD], in_=vf[:]), reads=["vf"], writes=["vx"])
    P.op("pool", lambda e: e.tensor_copy(out=vcx[:, :, 0:HD], in_=vcf[:]), reads=["vcf"], writes=["vcx"])

    blocks = [("lat", j) for j in range(NBLK)]
    if need_ctx:
        blocks += [("ctx", j) for j in range(2)]
    for bi, (typ, j) in enumerate(blocks):
        b = bi % 2
        ts = slice(j * 128, (j + 1) * 128)
        if typ == "lat":
            P.dma("sp", qf[b][:], qd[:, :, ts], writes=[("qf", b)])
            P.dma("sp", qsf[b][:], qsd[:, :, ts], writes=[("qsf", b)])
            P.op("dve", lambda e, b=b, ts=ts: e.tensor_tensor(out=qt1[:], in0=qf[b][:], in1=cosT[:, None, ts].to_broadcast([HD, NQH, 128]),
                                                            op=ALU.mult), reads=[("qf", b), "cosT"], writes=["qt1"])
            P.op("pool", lambda e, b=b, ts=ts: e.tensor_tensor(out=qt2[:], in0=qsf[b][:], in1=sinT[:, None, ts].to_broadcast([HD, NQH, 128]),
                                                             op=ALU.mult), reads=[("qsf", b), "sinT"], writes=["qt2"])
            P.op("dve", lambda e, b=b: e.tensor_tensor(out=qr[b][:], in0=qt1[:], in1=qt2[:], op=ALU.add),
                 reads=["qt1", "qt2"], writes=[("qr", b)])
            kbs = []
            if j > 0:
                kbs.append(("lat", j - 1, "ml"))
            kbs.append(("lat", j, None))
            if j < NBLK - 1:
                kbs.append(("lat", j + 1, "mr"))
            kbs += [("ctx", 0, None), ("ctx", 1, None)]
            orow = j * 128
        else:
            P.dma("sp", qf[b][:], qcd[:, :, ts], writes=[("qf", b)])
            P.op("dve", lambda e, b=b: e.tensor_copy(out=qr[b][:], in_=qf[b][:]), reads=[("qf", b)], writes=[("qr", b)])
            kbs = [("ctx", 0, None), ("ctx", 1, None)]
            orow = T_LAT + j * 128
        for ki, (kt, kj, mk) in enumerate(kbs):
            p = ki % 2
            ksl = slice(kj * 128, (kj + 1) * 128)
            kap = kr[:, ksl] if kt == "lat" else kcb[:, ksl]
            kkey = "kr" if kt == "lat" else "kcb"
            for hh in range(2):
                P.op("pe", lambda e, b=b, p=p, hh=hh, kap=kap: e.matmul(pS[p][:, hh * 512:(hh + 1) * 512], lhsT=kap,
                                                                      rhs=qr[b][:, hh * 4:(hh + 1) * 4, :], start=True, stop=True),
                     reads=[kkey, ("qr", b)], writes=[("pS", p)])
            P.op("act", lambda e, p=p, ki=ki: e.activation(out=E[ki][:].rearrange("p h q -> p (h q)"), in_=pS[p][:], func=AF.Exp, scale=HD ** -0.5),
                 reads=[("pS", p)], writes=[("E", ki)])
            if mk is not None:
                mt = ml if mk == "ml" else mr
                P.op("dve", lambda e, ki=ki, mt=mt: e.tensor_tensor(out=E[ki][:], in0=E[ki][:], in1=mt[:, None, :].to_broadcast([128, NQH, 128]),
                                                                   op=ALU.mult), reads=[("E", ki), mk], writes=[("E", ki)])
        nk = len(kbs)
        for h in range(NQH):
            po = h // 4
            for ki, (kt, kj, mk) in enumerate(kbs):
                vap = vx[:, kj, :] if kt == "lat" else vcx[:, kj, :]
                vkey = "vx" if kt == "lat" else "vcx"
                P.op("pe", lambda e, h=h, po=po, ki=ki, vap=vap: e.matmul(pO[po][:, h % 4, 0:HD + 1], lhsT=E[ki][:, h, :], rhs=vap,
                                                                        start=(ki == 0), stop=(ki == nk - 1)),
                     reads=[("E", ki), vkey], writes=[("pO", po)])
        for po in range(2):
            hs = slice(po * 4, (po + 1) * 4)
            P.op("dve", lambda e, po=po, hs=hs: e.tensor_tensor(out=den[:, hs], in0=pO[po][:, :, HD], in1=esink[:, hs], op=ALU.add),
                 reads=[("pO", po), "esink"], writes=["den"])
            P.op("dve", lambda e, hs=hs: e.reciprocal(out=den[:, hs], in_=den[:, hs]), reads=["den"], writes=["den"])
            P.op("dve", lambda e, po=po, hs=hs, b=b: e.tensor_tensor(out=osb[b][:, hs, :], in0=pO[po][:, :, 0:HD],
                                                                    in1=den[:, hs].unsqueeze(2).to_broadcast([128, 4, HD]), op=ALU.mult),
                 reads=[("pO", po), "den"], writes=[("osb", b)])
        P.dma("pool", out[orow:orow + 128, :], osb[b][:].rearrange("p h d -> p (h d)"), reads=[("osb", b)], writes=["out"])
    P.finish(["out"])
    return nc


def _rope_swap(a):
    hd = a.shape[-1]
    idx = np.arange(hd)
    idx = (idx // 32) * 32 + ((idx % 32) + 16) % 32
    return a[..., idx]


def run_Mb(pl, pc, sink, need_ctx):
    nc = build_Mb(need_ctx)
    t = np.arange(T_LAT)
    pos = np.empty((HD, T_LAT), np.float32)
    pos[:32] = (t // 64)[None, :]
    pos[32:] = (t % 64)[None, :]
    d = np.arange(HD)
    fidx = np.stack([(d % 16).astype(np.float32), np.where((d % 32) < 16, -1.0, 1.0).astype(np.float32)], axis=1)
    jj, ii = np.meshgrid(np.arange(128), np.arange(128), indexing="ij")
    maskl = (jj >= ii).astype(np.float32)
    maskr = (jj <= ii).astype(np.float32)
    in_maps = []
    for cid in range(NCORES):
        b, g = cid // 4, cid % 4
        q = pl[b][:, g * 512:(g + 1) * 512].reshape(T_LAT, NQH, HD)
        k = pl[b][:, D + g * HD:D + (g + 1) * HD]
        v = pl[b][:, D + 256 + g * HD:D + 256 + (g + 1) * HD]
        qc = pc[b][:, g * 512:(g + 1) * 512].reshape(T_CTX, NQH, HD)
        kc = pc[b][:, D + g * HD:D + (g + 1) * HD]
        vc = pc[b][:, D + 256 + g * HD:D + 256 + (g + 1) * HD]
        m = {"qT": np.ascontiguousarray(q.transpose(2, 1, 0)), "qsT": np.ascontiguousarray(_rope_swap(q).transpose(2, 1, 0)),
             "kT": np.ascontiguousarray(k.T), "ksT": np.ascontiguousarray(_rope_swap(k).T),
             "v": np.ascontiguousarray(v.reshape(NBLK, 128, HD).transpose(1, 0, 2)),
             "qcT": np.ascontiguousarray(qc.transpose(2, 1, 0)), "kcT": np.ascontiguousarray(kc.T),
             "vc": np.ascontiguousarray(vc.reshape(2, 128, HD).transpose(1, 0, 2)),
             "pos": pos, "fidx": fidx, "sink": _bc(sink[g * NQH:(g + 1) * NQH]), "maskl": maskl, "maskr": maskr}
        in_maps.append(m)
    res = _run(nc, in_maps)
    ol = np.empty((2, T_LAT, D), np.float32)
    oc = np.empty((2, T_CTX, D), np.float32) if need_ctx else None
    for cid in range(NCORES):
        b, g = cid // 4, cid % 4
        o = res[cid]["o"]
        ol[b][:, g * 512:(g + 1) * 512] = o[:T_LAT]
        if need_ctx:
            oc[b][:, g * 512:(g + 1) * 512] = o[T_LAT:]
    return ol, oc


def layer_b(x, ctx, mod_i, norm_w_i, w_in, sink, w_out, need_ctx, final_w=None):
    pl, pc = run_P(x, ctx, mod_i, norm_w_i, w_in)
    ol, oc = run_Mb(pl, pc, sink, need_ctx)
    ml = {"m0": ol, "gate": pl[:, :, D + 512:]}
    mc = {"m0": oc, "gate": pc[:, :, D + 512:]}
    bcs = {}
    if final_w is not None:
        bcs["fnw"] = final_w
    return run_O(x, ctx, "b", ml, mc, w_out, mod_i[0:2, 2 * D:3 * D], mod_i[2, 2 * D:3 * D], bcs,
                 final=final_w is not None, need_ctx=need_ctx)


NH_C = 8
MC_INTERLEAVE = False
MC_FP32R = True
MC_BF16INV = True
MC_PIPE = True


def R32(ap):
    if MC_BF16INV:
        return ap
    return ap.bitcast(mybir.dt.float32r) if MC_FP32R else ap
HC = 64
LORA = 96


def build_Mc(L):
    ntile = L // 128
    W = NH_C * HC
    nc = bass.Bass("TRN2", target_bir_lowering=False)
    P = Prog(nc)
    rd = P.dram("r", [2, L, W])
    kd = P.dram("k", [2, L, W])
    vd = P.dram("v", [2, L, W])
    lwd = P.dram("lwT", [2, LORA, L])
    lad = P.dram("laT", [2, LORA, L])
    w2d = P.dram("w2", [2, LORA, W])
    a2d = P.dram("a2", [2, LORA, W])
    w0d = P.dram("w0b", [2, 128, W])
    a0d = P.dram("a0b", [2, 128, W])
    kkd = P.dram("kkb", [128, W])
    kad = P.dram("kab", [128, W])
    rkd = P.dram("rkb", [128, W])
    trid = P.dram("tri", [128, 128])
    mupd = P.dram("mup", [128, 128])
    mlod = P.dram("mlo", [128, 128])
    identd = P.dram("ident", [128, 128])
    seld = P.dram("sel", [128, 1])
    yout = P.dram("y", [2, L, W], kind="ExternalOutput")
    bout = P.dram("bon", [2, L, W], kind="ExternalOutput")

    def T2(n, dt=F32, shape=(128, W)):
        return [P.sb(list(shape), dt) for _ in range(n)]

    ident = P.sb([128, 128])
    identb = P.sb([128, 128], BF16)
    tri = P.sb([128, 128])
    mup = P.sb([128, 128])
    mupi = P.sb([128, 128])
    mlo = P.sb([128, 128])
    sel = P.sb([128, 1])
    w2 = T2(2, F32, (LORA, W))
    a2 = T2(2, F32, (LORA, W))
    w0b = T2(2)
    a0b = T2(2)
    kkb = P.sb([128, W])
    kab = P.sb([128, W])
    omka = P.sb([128, W])
    rkb = P.sb([128, W])
    rt, kt, vt = T2(2), T2(2), T2(2)
    lw = T2(2, F32, (LORA, 128))
    la = T2(2, F32, (LORA, 128))
    th_ = T2(2, F32, (LORA, 128))
    zt_, ld_, iclr_, kkr_, kk_, t1_, kdt_ = T2(2), T2(2), T2(2), T2(2), T2(2), T2(2), T2(2)
    ein_, eneg_ = T2(2), T2(2)
    st8_ = [[P.sb([128, NH_C]) for _ in range(3)] for _ in range(2)]
    Ah_, Bh_, Kh_, Rh_, Vb_ = T2(4, BF16), T2(4, BF16), T2(4, BF16), T2(4, BF16), T2(4, BF16)
    XT_ = [{nm: P.sb([HC, NH_C, 128], BF16) for nm in ("a", "b", "k", "r")} for _ in range(4)]
    gamT_ = [P.sb([HC, NH_C]) for _ in range(4)]
    IDT = BF16 if MC_BF16INV else F32
    Nf_ = [[[P.sb([128, 4, 128], IDT) for _ in range(2)] for _ in range(2)] for _ in range(2)]
    NTf_ = [[[P.sb([128, 4, 128], IDT) for _ in range(2)] for _ in range(2)] for _ in range(2)]
    TTh_ = [[P.sb([128, 4, 128], BF16) for _ in range(2)] for _ in range(2)]
    TT_ = [[P.sb([128, 4, 128]) for _ in range(2)] for _ in range(2)]
    TTb_ = [[P.sb([128, 4, 128], BF16) for _ in range(2)] for _ in range(2)]
    AakT_ = [[P.sb([128, 4, 128], BF16) for _ in range(2)] for _ in range(2)]
    AVb_ = [[P.sb([128, 4, HC], BF16) for _ in range(2)] for _ in range(2)]
    ArbT_ = [P.sb([128, NH_C, 128], BF16) for _ in range(2)]
    ArkT_ = [P.sb([128, NH_C, 128], BF16) for _ in range(2)]
    TAb_ = [P.sb([128, NH_C, HC], BF16) for _ in range(2)]
    TVb_ = [P.sb([128, NH_C, HC], BF16) for _ in range(2)]
    MTb_ = [P.sb([HC, NH_C, HC], BF16) for _ in range(2)]
    RQTb_ = [P.sb([HC, NH_C, 128], BF16) for _ in range(2)]
    Pb = [P.sb([HC, NH_C, HC], BF16) for _ in range(2)]
    ysb = T2(2, F32, (128, NH_C, HC))
    pz = P.ps([128, 512])
    ptr = [P.ps([128, 1024], BF16) for _ in range(2)]
    big = [P.ps([128, 4, 128]) for _ in range(2)]
    py = P.ps([128, NH_C, HC])
    pp = P.ps([128, NH_C, HC])
    pg = P.ps([128, 512])

    for (t_, d_, k_) in [(ident, identd, "ident"), (tri, trid, "tri"), (mup, mupd, "mup"), (mlo, mlod, "mlo"), (sel, seld, "sel"),
                         (kkb, kkd, "kkb"), (kab, kad, "kab"), (rkb, rkd, "rkb")]:
        P.dma("sp", t_[:], d_[:], writes=[k_])
    for d in range(2):
        P.dma("sp", w2[d][:], w2d[d], writes=[("w2", d)])
        P.dma("sp", a2[d][:], a2d[d], writes=[("a2", d)])
        P.dma("sp", w0b[d][:], w0d[d], writes=[("w0b", d)])
        P.dma("sp", a0b[d][:], a0d[d], writes=[("a0b", d)])
        P.op("pool", lambda e, d=d: e.memset(Pb[d][:], 0.0), writes=[("Pb", d)])
    P.op("dve", lambda e: e.tensor_copy(out=identb[:], in_=ident[:]), reads=["ident"], writes=["identb"])
    P.op("dve", lambda e: e.tensor_tensor(out=mupi[:], in0=mup[:], in1=ident[:], op=ALU.add), reads=["mup", "ident"], writes=["mupi"])
    P.op("dve", lambda e: e.tensor_scalar(out=omka[:], in0=kab[:], scalar1=-1.0, scalar2=None, op0=ALU.mult),
         reads=["kab"], writes=["omka"])
    P.op("dve", lambda e: e.tensor_scalar(out=omka[:], in0=omka[:], scalar1=1.0, scalar2=None, op0=ALU.add),
         reads=["omka"], writes=["omka"])

    bi = [0]

    def emit_dir(t, d):
        rows = slice(t * 128, (t + 1) * 128)
        b = d
        dp = d * 2 + (t % 2)
        IFACE = ("Ah", "Bh", "Kh", "Rh", "Vb", "XTa", "XTb", "XTk", "XTr", "gamT")
        K = lambda nm: (nm, dp) if nm in IFACE else (nm, d)
        th, zt, ld, iclr, kkr, kk, t1, kdt = th_[d], zt_[d], ld_[d], iclr_[d], kkr_[d], kk_[d], t1_[d], kdt_[d]
        bt = kkr
        sq = zt
        ein, eneg, st8 = ein_[d], eneg_[d], st8_[d]
        eex = zt
        Ah, Bh, Kh, Rh, Vb, XT, gamT = Ah_[dp], Bh_[dp], Kh_[dp], Rh_[dp], Vb_[dp], XT_[dp], gamT_[dp]
        ArbT, ArkT, TAb, TVb, MTb, RQTb = ArbT_[d], ArkT_[d], TAb_[d], TVb_[d], MTb_[d], RQTb_[d]
        main = []
        P._defer = main

        def sigmoid_tail(dst, key):
            P.op("act", lambda e: e.activation(out=zt[:], in_=zt[:], func=AF.Exp, scale=-1.0), reads=[K("zt")], writes=[K("zt")])
            P.op("dve", lambda e: e.tensor_scalar(out=zt[:], in0=zt[:], scalar1=1.0, scalar2=None, op0=ALU.add), reads=[K("zt")], writes=[K("zt")])
            P.op("dve", lambda e: e.reciprocal(out=dst, in_=zt[:]), reads=[K("zt")], writes=[key])

        P.dma("sp", rt[b][:], rd[d, rows, :], writes=[K("rt")])
        P.dma("sp", kt[b][:], kd[d, rows, :], writes=[K("kt")])
        P.dma("sp", vt[b][:], vd[d, rows, :], writes=[K("vt")])
        P.dma("sp", lw[b][:], lwd[d, :, rows], writes=[K("lw")])
        P.dma("sp", la[b][:], lad[d, :, rows], writes=[K("la")])
        P.op("act", lambda e: e.activation(out=th[:], in_=lw[b][:], func=AF.Tanh), reads=[K("lw")], writes=[K("th")])
        P.atomic_begin()
        P.op("pe", lambda e: e.matmul(pz[:], lhsT=th[:], rhs=w2[d][:], start=True, stop=True),
             reads=[K("th"), ("w2", d)], writes=["pz"])
        P.op("dve", lambda e: e.tensor_tensor(out=zt[:], in0=pz[:], in1=w0b[d][:], op=ALU.add), reads=["pz", ("w0b", d)], writes=[K("zt")])
        P.atomic_end()
        sigmoid_tail(ld[:], K("ld"))
        P.op("pool", lambda e: e.tensor_scalar(out=ld[:], in0=ld[:], scalar1=-float(np.exp(-0.5)), scalar2=None, op0=ALU.mult),
             reads=[K("ld")], writes=[K("ld")])
        P.atomic_begin()
        P.op("pe", lambda e: e.matmul(pz[:], lhsT=la[b][:], rhs=a2[d][:], start=True, stop=True),
             reads=[K("la"), ("a2", d)], writes=["pz"])
        P.op("dve", lambda e: e.tensor_tensor(out=zt[:], in0=pz[:], in1=a0b[d][:], op=ALU.add), reads=["pz", ("a0b", d)], writes=[K("zt")])
        P.atomic_end()
        sigmoid_tail(iclr[:], K("iclr"))
        h3 = lambda ap: ap.rearrange("p (h c) -> p h c", c=HC)
        P.op("pool", lambda e: e.tensor_tensor(out=kkr[:], in0=kt[b][:], in1=kkb[:], op=ALU.mult), reads=[K("kt"), "kkb"], writes=[K("kkr")])
        P.op("pool", lambda e: e.tensor_tensor(out=sq[:], in0=kkr[:], in1=kkr[:], op=ALU.mult), reads=[K("kkr")], writes=[K("zt")])
        P.op("dve", lambda e: e.tensor_reduce(out=st8[0][:], in_=h3(sq[:]), axis=AX.X, op=ALU.add), reads=[K("zt")], writes=[K("st0")])
        P.op("act", lambda e: e.activation(out=st8[0][:], in_=st8[0][:], func=AF.Sqrt), reads=[K("st0")], writes=[K("st0")])
        P.op("dve", lambda e: e.tensor_scalar(out=st8[0][:], in0=st8[0][:], scalar1=1e-12, scalar2=None, op0=ALU.max), reads=[K("st0")], writes=[K("st0")])
        P.op("dve", lambda e: e.reciprocal(out=st8[1][:], in_=st8[0][:]), reads=[K("st0")], writes=[K("st1")])
        P.op("dve", lambda e: e.tensor_tensor(out=h3(kk[:]), in0=h3(kkr[:]), in1=st8[1][:].unsqueeze(2).to_broadcast([128, NH_C, HC]), op=ALU.mult),
             reads=[K("kkr"), K("st1")], writes=[K("kk")])
        P.op("dve", lambda e: e.tensor_tensor(out=t1[:], in0=iclr[:], in1=kab[:], op=ALU.mult), reads=[K("iclr"), "kab"], writes=[K("t1")])
        P.op("pool", lambda e: e.tensor_tensor(out=t1[:], in0=t1[:], in1=omka[:], op=ALU.add), reads=[K("t1"), "omka"], writes=[K("t1")])
        P.op("dve", lambda e: e.tensor_tensor(out=kdt[:], in0=kt[b][:], in1=t1[:], op=ALU.mult), reads=[K("kt"), K("t1")], writes=[K("kdt")])
        P.op("pool", lambda e: e.tensor_tensor(out=bt[:], in0=kk[:], in1=iclr[:], op=ALU.mult), reads=[K("kk"), K("iclr")], writes=[K("kkr")])
        P.op("dve", lambda e: e.tensor_tensor(out=t1[:], in0=rt[b][:], in1=kdt[:], op=ALU.mult), reads=[K("rt"), K("kdt"), K("t1")], writes=[K("t1")])
        P.op("pool", lambda e: e.tensor_tensor(out=t1[:], in0=t1[:], in1=rkb[:], op=ALU.mult), reads=[K("t1"), "rkb"], writes=[K("t1")])
        P.op("dve", lambda e: e.tensor_reduce(out=st8[2][:], in_=h3(t1[:]), axis=AX.X, op=ALU.add), reads=[K("t1")], writes=[K("st2")])
        P.op("dve", lambda e: e.tensor_tensor(out=h3(t1[:]), in0=h3(vt[b][:]), in1=st8[2][:].unsqueeze(2).to_broadcast([128, NH_C, HC]), op=ALU.mult),
             reads=[K("vt"), K("st2"), K("t1")], writes=[K("t1")])
        P.dma("pool", bout[d, rows, :], t1[:], reads=[K("t1")], writes=["bout"])
        P.atomic_begin()
        P.op("pe", lambda e: e.matmul(pz[:], lhsT=tri[:], rhs=ld[:], start=True, stop=True), reads=["tri", K("ld")], writes=["pz"])
        P.op("act", lambda e: e.activation(out=ein[:], in_=pz[:], func=AF.Exp), reads=["pz"], writes=[K("ein")])
        P.op("act", lambda e: e.activation(out=eneg[:], in_=pz[:], func=AF.Exp, scale=-1.0), reads=["pz"], writes=[K("eneg")])
        P.op("dve", lambda e: e.tensor_tensor(out=eex[:], in0=pz[:], in1=ld[:], op=ALU.subtract), reads=["pz", K("ld")], writes=[K("zt")])
        P.atomic_end()
        P.op("act", lambda e: e.activation(out=eex[:], in_=eex[:], func=AF.Exp), reads=[K("zt")], writes=[K("zt")])
        P.op("dve", lambda e: e.scalar_tensor_tensor(out=Ah[:], in0=kk[:], scalar=-1.0, in1=eex[:], op0=ALU.mult, op1=ALU.mult),
             reads=[K("kk"), K("zt")], writes=[K("Ah")])
        P.op("pool", lambda e: e.tensor_tensor(out=Bh[:], in0=bt[:], in1=eneg[:], op=ALU.mult), reads=[K("kkr"), K("eneg")], writes=[K("Bh")])
        P.op("dve", lambda e: e.tensor_tensor(out=Kh[:], in0=kdt[:], in1=eneg[:], op=ALU.mult), reads=[K("kdt"), K("eneg")], writes=[K("Kh")])
        P.op("pool", lambda e: e.tensor_tensor(out=Rh[:], in0=rt[b][:], in1=ein[:], op=ALU.mult), reads=[K("rt"), K("ein")], writes=[K("Rh")])
        P.op("act", lambda e: e.copy(out=Vb[:], in_=vt[b][:]), reads=[K("vt")], writes=[K("Vb")])
        P.atomic_begin()
        for h in range(NH_C):
            P.op("pe", lambda e, h=h: e.matmul(pg[:HC, h:h + 1], lhsT=ein[:, h * HC:(h + 1) * HC], rhs=sel[:], start=True, stop=True),
                 reads=[K("ein"), "sel"], writes=["pg"])
        P.op("act", lambda e: e.copy(out=gamT[:], in_=pg[:HC, :NH_C]), reads=["pg"], writes=[K("gamT")])
        P.atomic_end()
        for xi, (nm, src_t, skey) in enumerate([("a", Ah, "Ah"), ("b", Bh, "Bh"), ("k", Kh, "Kh"), ("r", Rh, "Rh")]):
            pt = ptr[xi % 2]
            P.atomic_begin()
            for h in range(NH_C):
                P.op("pe", lambda e, h=h, pt=pt, src_t=src_t: e.transpose(out=pt[:HC, h * 128:(h + 1) * 128], in_=src_t[:, h * HC:(h + 1) * HC],
                                                                          identity=identb[:]),
                     reads=[K(skey), "identb"], writes=[("ptr", xi % 2)])
            if xi % 2 == 0:
                P.op("act", lambda e, nm=nm, pt=pt: e.copy(out=XT[nm][:].rearrange("p h t -> p (h t)"), in_=pt[:HC, :]),
                     reads=[("ptr", xi % 2)], writes=[K("XT" + nm)])
                P.atomic_end()
            else:
                P.op("dve", lambda e, nm=nm, pt=pt: e.tensor_copy(out=XT[nm][:].rearrange("p h t -> p (h t)"), in_=pt[:HC, :]),
                     reads=[("ptr", xi % 2)], writes=[K("XT" + nm)])
                P.atomic_end()
        qlists = []

        def emit_quad(qd_):
            ql = []
            P._defer = ql
            qlists.append(ql)
            Q = lambda nm, qd_=qd_: (nm, d, qd_)
            hs = [qd_ * 4 + i for i in range(4)]
            Nf, NTf, TT, TTb, AakT, AVb = Nf_[d][qd_], NTf_[d][qd_], TT_[d][qd_], TTb_[d][qd_], AakT_[d][qd_], AVb_[d][qd_]
            TTh = TTh_[d][qd_]

            def mm4(lhs_fn, rhs_fn, rows_, cols_, rkeys, hs=hs):
                p = bi[0] % 2
                bi[0] += 1
                P.atomic_begin()
                for i, h in enumerate(hs):
                    P.op("pe", lambda e, i=i, h=h, p=p: e.matmul(big[p][:rows_, i, :cols_], lhsT=lhs_fn(i, h), rhs=rhs_fn(i, h),
                                                               start=True, stop=True), reads=rkeys, writes=[("big", p)])
                return p

            def EV(*a_, **k_):
                P.op(*a_, **k_)
                P.atomic_end()

            msk = lambda m_: m_[:, None, :].to_broadcast([128, 4, 128])
            p = mm4(lambda i, h: XT["a"][:, h, :], lambda i, h: XT["b"][:, h, :], 128, 128, [K("XTa"), K("XTb")])
            EV("dve", lambda e, p=p: e.tensor_tensor(out=R32(Nf[0][:]), in0=big[p][:], in1=msk(mlo), op=ALU.mult),
                 reads=[("big", p), "mlo"], writes=[Q("Nf0")])
            p = mm4(lambda i, h: XT["b"][:, h, :], lambda i, h: XT["a"][:, h, :], 128, 128, [K("XTa"), K("XTb")])
            EV("dve", lambda e, p=p: e.tensor_tensor(out=R32(NTf[0][:]), in0=big[p][:], in1=msk(mup), op=ALU.mult),
                 reads=[("big", p), "mup"], writes=[Q("NTf0")])
            P.op("dve", lambda e: e.tensor_tensor(out=R32(TT[:]), in0=NTf[0][:], in1=msk(ident), op=ALU.add),
                 reads=[Q("NTf0"), "ident"], writes=[Q("TT")])
            if MC_BF16INV:
                P.op("act", lambda e: e.copy(out=TTh[:], in_=TT[:]), reads=[Q("TT")], writes=[Q("TTh")])
            p = mm4(lambda i, h: XT["k"][:, h, :], lambda i, h: XT["a"][:, h, :], 128, 128, [K("XTa"), K("XTk")])
            EV("dve", lambda e, p=p: e.tensor_tensor(out=AakT[:], in0=big[p][:], in1=msk(mup), op=ALU.mult),
                 reads=[("big", p), "mup"], writes=[Q("AakT")])
            p = mm4(lambda i, h: XT["b"][:, h, :], lambda i, h: XT["r"][:, h, :], 128, 128, [K("XTr"), K("XTb")])
            EV("dve", lambda e, p=p, qd_=qd_: e.tensor_tensor(out=ArbT[:, qd_ * 4:(qd_ + 1) * 4, :], in0=big[p][:], in1=msk(mupi), op=ALU.mult),
                 reads=[("big", p), "mupi"], writes=[Q("ArbT")])
            p = mm4(lambda i, h: XT["k"][:, h, :], lambda i, h: XT["r"][:, h, :], 128, 128, [K("XTr"), K("XTk")])
            EV("dve", lambda e, p=p, qd_=qd_: e.tensor_tensor(out=ArkT[:, qd_ * 4:(qd_ + 1) * 4, :], in0=big[p][:], in1=msk(mupi), op=ALU.mult),
                 reads=[("big", p), "mupi"], writes=[Q("ArkT")])
            cur = 0
            for lvl in range(1, 7):
                nxt = 1 - cur
                p = mm4(lambda i, h, cur=cur: R32(NTf[cur][:, i, :]), lambda i, h, cur=cur: R32(Nf[cur][:, i, :]), 128, 128, [Q("Nf%d" % cur), Q("NTf%d" % cur)])
                EV("act", lambda e, p=p, nxt=nxt: e.copy(out=R32(Nf[nxt][:]), in_=big[p][:]), reads=[("big", p)], writes=[Q("Nf%d" % nxt)])
                if lvl < 6:
                    p = mm4(lambda i, h, cur=cur: R32(Nf[cur][:, i, :]), lambda i, h, cur=cur: R32(NTf[cur][:, i, :]), 128, 128, [Q("Nf%d" % cur), Q("NTf%d" % cur)])
                    EV("act", lambda e, p=p, nxt=nxt: e.copy(out=R32(NTf[nxt][:]), in_=big[p][:]), reads=[("big", p)], writes=[Q("NTf%d" % nxt)])
                if MC_BF16INV:
                    p = mm4(lambda i, h, nxt=nxt: Nf[nxt][:, i, :], lambda i, h: TTh[:, i, :], 128, 128, [Q("Nf%d" % nxt), Q("TTh")])
                else:
                    p = mm4(lambda i, h, nxt=nxt: R32(Nf[nxt][:, i, :]), lambda i, h: R32(TT[:, i, :]), 128, 128, [Q("Nf%d" % nxt), Q("TT")])
                EV("dve", lambda e, p=p: e.tensor_tensor(out=R32(TT[:]), in0=big[p][:], in1=TT[:], op=ALU.add), reads=[("big", p), Q("TT")], writes=[Q("TT")])
                if MC_BF16INV and lvl < 6:
                    P.op("act", lambda e: e.copy(out=TTh[:], in_=TT[:]), reads=[Q("TT")], writes=[Q("TTh")])
                cur = nxt
            P.op("act", lambda e: e.copy(out=TTb[:], in_=TT[:]), reads=[Q("TT")], writes=[Q("TTb")])
            qs = slice(qd_ * 4, (qd_ + 1) * 4)
            p = mm4(lambda i, h: TTb[:, i, :], lambda i, h: Ah[:, h * HC:(h + 1) * HC], 128, HC, [Q("TTb"), K("Ah")])
            EV("act", lambda e, p=p, qs=qs: e.copy(out=TAb[:, qs, :], in_=big[p][:, :, :HC]), reads=[("big", p)], writes=[Q("TAb")])
            p = mm4(lambda i, h: AakT[:, i, :], lambda i, h: Vb[:, h * HC:(h + 1) * HC], 128, HC, [Q("AakT"), K("Vb")])
            EV("dve", lambda e, p=p: e.tensor_copy(out=AVb[:], in_=big[p][:, :, :HC]), reads=[("big", p)], writes=[Q("AVb")])
            p = mm4(lambda i, h: TTb[:, i, :], lambda i, h: AVb[:, i, :], 128, HC, [Q("TTb"), Q("AVb")])
            EV("act", lambda e, p=p, qs=qs: e.copy(out=TVb[:, qs, :], in_=big[p][:, :, :HC]), reads=[("big", p)], writes=[Q("TVb")])
            p = mm4(lambda i, h: TAb[:, h, :], lambda i, h: Bh[:, h * HC:(h + 1) * HC], HC, HC, [Q("TAb"), K("Bh")])
            EV("dve", lambda e, p=p, qs=qs: e.tensor_tensor(out=MTb[:, qs, :], in0=big[p][:HC, :, :HC],
                                                            in1=ident[:HC, None, :HC].to_broadcast([HC, 4, HC]), op=ALU.add),
                 reads=[("big", p), "ident"], writes=[Q("MTb")])
            p = mm4(lambda i, h: TAb[:, h, :], lambda i, h: ArbT[:, h, :], HC, 128, [Q("TAb"), Q("ArbT")])
            EV("dve", lambda e, p=p, qs=qs: e.tensor_tensor(out=RQTb[:, qs, :], in0=big[p][:HC, :, :], in1=XT["r"][:, qs, :], op=ALU.add),
                 reads=[("big", p), K("XTr")], writes=[Q("RQTb")])
        for qd_ in range(2):
            emit_quad(qd_)
        tail = []
        P._defer = tail
        allq = lambda nm: [(nm, d, 0), (nm, d, 1)]
        P.atomic_begin()
        for h in range(NH_C):
            hc = slice(h * HC, (h + 1) * HC)
            P.op("pe", lambda e, h=h: e.matmul(py[:, h, :], lhsT=ArbT[:, h, :], rhs=TVb[:, h, :], start=True, stop=False),
                 reads=allq("ArbT") + allq("TVb") + allq("ArkT") + allq("RQTb") + [K("Vb"), ("Pb", d)], writes=["py"])
            P.op("pe", lambda e, h=h, hc=hc: e.matmul(py[:, h, :], lhsT=ArkT[:, h, :], rhs=Vb[:, hc], start=False, stop=False),
                 reads=[], writes=["py"])
            P.op("pe", lambda e, h=h: e.matmul(py[:, h, :], lhsT=RQTb[:, h, :], rhs=Pb[d][:, h, :], start=False, stop=True),
                 reads=[("Pb", d)], writes=["py"])
        P.op("act", lambda e: e.copy(out=ysb[b][:], in_=py[:]), reads=["py"], writes=[K("ysb")])
        P.atomic_end()
        P.dma("pool", yout[d, rows, :], ysb[b][:].rearrange("p h c -> p (h c)"), reads=[K("ysb")], writes=["yout"])
        P.atomic_begin()
        for h in range(NH_C):
            hc = slice(h * HC, (h + 1) * HC)
            P.op("pe", lambda e, h=h, hc=hc: e.matmul(pp[:HC, h, :], lhsT=Bh[:, hc], rhs=TVb[:, h, :], start=True, stop=False),
                 reads=allq("TVb") + allq("MTb") + [K("Bh"), K("Kh"), K("Vb"), ("Pb", d)], writes=["pp"])
            P.op("pe", lambda e, h=h, hc=hc: e.matmul(pp[:HC, h, :], lhsT=Kh[:, hc], rhs=Vb[:, hc], start=False, stop=False),
                 reads=[], writes=["pp"])
            P.op("pe", lambda e, h=h: e.matmul(pp[:HC, h, :], lhsT=MTb[:, h, :], rhs=Pb[d][:, h, :], start=False, stop=True),
                 reads=[("Pb", d)], writes=["pp"])
        P.op("dve", lambda e: e.tensor_tensor(out=Pb[d][:], in0=pp[:HC, :, :], in1=gamT[:].unsqueeze(2).to_broadcast([HC, NH_C, HC]),
                                              op=ALU.mult), reads=["pp", K("gamT")], writes=[("Pb", d)])
        P.atomic_end()
        P._defer = None
        return main, qlists, tail

    for t in range(ntile):
        if t == 0:
            nxt_parts = [emit_dir(0, d) for d in range(2)]
            for d in range(2):
                P.interleave([nxt_parts[d][0]])
        parts = nxt_parts
        streams = parts[0][1] + parts[1][1]
        if t + 1 < ntile:
            nxt_parts = [emit_dir(t + 1, d) for d in range(2)]
            if MC_PIPE:
                streams = streams + [nxt_parts[0][0] + nxt_parts[1][0]]
        P.interleave(streams)
        if t + 1 < ntile and not MC_PIPE:
            for d in range(2):
                P.interleave([nxt_parts[d][0]])
        for d in range(2):
            P.interleave([parts[d][2]])
    P.finish(["yout", "bout"])
    return nc


def run_Mc(pl, pc, prm):
    W = NH_C * HC
    nc = build_Mc(L_SEQ)
    tt, ss = np.meshgrid(np.arange(128), np.arange(128), indexing="xy")
    consts = {"tri": (ss <= tt).astype(np.float32), "mup": (ss < tt).astype(np.float32), "mlo": (ss > tt).astype(np.float32),
              "ident": np.eye(128, dtype=np.float32), "sel": (np.arange(128) == 127).astype(np.float32)[:, None]}
    rk_flat = prm["r_k"].reshape(-1)
    in_maps = []
    for cid in range(NCORES):
        b, hg = cid // 4, cid % 4
        cols = slice(hg * W, (hg + 1) * W)
        m = dict(consts)
        for nm, off in (("r", 0), ("k", D), ("v", 2 * D)):
            m[nm] = np.stack([_seq(pl[b][:, off + hg * W:off + (hg + 1) * W], pc[b][:, off + hg * W:off + (hg + 1) * W], dr == 1)
                              for dr in range(2)])
        m["lwT"] = np.stack([np.ascontiguousarray(_seq(pl[b][:, 4 * D + dr * 128:4 * D + dr * 128 + LORA],
                                                       pc[b][:, 4 * D + dr * 128:4 * D + dr * 128 + LORA], dr == 1).T) for dr in range(2)])
        m["laT"] = np.stack([np.ascontiguousarray(_seq(pl[b][:, 4 * D + 256 + dr * 128:4 * D + 256 + dr * 128 + LORA],
                                                       pc[b][:, 4 * D + 256 + dr * 128:4 * D + 256 + dr * 128 + LORA], dr == 1).T) for dr in range(2)])
        m["w2"] = np.ascontiguousarray(prm["w2"][:, :, cols])
        m["a2"] = np.ascontiguousarray(prm["a2"][:, :, cols])
        m["w0b"] = np.stack([_bc(prm["w0"][dr, cols]) for dr in range(2)])
        m["a0b"] = np.stack([_bc(prm["a0"][dr, cols]) for dr in range(2)])
        m["kkb"] = _bc(prm["k_k"][cols])
        m["kab"] = _bc(prm["k_a"][cols])
        m["rkb"] = _bc(rk_flat[cols])
        in_maps.append(m)
    res = _run(nc, in_maps)
    names = ["m0", "m1", "m2", "m3"]
    lat = {nm: np.empty((2, T_LAT, D), np.float32) for nm in names}
    cx = {nm: np.empty((2, T_CTX, D), np.float32) for nm in names}
    for cid in range(NCORES):
        b, hg = cid // 4, cid % 4
        cols = slice(hg * W, (hg + 1) * W)
        for dr in range(2):
            for key, nm in (("y", "m%d" % dr), ("bon", "m%d" % (2 + dr))):
                l, c = _unseq(res[cid][key][dr], dr == 1)
                lat[nm][b][:, cols] = l
                cx[nm][b][:, cols] = c
    return lat, cx


def layer_c(x, ctx, mod_i, norm_w_i, prm, need_ctx, final_w=None):
    pad = lambda w: np.concatenate([w, np.zeros((D, 128 - w.shape[1]), np.float32)], axis=1)
    wcat = np.concatenate([prm["w_in"][0], prm["w_in"][1], prm["w_in"][2], prm["w_in"][3],
                           pad(prm["w1"][0]), pad(prm["w1"][1]), pad(prm["a1"][0]), pad(prm["a1"][1])], axis=1)
    lerp = [0] * 16 + [1] * 16 + [2] * 16 + [3] * 16 + [4, 4, 5, 5]
    pl, pc = run_P(x, ctx, mod_i, norm_w_i, wcat, lerp=lerp, mu=prm["mu"])
    lat, cx = run_Mc(pl, pc, prm)
    lat["gate"] = pl[:, :, 3 * D:4 * D]
    cx["gate"] = pc[:, :, 3 * D:4 * D]
    bcs = {"lnw": prm["ln_w"], "lnb": prm["ln_b"]}
    if final_w is not None:
        bcs["fnw"] = final_w
    return run_O(x, ctx, "c", lat, cx, prm["w_out"], mod_i[0:2, 2 * D:3 * D], mod_i[2, 2 * D:3 * D], bcs,
                 final=final_w is not None, need_ctx=need_ctx)


def kernel(x, c, ctx, c_ctx, norm_w, mod_w, mod_b, a_w_in, a_lb_raw, a_onorm_w, a_w_out,
           b_w_in, b_sink, b_w_out, c_mu, c_w_in, c_w0, c_w1, c_w2, c_a0, c_a1, c_a2,
           c_k_k, c_k_a, c_r_k, c_ln_w, c_ln_b, c_w_out, final_norm_w):
    f = lambda a: np.asarray(a, dtype=np.float32)
    x, c, ctx, c_ctx = f(x), f(c), f(ctx), f(c_ctx)
    mod = run_mod(c, f(c_ctx), f(mod_w), f(mod_b))
    depth = 4
    for i in range(depth):
        j, kind = i // 3, i % 3
        need_ctx = i < depth - 1
        fw = f(final_norm_w) if i == depth - 1 else None
        mod_i = np.ascontiguousarray(mod[:, i])
        if kind == 0:
            x, ctx = layer_a(x, ctx, mod_i, f(norm_w[i]), f(a_w_in[j]), f(a_lb_raw), j, f(a_onorm_w[j]), f(a_w_out[j]), need_ctx, fw)
        elif kind == 1:
            x, ctx_n = layer_b(x, ctx, mod_i, f(norm_w[i]), f(b_w_in[j]), f(b_sink[j]), f(b_w_out[j]), need_ctx, fw)
            ctx = ctx_n if need_ctx else ctx
        else:
            prm = {"mu": f(c_mu[j]), "w_in": f(c_w_in[j]), "w0": f(c_w0[j]), "w1": f(c_w1[j]), "w2": f(c_w2[j]),
                   "a0": f(c_a0[j]), "a1": f(c_a1[j]), "a2": f(c_a2[j]), "k_k": f(c_k_k[j]), "k_a": f(c_k_a[j]),
                   "r_k": f(c_r_k[j]), "ln_w": f(c_ln_w[j]), "ln_b": f(c_ln_b[j]), "w_out": f(c_w_out[j])}
            x, ctx_n = layer_c(x, ctx, mod_i, f(norm_w[i]), prm, need_ctx, fw)
            ctx = ctx_n if need_ctx else ctx
    return x.astype(np.float32)
```

```python
import numpy as np
from contextlib import ExitStack
import concourse.bass as bass
import concourse.mybir as mybir
from concourse.bass_utils import run_bass_kernel_spmd

F32 = mybir.dt.float32
BF16 = mybir.dt.bfloat16
AF = mybir.ActivationFunctionType
ALU = mybir.AluOpType
AX = mybir.AxisListType

NCORES = 8
D = 2048
KC = D // 128


class Prog:
    CE = ("pe", "act", "dve", "pool")

    def __init__(self, nc, ndma=8):
        self.nc = nc
        self.es = ExitStack()
        self.eng = {"pe": nc.tensor, "act": nc.scalar, "dve": nc.vector, "pool": nc.gpsimd, "sp": nc.sync}
        self.sem = {}
        self.cnt = {}
        for e in self.CE:
            self.sem[("e", e)] = self.es.enter_context(nc.semaphore("s_" + e))
            self.cnt[("e", e)] = 0
        self.ndma = ndma
        self.dma_rr = {}
        for q in ("sp", "pool", "act"):
            self.dma_rr[q] = 0
            for i in range(ndma):
                k = ("d", q, i)
                self.sem[k] = self.es.enter_context(nc.semaphore("d_%s%d" % (q, i)))
                self.cnt[k] = 0
        self.known = {e: {} for e in self.eng}
        self.last_w = {}
        self.readers = {}
        self.ninstr = 0
        self.uid = 0

    def sb(self, shape, dtype=F32, name=None, stack=None):
        self.uid += 1
        return (stack or self.es).enter_context(self.nc.sbuf_tensor(name or ("sb%d" % self.uid), list(shape), dtype))

    def barrier(self):
        for e in ("pe", "act", "dve", "pool", "sp"):
            kn = self.known[e]
            for k, v in self.cnt.items():
                if v > 0 and kn.get(k, 0) < v:
                    self.eng[e].wait_ge(self.sem[k], v)
                    kn[k] = v
                    self.ninstr += 1

    def ps(self, shape, dtype=F32, name=None):
        self.uid += 1
        return self.es.enter_context(self.nc.psum_tensor(name or ("ps%d" % self.uid), list(shape), dtype))

    def dram(self, name, shape, dtype=F32, kind="ExternalInput"):
        return self.nc.dram_tensor(name, list(shape), dtype, kind=kind).ap()

    def _deps(self, e, reads, writes):
        deps = {}

        def add(kv):
            k, v = kv
            if e == "pe" and k == ("e", "pe"):
                return
            if deps.get(k, 0) < v:
                deps[k] = v

        for b in reads:
            if b in self.last_w:
                add(self.last_w[b])
        for b in writes:
            if b in self.last_w:
                add(self.last_w[b])
            for r in self.readers.get(b, ()):
                add(r)
        kn = self.known[e]
        for k, v in deps.items():
            if kn.get(k, 0) < v:
                self.eng[e].wait_ge(self.sem[k], v)
                self.ninstr += 1
                kn[k] = v

    def _record(self, key, val, reads, writes):
        for b in writes:
            self.last_w[b] = (key, val)
            self.readers[b] = []
        for b in reads:
            self.readers.setdefault(b, []).append((key, val))
            if len(self.readers[b]) > 64:
                mx = {}
                for k, v in self.readers[b]:
                    if mx.get(k, 0) < v:
                        mx[k] = v
                self.readers[b] = list(mx.items())

    _defer = None
    _atomic = None

    def begin(self):
        self._defer = []

    def end(self):
        l, self._defer = self._defer, None
        return l

    def atomic_begin(self):
        if self._defer is not None:
            self._atomic = []

    def atomic_end(self):
        if self._defer is not None:
            self._defer.append(self._atomic)
            self._atomic = None

    def interleave(self, lists):
        n = max(len(l) for l in lists)
        for i in range(n):
            for l in lists:
                if i < len(l):
                    for (kind, args, kw) in l[i]:
                        if kind == "op":
                            self.op(*args, **kw)
                        else:
                            self.dma(*args, **kw)

    def _rec(self, kind, args, kw):
        item = (kind, args, kw)
        if self._atomic is not None:
            self._atomic.append(item)
        else:
            self._defer.append([item])

    def op(self, e, fn, reads=(), writes=()):
        if self._defer is not None:
            return self._rec("op", (e, fn, list(reads), list(writes)), {})
        self._deps(e, reads, writes)
        key = ("e", e)
        self.cnt[key] += 1
        ins = fn(self.eng[e])
        ins.then_inc(self.sem[key], 1)
        self.ninstr += 1
        self._record(key, self.cnt[key], reads, writes)
        return ins

    def dma(self, q, out, in_, reads=(), writes=(), **kw):
        if self._defer is not None:
            return self._rec("dma", (q, out, in_, list(reads), list(writes)), kw)
        i = self.dma_rr[q]
        self.dma_rr[q] = (i + 1) % self.ndma
        key = ("d", q, i)
        kn = self.known[q]
        if kn.get(key, 0) < self.cnt[key]:
            self.eng[q].wait_ge(self.sem[key], self.cnt[key])
            kn[key] = self.cnt[key]
            self.ninstr += 1
        self._deps(q, reads, writes)
        self.cnt[key] += 16
        self.eng[q].dma_start(out=out, in_=in_, **kw).then_inc(self.sem[key], 16)
        self.ninstr += 1
        self._record(key, self.cnt[key], reads, writes)

    def finish(self, out_keys):
        self._deps("pool", out_keys, ())
        for k, v in self.cnt.items():
            if k[0] == "d" and v > 0 and self.known["pool"].get(k, 0) < v:
                self.eng["pool"].wait_ge(self.sem[k], v)
                self.known["pool"][k] = v


def _run(nc, in_maps):
    res = run_bass_kernel_spmd(nc, in_maps, core_ids=list(range(NCORES)))
    return res.results


MOD_NCOL = 4 * 3 * D // NCORES


def build_mod():
    nc = bass.Bass("TRN2", target_bir_lowering=False)
    P = Prog(nc)
    ccT = P.dram("ccT", [128, KC, 3])
    w = P.dram("w", [128, KC, MOD_NCOL])
    b3 = P.dram("b3", [3, MOD_NCOL])
    out = P.dram("out", [3, MOD_NCOL], kind="ExternalOutput")
    s_in = P.sb([128, KC, 3])
    s_act = P.sb([128, KC, 3])
    bias = P.sb([3, MOD_NCOL])
    res = P.sb([3, MOD_NCOL])
    wt = [P.sb([128, KC, 512]) for _ in range(2)]
    pt = [P.ps([3, 512]) for _ in range(2)]
    P.dma("sp", s_in[:], ccT[:], writes=["s_in"])
    P.dma("sp", bias[:], b3[:], writes=["bias"])
    P.op("act", lambda e: e.activation(out=s_act[:], in_=s_in[:], func=AF.Silu), reads=["s_in"], writes=["s_act"])
    nb = MOD_NCOL // 512
    for j in range(nb):
        wb = wt[j % 2]
        pb = pt[j % 2]
        P.dma("sp", wb[:], w[:, :, j * 512:(j + 1) * 512], writes=[("w", j % 2)])
        for k in range(KC):
            P.op("pe", lambda e, k=k, wb=wb, pb=pb: e.matmul(pb[:], lhsT=s_act[:, k, :], rhs=wb[:, k, :],
                                                        start=(k == 0), stop=(k == KC - 1)),
                 reads=["s_act", ("w", j % 2)], writes=[("p", j % 2)])
        P.op("dve", lambda e, j=j, pb=pb: e.tensor_tensor(out=res[:, j * 512:(j + 1) * 512], in0=pb[:],
                                                       in1=bias[:, j * 512:(j + 1) * 512], op=ALU.add),
             reads=[("p", j % 2), "bias"], writes=["res"])
    P.dma("pool", out[:], res[:], reads=["res"], writes=["out"])
    P.finish(["out"])
    return nc


def run_mod(c, c_ctx, mod_w, mod_b):
    cc = np.concatenate([c, c_ctx[None, :]], axis=0).astype(np.float32)
    ccT = np.ascontiguousarray(cc.T.reshape(KC, 128, 3).transpose(1, 0, 2))
    wall = np.concatenate([mod_w[l] for l in range(4)], axis=1)
    ball = np.concatenate([mod_b[l] for l in range(4)], axis=0)
    in_maps = []
    for cid in range(NCORES):
        cols = slice(cid * MOD_NCOL, (cid + 1) * MOD_NCOL)
        wc = np.ascontiguousarray(wall[:, cols].reshape(KC, 128, MOD_NCOL).transpose(1, 0, 2))
        bc = np.ascontiguousarray(np.broadcast_to(ball[cols][None, :], (3, MOD_NCOL)))
        in_maps.append({"ccT": ccT, "w": wc, "b3": bc})
    nc = build_mod()
    res = _run(nc, in_maps)
    mod = np.concatenate([r["out"] for r in res], axis=1)
    return mod.reshape(3, 4, 3 * D)


def _segments(r0, n):
    segs = []
    o = 0
    while o < n:
        m = min(128, n - o)
        segs.append((r0 + o, m))
        o += m
    return segs


def build_P(n_lat, n_ctx, nblk, lerp=None):
    halo = 1 if lerp is not None else 0
    r_lat = n_lat + 2 * halo
    r_ctx = n_ctx + 2 * halo
    R = r_lat + r_ctx
    n_int = n_lat + n_ctx
    nc = bass.Bass("TRN2", target_bir_lowering=False)
    P = Prog(nc)
    xin = P.dram("xin", [R, D])
    nw = P.dram("nw", [128, D])
    scb = P.dram("scb", [2, 128, D])
    shb = P.dram("shb", [2, 128, D])
    wd = P.dram("w", [nblk, 128, KC, 128])
    identd = P.dram("ident", [128, 128])
    if lerp is not None:
        mud = P.dram("mu", [128, KC, 6])
        bmd = P.dram("bmask", [128, 4])
    out = P.dram("projT", [nblk * 128, n_int], kind="ExternalOutput")

    hT = P.sb([128, KC, R], BF16)
    tmp = P.sb([128, D])
    if lerp is not None:
        xxT = P.sb([128, KC, R], BF16)
        mu = P.sb([128, KC, 6])
        bm = P.sb([128, 4])
    tps = [P.ps([128, 4, 128], BF16) for _ in range(2)]
    mps = [P.ps([128, 512]) for _ in range(4)]
    es1 = ExitStack()
    ident_f = P.sb([128, 128], stack=es1)
    ident = P.sb([128, 128], BF16, stack=es1)
    S = [P.sb([128, D], stack=es1) for _ in range(2)]
    SH = [P.sb([128, D], stack=es1) for _ in range(2)]
    nwt = tmp
    xt = [P.sb([128, D], stack=es1) for _ in range(2)]
    sq = P.sb([128, D], BF16, stack=es1)
    hb = [P.sb([128, D], BF16, stack=es1) for _ in range(2)]
    ss = [P.sb([128, 1], stack=es1) for _ in range(2)]
    rstd = [P.sb([128, 1], stack=es1) for _ in range(2)]

    P.dma("sp", ident_f[:], identd[:], writes=["identf"])
    P.op("dve", lambda e: e.tensor_copy(out=ident[:], in_=ident_f[:]), reads=["identf"], writes=["ident"])
    P.dma("sp", nwt[:], nw[:], writes=["tmp"])
    for i in range(2):
        P.dma("sp", S[i][:], scb[i], writes=[("S", i)])
        P.dma("sp", SH[i][:], shb[i], writes=[("SH", i)])
        P.op("dve", lambda e, i=i: e.scalar_tensor_tensor(out=S[i][:], in0=S[i][:], scalar=1.0, in1=nwt[:],
                                                          op0=ALU.add, op1=ALU.mult),
             reads=[("S", i), "tmp"], writes=[("S", i)])
    if lerp is not None:
        P.dma("sp", mu[:], mud[:], writes=["mu"])
        P.dma("sp", bm[:], bmd[:], writes=["bm"])

    segs = [(r, m, 0) for (r, m) in _segments(0, r_lat)] + [(r, m, 1) for (r, m) in _segments(r_lat, r_ctx)]
    for si, (r0, m, mi) in enumerate(segs):
        b = si % 2
        P.dma("sp", xt[b][:m, :], xin[r0:r0 + m, :], writes=[("xt", b)])
        P.op("act", lambda e, b=b, m=m: e.activation(out=sq[:m, :], in_=xt[b][:m, :], func=AF.Square,
                                                      accum_out=ss[b][:m, :]),
             reads=[("xt", b)], writes=["sq", ("ss", b)])
        P.op("dve", lambda e, b=b, m=m: e.tensor_scalar(out=rstd[b][:m, :], in0=ss[b][:m, :], scalar1=1.0 / D,
                                                         scalar2=1e-6, op0=ALU.mult, op1=ALU.add),
             reads=[("ss", b)], writes=[("rstd", b)])
        P.op("act", lambda e, b=b, m=m: e.activation(out=rstd[b][:m, :], in_=rstd[b][:m, :], func=AF.Sqrt),
             reads=[("rstd", b)], writes=[("rstd", b)])
        P.op("dve", lambda e, b=b, m=m: e.reciprocal(out=rstd[b][:m, :], in_=rstd[b][:m, :]),
             reads=[("rstd", b)], writes=[("rstd", b)])
        P.op("dve", lambda e, b=b, m=m, mi=mi: e.scalar_tensor_tensor(out=tmp[:m, :], in0=xt[b][:m, :],
                                                                     scalar=rstd[b][:m, :], in1=S[mi][:m, :],
                                                                     op0=ALU.mult, op1=ALU.mult),
             reads=[("xt", b), ("rstd", b), ("S", mi)], writes=["tmp"])
        P.op("dve", lambda e, b=b, m=m, mi=mi: e.tensor_tensor(out=hb[b][:m, :], in0=tmp[:m, :], in1=SH[mi][:m, :],
                                                              op=ALU.add),
             reads=["tmp", ("SH", mi)], writes=[("hb", b)])
        for kg in range(KC // 4):
            tb = (si * 4 + kg) % 2
            for kk in range(4):
                k = kg * 4 + kk
                P.op("pe", lambda e, b=b, m=m, k=k, kk=kk, tb=tb: e.transpose(out=tps[tb][:, kk, :m],
                                                                              in_=hb[b][:m, k * 128:(k + 1) * 128],
                                                                              identity=ident[:m, :m]),
                     reads=[("hb", b), "ident"], writes=[("tps", tb)])
            eng = "act" if kg % 2 == 0 else "dve"
            if eng == "act":
                P.op("act", lambda e, m=m, kg=kg, tb=tb, r0=r0: e.copy(out=hT[:, kg * 4:(kg + 1) * 4, r0:r0 + m],
                                                                       in_=tps[tb][:, :, :m]),
                     reads=[("tps", tb)], writes=["hT"])
            else:
                P.op("dve", lambda e, m=m, kg=kg, tb=tb, r0=r0: e.tensor_copy(out=hT[:, kg * 4:(kg + 1) * 4, r0:r0 + m],
                                                                              in_=tps[tb][:, :, :m]),
                     reads=[("tps", tb)], writes=["hT"])

    if lerp is not None:
        for ci, col in enumerate([0, r_lat - 1, r_lat, R - 1]):
            P.op("dve", lambda e, ci=ci, col=col: e.tensor_scalar(out=hT[:, :, col:col + 1], in0=hT[:, :, col:col + 1],
                                                                   scalar1=bm[:, ci:ci + 1], scalar2=None, op0=ALU.mult),
                 reads=["hT", "bm"], writes=["hT"])
        for (c0, n) in [(0, r_lat), (r_lat, r_ctx)]:
            for k in range(KC):
                w_ = n - 2
                P.op("dve", lambda e, k=k, c0=c0, w_=w_: e.tensor_tensor(out=tmp[:, :w_], in0=hT[:, k, c0:c0 + w_],
                                                                        in1=hT[:, k, c0 + 2:c0 + 2 + w_], op=ALU.add),
                     reads=["hT"], writes=["tmp"])
                P.op("dve", lambda e, k=k, c0=c0, w_=w_: e.scalar_tensor_tensor(out=xxT[:, k, c0 + 1:c0 + 1 + w_],
                                                                               in0=tmp[:, :w_], scalar=0.5,
                                                                               in1=hT[:, k, c0 + 1:c0 + 1 + w_],
                                                                               op0=ALU.mult, op1=ALU.subtract),
                     reads=["tmp", "hT"], writes=["xxT"])

    P.barrier()
    es1.close()
    wf = [P.sb([128, KC, 128]) for _ in range(2)]
    wb = [P.sb([128, KC, 128], BF16) for _ in range(2)]
    stage = [P.sb([128, n_int]) for _ in range(2)]
    if lerp is not None:
        wb2 = [P.sb([128, KC, 128], BF16) for _ in range(2)]
    groups = []
    o = 0
    while o < n_lat:
        n = min(512, n_lat - o)
        groups.append((halo + o, o, n))
        o += n
    groups.append((r_lat + halo, n_lat, n_ctx))
    gi = 0
    for j in range(nblk):
        b = j % 2
        P.dma("sp", wf[b][:], wd[j], writes=[("wf", b)])
        if j % 2 == 0:
            P.op("act", lambda e, b=b: e.copy(out=wb[b][:], in_=wf[b][:]), reads=[("wf", b)], writes=[("wb", b)])
        else:
            P.op("pool", lambda e, b=b: e.tensor_copy(out=wb[b][:], in_=wf[b][:]), reads=[("wf", b)], writes=[("wb", b)])
        if lerp is not None:
            n_mu = lerp[j]
            P.op("pool", lambda e, b=b, n_mu=n_mu: e.tensor_tensor(out=wb2[b][:], in0=wf[b][:],
                                                                   in1=mu[:, :, n_mu:n_mu + 1].to_broadcast([128, KC, 128]),
                                                                   op=ALU.mult),
                 reads=[("wf", b), "mu"], writes=[("wb2", b)])
        for (hc, oc, n) in groups:
            pb = gi % 4
            gi += 1
            nmm = KC * (2 if lerp is not None else 1)
            for k in range(KC):
                P.op("pe", lambda e, b=b, k=k, pb=pb, hc=hc, n=n: e.matmul(mps[pb][:, :n], lhsT=wb[b][:, k, :],
                                                                         rhs=hT[:, k, hc:hc + n],
                                                                         start=(k == 0), stop=(k == nmm - 1)),
                     reads=[("wb", b), "hT"], writes=[("mps", pb)])
            if lerp is not None:
                for k in range(KC):
                    P.op("pe", lambda e, b=b, k=k, pb=pb, hc=hc, n=n: e.matmul(mps[pb][:, :n], lhsT=wb2[b][:, k, :],
                                                                             rhs=xxT[:, k, hc:hc + n],
                                                                             start=False, stop=(k == KC - 1)),
                         reads=[("wb2", b), "xxT"], writes=[("mps", pb)])
            if gi % 2 == 0:
                P.op("act", lambda e, b=b, pb=pb, oc=oc, n=n: e.copy(out=stage[b][:, oc:oc + n], in_=mps[pb][:, :n]),
                     reads=[("mps", pb)], writes=[("stage", b)])
            else:
                P.op("dve", lambda e, b=b, pb=pb, oc=oc, n=n: e.tensor_copy(out=stage[b][:, oc:oc + n], in_=mps[pb][:, :n]),
                     reads=[("mps", pb)], writes=[("stage", b)])
        P.dma("pool", out[j * 128:(j + 1) * 128, :], stage[b][:], reads=[("stage", b)], writes=["out"])
    P.finish(["out"])
    return nc


def _bc(v):
    return np.ascontiguousarray(np.broadcast_to(np.asarray(v, np.float32)[None, :], (128, v.shape[-1])))


def _wblocks(w):
    ncols = w.shape[1]
    nblk = (ncols + 127) // 128
    if nblk * 128 != ncols:
        w = np.concatenate([w, np.zeros((D, nblk * 128 - ncols), np.float32)], axis=1)
    return np.ascontiguousarray(w.reshape(KC, 128, nblk, 128).transpose(2, 1, 0, 3))


N_LAT = 2048
N_CTX = 64
T_LAT = 8192
T_CTX = 256


def _core_rows(x, ctx, cid, halo=0):
    b, q = cid // 4, cid % 4

    def take(a, lo, hi):
        T = a.shape[0]
        rows = []
        if lo < 0:
            rows.append(np.zeros((-lo, a.shape[1]), a.dtype))
        rows.append(a[max(lo, 0):min(hi, T)])
        if hi > T:
            rows.append(np.zeros((hi - T, a.shape[1]), a.dtype))
        return np.concatenate(rows, axis=0) if len(rows) > 1 else rows[0]

    lat = take(x[b], q * N_LAT - halo, (q + 1) * N_LAT + halo)
    cx = take(ctx[b], q * N_CTX - halo, (q + 1) * N_CTX + halo)
    return np.ascontiguousarray(np.concatenate([lat, cx], axis=0))


def _gather_rows(outs, width):
    x = np.empty((2, T_LAT, width), np.float32)
    ctx = np.empty((2, T_CTX, width), np.float32)
    for cid in range(NCORES):
        b, q = cid // 4, cid % 4
        x[b, q * N_LAT:(q + 1) * N_LAT] = outs[cid][:N_LAT]
        ctx[b, q * N_CTX:(q + 1) * N_CTX] = outs[cid][N_LAT:]
    return x, ctx


def run_P(x, ctx, mod_i, norm_w_i, wcat, lerp=None, mu=None):
    wb = _wblocks(wcat)
    nblk = wb.shape[0]
    halo = 1 if lerp is not None else 0
    nc = build_P(N_LAT, N_CTX, nblk, lerp)
    ident = np.eye(128, dtype=np.float32)
    nw = _bc(norm_w_i)
    in_maps = []
    for cid in range(NCORES):
        b, q = cid // 4, cid % 4
        m = {"xin": _core_rows(x, ctx, cid, halo), "nw": nw, "w": wb, "ident": ident,
             "scb": np.stack([_bc(mod_i[b, D:2 * D]), _bc(mod_i[2, D:2 * D])]),
             "shb": np.stack([_bc(mod_i[b, 0:D]), _bc(mod_i[2, 0:D])])}
        if lerp is not None:
            m["mu"] = np.ascontiguousarray(mu.T.reshape(KC, 128, 6).transpose(1, 0, 2))
            bm = np.ones((128, 4), np.float32)
            if q == 0:
                bm[:, 0] = 0.0
                bm[:, 2] = 0.0
            if q == 3:
                bm[:, 1] = 0.0
                bm[:, 3] = 0.0
            m["bmask"] = bm
        in_maps.append(m)
    res = _run(nc, in_maps)
    outs = [np.ascontiguousarray(r["projT"].T) for r in res]
    return _gather_rows(outs, nblk * 128)


O_INS = {"a": ["m0", "m1", "gate"], "b": ["m0", "gate"], "c": ["m0", "m1", "m2", "m3", "gate"]}


def build_O(n_lat, n_ctx, kind, final=False):
    R = n_lat + n_ctx
    nc = bass.Bass("TRN2", target_bir_lowering=False)
    P = Prog(nc)
    xin = P.dram("xin", [R, D])
    ins_d = {nm: P.dram(nm, [R, D]) for nm in O_INS[kind]}
    wod = P.dram("wo", [128, KC, D])
    gbd = P.dram("gb", [2, 128, D])
    identd = P.dram("ident", [128, 128])
    nbc = {"a": ["onw"], "b": [], "c": ["lnw", "lnb"]}[kind] + (["fnw"] if final else [])
    bc_d = {nm: P.dram(nm, [128, D]) for nm in nbc}
    out = P.dram("xout", [R, D], kind="ExternalOutput")

    ident_f = P.sb([128, 128])
    ident = P.sb([128, 128], BF16)
    wo = P.sb([128, KC, D], BF16)
    gb = [P.sb([128, D]) for _ in range(2)]
    bc = {nm: P.sb([128, D]) for nm in nbc}
    xt = [P.sb([128, D]) for _ in range(2)]
    xo = [P.sb([128, D]) for _ in range(2)]
    NS = 2 if kind == "c" else 4
    it = {nm: [P.sb([128, 512]) for _ in range(NS)] for nm in O_INS[kind]}
    t1s = [P.sb([128, 512]) for _ in range(NS)]
    t2s = [P.sb([128, 512]) for _ in range(NS)]
    t3s = [P.sb([128, 512]) for _ in range(NS)]
    sgs = [P.sb([128, 512]) for _ in range(NS)]
    sts = [[P.sb([128, 8]) for _ in range(3)] for _ in range(NS)]
    zb = [P.sb([128, D], BF16) for _ in range(2)]
    zT = [P.sb([128, KC, 128], BF16) for _ in range(2)]
    tps = [P.ps([128, 4, 128], BF16) for _ in range(2)]
    mps = [P.ps([128, 512]) for _ in range(4)]
    fs = [P.sb([128, 1]) for _ in range(2)]

    P.dma("sp", ident_f[:], identd[:], writes=["identf"])
    P.op("dve", lambda e: e.tensor_copy(out=ident[:], in_=ident_f[:]), reads=["identf"], writes=["ident"])
    for i in range(2):
        P.dma("sp", gb[i][:], gbd[i], writes=[("gb", i)])
    for nm in nbc:
        P.dma("sp", bc[nm][:], bc_d[nm], writes=[nm])
    for k in range(KC):
        b = k % 2
        P.dma("sp", xt[b][:], wod[:, k, :], writes=[("xt", b)])
        if k % 2 == 0:
            P.op("act", lambda e, b=b, k=k: e.copy(out=wo[:, k, :], in_=xt[b][:]), reads=[("xt", b)], writes=["wo"])
        else:
            P.op("pool", lambda e, b=b, k=k: e.tensor_copy(out=wo[:, k, :], in_=xt[b][:]), reads=[("xt", b)], writes=["wo"])

    G = 128 if kind == "a" else 64
    ng = 512 // G
    segs = [(r, m, 0) for (r, m) in _segments(0, n_lat)] + [(r, m, 1) for (r, m) in _segments(n_lat, n_ctx)]
    li = 0
    for si, (r0, m, mi) in enumerate(segs):
        b = si % 2
        P.dma("sp", xt[b][:m, :], xin[r0:r0 + m, :], writes=[("xt", b)])
        gstreams = [[] for _ in range(NS)]

        def emit_group(cg):
            sidx = cg * NS // 4
            t1, t2, t3, sg, st = t1s[sidx], t2s[sidx], t3s[sidx], sgs[sidx], sts[sidx]
            SK = lambda nm: (nm, sidx)
            cs = slice(cg * 512, (cg + 1) * 512)
            lb = sidx
            P._defer = gstreams[sidx]
            T = {}
            for nm in O_INS[kind]:
                P.dma("sp", it[nm][lb][:m, :], ins_d[nm][r0:r0 + m, cs], writes=[(nm, lb)])
                T[nm] = it[nm][lb]
            gk = ("gate", lb)
            P.op("act", lambda e, m=m, T=T: e.activation(out=sg[:m, :], in_=T["gate"][:m, :], func=AF.Silu),
                 reads=[gk], writes=[SK("sg")])
            if kind == "b":
                P.op("dve", lambda e, m=m, T=T, b=b, cs=cs: e.tensor_tensor(out=zb[b][:m, cs], in0=T["m0"][:m, :],
                                                                          in1=sg[:m, :], op=ALU.mult),
                     reads=[("m0", lb), SK("sg")], writes=[("zb", b)])
                P._defer = None
                return
            P.op("dve", lambda e, m=m, T=T: e.tensor_tensor(out=t1[:m, :], in0=T["m0"][:m, :], in1=T["m1"][:m, :], op=ALU.add),
                 reads=[("m0", lb), ("m1", lb)], writes=[SK("t1")])
            y3 = t1[:m, :].rearrange("p (g c) -> p g c", c=G)
            if kind == "c":
                P.op("dve", lambda e, m=m, y3=y3: e.tensor_reduce(out=st[0][:m, :ng], in_=y3, axis=AX.X, op=ALU.add),
                     reads=[SK("t1")], writes=[SK("st0")])
                P.op("dve", lambda e, m=m: e.tensor_scalar(out=st[0][:m, :ng], in0=st[0][:m, :ng], scalar1=-1.0 / G,
                                                           scalar2=None, op0=ALU.mult),
                     reads=[SK("st0")], writes=[SK("st0")])
                P.op("dve", lambda e, m=m, y3=y3: e.tensor_tensor(out=y3, in0=y3,
                                                                in1=st[0][:m, :ng].unsqueeze(2).to_broadcast([m, ng, G]),
                                                                op=ALU.add),
                     reads=[SK("t1"), SK("st0")], writes=[SK("t1")])
            P.op("pool", lambda e, m=m: e.tensor_tensor(out=t2[:m, :], in0=t1[:m, :], in1=t1[:m, :], op=ALU.mult),
                 reads=[SK("t1")], writes=[SK("t2")])
            P.op("dve", lambda e, m=m: e.tensor_reduce(out=st[1][:m, :ng], in_=t2[:m, :].rearrange("p (g c) -> p g c", c=G),
                                                       axis=AX.X, op=ALU.add),
                 reads=[SK("t2")], writes=[SK("st1")])
            eps = 1e-6 if kind == "a" else 64e-5
            P.op("dve", lambda e, m=m, eps=eps: e.tensor_scalar(out=st[1][:m, :ng], in0=st[1][:m, :ng], scalar1=1.0 / G,
                                                                scalar2=eps, op0=ALU.mult, op1=ALU.add),
                 reads=[SK("st1")], writes=[SK("st1")])
            P.op("act", lambda e, m=m: e.activation(out=st[1][:m, :ng], in_=st[1][:m, :ng], func=AF.Sqrt),
                 reads=[SK("st1")], writes=[SK("st1")])
            P.op("dve", lambda e, m=m: e.reciprocal(out=st[2][:m, :ng], in_=st[1][:m, :ng]),
                 reads=[SK("st1")], writes=[SK("st2")])
            P.op("dve", lambda e, m=m, y3=y3: e.tensor_tensor(out=y3, in0=y3,
                                                            in1=st[2][:m, :ng].unsqueeze(2).to_broadcast([m, ng, G]),
                                                            op=ALU.mult),
                 reads=[SK("t1"), SK("st2")], writes=[SK("t1")])
            if kind == "a":
                P.op("pool", lambda e, m=m, cs=cs: e.tensor_tensor(out=t2[:m, :], in0=t1[:m, :], in1=bc["onw"][:m, cs], op=ALU.mult),
                     reads=[SK("t1"), "onw"], writes=[SK("t2")])
            else:
                P.op("pool", lambda e, m=m, cs=cs: e.tensor_tensor(out=t2[:m, :], in0=t1[:m, :], in1=bc["lnw"][:m, cs], op=ALU.mult),
                     reads=[SK("t1"), "lnw"], writes=[SK("t2")])
                P.op("pool", lambda e, m=m, T=T: e.tensor_tensor(out=t3[:m, :], in0=T["m2"][:m, :], in1=T["m3"][:m, :], op=ALU.add),
                     reads=[("m2", lb), ("m3", lb)], writes=[SK("t3")])
                P.op("pool", lambda e, m=m, cs=cs: e.tensor_tensor(out=t3[:m, :], in0=t3[:m, :], in1=bc["lnb"][:m, cs], op=ALU.add),
                     reads=[SK("t3"), "lnb"], writes=[SK("t3")])
                P.op("dve", lambda e, m=m: e.tensor_tensor(out=t2[:m, :], in0=t2[:m, :], in1=t3[:m, :], op=ALU.add),
                     reads=[SK("t2"), SK("t3")], writes=[SK("t2")])
            P.op("dve", lambda e, m=m, b=b, cs=cs: e.tensor_tensor(out=zb[b][:m, cs], in0=t2[:m, :], in1=sg[:m, :], op=ALU.mult),
                 reads=[SK("t2"), SK("sg")], writes=[("zb", b)])
            P._defer = None

        for cg in range(4):
            emit_group(cg)
        P.interleave(gstreams)
        for kg in range(KC // 4):
            tb = (si * 4 + kg) % 2
            for kk in range(4):
                k = kg * 4 + kk
                P.op("pe", lambda e, b=b, m=m, k=k, kk=kk, tb=tb: e.transpose(out=tps[tb][:, kk, :m],
                                                                              in_=zb[b][:m, k * 128:(k + 1) * 128],
                                                                              identity=ident[:m, :m]),
                     reads=[("zb", b), "ident"], writes=[("tps", tb)])
            if kg % 2 == 0:
                P.op("act", lambda e, m=m, kg=kg, tb=tb, b=b: e.copy(out=zT[b][:, kg * 4:(kg + 1) * 4, :m], in_=tps[tb][:, :, :m]),
                     reads=[("tps", tb)], writes=[("zT", b)])
            else:
                P.op("dve", lambda e, m=m, kg=kg, tb=tb, b=b: e.tensor_copy(out=zT[b][:, kg * 4:(kg + 1) * 4, :m], in_=tps[tb][:, :, :m]),
                     reads=[("tps", tb)], writes=[("zT", b)])
        for cg in range(4):
            cs = slice(cg * 512, (cg + 1) * 512)
            pb = cg
            for k in range(KC):
                P.op("pe", lambda e, b=b, m=m, k=k, pb=pb, cs=cs: e.matmul(mps[pb][:m, :], lhsT=zT[b][:, k, :m], rhs=wo[:, k, cs],
                                                                         start=(k == 0), stop=(k == KC - 1)),
                     reads=[("zT", b), "wo"], writes=[("mps", pb)])
            te = t1s[cg % 2]
            P.op("dve", lambda e, m=m, pb=pb, cs=cs, mi=mi, te=te: e.tensor_tensor(out=te[:m, :], in0=mps[pb][:m, :], in1=gb[mi][:m, cs], op=ALU.mult),
                 reads=[("mps", pb), ("gb", mi)], writes=[("t1", cg % 2)])
            P.op("pool", lambda e, m=m, b=b, cs=cs, te=te: e.tensor_tensor(out=xo[b][:m, cs], in0=te[:m, :], in1=xt[b][:m, cs], op=ALU.add),
                 reads=[("t1", cg % 2), ("xt", b)], writes=[("xo", b)])
        if final:
            P.op("act", lambda e, b=b, m=m: e.activation(out=xt[b][:m, :], in_=xo[b][:m, :], func=AF.Square, accum_out=fs[0][:m, :]),
                 reads=[("xo", b)], writes=[("xt", b), "fs0"])
            P.op("dve", lambda e, m=m: e.tensor_scalar(out=fs[0][:m, :], in0=fs[0][:m, :], scalar1=1.0 / D, scalar2=1e-6,
                                                       op0=ALU.mult, op1=ALU.add), reads=["fs0"], writes=["fs0"])
            P.op("act", lambda e, m=m: e.activation(out=fs[0][:m, :], in_=fs[0][:m, :], func=AF.Sqrt), reads=["fs0"], writes=["fs0"])
            P.op("dve", lambda e, m=m: e.reciprocal(out=fs[1][:m, :], in_=fs[0][:m, :]), reads=["fs0"], writes=["fs1"])
            P.op("dve", lambda e, m=m, b=b: e.scalar_tensor_tensor(out=xo[b][:m, :], in0=xo[b][:m, :], scalar=fs[1][:m, :],
                                                                   in1=bc["fnw"][:m, :], op0=ALU.mult, op1=ALU.mult),
                 reads=[("xo", b), "fs1", "fnw"], writes=[("xo", b)])
        P.dma("pool", out[r0:r0 + m, :], xo[b][:m, :], reads=[("xo", b)], writes=["out"])
    P.finish(["out"])
    return nc


def run_O(x, ctx, kind, mix_lat, mix_ctx, w_out, g_lat, g_ctx, bcs, final=False, need_ctx=True):
    n_ctx = N_CTX if need_ctx else 0
    nc = build_O(N_LAT, n_ctx, kind, final)
    ident = np.eye(128, dtype=np.float32)
    wo = np.ascontiguousarray(w_out.reshape(KC, 128, D).transpose(1, 0, 2))
    in_maps = []
    for cid in range(NCORES):
        b, q = cid // 4, cid % 4

        def rows(lat, cx):
            parts = [lat[b, q * N_LAT:(q + 1) * N_LAT]]
            if need_ctx:
                parts.append(cx[b, q * N_CTX:(q + 1) * N_CTX])
            return np.ascontiguousarray(np.concatenate(parts, axis=0))

        m = {"xin": rows(x, ctx), "wo": wo, "ident": ident, "gb": np.stack([_bc(g_lat[b]), _bc(g_ctx)])}
        for nm in O_INS[kind]:
            m[nm] = rows(mix_lat[nm], mix_ctx[nm] if need_ctx else None)
        for nm, v in bcs.items():
            m[nm] = _bc(v)
        in_maps.append(m)
    res = _run(nc, in_maps)
    xo = np.empty((2, T_LAT, D), np.float32)
    co = np.empty((2, T_CTX, D), np.float32) if need_ctx else None
    for cid in range(NCORES):
        b, q = cid // 4, cid % 4
        o = res[cid]["xout"]
        xo[b, q * N_LAT:(q + 1) * N_LAT] = o[:N_LAT]
        if need_ctx:
            co[b, q * N_CTX:(q + 1) * N_CTX] = o[N_LAT:]
    return xo, co


L_SEQ = T_CTX + T_LAT
MA_SHARE_PSUM = False
CH = 64


def build_Ma(nrec, L, jlayer):
    ntile = L // 128
    NR = nrec
    WN = NR * 128
    NG = NR * 2
    nc = bass.Bass("TRN2", target_bir_lowering=False)
    P = Prog(nc)
    qd = P.dram("qT", [nrec, 128, L])
    zd = P.dram("zT", [nrec, 128, L])
    vd = P.dram("v", [nrec, L // CH, CH, 128])
    lbd = P.dram("lbr", [nrec, 128, 2])
    maskd = P.dram("mask", [CH, CH])
    identd = P.dram("ident", [128, 128])
    out = P.dram("oT", [nrec, 128, L], kind="ExternalOutput")

    ident_f = P.sb([128, 128])
    ident = P.sb([128, 128], BF16)
    mask = P.sb([CH, CH])
    m01 = P.sb([128, WN])
    lbr = P.sb([128, nrec, 2])
    lb = P.sb([128, nrec])
    oml = P.sb([128, nrec])
    S = [P.sb([128, 128]) for _ in range(nrec)]
    Sb = [P.sb([128, 128], BF16) for _ in range(nrec)]
    Z = [P.sb([128, NR, 128]) for _ in range(2)]
    Qw = [P.sb([128, NR, 128]) for _ in range(2)]
    Vt = [P.sb([CH, NR, 2, 128]) for _ in range(2)]
    Vb = [P.sb([CH, NR, 2, 128], BF16) for _ in range(2)]
    e1, ft, gt, kq, bc, bm, be, ex0, ex2, ex3, bk = [P.sb([128, WN]) for _ in range(11)]
    qh = [P.sb([128, WN], BF16) for _ in range(2)]
    qtl = [P.sb([128, WN], BF16) for _ in range(2)]
    ktl = [P.sb([128, WN], BF16) for _ in range(2)]
    khI = [[P.sb([128, WN], BF16) for _ in range(4)] for _ in range(2)]
    gam = [P.sb([128, NG]) for _ in range(2)]
    osb = [P.sb([128, NR, 128]) for _ in range(2)]
    att_all = [[P.sb([CH, CH], BF16) for _ in range(2)] for _ in range(nrec)]
    ktm_all = [[P.sb([CH, 128], BF16) for _ in range(2)] for _ in range(nrec)]
    pA_bank = [P.ps([128, 512]) for _ in range(2)]
    pO_bank = [P.ps([128, 512]) for _ in range(2)]
    pT_bank = [P.ps([128, 1024], BF16) for _ in range(2)]
    pS_bank = [P.ps([128, 512]) for _ in range(2)]

    P.dma("sp", ident_f[:], identd[:], writes=["identf"])
    P.op("dve", lambda e: e.tensor_copy(out=ident[:], in_=ident_f[:]), reads=["identf"], writes=["ident"])
    P.dma("sp", mask[:], maskd[:], writes=["mask"])
    P.op("pool", lambda e: e.memset(m01[:], 1.0), writes=["m01"])
    P.op("pool", lambda e: e.memset(m01[:].rearrange("p (g s) -> p g s", s=CH)[:, :, 0:1], 0.0), reads=["m01"], writes=["m01"])
    for p_ in range(2):
        P.op("dve", lambda e, p_=p_: e.memset(pA_bank[p_][:], 0.0), writes=[("pA", p_)])
    for r in range(nrec):
        P.dma("sp", lbr[:, r, :], lbd[r], writes=["lbr"])
        P.op("pool", lambda e, r=r: e.memset(S[r][:], 0.0), writes=[("S", r)])
        P.op("pool", lambda e, r=r: e.memset(Sb[r][:], 0.0), writes=[("Sb", r)])
    if jlayer != 0:
        P.op("dve", lambda e: e.tensor_tensor(out=lb[:], in0=lbr[:, :, 0], in1=lbr[:, :, 1], op=ALU.subtract),
             reads=["lbr"], writes=["lb"])
        P.op("act", lambda e: e.activation(out=lb[:], in_=lb[:], func=AF.Exp), reads=["lb"], writes=["lb"])
        P.op("dve", lambda e: e.tensor_scalar(out=lb[:], in0=lb[:], scalar1=1.0, scalar2=None, op0=ALU.add),
             reads=["lb"], writes=["lb"])
        P.op("dve", lambda e: e.reciprocal(out=lb[:], in_=lb[:]), reads=["lb"], writes=["lb"])
        P.op("dve", lambda e: e.tensor_scalar(out=oml[:], in0=lb[:], scalar1=-1.0, scalar2=1.0, op0=ALU.mult, op1=ALU.add),
             reads=["lb"], writes=["oml"])

    QS = 128.0 ** -0.5

    NH = 2
    HW_ = WN // NH
    GH = NG // NH

    def emit_W(t, hf):
        tp = t % 2
        ts = slice(t * 128, (t + 1) * 128)
        KT = lambda nm: (nm, tp, hf)
        KH = lambda nm: (nm, hf)
        cs = slice(hf * HW_, (hf + 1) * HW_)
        rs = slice(hf * (NR // NH), (hf + 1) * (NR // NH))
        nr = NR // NH
        lst = []
        P._defer = lst
        z2 = Z[tp][:, rs, :].rearrange("p r t -> p (r t)")
        q2 = Qw[tp][:, rs, :].rearrange("p r t -> p (r t)")
        g3 = lambda ap: ap.rearrange("p (g s) -> p g s", s=CH)
        P.dma("sp", Z[tp][:, rs, :], zd[rs, :, ts].rearrange("r p t -> p r t"), writes=[KT("Z")])
        P.dma("sp", Qw[tp][:, rs, :], qd[rs, :, ts].rearrange("r p t -> p r t"), writes=[KT("Q")])
        for r_ in range(rs.start, rs.stop):
            P.dma("sp", Vt[tp][:, r_], vd[r_, 2 * t:2 * t + 2].rearrange("c s d -> s c d"), writes=[KT("Vt")])
        P.op("pool", lambda e: e.tensor_copy(out=Vb[tp][:, rs], in_=Vt[tp][:, rs]), reads=[KT("Vt")], writes=[KT("Vb")])
        P.op("act", lambda e: e.activation(out=e1[:, cs], in_=z2, func=AF.Exp, scale=-1.0), reads=[KT("Z")], writes=[KH("e1")])
        P.op("dve", lambda e: e.tensor_scalar(out=e1[:, cs], in0=e1[:, cs], scalar1=1.0, scalar2=None, op0=ALU.add), reads=[KH("e1")], writes=[KH("e1")])
        P.op("act", lambda e: e.activation(out=e1[:, cs], in_=e1[:, cs], func=AF.Ln), reads=[KH("e1")], writes=[KH("e1")])
        P.op("act", lambda e: e.activation(out=ft[:, cs], in_=e1[:, cs], func=AF.Exp, scale=-1.0), reads=[KH("e1")], writes=[KH("ft")])
        if jlayer != 0:
            f3 = ft[:, cs].rearrange("p (r t) -> p r t", t=128)
            P.op("dve", lambda e: e.tensor_tensor(out=f3, in0=f3, in1=oml[:, rs].unsqueeze(2).to_broadcast([128, nr, 128]), op=ALU.mult),
                 reads=[KH("ft"), "oml"], writes=[KH("ft")])
            P.op("dve", lambda e: e.tensor_tensor(out=f3, in0=f3, in1=lb[:, rs].unsqueeze(2).to_broadcast([128, nr, 128]), op=ALU.add),
                 reads=[KH("ft"), "lb"], writes=[KH("ft")])
            P.op("act", lambda e: e.activation(out=gt[:, cs], in_=ft[:, cs], func=AF.Ln), reads=[KH("ft")], writes=[KH("gt")])
        else:
            P.op("dve", lambda e: e.tensor_scalar(out=gt[:, cs], in0=e1[:, cs], scalar1=-1.0, scalar2=None, op0=ALU.mult),
                 reads=[KH("e1")], writes=[KH("gt")])
        P.op("pool", lambda e: e.tensor_scalar(out=kq[:, cs], in0=ft[:, cs], scalar1=-1.0, scalar2=1.0, op0=ALU.mult, op1=ALU.add),
             reads=[KH("ft")], writes=[KH("kq")])
        P.op("dve", lambda e: e.tensor_tensor_scan(out=bc[:, cs], data0=m01[:, cs], data1=gt[:, cs], initial=0.0, op0=ALU.mult, op1=ALU.add),
             reads=[KH("gt"), "m01"], writes=[KH("bc")])
        bc3 = g3(bc[:, cs])
        bc4 = bc[:, cs].rearrange("p (g i s) -> p g i s", i=4, s=16)
        P.op("dve", lambda e: e.tensor_tensor(out=bm[:, cs].rearrange("p (g i s) -> p g i s", i=4, s=16), in0=bc4,
                                              in1=bc4[:, :, :, 0:1].to_broadcast([128, GH, 4, 16]), op=ALU.subtract),
             reads=[KH("bc")], writes=[KH("bm")])
        P.op("dve", lambda e: e.tensor_tensor(out=g3(be[:, cs]), in0=bc3, in1=bc3[:, :, CH - 1:CH].to_broadcast([128, GH, CH]), op=ALU.subtract),
             reads=[KH("bc")], writes=[KH("be")])
        P.op("act", lambda e: e.activation(out=ex0[:, cs], in_=bm[:, cs], func=AF.Exp), reads=[KH("bm")], writes=[KH("ex0")])
        P.op("act", lambda e: e.activation(out=ex2[:, cs], in_=bc[:, cs], func=AF.Exp), reads=[KH("bc")], writes=[KH("ex2")])
        P.op("act", lambda e: e.activation(out=ex3[:, cs], in_=be[:, cs], func=AF.Exp, scale=-1.0), reads=[KH("be")], writes=[KH("ex3")])
        P.op("act", lambda e: e.activation(out=gam[tp][:, hf * GH:(hf + 1) * GH], in_=bc3[:, :, CH - 1], func=AF.Exp), reads=[KH("bc")], writes=[KT("gam")])
        P.op("dve", lambda e: e.scalar_tensor_tensor(out=qh[tp][:, cs], in0=q2, scalar=QS, in1=ex0[:, cs], op0=ALU.mult, op1=ALU.mult),
             reads=[KT("Q"), KH("ex0")], writes=[KT("qh")])
        P.op("dve", lambda e: e.scalar_tensor_tensor(out=qtl[tp][:, cs], in0=q2, scalar=QS, in1=ex2[:, cs], op0=ALU.mult, op1=ALU.mult),
             reads=[KT("Q"), KH("ex2")], writes=[KT("qtl")])
        P.op("pool", lambda e: e.tensor_tensor(out=ktl[tp][:, cs], in0=kq[:, cs], in1=ex3[:, cs], op=ALU.mult), reads=[KH("kq"), KH("ex3")], writes=[KT("ktl")])
        kq3 = g3(kq[:, cs])
        bk3 = g3(bk[:, cs])
        for I in range(4):
            n = 16 * (I + 1)
            P.op("dve", lambda e, n=n, I=I: e.tensor_tensor(out=bk3[:, :, :n], in0=bc3[:, :, :n],
                                                          in1=bc3[:, :, 16 * I:16 * I + 1].to_broadcast([128, GH, n]), op=ALU.subtract),
                 reads=[KH("bc"), KH("bk")], writes=[KH("bk")])
            P.op("act", lambda e, n=n: e.activation(out=bk3[:, :, :n], in_=bk3[:, :, :n], func=AF.Exp, scale=-1.0), reads=[KH("bk")], writes=[KH("bk")])
            eng_ = "pool" if I % 2 == 0 else "dve"
            P.op(eng_, lambda e, n=n, I=I: e.tensor_tensor(out=g3(khI[tp][I][:, cs])[:, :, :n], in0=kq3[:, :, :n], in1=bk3[:, :, :n], op=ALU.mult),
                 reads=[KH("kq"), KH("bk")], writes=[KT("kh%d" % I)])
        P._defer = None
        return lst

    def emit_R(t, r):
        tp = t % 2
        hf_ = r // (NR // NH)
        KT = lambda nm: (nm, tp, hf_) if nm != "osb" else (nm, tp)
        lst = []
        P._defer = lst
        for c in range(2):
            g = r * 2 + c
            go = g * CH
            p = (r + c) % 2
            pA_t = pA_bank[p][:, 0:CH]
            pO_t = pO_bank[p][:, 0:CH]
            pT_t = pT_bank[p][:, 0:128]
            pS_t = pS_bank[p][:, 0:128]
            att_t = att_all[r][c]
            ktm_t = ktm_all[r][c]
            vb_t = Vb[tp][:, r, c, :]
            P.atomic_begin()
            for I in range(4):
                n = 16 * (I + 1)
                P.op("pe", lambda e, I=I, n=n, pA_t=pA_t, go=go: e.matmul(pA_t[:n, 16 * I:16 * I + 16], lhsT=khI[tp][I][:, go:go + n],
                                                                         rhs=qh[tp][:, go + 16 * I:go + 16 * I + 16], start=True, stop=True),
                     reads=[KT("kh%d" % I), KT("qh")], writes=[("pA", p)])
            P.op("dve", lambda e, att_t=att_t, pA_t=pA_t: e.tensor_tensor(out=att_t[:], in0=pA_t[:CH, :], in1=mask[:], op=ALU.mult),
                 reads=[("pA", p), "mask"], writes=[("att", r, c)])
            P.atomic_end()
            P.atomic_begin()
            P.op("pe", lambda e, att_t=att_t, pO_t=pO_t, vb_t=vb_t: e.matmul(pO_t, lhsT=vb_t, rhs=att_t[:], start=True, stop=False),
                 reads=[KT("Vb"), ("att", r, c), ("Sb", r), KT("qtl")], writes=[("pO", p)])
            P.op("pe", lambda e, pO_t=pO_t, go=go: e.matmul(pO_t, lhsT=Sb[r][:], rhs=qtl[tp][:, go:go + CH], start=False, stop=True),
                 reads=[("Sb", r), KT("qtl")], writes=[("pO", p)])
            P.op("act", lambda e, pO_t=pO_t, c=c: e.copy(out=osb[tp][:, r, c * CH:(c + 1) * CH], in_=pO_t),
                 reads=[("pO", p)], writes=[KT("osb")])
            P.atomic_end()
            P.atomic_begin()
            P.op("pe", lambda e, pT_t=pT_t, go=go: e.transpose(out=pT_t[:CH, :], in_=ktl[tp][:, go:go + CH], identity=ident[:]),
                 reads=[KT("ktl"), "ident"], writes=[("pT", p)])
            P.op("act", lambda e, ktm_t=ktm_t, pT_t=pT_t: e.copy(out=ktm_t[:], in_=pT_t[:CH, :]), reads=[("pT", p)], writes=[("ktm", r, c)])
            P.atomic_end()
            P.atomic_begin()
            P.op("pe", lambda e, ktm_t=ktm_t, pS_t=pS_t, vb_t=vb_t: e.matmul(pS_t, lhsT=ktm_t[:], rhs=vb_t, start=True, stop=True),
                 reads=[("ktm", r, c), KT("Vb")], writes=[("pS", p)])
            P.op("dve", lambda e, pS_t=pS_t, g=g: e.scalar_tensor_tensor(out=S[r][:], in0=S[r][:], scalar=gam[tp][:, g:g + 1],
                                                                        in1=pS_t, op0=ALU.mult, op1=ALU.add),
                 reads=[("S", r), KT("gam"), ("pS", p)], writes=[("S", r)])
            P.atomic_end()
            P.op("pool", lambda e: e.tensor_copy(out=Sb[r][:], in_=S[r][:]), reads=[("S", r)], writes=[("Sb", r)])
        P._defer = None
        return lst

    P.interleave([emit_W(0, hf) for hf in range(NH)])
    for t in range(ntile):
        ts = slice(t * 128, (t + 1) * 128)
        streams = [emit_R(t, r) for r in range(nrec)]
        if t + 1 < ntile:
            streams += [emit_W(t + 1, hf) for hf in range(NH)]
        P.interleave(streams)
        P.dma("pool", out[:, :, ts].rearrange("r p t -> p r t"), osb[t % 2][:], reads=[("osb", t % 2)], writes=["out"])
    P.finish(["out"])
    return nc


def _seq(lat_b, ctx_b, rev):
    if rev:
        return np.concatenate([ctx_b[::-1], lat_b[::-1]], axis=0)
    return np.concatenate([ctx_b, lat_b], axis=0)


def _unseq(s, rev):
    c, l = s[:T_CTX], s[T_CTX:]
    if rev:
        return l[::-1], c[::-1]
    return l, c


def run_Ma(pl, pc, a_lb_raw, jlayer):
    nrec = 8
    nc = build_Ma(nrec, L_SEQ, jlayer)
    ident = np.eye(128, dtype=np.float32)
    mask = np.triu(np.ones((CH, CH), np.float32))
    in_maps = []
    for cid in range(NCORES):
        b, hg = cid // 4, cid % 4
        qT = np.empty((nrec, 128, L_SEQ), np.float32)
        zT = np.empty((nrec, 128, L_SEQ), np.float32)
        v = np.empty((nrec, L_SEQ // CH, CH, 128), np.float32)
        lbr = np.empty((nrec, 128, 2), np.float32)
        for hl in range(4):
            h = hg * 4 + hl
            hc = slice(h * 128, (h + 1) * 128)
            for dr in range(2):
                r = hl * 2 + dr
                zoff = 2 * D + dr * D
                sq = _seq(pl[b][:, hc], pc[b][:, hc], dr == 1)
                sv = _seq(pl[b][:, D + h * 128:D + (h + 1) * 128], pc[b][:, D + h * 128:D + (h + 1) * 128], dr == 1)
                sz = _seq(pl[b][:, zoff + h * 128:zoff + (h + 1) * 128], pc[b][:, zoff + h * 128:zoff + (h + 1) * 128], dr == 1)
                qT[r] = sq.T
                zT[r] = sz.T
                v[r] = sv.reshape(L_SEQ // CH, CH, 128)
                lbr[r] = a_lb_raw[:, hc].T
        in_maps.append({"qT": qT, "zT": zT, "v": v, "lbr": lbr, "mask": mask, "ident": ident})
    res = _run(nc, in_maps)
    o_lat = [np.empty((2, T_LAT, D), np.float32) for _ in range(2)]
    o_ctx = [np.empty((2, T_CTX, D), np.float32) for _ in range(2)]
    for cid in range(NCORES):
        b, hg = cid // 4, cid % 4
        oT = res[cid]["oT"]
        for hl in range(4):
            h = hg * 4 + hl
            for dr in range(2):
                l, c = _unseq(oT[hl * 2 + dr].T, dr == 1)
                o_lat[dr][b][:, h * 128:(h + 1) * 128] = l
                o_ctx[dr][b][:, h * 128:(h + 1) * 128] = c
    return o_lat, o_ctx


def layer_a(x, ctx, mod_i, norm_w_i, w_in, a_lb_raw, jlayer, onorm_w, w_out, need_ctx, final_w=None):
    pl, pc = run_P(x, ctx, mod_i, norm_w_i, w_in)
    o_lat, o_ctx = run_Ma(pl, pc, a_lb_raw, jlayer)
    ml = {"m0": o_lat[0], "m1": o_lat[1], "gate": pl[:, :, 4 * D:5 * D]}
    mc = {"m0": o_ctx[0], "m1": o_ctx[1], "gate": pc[:, :, 4 * D:5 * D]}
    bcs = {"onw": np.tile(onorm_w, 16)}
    if final_w is not None:
        bcs["fnw"] = final_w
    return run_O(x, ctx, "a", ml, mc, w_out, mod_i[0:2, 2 * D:3 * D], mod_i[2, 2 * D:3 * D], bcs,
                 final=final_w is not None, need_ctx=need_ctx)


NQH = 8
HD = 64
NBLK = T_LAT // 128


def build_Mb(need_ctx):
    nc = bass.Bass("TRN2", target_bir_lowering=False)
    P = Prog(nc)
    qd = P.dram("qT", [HD, NQH, T_LAT])
    qsd = P.dram("qsT", [HD, NQH, T_LAT])
    kd = P.dram("kT", [HD, T_LAT])
    ksd = P.dram("ksT", [HD, T_LAT])
    vd = P.dram("v", [128, NBLK, HD])
    qcd = P.dram("qcT", [HD, NQH, T_CTX])
    kcd = P.dram("kcT", [HD, T_CTX])
    vcd = P.dram("vc", [128, 2, HD])
    posd = P.dram("pos", [HD, T_LAT])
    fid = P.dram("fidx", [HD, 2])
    sinkd = P.dram("sink", [128, NQH])
    mld = P.dram("maskl", [128, 128])
    mrd = P.dram("maskr", [128, 128])
    n_out = T_LAT + (T_CTX if need_ctx else 0)
    out = P.dram("o", [n_out, NQH * HD], kind="ExternalOutput")

    PI = float(np.pi)
    CW = 2048
    cosT = P.sb([HD, T_LAT])
    sinT = P.sb([HD, T_LAT])
    posc = P.sb([HD, CW])
    tmpA = P.sb([HD, CW])
    tmpB = P.sb([HD, CW])
    fidx = P.sb([HD, 2])
    inv = P.sb([HD, 1])
    kr = P.sb([HD, T_LAT], BF16)
    kcb = P.sb([HD, T_CTX], BF16)
    kcf = P.sb([HD, T_CTX])
    vf = P.sb([128, NBLK, HD])
    vx = P.sb([128, NBLK, HD + 1], BF16)
    vcf = P.sb([128, 2, HD])
    vcx = P.sb([128, 2, HD + 1], BF16)
    esink = P.sb([128, NQH])
    ml = P.sb([128, 128], BF16)
    mr = P.sb([128, 128], BF16)
    mlf = P.sb([128, 128])
    mrf = P.sb([128, 128])
    qf = [P.sb([HD, NQH, 128]) for _ in range(2)]
    qsf = [P.sb([HD, NQH, 128]) for _ in range(2)]
    qt1 = P.sb([HD, NQH, 128])
    qt2 = P.sb([HD, NQH, 128])
    qr = [P.sb([HD, NQH, 128], BF16) for _ in range(2)]
    E = [P.sb([128, NQH, 128], BF16) for _ in range(5)]
    pS = [P.ps([128, 1024]) for _ in range(2)]
    pO = [P.ps([128, 4, 128]) for _ in range(2)]
    den = P.sb([128, NQH])
    osb = [P.sb([128, NQH, HD]) for _ in range(2)]

    P.dma("sp", fidx[:], fid[:], writes=["fidx"])
    P.dma("sp", esink[:], sinkd[:], writes=["esink"])
    P.dma("sp", mlf[:], mld[:], writes=["mlf"])
    P.dma("sp", mrf[:], mrd[:], writes=["mrf"])
    P.op("dve", lambda e: e.tensor_copy(out=ml[:], in_=mlf[:]), reads=["mlf"], writes=["ml"])
    P.op("dve", lambda e: e.tensor_copy(out=mr[:], in_=mrf[:]), reads=["mrf"], writes=["mr"])
    P.op("act", lambda e: e.activation(out=esink[:], in_=esink[:], func=AF.Exp), reads=["esink"], writes=["esink"])
    P.op("act", lambda e: e.activation(out=inv[:], in_=fidx[:, 0:1], func=AF.Exp, scale=-float(np.log(10000.0)) / 16.0),
         reads=["fidx"], writes=["inv"])

    def sincos(dst_full, shift, key, cc):
        cs_ = slice(cc * CW, (cc + 1) * CW)
        dst = dst_full[:, cs_]
        P.op("dve", lambda e: e.tensor_scalar(out=tmpA[:], in0=posc[:], scalar1=inv[:, 0:1], scalar2=shift, op0=ALU.mult, op1=ALU.add),
             reads=["pos", "inv"], writes=["tmpA"])
        ni = tmpB[:].bitcast(mybir.dt.int32)
        P.op("dve", lambda e: e.tensor_scalar(out=dst, in0=tmpA[:], scalar1=1.0 / (2 * PI), scalar2=0.5, op0=ALU.mult, op1=ALU.add),
             reads=["tmpA"], writes=[key])
        P.op("dve", lambda e: e.tensor_copy(out=ni, in_=dst), reads=[key], writes=["tmpB"])
        P.op("dve", lambda e: e.tensor_copy(out=dst, in_=ni), reads=["tmpB"], writes=[key])
        P.op("dve", lambda e: e.scalar_tensor_tensor(out=tmpA[:], in0=dst, scalar=-2 * PI, in1=tmpA[:], op0=ALU.mult, op1=ALU.add),
             reads=[key, "tmpA"], writes=["tmpA"])
        P.op("dve", lambda e: e.tensor_scalar(out=tmpB[:], in0=tmpA[:], scalar1=-PI, scalar2=2 * PI, op0=ALU.is_lt, op1=ALU.mult),
             reads=["tmpA"], writes=["tmpB"])
        P.op("dve", lambda e: e.tensor_tensor(out=tmpA[:], in0=tmpA[:], in1=tmpB[:], op=ALU.add), reads=["tmpA", "tmpB"], writes=["tmpA"])
        P.op("dve", lambda e: e.tensor_scalar(out=tmpB[:], in0=tmpA[:], scalar1=PI, scalar2=-2 * PI, op0=ALU.is_gt, op1=ALU.mult),
             reads=["tmpA"], writes=["tmpB"])
        P.op("dve", lambda e: e.tensor_tensor(out=tmpA[:], in0=tmpA[:], in1=tmpB[:], op=ALU.add), reads=["tmpA", "tmpB"], writes=["tmpA"])
        P.op("dve", lambda e: e.tensor_scalar(out=tmpA[:], in0=tmpA[:], scalar1=PI, scalar2=-PI, op0=ALU.min, op1=ALU.max),
             reads=["tmpA"], writes=["tmpA"])
        P.op("act", lambda e: e.activation(out=dst, in_=tmpA[:], func=AF.Sin), reads=["tmpA"], writes=[key])

    for cc in range(T_LAT // CW):
        P.dma("sp", posc[:], posd[:, cc * CW:(cc + 1) * CW], writes=["pos"])
        sincos(sinT, 0.0, "sinT", cc)
        sincos(cosT, PI / 2, "cosT", cc)
    P.op("dve", lambda e: e.tensor_scalar(out=sinT[:], in0=sinT[:], scalar1=fidx[:, 1:2], scalar2=None, op0=ALU.mult),
         reads=["sinT", "fidx"], writes=["sinT"])

    for cc in range(T_LAT // CW):
        cs_ = slice(cc * CW, (cc + 1) * CW)
        P.dma("sp", tmpA[:], kd[:, cs_], writes=["tmpA"])
        P.dma("sp", tmpB[:], ksd[:, cs_], writes=["tmpB"])
        P.op("dve", lambda e, cs_=cs_: e.tensor_tensor(out=tmpA[:], in0=tmpA[:], in1=cosT[:, cs_], op=ALU.mult), reads=["tmpA", "cosT"], writes=["tmpA"])
        P.op("pool", lambda e, cs_=cs_: e.tensor_tensor(out=tmpB[:], in0=tmpB[:], in1=sinT[:, cs_], op=ALU.mult), reads=["tmpB", "sinT"], writes=["tmpB"])
        P.op("dve", lambda e, cs_=cs_: e.tensor_tensor(out=kr[:, cs_], in0=tmpA[:], in1=tmpB[:], op=ALU.add), reads=["tmpA", "tmpB"], writes=["kr"])
    P.dma("sp", kcf[:], kcd[:], writes=["kcf"])
    P.op("dve", lambda e: e.tensor_copy(out=kcb[:], in_=kcf[:]), reads=["kcf"], writes=["kcb"])
    P.dma("sp", vf[:], vd[:], writes=["vf"])
    P.dma("sp", vcf[:], vcd[:], writes=["vcf"])
    P.op("pool", lambda e: e.memset(vx[:], 1.0), writes=["vx"])
    P.op("pool", lambda e: e.memset(vcx[:], 1.0), writes=["vcx"])
    P.op("pool", lambda e: e.tensor_copy(out=vx[:, :, 0:HD], in_=vf[:]), reads=["vf"], writes=["vx"])
    P.op("pool", lambda e: e.tensor_copy(out=vcx[:, :, 0:HD], in_=vcf[:]), reads=["vcf"], writes=["vcx"])

    blocks = [("lat", j) for j in range(NBLK)]
    if need_ctx:
        blocks += [("ctx", j) for j in range(2)]
    for bi, (typ, j) in enumerate(blocks):
        b = bi % 2
        ts = slice(j * 128, (j + 1) * 128)
        if typ == "lat":
            P.dma("sp", qf[b][:], qd[:, :, ts], writes=[("qf", b)])
            P.dma("sp", qsf[b][:], qsd[:, :, ts], writes=[("qsf", b)])
            P.op("dve", lambda e, b=b, ts=ts: e.tensor_tensor(out=qt1[:], in0=qf[b][:], in1=cosT[:, None, ts].to_broadcast([HD, NQH, 128]),
                                                            op=ALU.mult), reads=[("qf", b), "cosT"], writes=["qt1"])
            P.op("pool", lambda e, b=b, ts=ts: e.tensor_tensor(out=qt2[:], in0=qsf[b][:], in1=sinT[:, None, ts].to_broadcast([HD, NQH, 128]),
                                                             op=ALU.mult), reads=[("qsf", b), "sinT"], writes=["qt2"])
            P.op("dve", lambda e, b=b: e.tensor_tensor(out=qr[b][:], in0=qt1[:], in1=qt2[:], op=ALU.add),
                 reads=["qt1", "qt2"], writes=[("qr", b)])
            kbs = []
            if j > 0:
                kbs.append(("lat", j - 1, "ml"))
            kbs.append(("lat", j, None))
            if j < NBLK - 1:
                kbs.append(("lat", j + 1, "mr"))
            kbs += [("ctx", 0, None), ("ctx", 1, None)]
            orow = j * 128
        else:
            P.dma("sp", qf[b][:], qcd[:, :, ts], writes=[("qf", b)])
            P.op("dve", lambda e, b=b: e.tensor_copy(out=qr[b][:], in_=qf[b][:]), reads=[("qf", b)], writes=[("qr", b)])
            kbs = [("ctx", 0, None), ("ctx", 1, None)]
            orow = T_LAT + j * 128
        for ki, (kt, kj, mk) in enumerate(kbs):
            p = ki % 2
            ksl = slice(kj * 128, (kj + 1) * 128)
            kap = kr[:, ksl] if kt == "lat" else kcb[:, ksl]
            kkey = "kr" if kt == "lat" else "kcb"
            for hh in range(2):
                P.op("pe", lambda e, b=b, p=p, hh=hh, kap=kap: e.matmul(pS[p][:, hh * 512:(hh + 1) * 512], lhsT=kap,
                                                                      rhs=qr[b][:, hh * 4:(hh + 1) * 4, :], start=True, stop=True),
                     reads=[kkey, ("qr", b)], writes=[("pS", p)])
            P.op("act", lambda e, p=p, ki=ki: e.activation(out=E[ki][:].rearrange("p h q -> p (h q)"), in_=pS[p][:], func=AF.Exp, scale=HD ** -0.5),
                 reads=[("pS", p)], writes=[("E", ki)])
            if mk is not None:
                mt = ml if mk == "ml" else mr
                P.op("dve", lambda e, ki=ki, mt=mt: e.tensor_tensor(out=E[ki][:], in0=E[ki][:], in1=mt[:, None, :].to_broadcast([128, NQH, 128]),
                                                                   op=ALU.mult), reads=[("E", ki), mk], writes=[("E", ki)])
        nk = len(kbs)
        for h in range(NQH):
            po = h // 4
            for ki, (kt, kj, mk) in enumerate(kbs):
                vap = vx[:, kj, :] if kt == "lat" else vcx[:, kj, :]
                vkey = "vx" if kt == "lat" else "vcx"
                P.op("pe", lambda e, h=h, po=po, ki=ki, vap=vap: e.matmul(pO[po][:, h % 4, 0:HD + 1], lhsT=E[ki][:, h, :], rhs=vap,
                                                                        start=(ki == 0), stop=(ki == nk - 1)),
                     reads=[("E", ki), vkey], writes=[("pO", po)])
        for po in range(2):
            hs = slice(po * 4, (po + 1) * 4)
            P.op("dve", lambda e, po=po, hs=hs: e.tensor_tensor(out=den[:, hs], in0=pO[po][:, :, HD], in1=esink[:, hs], op=ALU.add),
                 reads=[("pO", po), "esink"], writes=["den"])
            P.op("dve", lambda e, hs=hs: e.reciprocal(out=den[:, hs], in_=den[:, hs]), reads=["den"], writes=["den"])
            P.op("dve", lambda e, po=po, hs=hs, b=b: e.tensor_tensor(out=osb[b][:, hs, :], in0=pO[po][:, :, 0:HD],
                                                                    in1=den[:, hs].unsqueeze(2).to_broadcast([128, 4, HD]), op=ALU.mult),
                 reads=[("pO", po), "den"], writes=[("osb", b)])
        P.dma("pool", out[orow:orow + 128, :], osb[b][:].rearrange("p h d -> p (h d)"), reads=[("osb", b)], writes=["out"])
    P.finish(["out"])
    return nc


def _rope_swap(a):
    hd = a.shape[-1]
    idx = np.arange(hd)
    idx = (idx // 32) * 32 + ((idx % 32) + 16) % 32
    return a[..., idx]


def run_Mb(pl, pc, sink, need_ctx):
    nc = build_Mb(need_ctx)
    t = np.arange(T_LAT)
    pos = np.empty((HD, T_LAT), np.float32)
    pos[:32] = (t // 64)[None, :]
    pos[32:] = (t % 64)[None, :]
    d = np.arange(HD)
    fidx = np.stack([(d % 16).astype(np.float32), np.where((d % 32) < 16, -1.0, 1.0).astype(np.float32)], axis=1)
    jj, ii = np.meshgrid(np.arange(128), np.arange(128), indexing="ij")
    maskl = (jj >= ii).astype(np.float32)
    maskr = (jj <= ii).astype(np.float32)
    in_maps = []
    for cid in range(NCORES):
        b, g = cid // 4, cid % 4
        q = pl[b][:, g * 512:(g + 1) * 512].reshape(T_LAT, NQH, HD)
        k = pl[b][:, D + g * HD:D + (g + 1) * HD]
        v = pl[b][:, D + 256 + g * HD:D + 256 + (g + 1) * HD]
        qc = pc[b][:, g * 512:(g + 1) * 512].reshape(T_CTX, NQH, HD)
        kc = pc[b][:, D + g * HD:D + (g + 1) * HD]
        vc = pc[b][:, D + 256 + g * HD:D + 256 + (g + 1) * HD]
        m = {"qT": np.ascontiguousarray(q.transpose(2, 1, 0)), "qsT": np.ascontiguousarray(_rope_swap(q).transpose(2, 1, 0)),
             "kT": np.ascontiguousarray(k.T), "ksT": np.ascontiguousarray(_rope_swap(k).T),
             "v": np.ascontiguousarray(v.reshape(NBLK, 128, HD).transpose(1, 0, 2)),
             "qcT": np.ascontiguousarray(qc.transpose(2, 1, 0)), "kcT": np.ascontiguousarray(kc.T),
             "vc": np.ascontiguousarray(vc.reshape(2, 128, HD).transpose(1, 0, 2)),
             "pos": pos, "fidx": fidx, "sink": _bc(sink[g * NQH:(g + 1) * NQH]), "maskl": maskl, "maskr": maskr}
        in_maps.append(m)
    res = _run(nc, in_maps)
    ol = np.empty((2, T_LAT, D), np.float32)
    oc = np.empty((2, T_CTX, D), np.float32) if need_ctx else None
    for cid in range(NCORES):
        b, g = cid // 4, cid % 4
        o = res[cid]["o"]
        ol[b][:, g * 512:(g + 1) * 512] = o[:T_LAT]
        if need_ctx:
            oc[b][:, g * 512:(g + 1) * 512] = o[T_LAT:]
    return ol, oc


def layer_b(x, ctx, mod_i, norm_w_i, w_in, sink, w_out, need_ctx, final_w=None):
    pl, pc = run_P(x, ctx, mod_i, norm_w_i, w_in)
    ol, oc = run_Mb(pl, pc, sink, need_ctx)
    ml = {"m0": ol, "gate": pl[:, :, D + 512:]}
    mc = {"m0": oc, "gate": pc[:, :, D + 512:]}
    bcs = {}
    if final_w is not None:
        bcs["fnw"] = final_w
    return run_O(x, ctx, "b", ml, mc, w_out, mod_i[0:2, 2 * D:3 * D], mod_i[2, 2 * D:3 * D], bcs,
                 final=final_w is not None, need_ctx=need_ctx)


NH_C = 8
MC_INTERLEAVE = False
MC_FP32R = True
MC_BF16INV = True
MC_PIPE = True


def R32(ap):
    if MC_BF16INV:
        return ap
    return ap.bitcast(mybir.dt.float32r) if MC_FP32R else ap
HC = 64
LORA = 96


def build_Mc(L):
    ntile = L // 128
    W = NH_C * HC
    nc = bass.Bass("TRN2", target_bir_lowering=False)
    P = Prog(nc)
    rd = P.dram("r", [2, L, W])
    kd = P.dram("k", [2, L, W])
    vd = P.dram("v", [2, L, W])
    lwd = P.dram("lwT", [2, LORA, L])
    lad = P.dram("laT", [2, LORA, L])
    w2d = P.dram("w2", [2, LORA, W])
    a2d = P.dram("a2", [2, LORA, W])
    w0d = P.dram("w0b", [2, 128, W])
    a0d = P.dram("a0b", [2, 128, W])
    kkd = P.dram("kkb", [128, W])
    kad = P.dram("kab", [128, W])
    rkd = P.dram("rkb", [128, W])
    trid = P.dram("tri", [128, 128])
    mupd = P.dram("mup", [128, 128])
    mlod = P.dram("mlo", [128, 128])
    identd = P.dram("ident", [128, 128])
    seld = P.dram("sel", [128, 1])
    yout = P.dram("y", [2, L, W], kind="ExternalOutput")
    bout = P.dram("bon", [2, L, W], kind="ExternalOutput")

    def T2(n, dt=F32, shape=(128, W)):
        return [P.sb(list(shape), dt) for _ in range(n)]

    ident = P.sb([128, 128])
    identb = P.sb([128, 128], BF16)
    tri = P.sb([128, 128])
    mup = P.sb([128, 128])
    mupi = P.sb([128, 128])
    mlo = P.sb([128, 128])
    sel = P.sb([128, 1])
    w2 = T2(2, F32, (LORA, W))
    a2 = T2(2, F32, (LORA, W))
    w0b = T2(2)
    a0b = T2(2)
    kkb = P.sb([128, W])
    kab = P.sb([128, W])
    omka = P.sb([128, W])
    rkb = P.sb([128, W])
    rt, kt, vt = T2(2), T2(2), T2(2)
    lw = T2(2, F32, (LORA, 128))
    la = T2(2, F32, (LORA, 128))
    th_ = T2(2, F32, (LORA, 128))
    zt_, ld_, iclr_, kkr_, kk_, t1_, kdt_ = T2(2), T2(2), T2(2), T2(2), T2(2), T2(2), T2(2)
    ein_, eneg_ = T2(2), T2(2)
    st8_ = [[P.sb([128, NH_C]) for _ in range(3)] for _ in range(2)]
    Ah_, Bh_, Kh_, Rh_, Vb_ = T2(4, BF16), T2(4, BF16), T2(4, BF16), T2(4, BF16), T2(4, BF16)
    XT_ = [{nm: P.sb([HC, NH_C, 128], BF16) for nm in ("a", "b", "k", "r")} for _ in range(4)]
    gamT_ = [P.sb([HC, NH_C]) for _ in range(4)]
    IDT = BF16 if MC_BF16INV else F32
    Nf_ = [[[P.sb([128, 4, 128], IDT) for _ in range(2)] for _ in range(2)] for _ in range(2)]
    NTf_ = [[[P.sb([128, 4, 128], IDT) for _ in range(2)] for _ in range(2)] for _ in range(2)]
    TTh_ = [[P.sb([128, 4, 128], BF16) for _ in range(2)] for _ in range(2)]
    TT_ = [[P.sb([128, 4, 128]) for _ in range(2)] for _ in range(2)]
    TTb_ = [[P.sb([128, 4, 128], BF16) for _ in range(2)] for _ in range(2)]
    AakT_ = [[P.sb([128, 4, 128], BF16) for _ in range(2)] for _ in range(2)]
    AVb_ = [[P.sb([128, 4, HC], BF16) for _ in range(2)] for _ in range(2)]
    ArbT_ = [P.sb([128, NH_C, 128], BF16) for _ in range(2)]
    ArkT_ = [P.sb([128, NH_C, 128], BF16) for _ in range(2)]
    TAb_ = [P.sb([128, NH_C, HC], BF16) for _ in range(2)]
    TVb_ = [P.sb([128, NH_C, HC], BF16) for _ in range(2)]
    MTb_ = [P.sb([HC, NH_C, HC], BF16) for _ in range(2)]
    RQTb_ = [P.sb([HC, NH_C, 128], BF16) for _ in range(2)]
    Pb = [P.sb([HC, NH_C, HC], BF16) for _ in range(2)]
    ysb = T2(2, F32, (128, NH_C, HC))
    pz = P.ps([128, 512])
    ptr = [P.ps([128, 1024], BF16) for _ in range(2)]
    big = [P.ps([128, 4, 128]) for _ in range(2)]
    py = P.ps([128, NH_C, HC])
    pp = P.ps([128, NH_C, HC])
    pg = P.ps([128, 512])

    for (t_, d_, k_) in [(ident, identd, "ident"), (tri, trid, "tri"), (mup, mupd, "mup"), (mlo, mlod, "mlo"), (sel, seld, "sel"),
                         (kkb, kkd, "kkb"), (kab, kad, "kab"), (rkb, rkd, "rkb")]:
        P.dma("sp", t_[:], d_[:], writes=[k_])
    for d in range(2):
        P.dma("sp", w2[d][:], w2d[d], writes=[("w2", d)])
        P.dma("sp", a2[d][:], a2d[d], writes=[("a2", d)])
        P.dma("sp", w0b[d][:], w0d[d], writes=[("w0b", d)])
        P.dma("sp", a0b[d][:], a0d[d], writes=[("a0b", d)])
        P.op("pool", lambda e, d=d: e.memset(Pb[d][:], 0.0), writes=[("Pb", d)])
    P.op("dve", lambda e: e.tensor_copy(out=identb[:], in_=ident[:]), reads=["ident"], writes=["identb"])
    P.op("dve", lambda e: e.tensor_tensor(out=mupi[:], in0=mup[:], in1=ident[:], op=ALU.add), reads=["mup", "ident"], writes=["mupi"])
    P.op("dve", lambda e: e.tensor_scalar(out=omka[:], in0=kab[:], scalar1=-1.0, scalar2=None, op0=ALU.mult),
         reads=["kab"], writes=["omka"])
    P.op("dve", lambda e: e.tensor_scalar(out=omka[:], in0=omka[:], scalar1=1.0, scalar2=None, op0=ALU.add),
         reads=["omka"], writes=["omka"])

    bi = [0]

    def emit_dir(t, d):
        rows = slice(t * 128, (t + 1) * 128)
        b = d
        dp = d * 2 + (t % 2)
        IFACE = ("Ah", "Bh", "Kh", "Rh", "Vb", "XTa", "XTb", "XTk", "XTr", "gamT")
        K = lambda nm: (nm, dp) if nm in IFACE else (nm, d)
        th, zt, ld, iclr, kkr, kk, t1, kdt = th_[d], zt_[d], ld_[d], iclr_[d], kkr_[d], kk_[d], t1_[d], kdt_[d]
        bt = kkr
        sq = zt
        ein, eneg, st8 = ein_[d], eneg_[d], st8_[d]
        eex = zt
        Ah, Bh, Kh, Rh, Vb, XT, gamT = Ah_[dp], Bh_[dp], Kh_[dp], Rh_[dp], Vb_[dp], XT_[dp], gamT_[dp]
        ArbT, ArkT, TAb, TVb, MTb, RQTb = ArbT_[d], ArkT_[d], TAb_[d], TVb_[d], MTb_[d], RQTb_[d]
        main = []
        P._defer = main

        def sigmoid_tail(dst, key):
            P.op("act", lambda e: e.activation(out=zt[:], in_=zt[:], func=AF.Exp, scale=-1.0), reads=[K("zt")], writes=[K("zt")])
            P.op("dve", lambda e: e.tensor_scalar(out=zt[:], in0=zt[:], scalar1=1.0, scalar2=None, op0=ALU.add), reads=[K("zt")], writes=[K("zt")])
            P.op("dve", lambda e: e.reciprocal(out=dst, in_=zt[:]), reads=[K("zt")], writes=[key])

        P.dma("sp", rt[b][:], rd[d, rows, :], writes=[K("rt")])
        P.dma("sp", kt[b][:], kd[d, rows, :], writes=[K("kt")])
        P.dma("sp", vt[b][:], vd[d, rows, :], writes=[K("vt")])
        P.dma("sp", lw[b][:], lwd[d, :, rows], writes=[K("lw")])
        P.dma("sp", la[b][:], lad[d, :, rows], writes=[K("la")])
        P.op("act", lambda e: e.activation(out=th[:], in_=lw[b][:], func=AF.Tanh), reads=[K("lw")], writes=[K("th")])
        P.atomic_begin()
        P.op("pe", lambda e: e.matmul(pz[:], lhsT=th[:], rhs=w2[d][:], start=True, stop=True),
             reads=[K("th"), ("w2", d)], writes=["pz"])
        P.op("dve", lambda e: e.tensor_tensor(out=zt[:], in0=pz[:], in1=w0b[d][:], op=ALU.add), reads=["pz", ("w0b", d)], writes=[K("zt")])
        P.atomic_end()
        sigmoid_tail(ld[:], K("ld"))
        P.op("pool", lambda e: e.tensor_scalar(out=ld[:], in0=ld[:], scalar1=-float(np.exp(-0.5)), scalar2=None, op0=ALU.mult),
             reads=[K("ld")], writes=[K("ld")])
        P.atomic_begin()
        P.op("pe", lambda e: e.matmul(pz[:], lhsT=la[b][:], rhs=a2[d][:], start=True, stop=True),
             reads=[K("la"), ("a2", d)], writes=["pz"])
        P.op("dve", lambda e: e.tensor_tensor(out=zt[:], in0=pz[:], in1=a0b[d][:], op=ALU.add), reads=["pz", ("a0b", d)], writes=[K("zt")])
        P.atomic_end()
        sigmoid_tail(iclr[:], K("iclr"))
        h3 = lambda ap: ap.rearrange("p (h c) -> p h c", c=HC)
        P.op("pool", lambda e: e.tensor_tensor(out=kkr[:], in0=kt[b][:], in1=kkb[:], op=ALU.mult), reads=[K("kt"), "kkb"], writes=[K("kkr")])
        P.op("pool", lambda e: e.tensor_tensor(out=sq[:], in0=kkr[:], in1=kkr[:], op=ALU.mult), reads=[K("kkr")], writes=[K("zt")])
        P.op("dve", lambda e: e.tensor_reduce(out=st8[0][:], in_=h3(sq[:]), axis=AX.X, op=ALU.add), reads=[K("zt")], writes=[K("st0")])
        P.op("act", lambda e: e.activation(out=st8[0][:], in_=st8[0][:], func=AF.Sqrt), reads=[K("st0")], writes=[K("st0")])
        P.op("dve", lambda e: e.tensor_scalar(out=st8[0][:], in0=st8[0][:], scalar1=1e-12, scalar2=None, op0=ALU.max), reads=[K("st0")], writes=[K("st0")])
        P.op("dve", lambda e: e.reciprocal(out=st8[1][:], in_=st8[0][:]), reads=[K("st0")], writes=[K("st1")])
        P.op("dve", lambda e: e.tensor_tensor(out=h3(kk[:]), in0=h3(kkr[:]), in1=st8[1][:].unsqueeze(2).to_broadcast([128, NH_C, HC]), op=ALU.mult),
             reads=[K("kkr"), K("st1")], writes=[K("kk")])
        P.op("dve", lambda e: e.tensor_tensor(out=t1[:], in0=iclr[:], in1=kab[:], op=ALU.mult), reads=[K("iclr"), "kab"], writes=[K("t1")])
        P.op("pool", lambda e: e.tensor_tensor(out=t1[:], in0=t1[:], in1=omka[:], op=ALU.add), reads=[K("t1"), "omka"], writes=[K("t1")])
        P.op("dve", lambda e: e.tensor_tensor(out=kdt[:], in0=kt[b][:], in1=t1[:], op=ALU.mult), reads=[K("kt"), K("t1")], writes=[K("kdt")])
        P.op("pool", lambda e: e.tensor_tensor(out=bt[:], in0=kk[:], in1=iclr[:], op=ALU.mult), reads=[K("kk"), K("iclr")], writes=[K("kkr")])
        P.op("dve", lambda e: e.tensor_tensor(out=t1[:], in0=rt[b][:], in1=kdt[:], op=ALU.mult), reads=[K("rt"), K("kdt"), K("t1")], writes=[K("t1")])
        P.op("pool", lambda e: e.tensor_tensor(out=t1[:], in0=t1[:], in1=rkb[:], op=ALU.mult), reads=[K("t1"), "rkb"], writes=[K("t1")])
        P.op("dve", lambda e: e.tensor_reduce(out=st8[2][:], in_=h3(t1[:]), axis=AX.X, op=ALU.add), reads=[K("t1")], writes=[K("st2")])
        P.op("dve", lambda e: e.tensor_tensor(out=h3(t1[:]), in0=h3(vt[b][:]), in1=st8[2][:].unsqueeze(2).to_broadcast([128, NH_C, HC]), op=ALU.mult),
             reads=[K("vt"), K("st2"), K("t1")], writes=[K("t1")])
        P.dma("pool", bout[d, rows, :], t1[:], reads=[K("t1")], writes=["bout"])
        P.atomic_begin()
        P.op("pe", lambda e: e.matmul(pz[:], lhsT=tri[:], rhs=ld[:], start=True, stop=True), reads=["tri", K("ld")], writes=["pz"])
        P.op("act", lambda e: e.activation(out=ein[:], in_=pz[:], func=AF.Exp), reads=["pz"], writes=[K("ein")])
        P.op("act", lambda e: e.activation(out=eneg[:], in_=pz[:], func=AF.Exp, scale=-1.0), reads=["pz"], writes=[K("eneg")])
        P.op("dve", lambda e: e.tensor_tensor(out=eex[:], in0=pz[:], in1=ld[:], op=ALU.subtract), reads=["pz", K("ld")], writes=[K("zt")])
        P.atomic_end()
        P.op("act", lambda e: e.activation(out=eex[:], in_=eex[:], func=AF.Exp), reads=[K("zt")], writes=[K("zt")])
        P.op("dve", lambda e: e.scalar_tensor_tensor(out=Ah[:], in0=kk[:], scalar=-1.0, in1=eex[:], op0=ALU.mult, op1=ALU.mult),
             reads=[K("kk"), K("zt")], writes=[K("Ah")])
        P.op("pool", lambda e: e.tensor_tensor(out=Bh[:], in0=bt[:], in1=eneg[:], op=ALU.mult), reads=[K("kkr"), K("eneg")], writes=[K("Bh")])
        P.op("dve", lambda e: e.tensor_tensor(out=Kh[:], in0=kdt[:], in1=eneg[:], op=ALU.mult), reads=[K("kdt"), K("eneg")], writes=[K("Kh")])
        P.op("pool", lambda e: e.tensor_tensor(out=Rh[:], in0=rt[b][:], in1=ein[:], op=ALU.mult), reads=[K("rt"), K("ein")], writes=[K("Rh")])
        P.op("act", lambda e: e.copy(out=Vb[:], in_=vt[b][:]), reads=[K("vt")], writes=[K("Vb")])
        P.atomic_begin()
        for h in range(NH_C):
            P.op("pe", lambda e, h=h: e.matmul(pg[:HC, h:h + 1], lhsT=ein[:, h * HC:(h + 1) * HC], rhs=sel[:], start=True, stop=True),
                 reads=[K("ein"), "sel"], writes=["pg"])
        P.op("act", lambda e: e.copy(out=gamT[:], in_=pg[:HC, :NH_C]), reads=["pg"], writes=[K("gamT")])
        P.atomic_end()
        for xi, (nm, src_t, skey) in enumerate([("a", Ah, "Ah"), ("b", Bh, "Bh"), ("k", Kh, "Kh"), ("r", Rh, "Rh")]):
            pt = ptr[xi % 2]
            P.atomic_begin()
            for h in range(NH_C):
                P.op("pe", lambda e, h=h, pt=pt, src_t=src_t: e.transpose(out=pt[:HC, h * 128:(h + 1) * 128], in_=src_t[:, h * HC:(h + 1) * HC],
                                                                          identity=identb[:]),
                     reads=[K(skey), "identb"], writes=[("ptr", xi % 2)])
            if xi % 2 == 0:
                P.op("act", lambda e, nm=nm, pt=pt: e.copy(out=XT[nm][:].rearrange("p h t -> p (h t)"), in_=pt[:HC, :]),
                     reads=[("ptr", xi % 2)], writes=[K("XT" + nm)])
                P.atomic_end()
            else:
                P.op("dve", lambda e, nm=nm, pt=pt: e.tensor_copy(out=XT[nm][:].rearrange("p h t -> p (h t)"), in_=pt[:HC, :]),
                     reads=[("ptr", xi % 2)], writes=[K("XT" + nm)])
                P.atomic_end()
        qlists = []

        def emit_quad(qd_):
            ql = []
            P._defer = ql
            qlists.append(ql)
            Q = lambda nm, qd_=qd_: (nm, d, qd_)
            hs = [qd_ * 4 + i for i in range(4)]
            Nf, NTf, TT, TTb, AakT, AVb = Nf_[d][qd_], NTf_[d][qd_], TT_[d][qd_], TTb_[d][qd_], AakT_[d][qd_], AVb_[d][qd_]
            TTh = TTh_[d][qd_]

            def mm4(lhs_fn, rhs_fn, rows_, cols_, rkeys, hs=hs):
                p = bi[0] % 2
                bi[0] += 1
                P.atomic_begin()
                for i, h in enumerate(hs):
                    P.op("pe", lambda e, i=i, h=h, p=p: e.matmul(big[p][:rows_, i, :cols_], lhsT=lhs_fn(i, h), rhs=rhs_fn(i, h),
                                                               start=True, stop=True), reads=rkeys, writes=[("big", p)])
                return p

            def EV(*a_, **k_):
                P.op(*a_, **k_)
                P.atomic_end()

            msk = lambda m_: m_[:, None, :].to_broadcast([128, 4, 128])
            p = mm4(lambda i, h: XT["a"][:, h, :], lambda i, h: XT["b"][:, h, :], 128, 128, [K("XTa"), K("XTb")])
            EV("dve", lambda e, p=p: e.tensor_tensor(out=R32(Nf[0][:]), in0=big[p][:], in1=msk(mlo), op=ALU.mult),
                 reads=[("big", p), "mlo"], writes=[Q("Nf0")])
            p = mm4(lambda i, h: XT["b"][:, h, :], lambda i, h: XT["a"][:, h, :], 128, 128, [K("XTa"), K("XTb")])
            EV("dve", lambda e, p=p: e.tensor_tensor(out=R32(NTf[0][:]), in0=big[p][:], in1=msk(mup), op=ALU.mult),
                 reads=[("big", p), "mup"], writes=[Q("NTf0")])
            P.op("dve", lambda e: e.tensor_tensor(out=R32(TT[:]), in0=NTf[0][:], in1=msk(ident), op=ALU.add),
                 reads=[Q("NTf0"), "ident"], writes=[Q("TT")])
            if MC_BF16INV:
                P.op("act", lambda e: e.copy(out=TTh[:], in_=TT[:]), reads=[Q("TT")], writes=[Q("TTh")])
            p = mm4(lambda i, h: XT["k"][:, h, :], lambda i, h: XT["a"][:, h, :], 128, 128, [K("XTa"), K("XTk")])
            EV("dve", lambda e, p=p: e.tensor_tensor(out=AakT[:], in0=big[p][:], in1=msk(mup), op=ALU.mult),
                 reads=[("big", p), "mup"], writes=[Q("AakT")])
            p = mm4(lambda i, h: XT["b"][:, h, :], lambda i, h: XT["r"][:, h, :], 128, 128, [K("XTr"), K("XTb")])
            EV("dve", lambda e, p=p, qd_=qd_: e.tensor_tensor(out=ArbT[:, qd_ * 4:(qd_ + 1) * 4, :], in0=big[p][:], in1=msk(mupi), op=ALU.mult),
                 reads=[("big", p), "mupi"], writes=[Q("ArbT")])
            p = mm4(lambda i, h: XT["k"][:, h, :], lambda i, h: XT["r"][:, h, :], 128, 128, [K("XTr"), K("XTk")])
            EV("dve", lambda e, p=p, qd_=qd_: e.tensor_tensor(out=ArkT[:, qd_ * 4:(qd_ + 1) * 4, :], in0=big[p][:], in1=msk(mupi), op=ALU.mult),
                 reads=[("big", p), "mupi"], writes=[Q("ArkT")])
            cur = 0
            for lvl in range(1, 7):
                nxt = 1 - cur
                p = mm4(lambda i, h, cur=cur: R32(NTf[cur][:, i, :]), lambda i, h, cur=cur: R32(Nf[cur][:, i, :]), 128, 128, [Q("Nf%d" % cur), Q("NTf%d" % cur)])
                EV("act", lambda e, p=p, nxt=nxt: e.copy(out=R32(Nf[nxt][:]), in_=big[p][:]), reads=[("big", p)], writes=[Q("Nf%d" % nxt)])
                if lvl < 6:
                    p = mm4(lambda i, h, cur=cur: R32(Nf[cur][:, i, :]), lambda i, h, cur=cur: R32(NTf[cur][:, i, :]), 128, 128, [Q("Nf%d" % cur), Q("NTf%d" % cur)])
                    EV("act", lambda e, p=p, nxt=nxt: e.copy(out=R32(NTf[nxt][:]), in_=big[p][:]), reads=[("big", p)], writes=[Q("NTf%d" % nxt)])
                if MC_BF16INV:
                    p = mm4(lambda i, h, nxt=nxt: Nf[nxt][:, i, :], lambda i, h: TTh[:, i, :], 128, 128, [Q("Nf%d" % nxt), Q("TTh")])
                else:
                    p = mm4(lambda i, h, nxt=nxt: R32(Nf[nxt][:, i, :]), lambda i, h: R32(TT[:, i, :]), 128, 128, [Q("Nf%d" % nxt), Q("TT")])
                EV("dve", lambda e, p=p: e.tensor_tensor(out=R32(TT[:]), in0=big[p][:], in1=TT[:], op=ALU.add), reads=[("big", p), Q("TT")], writes=[Q("TT")])
                if MC_BF16INV and lvl < 6:
                    P.op("act", lambda e: e.copy(out=TTh[:], in_=TT[:]), reads=[Q("TT")], writes=[Q("TTh")])
                cur = nxt
            P.op("act", lambda e: e.copy(out=TTb[:], in_=TT[:]), reads=[Q("TT")], writes=[Q("TTb")])
            qs = slice(qd_ * 4, (qd_ + 1) * 4)
            p = mm4(lambda i, h: TTb[:, i, :], lambda i, h: Ah[:, h * HC:(h + 1) * HC], 128, HC, [Q("TTb"), K("Ah")])
            EV("act", lambda e, p=p, qs=qs: e.copy(out=TAb[:, qs, :], in_=big[p][:, :, :HC]), reads=[("big", p)], writes=[Q("TAb")])
            p = mm4(lambda i, h: AakT[:, i, :], lambda i, h: Vb[:, h * HC:(h + 1) * HC], 128, HC, [Q("AakT"), K("Vb")])
            EV("dve", lambda e, p=p: e.tensor_copy(out=AVb[:], in_=big[p][:, :, :HC]), reads=[("big", p)], writes=[Q("AVb")])
            p = mm4(lambda i, h: TTb[:, i, :], lambda i, h: AVb[:, i, :], 128, HC, [Q("TTb"), Q("AVb")])
            EV("act", lambda e, p=p, qs=qs: e.copy(out=TVb[:, qs, :], in_=big[p][:, :, :HC]), reads=[("big", p)], writes=[Q("TVb")])
            p = mm4(lambda i, h: TAb[:, h, :], lambda i, h: Bh[:, h * HC:(h + 1) * HC], HC, HC, [Q("TAb"), K("Bh")])
            EV("dve", lambda e, p=p, qs=qs: e.tensor_tensor(out=MTb[:, qs, :], in0=big[p][:HC, :, :HC],
                                                            in1=ident[:HC, None, :HC].to_broadcast([HC, 4, HC]), op=ALU.add),
                 reads=[("big", p), "ident"], writes=[Q("MTb")])
            p = mm4(lambda i, h: TAb[:, h, :], lambda i, h: ArbT[:, h, :], HC, 128, [Q("TAb"), Q("ArbT")])
            EV("dve", lambda e, p=p, qs=qs: e.tensor_tensor(out=RQTb[:, qs, :], in0=big[p][:HC, :, :], in1=XT["r"][:, qs, :], op=ALU.add),
                 reads=[("big", p), K("XTr")], writes=[Q("RQTb")])
        for qd_ in range(2):
            emit_quad(qd_)
        tail = []
        P._defer = tail
        allq = lambda nm: [(nm, d, 0), (nm, d, 1)]
        P.atomic_begin()
        for h in range(NH_C):
            hc = slice(h * HC, (h + 1) * HC)
            P.op("pe", lambda e, h=h: e.matmul(py[:, h, :], lhsT=ArbT[:, h, :], rhs=TVb[:, h, :], start=True, stop=False),
                 reads=allq("ArbT") + allq("TVb") + allq("ArkT") + allq("RQTb") + [K("Vb"), ("Pb", d)], writes=["py"])
            P.op("pe", lambda e, h=h, hc=hc: e.matmul(py[:, h, :], lhsT=ArkT[:, h, :], rhs=Vb[:, hc], start=False, stop=False),
                 reads=[], writes=["py"])
            P.op("pe", lambda e, h=h: e.matmul(py[:, h, :], lhsT=RQTb[:, h, :], rhs=Pb[d][:, h, :], start=False, stop=True),
                 reads=[("Pb", d)], writes=["py"])
        P.op("act", lambda e: e.copy(out=ysb[b][:], in_=py[:]), reads=["py"], writes=[K("ysb")])
        P.atomic_end()
        P.dma("pool", yout[d, rows, :], ysb[b][:].rearrange("p h c -> p (h c)"), reads=[K("ysb")], writes=["yout"])
        P.atomic_begin()
        for h in range(NH_C):
            hc = slice(h * HC, (h + 1) * HC)
            P.op("pe", lambda e, h=h, hc=hc: e.matmul(pp[:HC, h, :], lhsT=Bh[:, hc], rhs=TVb[:, h, :], start=True, stop=False),
                 reads=allq("TVb") + allq("MTb") + [K("Bh"), K("Kh"), K("Vb"), ("Pb", d)], writes=["pp"])
            P.op("pe", lambda e, h=h, hc=hc: e.matmul(pp[:HC, h, :], lhsT=Kh[:, hc], rhs=Vb[:, hc], start=False, stop=False),
                 reads=[], writes=["pp"])
            P.op("pe", lambda e, h=h: e.matmul(pp[:HC, h, :], lhsT=MTb[:, h, :], rhs=Pb[d][:, h, :], start=False, stop=True),
                 reads=[("Pb", d)], writes=["pp"])
        P.op("dve", lambda e: e.tensor_tensor(out=Pb[d][:], in0=pp[:HC, :, :], in1=gamT[:].unsqueeze(2).to_broadcast([HC, NH_C, HC]),
                                              op=ALU.mult), reads=["pp", K("gamT")], writes=[("Pb", d)])
        P.atomic_end()
        P._defer = None
        return main, qlists, tail

    for t in range(ntile):
        if t == 0:
            nxt_parts = [emit_dir(0, d) for d in range(2)]
            for d in range(2):
                P.interleave([nxt_parts[d][0]])
        parts = nxt_parts
        streams = parts[0][1] + parts[1][1]
        if t + 1 < ntile:
            nxt_parts = [emit_dir(t + 1, d) for d in range(2)]
            if MC_PIPE:
                streams = streams + [nxt_parts[0][0] + nxt_parts[1][0]]
        P.interleave(streams)
        if t + 1 < ntile and not MC_PIPE:
            for d in range(2):
                P.interleave([nxt_parts[d][0]])
        for d in range(2):
            P.interleave([parts[d][2]])
    P.finish(["yout", "bout"])
    return nc


def run_Mc(pl, pc, prm):
    W = NH_C * HC
    nc = build_Mc(L_SEQ)
    tt, ss = np.meshgrid(np.arange(128), np.arange(128), indexing="xy")
    consts = {"tri": (ss <= tt).astype(np.float32), "mup": (ss < tt).astype(np.float32), "mlo": (ss > tt).astype(np.float32),
              "ident": np.eye(128, dtype=np.float32), "sel": (np.arange(128) == 127).astype(np.float32)[:, None]}
    rk_flat = prm["r_k"].reshape(-1)
    in_maps = []
    for cid in range(NCORES):
        b, hg = cid // 4, cid % 4
        cols = slice(hg * W, (hg + 1) * W)
        m = dict(consts)
        for nm, off in (("r", 0), ("k", D), ("v", 2 * D)):
            m[nm] = np.stack([_seq(pl[b][:, off + hg * W:off + (hg + 1) * W], pc[b][:, off + hg * W:off + (hg + 1) * W], dr == 1)
                              for dr in range(2)])
        m["lwT"] = np.stack([np.ascontiguousarray(_seq(pl[b][:, 4 * D + dr * 128:4 * D + dr * 128 + LORA],
                                                       pc[b][:, 4 * D + dr * 128:4 * D + dr * 128 + LORA], dr == 1).T) for dr in range(2)])
        m["laT"] = np.stack([np.ascontiguousarray(_seq(pl[b][:, 4 * D + 256 + dr * 128:4 * D + 256 + dr * 128 + LORA],
                                                       pc[b][:, 4 * D + 256 + dr * 128:4 * D + 256 + dr * 128 + LORA], dr == 1).T) for dr in range(2)])
        m["w2"] = np.ascontiguousarray(prm["w2"][:, :, cols])
        m["a2"] = np.ascontiguousarray(prm["a2"][:, :, cols])
        m["w0b"] = np.stack([_bc(prm["w0"][dr, cols]) for dr in range(2)])
        m["a0b"] = np.stack([_bc(prm["a0"][dr, cols]) for dr in range(2)])
        m["kkb"] = _bc(prm["k_k"][cols])
        m["kab"] = _bc(prm["k_a"][cols])
        m["rkb"] = _bc(rk_flat[cols])
        in_maps.append(m)
    res = _run(nc, in_maps)
    names = ["m0", "m1", "m2", "m3"]
    lat = {nm: np.empty((2, T_LAT, D), np.float32) for nm in names}
    cx = {nm: np.empty((2, T_CTX, D), np.float32) for nm in names}
    for cid in range(NCORES):
        b, hg = cid // 4, cid % 4
        cols = slice(hg * W, (hg + 1) * W)
        for dr in range(2):
            for key, nm in (("y", "m%d" % dr), ("bon", "m%d" % (2 + dr))):
                l, c = _unseq(res[cid][key][dr], dr == 1)
                lat[nm][b][:, cols] = l
                cx[nm][b][:, cols] = c
    return lat, cx


def layer_c(x, ctx, mod_i, norm_w_i, prm, need_ctx, final_w=None):
    pad = lambda w: np.concatenate([w, np.zeros((D, 128 - w.shape[1]), np.float32)], axis=1)
    wcat = np.concatenate([prm["w_in"][0], prm["w_in"][1], prm["w_in"][2], prm["w_in"][3],
                           pad(prm["w1"][0]), pad(prm["w1"][1]), pad(prm["a1"][0]), pad(prm["a1"][1])], axis=1)
    lerp = [0] * 16 + [1] * 16 + [2] * 16 + [3] * 16 + [4, 4, 5, 5]
    pl, pc = run_P(x, ctx, mod_i, norm_w_i, wcat, lerp=lerp, mu=prm["mu"])
    lat, cx = run_Mc(pl, pc, prm)
    lat["gate"] = pl[:, :, 3 * D:4 * D]
    cx["gate"] = pc[:, :, 3 * D:4 * D]
    bcs = {"lnw": prm["ln_w"], "lnb": prm["ln_b"]}
    if final_w is not None:
        bcs["fnw"] = final_w
    return run_O(x, ctx, "c", lat, cx, prm["w_out"], mod_i[0:2, 2 * D:3 * D], mod_i[2, 2 * D:3 * D], bcs,
                 final=final_w is not None, need_ctx=need_ctx)


def kernel(x, c, ctx, c_ctx, norm_w, mod_w, mod_b, a_w_in, a_lb_raw, a_onorm_w, a_w_out,
           b_w_in, b_sink, b_w_out, c_mu, c_w_in, c_w0, c_w1, c_w2, c_a0, c_a1, c_a2,
           c_k_k, c_k_a, c_r_k, c_ln_w, c_ln_b, c_w_out, final_norm_w):
    f = lambda a: np.asarray(a, dtype=np.float32)
    x, c, ctx, c_ctx = f(x), f(c), f(ctx), f(c_ctx)
    mod = run_mod(c, f(c_ctx), f(mod_w), f(mod_b))
    depth = 4
    for i in range(depth):
        j, kind = i // 3, i % 3
        need_ctx = i < depth - 1
        fw = f(final_norm_w) if i == depth - 1 else None
        mod_i = np.ascontiguousarray(mod[:, i])
        if kind == 0:
            x, ctx = layer_a(x, ctx, mod_i, f(norm_w[i]), f(a_w_in[j]), f(a_lb_raw), j, f(a_onorm_w[j]), f(a_w_out[j]), need_ctx, fw)
        elif kind == 1:
            x, ctx_n = layer_b(x, ctx, mod_i, f(norm_w[i]), f(b_w_in[j]), f(b_sink[j]), f(b_w_out[j]), need_ctx, fw)
            ctx = ctx_n if need_ctx else ctx
        else:
            prm = {"mu": f(c_mu[j]), "w_in": f(c_w_in[j]), "w0": f(c_w0[j]), "w1": f(c_w1[j]), "w2": f(c_w2[j]),
                   "a0": f(c_a0[j]), "a1": f(c_a1[j]), "a2": f(c_a2[j]), "k_k": f(c_k_k[j]), "k_a": f(c_k_a[j]),
                   "r_k": f(c_r_k[j]), "ln_w": f(c_ln_w[j]), "ln_b": f(c_ln_b[j]), "w_out": f(c_w_out[j])}
            x, ctx_n = layer_c(x, ctx, mod_i, f(norm_w[i]), prm, need_ctx, fw)
            ctx = ctx_n if need_ctx else ctx
    return x.astype(np.float32)
```
